# Optimizing a Trainium2 kernel written in Bass

```python
import jax, jax.numpy as jnp
from jax import lax
import numpy as np

D_MODEL = 1024
BATCH = 16
SEQ = 256
DEPTH = 4
DEC_BATCH = 8
DEC_SEQ = 2048
PAST_LEN = 256

GRID_W = 64
N_EVEN = (DEPTH + 1) // 2
N_ODD = DEPTH // 2
A_HEADS = 8
A_KV_HEADS = 2
A_GROUP = A_HEADS // A_KV_HEADS
HEAD_DIM = 64
WINDOW = 128
BLOCK = 128
B_HEADS = 8
Q_LORA = 192
KV_LORA = 128
QK_NOPE = 64
QK_ROPE = 32
V_DIM = 64
MLA_SCALE = (QK_NOPE + QK_ROPE) ** -0.5
A_Q_W = A_HEADS * HEAD_DIM
A_KV_W = A_KV_HEADS * HEAD_DIM
ATTN_IN_SIZES = (A_Q_W, A_KV_W, A_KV_W, Q_LORA, KV_LORA, QK_ROPE)
ATTN_IN_W = sum(ATTN_IN_SIZES)
ATTN_OUT_W = A_HEADS * HEAD_DIM + B_HEADS * V_DIM
CONV_CH = D_MODEL // 2
CONV_WIDTH = 31
POOL_CH = D_MODEL // 2
POOL_SIZES = (2, 4, 8, 16)
N_POOL_GROUPS = len(POOL_SIZES)
POOL_GROUP_W = POOL_CH // N_POOL_GROUPS
CONV_IN_SIZES = (CONV_CH, CONV_CH, POOL_CH)
CONV_IN_W = sum(CONV_IN_SIZES)
CONV_OUT_W = CONV_CH + POOL_CH
D_FF = 4 * D_MODEL
ROPE_BASE = 10000.0
EPS = 1e-6
NEG_INF = -1e30

kernel_name = "hybrid_diffusion_prefix_trunk_step"


def _split(x, sizes):
    offs = [int(v) for v in np.cumsum(sizes)[:-1]]
    return jnp.split(x, offs, axis=-1)


def rms_norm(x, g):
    xf = x.astype(jnp.float32)
    y = xf * lax.rsqrt(jnp.mean(xf * xf, axis=-1, keepdims=True) + EPS)
    return (y * g.astype(jnp.float32)).astype(x.dtype)


def layer_norm(x, g, b):
    xf = x.astype(jnp.float32)
    mu = jnp.mean(xf, axis=-1, keepdims=True)
    var = jnp.mean(jnp.square(xf - mu), axis=-1, keepdims=True)
    y = (xf - mu) * lax.rsqrt(var + EPS)
    return (y * g.astype(jnp.float32) + b.astype(jnp.float32)).astype(x.dtype)


def adaln(cond, w, b):
    m = jax.nn.silu(cond) @ w + b
    return [t[:, None, :] for t in jnp.split(m, 6, axis=-1)]


def modulate(h, shift, scale):
    return h * (1.0 + scale) + shift


def axial_rope(n, dim):
    rows = n // GRID_W
    row = jnp.repeat(jnp.arange(rows), GRID_W).astype(jnp.float32)
    col = jnp.tile(jnp.arange(GRID_W), rows).astype(jnp.float32)
    quarter = dim // 4
    inv_freq = ROPE_BASE ** (-jnp.arange(quarter, dtype=jnp.float32) / quarter)
    ang = jnp.concatenate([row[:, None] * inv_freq, col[:, None] * inv_freq], axis=-1)
    return jnp.cos(ang), jnp.sin(ang)


def apply_rope(x, cos, sin):
    half = x.shape[-1] // 2
    xf = x.astype(jnp.float32)
    x1, x2 = xf[..., :half], xf[..., half:]
    cs, sn = cos[:, None, :], sin[:, None, :]
    return jnp.concatenate([x1 * cs - x2 * sn, x1 * sn + x2 * cs], axis=-1).astype(x.dtype)


def _softmax_with_sink(s, sink):
    if sink is None:
        return jax.nn.softmax(s, axis=-1)
    col = jnp.broadcast_to(sink.astype(jnp.float32), s.shape[:-1] + (1,))
    return jax.nn.softmax(jnp.concatenate([s, col], axis=-1), axis=-1)[..., :-1]


def dense_attention(q, k, v, scale, sink=None):
    b, n, kh, g, dq = q.shape
    nb = n // BLOCK
    qb = jnp.moveaxis(q.reshape(b, nb, BLOCK, kh, g, dq), 1, 0)
    sink_b = None if sink is None else sink[None, :, :, None, None]

    def one(qi):
        s = jnp.einsum('bqhgd,bkhd->bhgqk', qi, k, preferred_element_type=jnp.float32) * scale
        p = _softmax_with_sink(s, sink_b)
        return jnp.einsum('bhgqk,bkhd->bqhgd', p.astype(v.dtype), v)

    o = lax.map(one, qb)
    return jnp.moveaxis(o, 0, 1).reshape(b, n, kh, g, v.shape[-1])


def window_attention(q, k, v, k_ctx, v_ctx, sink):
    b, n, kh, g, d = q.shape
    nb = n // BLOCK
    scale = d ** -0.5
    qb = q.reshape(b, nb, BLOCK, kh, g, d)
    pad = ((0, 0), (BLOCK, BLOCK), (0, 0), (0, 0))
    kp, vp = jnp.pad(k, pad), jnp.pad(v, pad)
    idx = (jnp.arange(nb) * BLOCK)[:, None] + jnp.arange(3 * BLOCK)[None, :]
    kb, vb = kp[:, idx], vp[:, idx]
    qpos = (jnp.arange(nb) * BLOCK)[:, None] + jnp.arange(BLOCK)[None, :]
    kpos = idx - BLOCK
    valid = ((jnp.abs(qpos[:, :, None] - kpos[:, None, :]) <= WINDOW)
             & (kpos[:, None, :] >= 0) & (kpos[:, None, :] < n))
    s_loc = jnp.einsum('bnqhgd,bnkhd->bnhgqk', qb, kb, preferred_element_type=jnp.float32) * scale
    s_loc = jnp.where(valid[None, :, None, None], s_loc, NEG_INF)
    s_ctx = jnp.einsum('bnqhgd,bkhd->bnhgqk', qb, k_ctx, preferred_element_type=jnp.float32) * scale
    n_loc = 3 * BLOCK
    p = _softmax_with_sink(jnp.concatenate([s_loc, s_ctx], axis=-1), sink[None, None, :, :, None, None])
    p_loc = p[..., :n_loc].astype(v.dtype)
    p_ctx = p[..., n_loc:].astype(v.dtype)
    o = (jnp.einsum('bnhgqk,bnkhd->bnqhgd', p_loc, vb)
         + jnp.einsum('bnhgqk,bkhd->bnqhgd', p_ctx, v_ctx))
    return o.reshape(b, n, kh, g, d)


def _attn_project(h, w_in, q_norm, kv_norm, w_qb):
    b, n, _ = h.shape
    qa, ka, va, cq, ckv, kr = _split(h @ w_in, ATTN_IN_SIZES)
    qa = qa.reshape(b, n, A_HEADS, HEAD_DIM)
    ka = ka.reshape(b, n, A_KV_HEADS, HEAD_DIM)
    va = va.reshape(b, n, A_KV_HEADS, HEAD_DIM)
    q_mla = (rms_norm(cq, q_norm) @ w_qb).reshape(b, n, B_HEADS, QK_NOPE + QK_ROPE)
    ckv = rms_norm(ckv, kv_norm)
    return qa, ka, va, q_mla, ckv, kr


def _mla_keys(ckv, kr, w_kvb):
    b, m, _ = ckv.shape
    kv = (ckv @ w_kvb).reshape(b, m, B_HEADS, QK_NOPE + V_DIM)
    k = jnp.concatenate([kv[..., :QK_NOPE],
                         jnp.broadcast_to(kr[:, :, None, :], (b, m, B_HEADS, QK_ROPE))], axis=-1)
    return k, kv[..., QK_NOPE:]


def attn_mixer_context(h, w_in, sink, q_norm, kv_norm, w_qb, w_kvb, w_out):
    b, n, _ = h.shape
    qa, ka, va, q_mla, ckv, kr = _attn_project(h, w_in, q_norm, kv_norm, w_qb)
    oa = dense_attention(qa.reshape(b, n, A_KV_HEADS, A_GROUP, HEAD_DIM), ka, va, HEAD_DIM ** -0.5,
                         sink.reshape(A_KV_HEADS, A_GROUP))
    k_m, v_m = _mla_keys(ckv, kr, w_kvb)
    ob = dense_attention(q_mla[:, :, :, None, :], k_m, v_m, MLA_SCALE)
    out = jnp.concatenate([oa.reshape(b, n, -1), ob.reshape(b, n, -1)], axis=-1) @ w_out
    return out, ka, va, ckv, kr


def attn_mixer_latent(h, ck, cv, cckv, ckr, w_in, sink, q_norm, kv_norm, w_qb, w_kvb, w_out):
    b, n, _ = h.shape
    qa, ka, va, q_mla, ckv, kr = _attn_project(h, w_in, q_norm, kv_norm, w_qb)
    cos_a, sin_a = axial_rope(n, HEAD_DIM)
    qa = apply_rope(qa, cos_a, sin_a)
    ka = apply_rope(ka, cos_a, sin_a)
    oa = window_attention(qa.reshape(b, n, A_KV_HEADS, A_GROUP, HEAD_DIM), ka, va, ck, cv,
                          sink.reshape(A_KV_HEADS, A_GROUP))
    cos_b, sin_b = axial_rope(n, QK_ROPE)
    q_mla = jnp.concatenate([q_mla[..., :QK_NOPE], apply_rope(q_mla[..., QK_NOPE:], cos_b, sin_b)], axis=-1)
    kr = apply_rope(kr[:, :, None, :], cos_b, sin_b)[:, :, 0, :]
    k_lat, v_lat = _mla_keys(ckv, kr, w_kvb)
    k_ctx, v_ctx = _mla_keys(cckv, ckr, w_kvb)
    ob = dense_attention(q_mla[:, :, :, None, :], jnp.concatenate([k_lat, k_ctx], axis=1),
                         jnp.concatenate([v_lat, v_ctx], axis=1), MLA_SCALE)
    return jnp.concatenate([oa.reshape(b, n, -1), ob.reshape(b, n, -1)], axis=-1) @ w_out


def multiscale_pool(z, w_grp, scale):
    b, n, _ = z.shape
    zf = z.astype(jnp.float32)
    cs = jnp.concatenate([jnp.zeros((b, 1, POOL_CH), jnp.float32), jnp.cumsum(zf, axis=1)], axis=1)
    t = jnp.arange(n)
    outs = []
    for gi, w in enumerate(POOL_SIZES):
        lo = w // 2
        hi = w - lo - 1
        start = jnp.clip(t - lo, 0, n)
        end = jnp.clip(t + hi + 1, 0, n)
        sl = slice(gi * POOL_GROUP_W, (gi + 1) * POOL_GROUP_W)
        csg = cs[:, :, sl]
        mean = (csg[:, end] - csg[:, start]) / (end - start).astype(jnp.float32)[None, :, None]
        outs.append(mean - zf[:, :, sl])
    d = jnp.stack(outs, axis=2).astype(z.dtype)
    y = jnp.einsum('bngc,gcd->bngd', d, w_grp).reshape(b, n, POOL_CH)
    return y * scale


def conv_pool_mixer(h, w_in, w_dw, b_dw, ln_g, ln_b, w_grp, p_scale, w_out):
    a, gate, z = _split(h @ w_in, CONV_IN_SIZES)
    u = a * jax.nn.sigmoid(gate)
    u = lax.conv_general_dilated(u, w_dw[:, None, :], window_strides=(1,),
                                 padding=[(CONV_WIDTH // 2, CONV_WIDTH // 2)],
                                 dimension_numbers=('NWC', 'WIO', 'NWC'),
                                 feature_group_count=CONV_CH) + b_dw
    u = jax.nn.silu(layer_norm(u, ln_g, ln_b))
    pz = multiscale_pool(z, w_grp, p_scale)
    return jnp.concatenate([u, pz], axis=-1) @ w_out


def sq_relu_mlp(h, w1, w2):
    return jnp.square(jax.nn.relu(h @ w1)) @ w2


def setup_inputs(seed: int = 0) -> dict:
    key = jax.random.key(seed)
    keys = iter(jax.random.split(key, 40))

    def nrm(shape, s):
        return jax.random.normal(next(keys), shape, jnp.float32) * s

    def gain(shape):
        return 1.0 + nrm(shape, 0.05)

    return {
        'x_prompt': nrm((BATCH, SEQ, D_MODEL), 1.0),
        'x_sample': nrm((DEC_BATCH, DEC_SEQ, D_MODEL), 1.0),
        'cache_win_k': nrm((DEC_BATCH, N_EVEN, PAST_LEN, A_KV_HEADS, HEAD_DIM), 1.0),
        'cache_win_v': nrm((DEC_BATCH, N_EVEN, PAST_LEN, A_KV_HEADS, HEAD_DIM), 1.0),
        'cache_mla_ckv': nrm((DEC_BATCH, N_EVEN, PAST_LEN, KV_LORA), 1.0),
        'cache_mla_krope': nrm((DEC_BATCH, N_EVEN, PAST_LEN, QK_ROPE), 1.0),
        'c': nrm((DEC_BATCH, D_MODEL), 1.0),
        'c_ctx': nrm((D_MODEL,), 1.0),
        'w_mod': nrm((DEPTH, D_MODEL, 6 * D_MODEL), 0.5 * D_MODEL ** -0.5),
        'b_mod': nrm((DEPTH, 6 * D_MODEL), 0.02),
        'norm_g': gain((DEPTH, 2, D_MODEL)),
        'attn_w_in': nrm((N_EVEN, D_MODEL, ATTN_IN_W), D_MODEL ** -0.5),
        'attn_sink': nrm((N_EVEN, A_HEADS), 0.5),
        'mla_q_norm': gain((N_EVEN, Q_LORA)),
        'mla_kv_norm': gain((N_EVEN, KV_LORA)),
        'mla_w_qb': nrm((N_EVEN, Q_LORA, B_HEADS * (QK_NOPE + QK_ROPE)), Q_LORA ** -0.5),
        'mla_w_kvb': nrm((N_EVEN, KV_LORA, B_HEADS * (QK_NOPE + V_DIM)), KV_LORA ** -0.5),
        'attn_w_out': nrm((N_EVEN, ATTN_OUT_W, D_MODEL), ATTN_OUT_W ** -0.5),
        'conv_w_in': nrm((N_ODD, D_MODEL, CONV_IN_W), D_MODEL ** -0.5),
        'conv_dw': nrm((N_ODD, CONV_WIDTH, CONV_CH), CONV_WIDTH ** -0.5),
        'conv_dw_b': nrm((N_ODD, CONV_CH), 0.02),
        'conv_ln_g': gain((N_ODD, CONV_CH)),
        'conv_ln_b': nrm((N_ODD, CONV_CH), 0.02),
        'pool_w': nrm((N_ODD, N_POOL_GROUPS, POOL_GROUP_W, POOL_GROUP_W), POOL_GROUP_W ** -0.5),
        'pool_scale': 0.5 + nrm((N_ODD, POOL_CH), 0.05),
        'conv_w_out': nrm((N_ODD, CONV_OUT_W, D_MODEL), CONV_OUT_W ** -0.5),
        'mlp_w1': nrm((DEPTH, D_MODEL, D_FF), D_MODEL ** -0.5),
        'mlp_w2': nrm((DEPTH, D_FF, D_MODEL), D_FF ** -0.5),
        'final_g': gain((D_MODEL,)),
    }


def reference(x_prompt, x_sample, cache_win_k, cache_win_v, cache_mla_ckv, cache_mla_krope, c, c_ctx,
              w_mod, b_mod, norm_g, attn_w_in, attn_sink, mla_q_norm, mla_kv_norm, mla_w_qb, mla_w_kvb,
              attn_w_out, conv_w_in, conv_dw, conv_dw_b, conv_ln_g, conv_ln_b, pool_w, pool_scale,
              conv_w_out, mlp_w1, mlp_w2, final_g):
    xp, xs = x_prompt, x_sample
    ks, vs, ckvs, krs = [], [], [], []
    for l in range(DEPTH):
        mp = adaln(c_ctx[None, :], w_mod[l], b_mod[l])
        ms = adaln(c, w_mod[l], b_mod[l])
        hp = modulate(rms_norm(xp, norm_g[l, 0]), mp[0], mp[1])
        hs = modulate(rms_norm(xs, norm_g[l, 0]), ms[0], ms[1])
        if l % 2 == 0:
            i = l // 2
            ap = (attn_w_in[i], attn_sink[i], mla_q_norm[i], mla_kv_norm[i], mla_w_qb[i], mla_w_kvb[i],
                  attn_w_out[i])
            o_p, k_i, v_i, ckv_i, kr_i = attn_mixer_context(hp, *ap)
            o_s = attn_mixer_latent(hs, cache_win_k[:, i], cache_win_v[:, i], cache_mla_ckv[:, i],
                                    cache_mla_krope[:, i], *ap)
            ks.append(k_i)
            vs.append(v_i)
            ckvs.append(ckv_i)
            krs.append(kr_i)
        else:
            j = l // 2
            cp = (conv_w_in[j], conv_dw[j], conv_dw_b[j], conv_ln_g[j], conv_ln_b[j], pool_w[j],
                  pool_scale[j], conv_w_out[j])
            o_p = conv_pool_mixer(hp, *cp)
            o_s = conv_pool_mixer(hs, *cp)
        xp = xp + mp[2] * o_p
        xs = xs + ms[2] * o_s
        hp = modulate(rms_norm(xp, norm_g[l, 1]), mp[3], mp[4])
        hs = modulate(rms_norm(xs, norm_g[l, 1]), ms[3], ms[4])
        xp = xp + mp[5] * sq_relu_mlp(hp, mlp_w1[l], mlp_w2[l])
        xs = xs + ms[5] * sq_relu_mlp(hs, mlp_w1[l], mlp_w2[l])
    y_prompt = rms_norm(xp, final_g)
    y_sample = rms_norm(xs, final_g)
    new_win_k = jnp.stack(ks, axis=1)
    new_win_v = jnp.stack(vs, axis=1)
    new_mla_ckv = jnp.stack(ckvs, axis=1)
    new_mla_krope = jnp.stack(krs, axis=1)
    return (y_prompt, y_sample, new_win_k, new_win_v, new_mla_ckv, new_mla_krope)
```

```python
import contextlib
import os
import numpy as np
import concourse.bass as bass
import concourse.mybir as mybir
from concourse.bass_utils import run_bass_kernel_spmd

F32 = mybir.dt.float32
BF16 = mybir.dt.bfloat16
AF = mybir.ActivationFunctionType
ALU = mybir.AluOpType

D = 1024
DEPTH = 4
NTOK = 2560
NT = 5
TS = 512
NCTX = 256
EPS = 1e-6
NEG = -30000.0
SHIFT = 10.0
SLOT = 4096
NRING = 4
SEM_LIMIT = 30000

DEBUG_STOP = int(os.environ.get("KDEBUG_STOP", "-1"))


class Res:
    __slots__ = ("name", "w", "rd")

    def __init__(self, name):
        self.name = name
        self.w = None
        self.rd = {}


class Eng:
    def __init__(self, trk, name, handle, inc):
        self.trk = trk
        self.name = name
        self.h = handle
        self.inc = inc
        self.sem = None
        self.count = 0
        self.seen = {}
        self.sems = []

    def new_sem(self):
        self.sem = self.trk.alloc_sem(self.name)
        self.sems.append(self.sem)
        self.count = 0


class Tracker:
    def __init__(self, nc, es):
        self.nc = nc
        self.es = es
        self.nsem = 0
        self.E = {}
        for name, h, inc in (("pe", nc.tensor, 1), ("act", nc.scalar, 1), ("dve", nc.vector, 1),
                             ("pool", nc.gpsimd, 1)):
            e = Eng(self, name, h, inc)
            e.new_sem()
            self.E[name] = e
        self.Q = {"sp": Eng(self, "sp", nc.sync, 16), "poolq": self.E["pool"]}
        self.all_events = {}

    def alloc_sem(self, name):
        self.nsem += 1
        return self.es.enter_context(self.nc.semaphore(f"s_{name}_{self.nsem}"))

    def _wait(self, eng, ev):
        if ev is None:
            return
        sem, val, src = ev
        if src == "pe" and eng.name == "pe":
            return
        k = id(sem)
        if eng.seen.get(k, 0) >= val:
            return
        eng.seen[k] = val
        eng.h.wait_ge(sem, val)

    def _deps(self, eng, reads, writes, same_ok=False):
        for r in reads:
            if r.w is not None:
                self._wait(eng, r.w)
        for r in writes:
            if r.w is not None:
                self._wait(eng, r.w)
            for ev in r.rd.values():
                self._wait(eng, ev)

    def _record(self, ev, reads, writes):
        for r in reads:
            r.rd[id(ev[0])] = ev
        for r in writes:
            r.w = ev
            r.rd = {}
        self.all_events[id(ev[0])] = ev

    def op(self, ename, fn, reads=(), writes=()):
        eng = self.E[ename]
        if eng.count >= SEM_LIMIT:
            eng.new_sem()
        self._deps(eng, reads, writes)
        ins = fn(eng.h)
        eng.count += 1
        ins.then_inc(eng.sem, 1)
        ev = (eng.sem, eng.count, ename)
        self._record(ev, reads, writes)
        return ev

    def dma(self, qname, dsem, out, in_, reads=(), writes=()):
        q = self.Q[qname]
        self._deps(q, reads, writes)
        if dsem.last is not None:
            self._wait(q, dsem.last)
        if dsem.count + 16 > SEM_LIMIT:
            dsem.sem = self.alloc_sem("dma")
            dsem.count = 0
        ins = q.h.dma_start(out=out, in_=in_)
        dsem.count += 16
        ins.then_inc(dsem.sem, 16)
        ev = (dsem.sem, dsem.count, "dma")
        dsem.last = ev
        self._record(ev, reads, writes)
        return ev

    def barrier(self):
        evs = list(self.all_events.values())
        for e in list(self.E.values()) + [self.Q["sp"]]:
            for ev in evs:
                self._wait(e, ev)

    def final_wait(self, ename="sp"):
        q = self.Q[ename]
        for ev in list(self.all_events.values()):
            self._wait(q, ev)


class DmaSem:
    def __init__(self, trk, name):
        self.sem = trk.alloc_sem(name)
        self.count = 0
        self.last = None


class DmaSemPool:
    def __init__(self, trk, n, name):
        self.s = [DmaSem(trk, f"{name}{i}") for i in range(n)]
        self.i = 0

    def next(self):
        s = self.s[self.i % len(self.s)]
        self.i += 1
        return s


def _rope_tables(n, dim, grid_w=64, base=10000.0):
    rows = n // grid_w
    row = np.repeat(np.arange(rows), grid_w).astype(np.float32)
    col = np.tile(np.arange(grid_w), rows).astype(np.float32)
    quarter = dim // 4
    inv_freq = (base ** (-np.arange(quarter, dtype=np.float32) / quarter)).astype(np.float32)
    ang = np.concatenate([row[:, None] * inv_freq, col[:, None] * inv_freq], axis=-1).astype(np.float32)
    cos = np.cos(ang).astype(np.float32)
    sin = np.sin(ang).astype(np.float32)
    half = dim // 2
    cos2 = np.concatenate([cos, cos], axis=1).T
    sins = np.concatenate([-sin, sin], axis=1).T
    return np.ascontiguousarray(cos2), np.ascontiguousarray(sins)


def _consts():
    c = {}
    ca, sa = _rope_tables(2048, 64)
    c["ropeA"] = np.ascontiguousarray(np.stack([np.concatenate([ca, ca], 0), np.concatenate([sa, sa], 0)], 1))
    cb, sb = _rope_tables(2048, 32)
    c["ropeB"] = np.ascontiguousarray(np.stack([cb, sb], 1))
    b = np.arange(128)[:, None]
    a = np.arange(128)[None, :]
    m = np.zeros((128, 384), np.float32)
    m[:, 0:128] = np.where(b <= a, 0.0, NEG)
    m[:, 256:384] = np.where(a <= b, 0.0, NEG)
    c["maskb"] = m
    c["ident"] = np.eye(128, dtype=np.float32)
    corr = np.ones((128, 4, 2, 8), np.float32)
    for gi, w in enumerate((2, 4, 8, 16)):
        lo = w // 2
        hi = w - lo - 1
        for t in range(lo):
            corr[:, gi, 0, t] = w / float(t + hi + 1)
        for q in range(hi):
            corr[:, gi, 1, 7 - q] = w / float(lo + q + 1)
    c["pcorr"] = corr.reshape(128, 64)
    return c


def _pmap():
    m = {}
    o = 0

    def add(name, n):
        nonlocal o
        m[name] = (o, n)
        o += n
    add("bmod", 4 * 48)
    add("normg", 4 * 2 * 8)
    add("finalg", 8)
    add("cT", 16)
    add("qnorm", 4)
    add("kvnorm", 2)
    add("sink", 16)
    add("bdw", 8)
    add("lng", 8)
    add("lnb", 8)
    add("pscale", 8)
    add("dw", 2 * 4 * 31)
    m["_n"] = o
    return m


PM = _pmap()

POOL_W = (2, 4, 8, 16)
SEQS = [(0, 256), (256, 256), (512, 2048)]
PADB = [0, 288, 576]
NPAD = 2656


def _padpos(tok):
    for s, (o, n) in enumerate(SEQS):
        if o <= tok < o + n:
            return PADB[s] + 16 + (tok - o)
    raise ValueError


ARENA = 29568


def build_program():
    nc = bass.Bass("TRN2", target_bir_lowering=False)
    es = contextlib.ExitStack()
    wlist = []

    def dram(name, shape, kind="ExternalInput", dt=F32):
        return nc.dram_tensor(name, list(shape), dt, kind=kind).ap()

    d_xT = dram("xT", [128, 8, NTOK])
    d_params = dram("params", [128, PM["_n"]])
    d_ropeA = dram("ropeA", [128, 2, 2048])
    d_ropeB = dram("ropeB", [32, 2, 2048])
    d_maskb = dram("maskb", [128, 384])
    d_ident = dram("ident", [128, 128])
    d_pcorr = dram("pcorr", [128, 64])
    d_ckT = dram("ckT", [2, 128, NCTX])
    d_cv = dram("cv", [2, 2, 128, 128])
    d_cckvT = dram("cckvT", [2, 128, NCTX])
    d_ckrT = dram("ckrT", [2, 32, NCTX])
    d_poolw = dram("poolw", [2, 128, 4 * 128])
    d_w = dram("wstream", [N_IMAGES, 128, SLOT])
    o_yT = dram("yT", [128, 8, NTOK], kind="ExternalOutput")
    o_kT = dram("okT", [2, 128, 512], kind="ExternalOutput")
    o_v = dram("ov", [2, 512, 128], kind="ExternalOutput")
    o_ckvT = dram("ockvT", [2, 128, 512], kind="ExternalOutput")
    o_krT = dram("okrT", [2, 32, 512], kind="ExternalOutput")

    with es:
        trk = Tracker(nc, es)
        op = trk.op

        def sb(name, shape, dt):
            return es.enter_context(nc.sbuf_tensor(name, list(shape), dt))

        x = sb("x", [128, 8, NTOK], F32)
        X = [[Res(f"x{k}_{t}") for t in range(NT)] for k in range(8)]
        ring = [sb(f"ring{i}", [128, SLOT], BF16) for i in range(NRING)]
        RING = [Res(f"ring{i}") for i in range(NRING)]
        ring_sem = [DmaSem(trk, f"ring{i}") for i in range(NRING)]
        arena = sb("arena", [128, ARENA], BF16)
        params = sb("params_sb", [128, PM["_n"]], F32)
        P_ = Res("params")
        NDER = 4 * 6 * 8 * 2 + 64
        der = sb("der", [128, NDER], F32)
        DER = Res("der")
        mod = sb("mod", [128, 4, 48, 2], F32)
        MOD = [Res(f"mod{l}") for l in range(4)]
        scT = sb("scT", [128, 8, 2], BF16)
        SCT = Res("scT")
        ident = sb("ident_sb", [128, 128], BF16)
        onesm = sb("onesm", [128, 128], BF16)
        ones1 = sb("ones1", [128, 128], BF16)
        ones_lo = sb("ones_lo", [128, 128], BF16)
        maskb = sb("maskb_sb", [128, 384], BF16)
        pcorr = sb("pcorr_sb", [128, 64], F32)
        epst = sb("epst", [128, 1], F32)
        negc = sb("negc", [128, 1], F32)
        pw = sb("poolw_sb", [128, 512], BF16)
        PW = Res("pw")
        CONST = Res("const")
        rstd = sb("rstd", [128, 512], F32)
        RSTD = Res("rstd")
        sq = [sb(f"sq{i}", [128, 512], BF16) for i in range(2)]
        SQ = [Res(f"sq{i}") for i in range(2)]
        ft = [sb(f"ft{i}", [128, 512], F32) for i in range(7)]
        FT = [Res(f"ft{i}") for i in range(7)]
        tmp, TMP = ft[0:3], FT[0:3]
        cvt, CVT = ft[3:7], FT[3:7]
        rec, REC = ft[3], FT[3]
        ostage, OST = ft[4:6], FT[4:6]
        stage, STAGE = ft[6], FT[6]
        pt = [sb(f"pt{i}", [128, 512], BF16) for i in range(3)]
        PT = [Res(f"pt{i}") for i in range(3)]
        rope_t = sb("rope_t", [128, 2, TS], F32)
        ROPE = Res("rope")
        ropeB_all = sb("ropeB_all", [128, 2, TS], F32)
        ROPEB = Res("ropeB")
        ost_i = [0]

        banks = [es.enter_context(nc.psum_tensor(f"bank{i}", [128, 512], F32)) for i in range(8)]
        BANK = [Res(f"bank{i}") for i in range(8)]
        bank_free = list(range(8))

        side_free = []
        pool_sel = ["g"]

        def balloc():
            if pool_sel[0] == "side":
                assert side_free, "out of side PSUM banks"
                return side_free.pop(0)
            assert bank_free, "out of PSUM banks"
            return bank_free.pop(0)

        def bfree(b):
            if b in SIDE_POOL and pool_sel[0] == "side":
                side_free.append(b)
            else:
                bank_free.append(b)

        SIDE_POOL = [4, 5]

        def reserve_side(on):
            if on:
                for b in SIDE_POOL:
                    bank_free.remove(b)
                    side_free.append(b)
            else:
                for b in SIDE_POOL:
                    side_free.remove(b)
                    bank_free.append(b)

        sp_sems = DmaSemPool(trk, 6, "sp")
        out_sems = DmaSemPool(trk, 4, "out")

        rst = {"issued": 0, "got": 0, "held": [False] * NRING, "limit": None}

        def ring_pump():
            while rst["issued"] < N_IMAGES and rst["issued"] < rst["got"] + NRING:
                if rst["limit"] is not None and rst["issued"] >= rst["limit"]:
                    break
                i = rst["issued"]
                free = [k for k in range(NRING) if not rst["held"][k]]
                if not free:
                    break
                s_ = i % NRING if (i % NRING) in free else free[0]
                trk.dma("poolq", ring_sem[s_], ring[s_][:, :], d_w[i], reads=(), writes=(RING[s_],))
                rst["held"][s_] = True
                rst.setdefault("slot_of", {})[i] = s_
                rst["issued"] += 1

        def ring_get(key):
            i = rst["got"]
            wlist.append(key)
            ring_pump()
            assert rst["issued"] > i, ("ring stalled", key)
            rst["got"] += 1
            s_ = rst["slot_of"][i]
            return ring[s_], RING[s_], s_

        def ring_done(s_):
            rst["held"][s_] = False
            ring_pump()

        trk.dma("sp", sp_sems.next(), params[:, :], d_params[:, :], writes=(P_,))
        for k in range(8):
            trk.dma("sp", sp_sems.next(), x[:, k, :], d_xT[:, k, :], writes=tuple(X[k]))

        def load_cast(dst_ap, src_ap, nparts, ncols, wres):
            trk.dma("sp", sp_sems.next(), stage[0:nparts, 0:ncols], src_ap, writes=(STAGE,))
            op("dve", lambda e: e.tensor_copy(out=dst_ap, in_=stage[0:nparts, 0:ncols]), reads=(STAGE,), writes=wres)

        load_cast(ident[:, :], d_ident[:, :], 128, 128, (CONST,))
        load_cast(maskb[:, :], d_maskb[:, :], 128, 384, (CONST,))
        trk.dma("sp", sp_sems.next(), pcorr[:, :], d_pcorr[:, :], writes=(CONST,))
        op("dve", lambda e: e.memset(onesm[:, :], 1.0 / 1024.0), writes=(CONST,))
        op("dve", lambda e: e.memset(ones1[:, :], 1.0), writes=(CONST,))
        op("dve", lambda e: e.memset(ones_lo[:, :], 0.0), writes=(CONST,))
        op("dve", lambda e: e.memset(ones_lo[0:64, :], 1.0), writes=(CONST,))
        op("dve", lambda e: e.memset(epst[:, :], EPS), writes=(CONST,))
        op("dve", lambda e: e.memset(negc[:, :], -SHIFT), writes=(CONST,))

        o_cT = PM["cT"][0]
        op("act", lambda e: e.activation(out=scT[:, :, :], in_=params[:, o_cT:o_cT + 16].rearrange("p (k j) -> p k j", j=2),
                                         func=AF.Silu), reads=(P_,), writes=(SCT,))

        def dcol(l, which, k, g):
            return ((l * 6 + which) * 8 + k) * 2 + g

        o_es = 4 * 6 * 8 * 2
        o_np = o_es + 16
        o_psw = o_np + 8
        o_sink = PM["sink"][0]
        op("act", lambda e: e.activation(out=der[:, o_es:o_es + 16], in_=params[:, o_sink:o_sink + 16], func=AF.Exp, bias=negc[:, 0:1], scale=1.0),
           reads=(P_, CONST), writes=(DER,))
        o_ps = PM["pscale"][0]
        op("dve", lambda e: e.tensor_scalar(out=der[:, o_np:o_np + 8], in0=params[:, o_ps:o_ps + 8], scalar1=-1.0,
                                            scalar2=None, op0=ALU.mult), reads=(P_,), writes=(DER,))
        for j in range(2):
            for gi, w in enumerate(POOL_W):
                op("dve", lambda e, j=j, gi=gi, w=w: e.tensor_scalar(
                    out=der[:, o_psw + j * 4 + gi:o_psw + j * 4 + gi + 1],
                    in0=params[:, o_ps + j * 4 + gi:o_ps + j * 4 + gi + 1], scalar1=1.0 / w, scalar2=None,
                    op0=ALU.mult), reads=(P_,), writes=(DER,))

        def adaln_group(l, g):
            wt, WR, ws = ring_get(("wmod", l, g))
            b = balloc()

            def pe(e):
                ins = None
                for cc in range(4):
                    for kc in range(8):
                        ins = e.matmul(banks[b][:, cc * 2:cc * 2 + 2], lhsT=wt[:, kc * 512 + cc * 128:kc * 512 + cc * 128 + 128],
                                       rhs=scT[:, kc, :], start=(kc == 0), stop=(kc == 7), skip_group_check=True)
                return ins
            op("pe", pe, reads=(WR, SCT), writes=(BANK[b],))
            ring_done(ws)
            ob = PM["bmod"][0] + l * 48 + 4 * g
            for j in range(2):
                op("dve", lambda e, j=j: e.tensor_tensor(
                    out=mod[:, l, 4 * g:4 * g + 4, j],
                    in0=banks[b][:, 0:8].rearrange("p (c j) -> p c j", j=2)[:, :, j],
                    in1=params[:, ob:ob + 4], op=ALU.add), reads=(BANK[b], P_), writes=(MOD[l],))
            bfree(b)

        def adaln_finish(l, halves=(0, 1)):
            og = PM["normg"][0]
            for half in halves:
                for k in range(8):
                    gc = og + (l * 2 + half) * 8 + k
                    a0 = dcol(l, half * 3 + 0, k, 0)
                    op("dve", lambda e, k=k, half=half, gc=gc, a0=a0: e.tensor_scalar(
                        out=der[:, a0:a0 + 2], in0=mod[:, l, (half * 3 + 1) * 8 + k, :], scalar1=1.0, scalar2=params[:, gc:gc + 1],
                        op0=ALU.add, op1=ALU.mult), reads=(MOD[l], P_), writes=(DER,))
                    b0 = dcol(l, half * 3 + 1, k, 0)
                    op("dve", lambda e, k=k, half=half, b0=b0: e.tensor_copy(
                        out=der[:, b0:b0 + 2], in_=mod[:, l, (half * 3 + 0) * 8 + k, :]), reads=(MOD[l],), writes=(DER,))
                    g0 = dcol(l, half * 3 + 2, k, 0)
                    op("dve", lambda e, k=k, half=half, g0=g0: e.tensor_copy(
                        out=der[:, g0:g0 + 2], in_=mod[:, l, (half * 3 + 2) * 8 + k, :]), reads=(MOD[l],), writes=(DER,))

        def dv(l, which, k, g):
            c = dcol(l, which, k, g)
            return der[:, c:c + 1]

        def norm_tile(l, half, t, dest, DEST):
            grp = 0 if t == 0 else 1
            tok = slice(t * TS, (t + 1) * TS)
            b = balloc()
            for k in range(8):
                op("act", lambda e, k=k: e.activation(out=sq[k % 2][:, :], in_=x[:, k, tok], func=AF.Square),
                   reads=(X[k][t],), writes=(SQ[k % 2],))
                op("pe", lambda e, k=k: e.matmul(banks[b][:, :], lhsT=onesm[:, :], rhs=sq[k % 2][:, :], start=(k == 0),
                                                 stop=(k == 7), skip_group_check=True),
                   reads=(SQ[k % 2], CONST), writes=(BANK[b],))
            op("act", lambda e: e.activation(out=rstd[:, :], in_=banks[b][:, :], func=AF.Ln, bias=epst[:, 0:1], scale=1.0),
               reads=(BANK[b], CONST), writes=(RSTD,))
            bfree(b)
            op("act", lambda e: e.activation(out=rstd[:, :], in_=rstd[:, :], func=AF.Exp, scale=-0.5), reads=(RSTD,), writes=(RSTD,))
            for k in range(8):
                ti = k % 2
                op("dve", lambda e, k=k, ti=ti: e.tensor_tensor(out=tmp[ti][:, :], in0=x[:, k, tok], in1=rstd[:, :], op=ALU.mult),
                   reads=(X[k][t], RSTD), writes=(TMP[ti],))
                op("act", lambda e, k=k, ti=ti: e.activation(out=dest[:, k, :], in_=tmp[ti][:, :], func=AF.Identity,
                                                           bias=dv(l, half * 3 + 1, k, grp), scale=dv(l, half * 3 + 0, k, grp)),
                   reads=(TMP[ti], DER), writes=(DEST,))

        def norm_gen(l, half, t, dest, DEST):
            grp = 0 if t == 0 else 1
            tokx = slice(t * TS, (t + 1) * TS)
            b = balloc()
            for k in range(8):
                op("act", lambda e, k=k: e.activation(out=sq[k % 2][:, :], in_=x[:, k, tokx], func=AF.Square),
                   reads=(X[k][t],), writes=(SQ[k % 2],))
                op("pe", lambda e, k=k: e.matmul(banks[b][:, :], lhsT=onesm[:, :], rhs=sq[k % 2][:, :], start=(k == 0),
                                                 stop=(k == 7), skip_group_check=True),
                   reads=(SQ[k % 2], CONST), writes=(BANK[b],))
                if k % 2 == 1:
                    yield
            op("act", lambda e: e.activation(out=rstd[:, :], in_=banks[b][:, :], func=AF.Ln, bias=epst[:, 0:1], scale=1.0),
               reads=(BANK[b], CONST), writes=(RSTD,))
            bfree(b)
            op("act", lambda e: e.activation(out=rstd[:, :], in_=rstd[:, :], func=AF.Exp, scale=-0.5), reads=(RSTD,), writes=(RSTD,))
            yield
            for k in range(8):
                ti = k % 2
                op("dve", lambda e, k=k, ti=ti: e.tensor_tensor(out=tmp[ti][:, :], in0=x[:, k, tokx], in1=rstd[:, :], op=ALU.mult),
                   reads=(X[k][t], RSTD), writes=(TMP[ti],))
                op("act", lambda e, k=k, ti=ti: e.activation(out=dest[:, k, :], in_=tmp[ti][:, :], func=AF.Identity,
                                                           bias=dv(l, half * 3 + 1, k, grp), scale=dv(l, half * 3 + 0, k, grp)),
                   reads=(TMP[ti], DER), writes=(DEST,))
                if k % 4 == 3:
                    yield

        def drive(gens):
            gens = [g_ for g_ in gens if g_ is not None]
            while gens:
                for g_ in list(gens):
                    try:
                        next(g_)
                    except StopIteration:
                        gens.remove(g_)

        def resid_add(l, half, j, t, b):
            grp = 0 if t == 0 else 1
            xs = x[:, j, t * TS:(t + 1) * TS]
            op("dve", lambda e: e.scalar_tensor_tensor(out=xs, in0=banks[b][:, :], scalar=dv(l, half * 3 + 2, j, grp),
                                                       in1=xs, op0=ALU.mult, op1=ALU.add),
               reads=(BANK[b], DER, X[j][t]), writes=(X[j][t],))

        def mlp(l, next_adaln):
            hbuf = arena[:, 0:8 * NTOK].rearrange("p (k n) -> p k n", k=8)
            H = [Res(f"h{t}") for t in range(NT)]
            h1 = [arena[:, 8 * NTOK + i * 2048:8 * NTOK + (i + 1) * 2048].rearrange("p (c n) -> p c n", c=4) for i in range(2)]
            H1 = [Res("h1a"), Res("h1b")]
            norm_tile(l, 1, 0, hbuf[:, :, 0:TS], H[0])
            ada = list(next_adaln)
            seq = [(g, t) for g in range(8) for t in range(NT)]
            w1s, w2s = {}, {}

            def h1stage(k):
                g, t = seq[k]
                hi = k % 2
                if t == 0:
                    w1s[g] = ring_get(("w1", l, g))
                w1, W1, s1 = w1s[g]
                ng = None
                if g == 0 and t + 1 < NT:
                    ng = norm_gen(l, 1, t + 1, hbuf[:, :, (t + 1) * TS:(t + 2) * TS], H[t + 1])
                for c in range(4):
                    if ng is not None:
                        for _ in range(2):
                            next(ng, None)
                    b_ = balloc()

                    def pe(e, c=c, b_=b_):
                        ins = None
                        for kc in range(8):
                            ins = e.matmul(banks[b_][:, :], lhsT=w1[:, kc * 512 + c * 128:kc * 512 + c * 128 + 128],
                                           rhs=hbuf[:, kc, t * TS:(t + 1) * TS], start=(kc == 0), stop=(kc == 7))
                        return ins
                    op("pe", pe, reads=(W1, H[t]), writes=(BANK[b_],))
                    ti = c % 2
                    op("act", lambda e, b_=b_, ti=ti: e.activation(out=tmp[ti][:, :], in_=banks[b_][:, :], func=AF.Relu),
                       reads=(BANK[b_],), writes=(TMP[ti],))
                    bfree(b_)
                    op("dve", lambda e, c=c, ti=ti: e.tensor_tensor(out=h1[hi][:, c, :], in0=tmp[ti][:, :], in1=tmp[ti][:, :],
                                                                    op=ALU.mult), reads=(TMP[ti],), writes=(H1[hi],))
                if t == NT - 1:
                    ring_done(s1)
                if ng is not None:
                    for _ in ng:
                        pass

            def outstage(k):
                g, t = seq[k]
                hi = k % 2
                if t == 0:
                    w2s[g] = ring_get(("w2", l, g))
                w2, W2, s2 = w2s[g]
                for j in range(8):
                    b_ = balloc()

                    def pe2(e, j=j, b_=b_):
                        ins = None
                        for c in range(4):
                            ins = e.matmul(banks[b_][:, :], lhsT=w2[:, c * 1024 + j * 128:c * 1024 + j * 128 + 128],
                                           rhs=h1[hi][:, c, :], start=(c == 0), stop=(c == 3))
                        return ins
                    op("pe", pe2, reads=(W2, H1[hi]), writes=(BANK[b_],))
                    resid_add(l, 1, j, t, b_)
                    bfree(b_)
                if t == NT - 1:
                    ring_done(s2)
                    for _ in range(2):
                        if ada:
                            adaln_group(*ada.pop(0))
                            if not ada:
                                adaln_finish(l + 1)

            h1stage(0)
            for k in range(len(seq)):
                if k + 1 < len(seq):
                    h1stage(k + 1)
                outstage(k)
            assert not ada

        pti = [0]

        NPT = 3
        LA = 3
        OB_POOL = [6, 7]
        ob_i = [0]

        def reserve_obanks(on):
            if on:
                for b in OB_POOL:
                    bank_free.remove(b)
            else:
                bank_free.extend(OB_POOL)

        def run_attn(jobs, scale, side=None):
            items = []
            for ji, job in enumerate(jobs):
                for gi, g in enumerate(job["groups"]):
                    items.append((ji, gi, g))
            M = len(items)
            sbank = {}
            ptb = {}
            obank = {}
            for i in range(M + LA):
                if i < M:
                    ji, gi, g = items[i]
                    sbk = balloc()
                    sbank[i] = sbk
                    n = g["n"]

                    def pe(e, g=g, sbk=sbk, n=n):
                        ins = e.matmul(banks[sbk][:, 0:n], lhsT=g["k"], rhs=g["q"], start=True, stop=(g["mask"] is None), skip_group_check=True)
                        if g["mask"] is not None:
                            ins = e.matmul(banks[sbk][:, 0:n], lhsT=ident[:, :], rhs=g["mask"], start=False, stop=True, skip_group_check=True)
                        return ins
                    op("pe", pe, reads=(g["K"], g["Q"], CONST) + ((g["Q2"],) if "Q2" in g else ()), writes=(BANK[sbk],))
                j = i - (LA - 1)
                if 0 <= j < M:
                    ji, gi, g = items[j]
                    sbk = sbank.pop(j)
                    n = g["n"]
                    pi = pti[0] % NPT
                    pti[0] += 1
                    ptb[j] = pi
                    op("act", lambda e, sbk=sbk, n=n, pi=pi: e.activation(out=pt[pi][:, 0:n], in_=banks[sbk][:, 0:n], func=AF.Exp, bias=negc[:, 0:1], scale=scale),
                       reads=(BANK[sbk], CONST), writes=(PT[pi],))
                    bfree(sbk)
                k = i - LA
                if 0 <= k < M:
                    ji, gi, g = items[k]
                    if gi == 0:
                        obank[ji] = OB_POOL[ob_i[0] % 2]
                        ob_i[0] += 1
                    ob = obank[ji]
                    pi = ptb.pop(k)
                    n = g["n"]
                    c0 = g["c0"]
                    last = gi == len(jobs[ji]["groups"]) - 1
                    op("pe", lambda e, g=g, pi=pi, n=n, c0=c0, f=(gi == 0), la=last, ob=ob: e.matmul(
                        banks[ob][:, c0:c0 + n], lhsT=g["v"], rhs=pt[pi][:, 0:n], start=f, stop=la, skip_group_check=True),
                       reads=(g["V"], PT[pi]), writes=(BANK[ob],))
                    if last:
                        jobs[ji]["finish"](ob)
                        obank.pop(ji)
                if side is not None:
                    side()

        def normalize(obank, base, nq, sink_col, dst, wres, extra_reads=()):
            dbase = 64 - base
            ds = slice(dbase, dbase + 64)
            if sink_col is not None:
                op("act", lambda e: e.activation(out=rec[ds, 0:nq], in_=banks[obank][ds, 0:nq], func=AF.Ln,
                                                 bias=der[ds, sink_col:sink_col + 1], scale=1.0),
                   reads=(BANK[obank], DER), writes=(REC,))
            else:
                op("act", lambda e: e.activation(out=rec[ds, 0:nq], in_=banks[obank][ds, 0:nq], func=AF.Ln),
                   reads=(BANK[obank],), writes=(REC,))
            op("act", lambda e: e.activation(out=rec[ds, 0:nq], in_=rec[ds, 0:nq], func=AF.Exp, scale=-1.0), reads=(REC,), writes=(REC,))
            op("dve", lambda e: e.tensor_tensor(out=dst, in0=banks[obank][base:base + 64, 0:nq], in1=rec[ds, 0:nq], op=ALU.mult),
               reads=(BANK[obank], REC) + tuple(extra_reads), writes=wres)

        def rope_combine(b1, b2, nrows, dst, wres, p0=0, tab=None, TAB=None, tp0=None):
            ps = slice(p0, p0 + nrows)
            if tab is None:
                tab, TAB, tp0 = rope_t, ROPE, p0
            ts_ = slice(tp0, tp0 + nrows)
            op("dve", lambda e: e.tensor_tensor(out=tmp[0][ps, :], in0=banks[b1][ps, :], in1=tab[ts_, 0, :], op=ALU.mult),
               reads=(BANK[b1], TAB), writes=(TMP[0],))
            op("dve", lambda e: e.tensor_tensor(out=tmp[1][ps, :], in0=banks[b2][ps, :], in1=tab[ts_, 1, :], op=ALU.mult),
               reads=(BANK[b2], TAB), writes=(TMP[1],))
            op("dve", lambda e: e.tensor_tensor(out=dst, in0=tmp[0][ps, :], in1=tmp[1][ps, :], op=ALU.add),
               reads=(TMP[0], TMP[1]), writes=wres)

        def load_rope(which, T, p0=0):
            rsl = slice(T * TS, (T + 1) * TS)
            if which == "A":
                trk.dma("sp", sp_sems.next(), rope_t[:, :, :], d_ropeA[:, :, rsl], writes=(ROPE,))
            else:
                trk.dma("sp", sp_sems.next(), rope_t[p0:p0 + 32, :, :], d_ropeB[:, :, rsl], writes=(ROPE,))

        def attn_group(l, grp, extra=None):
            e_i = l // 2
            sample = grp == 1
            tiles = [1, 2, 3, 4] if sample else [0]
            t0 = tiles[0]
            ntok = TS * len(tiles)
            nctx = NCTX if sample else 0
            NKg = ntok + nctx
            nkt = NKg // 128
            nlt = ntok // 128
            nt = len(tiles)
            o = 0
            qa = arena[:, o:o + 4 * ntok].rearrange("p (c n) -> p c n", c=4); o += 4 * ntok
            cqn = arena[:, o:o + 2 * NKg].rearrange("p (c n) -> p c n", c=2); o += 2 * NKg
            ckvn = arena[:, o:o + NKg]; o += NKg
            r3 = o
            hts = [arena[:, o + i * 4096:o + (i + 1) * 4096].rearrange("p (k n) -> p k n", k=8) for i in range(2)]
            khb = arena[:, r3:r3 + NKg]
            vhb = arena[:, r3 + NKg:r3 + NKg + nkt * 128].rearrange("p (t c) -> p t c", c=128)
            qhbs = [arena[:, r3 + NKg + nkt * 128 + i * ntok:r3 + NKg + nkt * 128 + (i + 1) * ntok] for i in range(2)]
            o = r3 + max(8192, NKg + nkt * 128 + 2 * ntok)
            r4 = o
            ka = arena[:, o:o + NKg]
            va = arena[:, o + NKg:o + NKg + nkt * 192].rearrange("p (t c) -> p t c", c=192)
            obc = [arena[:, r4 + i * ntok:r4 + (i + 1) * ntok] for i in range(2)]
            o = r4 + max(NKg + nkt * 192, 2 * ntok)
            assert o <= ARENA, o
            HTs = [Res("ht0"), Res("ht1")]
            QAH = [[[Res(f"qah{c}_{hh}_{t}") for t in range(nt)] for hh in range(2)] for c in range(4)]
            CQ = [Res(f"cq{t}") for t in range(nt)]
            CKV = [Res(f"ckv{t}") for t in range(nt + 1)]
            KR = [Res(f"kr{t}") for t in range(nt + 1)]
            KA = [Res(f"ka{t}") for t in range(nt + 1)]
            VA = [Res(f"va{t}") for t in range(nt + 1)]
            OBC = [Res("obc0"), Res("obc1")]
            KH, VH = Res("kh"), Res("vh")
            QHs = [Res("qh0"), Res("qh1")]
            krv = cqn[64:96, 1, :]

            reserve_obanks(True)
            g0, G0, s0 = ring_get(("win", e_i, 0))
            g1, G1, s1 = ring_get(("win", e_i, 1))
            g2, G2, s2 = ring_get(("win", e_i, 2))
            g3, G3, s3 = ring_get(("win", e_i, 3))

            op("dve", lambda e: e.memset(va[:, :, 64:128], 1.0), writes=tuple(VA))
            op("dve", lambda e: e.memset(cqn[96:128, 1, :], 0.0), writes=tuple(CQ))
            if sample:
                for g_ in range(4):
                    trk.dma("sp", sp_sems.next(), ropeB_all[32 * g_:32 * g_ + 32, :, :], d_ropeB[:, :, g_ * TS:(g_ + 1) * TS], writes=(ROPEB,))
                load_cast(ka[:, ntok:NKg], d_ckT[e_i], 128, NCTX, (KA[nt],))
                load_cast(ckvn[:, ntok:NKg], d_cckvT[e_i], 128, NCTX, (CKV[nt],))
                load_cast(krv[:, ntok:NKg], d_ckrT[e_i], 32, NCTX, (KR[nt],))
                for kt in range(2):
                    trk.dma("sp", sp_sems.next(), stage[:, 0:128], d_cv[e_i, kt], writes=(STAGE,))
                    op("dve", lambda e, kt=kt: e.tensor_copy(out=va[:, nlt + kt, 0:64], in_=stage[:, 0:64]), reads=(STAGE,), writes=(VA[nt],))
                    op("dve", lambda e, kt=kt: e.tensor_copy(out=va[:, nlt + kt, 128:192], in_=stage[:, 64:128]), reads=(STAGE,), writes=(VA[nt],))

            oqn = PM["qnorm"][0] + e_i * 2
            okn = PM["kvnorm"][0] + e_i

            cur = {}

            def proj(bk, wt, WR, col0, ncol, ht, HT):

                def pe(e):
                    ins = None
                    for kc in range(8):
                        ins = e.matmul(banks[bk][0:ncol, :], lhsT=wt[:, kc * 512 + col0:kc * 512 + col0 + ncol], rhs=ht[:, kc, :],
                                       start=(kc == 0), stop=(kc == 7))
                    return ins
                op("pe", pe, reads=(WR, HT), writes=(BANK[bk],))

            def out_from(dst, src_ap, src_res, nrows, scale=None):
                i = ost_i[0] % 2
                ost_i[0] += 1
                if scale is None:
                    op("act", lambda e: e.activation(out=ostage[i][0:nrows, :], in_=src_ap, func=AF.Identity),
                       reads=src_res, writes=(OST[i],))
                else:
                    op("act", lambda e: e.activation(out=ostage[i][0:nrows, :], in_=src_ap, func=AF.Identity, scale=scale),
                       reads=src_res + (P_,), writes=(OST[i],))
                trk.dma("sp", out_sems.next(), dst, ostage[i][0:nrows, :], reads=(OST[i],))

            def norm_gen(t, dest, DEST):
                grp = 0 if t == 0 else 1
                tokx = slice(t * TS, (t + 1) * TS)
                b = balloc()
                for k in range(8):
                    op("act", lambda e, k=k: e.activation(out=sq[k % 2][:, :], in_=x[:, k, tokx], func=AF.Square),
                       reads=(X[k][t],), writes=(SQ[k % 2],))
                    op("pe", lambda e, k=k: e.matmul(banks[b][:, :], lhsT=onesm[:, :], rhs=sq[k % 2][:, :], start=(k == 0),
                                                     stop=(k == 7), skip_group_check=True),
                       reads=(SQ[k % 2], CONST), writes=(BANK[b],))
                    if k % 2 == 1:
                        yield
                op("act", lambda e: e.activation(out=rstd[:, :], in_=banks[b][:, :], func=AF.Ln, bias=epst[:, 0:1], scale=1.0),
                   reads=(BANK[b], CONST), writes=(RSTD,))
                bfree(b)
                op("act", lambda e: e.activation(out=rstd[:, :], in_=rstd[:, :], func=AF.Exp, scale=-0.5), reads=(RSTD,), writes=(RSTD,))
                yield
                for k in range(8):
                    ti = k % 2
                    op("dve", lambda e, k=k, ti=ti: e.tensor_tensor(out=tmp[ti][:, :], in0=x[:, k, tokx], in1=rstd[:, :], op=ALU.mult),
                       reads=(X[k][t], RSTD), writes=(TMP[ti],))
                    op("act", lambda e, k=k, ti=ti: e.activation(out=dest[:, k, :], in_=tmp[ti][:, :], func=AF.Identity,
                                                               bias=dv(l, 1, k, grp), scale=dv(l, 0, k, grp)),
                       reads=(TMP[ti], DER), writes=(DEST,))
                    if k % 4 == 3:
                        yield

            def stageA(lt):
                t = tiles[lt]
                tok = slice(lt * TS, (lt + 1) * TS)
                ht, HT = hts[lt % 2], HTs[lt % 2]
                cur["ht"], cur["HT"] = ht, HT
                yield from norm_gen(t, ht, HT)
                cur["ht"], cur["HT"] = ht, HT
                b0 = balloc(); b1 = balloc(); bs = balloc()
                proj(b0, g0, G0, 0, 128, ht, HT)
                proj(b1, g0, G0, 128, 128, ht, HT)
                op("act", lambda e, b0=b0: e.activation(out=sq[0][:, :], in_=banks[b0][:, :], func=AF.Square), reads=(BANK[b0],), writes=(SQ[0],))
                op("act", lambda e, b1=b1: e.activation(out=sq[1][:, :], in_=banks[b1][:, :], func=AF.Square), reads=(BANK[b1],), writes=(SQ[1],))

                def pe_ss(e, bs=bs):
                    e.matmul(banks[bs][:, :], lhsT=ones1[:, :], rhs=sq[0][:, :], start=True, stop=False, skip_group_check=True)
                    return e.matmul(banks[bs][:, :], lhsT=ones_lo[:, :], rhs=sq[1][:, :], start=False, stop=True, skip_group_check=True)
                op("pe", pe_ss, reads=(SQ[0], SQ[1], CONST), writes=(BANK[bs],))
                yield
                op("act", lambda e, bs=bs: e.activation(out=rstd[:, :], in_=banks[bs][:, :], func=AF.Ln, bias=epst[:, 0:1], scale=1.0 / 192.0),
                   reads=(BANK[bs], CONST), writes=(RSTD,))
                op("act", lambda e: e.activation(out=rstd[:, :], in_=rstd[:, :], func=AF.Exp, scale=-0.5), reads=(RSTD,), writes=(RSTD,))
                op("dve", lambda e, b0=b0: e.tensor_tensor(out=tmp[0][:, :], in0=banks[b0][:, :], in1=rstd[:, :], op=ALU.mult),
                   reads=(BANK[b0], RSTD), writes=(TMP[0],))
                op("act", lambda e, tok=tok: e.activation(out=cqn[:, 0, tok], in_=tmp[0][:, :], func=AF.Identity, scale=params[:, oqn:oqn + 1]),
                   reads=(TMP[0], P_), writes=(CQ[lt],))
                op("dve", lambda e, b1=b1: e.tensor_tensor(out=tmp[1][0:64, :], in0=banks[b1][0:64, :], in1=rstd[0:64, :], op=ALU.mult),
                   reads=(BANK[b1], RSTD), writes=(TMP[1],))
                op("act", lambda e, tok=tok: e.activation(out=cqn[0:64, 1, tok], in_=tmp[1][0:64, :], func=AF.Identity, scale=params[0:64, oqn + 1:oqn + 2]),
                   reads=(TMP[1], P_), writes=(CQ[lt],))
                bfree(b0); bfree(b1)
                yield
                b0 = balloc()
                proj(b0, g0, G0, 192, 128, ht, HT)
                op("act", lambda e, b0=b0: e.activation(out=sq[0][:, :], in_=banks[b0][:, :], func=AF.Square), reads=(BANK[b0],), writes=(SQ[0],))
                op("pe", lambda e, bs=bs: e.matmul(banks[bs][:, :], lhsT=ones1[:, :], rhs=sq[0][:, :], start=True, stop=True),
                   reads=(SQ[0], CONST), writes=(BANK[bs],))
                op("act", lambda e, bs=bs: e.activation(out=rstd[:, :], in_=banks[bs][:, :], func=AF.Ln, bias=epst[:, 0:1], scale=1.0 / 128.0),
                   reads=(BANK[bs], CONST), writes=(RSTD,))
                op("act", lambda e: e.activation(out=rstd[:, :], in_=rstd[:, :], func=AF.Exp, scale=-0.5), reads=(RSTD,), writes=(RSTD,))
                op("dve", lambda e, b0=b0: e.tensor_tensor(out=tmp[0][:, :], in0=banks[b0][:, :], in1=rstd[:, :], op=ALU.mult),
                   reads=(BANK[b0], RSTD), writes=(TMP[0],))
                op("act", lambda e, tok=tok: e.activation(out=ckvn[:, tok], in_=tmp[0][:, :], func=AF.Identity, scale=params[:, okn:okn + 1]),
                   reads=(TMP[0], P_), writes=(CKV[lt],))
                if not sample:
                    out_from(o_ckvT[e_i], tmp[0][:, :], (TMP[0],), 128, scale=params[:, okn:okn + 1])
                bfree(b0); bfree(bs)
                yield
                b0 = balloc()
                proj(b0, g0, G0, 320, 128, ht, HT)
                if sample:
                    b1 = balloc()
                    proj(b1, g0, G0, 352, 128, ht, HT)
                    rope_combine(b0, b1, 32, krv[:, tok], (KR[lt],), tab=ropeB_all, TAB=ROPEB, tp0=32 * (t - 1))
                    bfree(b1)
                else:
                    out_from(o_krT[e_i], banks[b0][0:32, :], (BANK[b0],), 32)
                    op("dve", lambda e, b0=b0, tok=tok: e.tensor_copy(out=krv[:, tok], in_=ostage[(ost_i[0] - 1) % 2][0:32, :]),
                       reads=(OST[(ost_i[0] - 1) % 2],), writes=(KR[lt],))
                bfree(b0)
                yield

            def stageB(lt):
                t = tiles[lt]
                tok = slice(lt * TS, (lt + 1) * TS)
                ht, HT = hts[lt % 2], HTs[lt % 2]
                cur["ht"], cur["HT"] = ht, HT
                if sample:
                    load_rope("A", t - 1)
                for c in range(4):
                    b0 = balloc()
                    proj(b0, g1, G1, c * 128, 128, ht, HT)
                    if sample:
                        b1 = balloc()
                        proj(b1, g2, G2, c * 128, 128, ht, HT)
                        rope_combine(b0, b1, 128, qa[:, c, tok], (QAH[c][0][lt], QAH[c][1][lt]))
                        bfree(b1)
                    else:
                        op("act", lambda e, c=c, b0=b0, tok=tok: e.activation(out=qa[:, c, tok], in_=banks[b0][:, :], func=AF.Identity),
                           reads=(BANK[b0],), writes=(QAH[c][0][lt], QAH[c][1][lt]))
                    bfree(b0)
                    yield
                b0 = balloc()
                proj(b0, g3, G3, 0, 128, ht, HT)
                if sample:
                    b1 = balloc()
                    proj(b1, g3, G3, 128, 128, ht, HT)
                    rope_combine(b0, b1, 128, ka[:, tok], (KA[lt],))
                    bfree(b1)
                else:
                    op("act", lambda e, b0=b0, tok=tok: e.activation(out=ka[:, tok], in_=banks[b0][:, :], func=AF.Identity), reads=(BANK[b0],), writes=(KA[lt],))
                    out_from(o_kT[e_i], banks[b0][:, :], (BANK[b0],), 128)
                bfree(b0)
                yield
                b0 = balloc()

                def pe_v(e, b0=b0, ht=ht):
                    ins = None
                    for tb in range(4):
                        for kc in range(8):
                            ins = e.matmul(banks[b0][:, tb * 128:(tb + 1) * 128], lhsT=ht[:, kc, tb * 128:(tb + 1) * 128],
                                           rhs=g3[:, kc * 512 + 256:kc * 512 + 384], start=(kc == 0), stop=(kc == 7), skip_group_check=True)
                    return ins
                op("pe", pe_v, reads=(G3, HT), writes=(BANK[b0],))
                bv = banks[b0][:, :].rearrange("p (t c) -> p t c", c=128)
                op("act", lambda e, bv=bv, lt=lt: e.activation(out=va[:, 4 * lt:4 * lt + 4, 0:64], in_=bv[:, :, 0:64], func=AF.Identity),
                   reads=(BANK[b0],), writes=(VA[lt],))
                op("act", lambda e, bv=bv, lt=lt: e.activation(out=va[:, 4 * lt:4 * lt + 4, 128:192], in_=bv[:, :, 64:128], func=AF.Identity),
                   reads=(BANK[b0],), writes=(VA[lt],))
                if not sample:
                    i = ost_i[0] % 2
                    ost_i[0] += 1
                    op("act", lambda e, i=i, b0=b0: e.activation(out=ostage[i][:, :], in_=banks[b0][:, :], func=AF.Identity),
                       reads=(BANK[b0],), writes=(OST[i],))
                    trk.dma("sp", out_sems.next(), o_v[e_i].rearrange("(t p) c -> p t c", p=128),
                            ostage[i][:, :].rearrange("p (t c) -> p t c", c=128), reads=(OST[i],))
                bfree(b0)
                yield

            def drive(gens):
                gens = [g_ for g_ in gens if g_ is not None]
                while gens:
                    for g_ in list(gens):
                        try:
                            next(g_)
                        except StopIteration:
                            gens.remove(g_)

            drive([stageA(0)])
            for lt in range(nt):
                drive([stageB(lt), stageA(lt + 1) if lt + 1 < nt else None])
            for s_ in (s0, s1, s2, s3):
                ring_done(s_)

            g4, G4, s4 = ring_get(("wqkv", e_i))
            wo0, WO0, so0 = ring_get(("wout", e_i, 0))
            wo1, WO1, so1 = ring_get(("wout", e_i, 1))

            kaz = [arena[:, r3 + i * NKg:r3 + (i + 1) * NKg] for i in range(2)]
            KZ = Res("kz")
            jobs = []
            for c in range(4):
                for hh in range(2):
                    h = c + 4 * hh
                    base = 64 * hh
                    vs = slice(0, 128) if hh == 0 else slice(64, 192)
                    sink_col = o_es + e_i * 8 + h
                    if not sample:
                        for s in range(2):
                            q0 = s * 256
                            groups = []
                            for kb in range(2):
                                k0 = s * 256 + kb * 128
                                groups.append(dict(k=kaz[hh][:, k0:k0 + 128], q=qa[:, c, q0:q0 + 256], K=KZ, Q=QAH[c][hh][0], Q2=QAH[c][1 - hh][0],
                                                   v=va[:, k0 // 128, vs], V=VA[0], c0=0, n=256, mask=None))
                            jobs.append(dict(groups=groups, finish=(lambda ob, base=base, sink_col=sink_col, c=c, q0=q0, hh=hh: normalize(
                                ob, base, 256, sink_col, qa[base:base + 64, c, q0:q0 + 256], (QAH[c][hh][0],)))))
                    else:
                        for T in range(4):
                            q0 = T * TS
                            groups = []
                            for kb in range(2):
                                groups.append(dict(k=kaz[hh][:, ntok + kb * 128:ntok + (kb + 1) * 128], q=qa[:, c, q0:q0 + TS],
                                                   K=KZ, Q=QAH[c][hh][T], Q2=QAH[c][1 - hh][T], v=va[:, nlt + kb, vs], V=VA[nt], c0=0, n=TS, mask=None))
                            for jb in range(4 * T - 1, 4 * T + 5):
                                if jb < 0 or jb > 15:
                                    continue
                                qlo = max(jb - 1, 4 * T)
                                qhi = min(jb + 1, 4 * T + 3)
                                n = (qhi - qlo + 1) * 128
                                c0 = (qlo - 4 * T) * 128
                                m0 = (qlo - (jb - 1)) * 128
                                k0 = jb * 128
                                groups.append(dict(k=kaz[hh][:, k0:k0 + 128], q=qa[:, c, q0 + c0:q0 + c0 + n],
                                                   K=KZ, Q=QAH[c][hh][T], Q2=QAH[c][1 - hh][T], v=va[:, jb, vs], V=VA[jb // 4],
                                                   c0=c0, n=n, mask=maskb[:, m0:m0 + n]))
                            jobs.append(dict(groups=groups, finish=(lambda ob, base=base, sink_col=sink_col, c=c, q0=q0, hh=hh, T=T: normalize(
                                ob, base, TS, sink_col, qa[base:base + 64, c, q0:q0 + TS], (QAH[c][hh][T],)))))
            side_q, side_o = [], []
            side_x = list(extra or [])

            sidestep = [0]
            armed = [False]

            def pop_side():
                pool_sel[0] = "side"
                sidestep[0] += 1
                if side_q:
                    side_q.pop(0)()
                elif side_x and sidestep[0] % 24 == 0 and armed[0]:
                    side_x.pop(0)()
                elif side_o and (sidestep[0] % 2 == 0 or not sample):
                    side_o.pop(0)[1]()
                pool_sel[0] = "g"

            def flush(lst):
                while lst:
                    it_ = lst.pop(0)
                    (it_[1] if isinstance(it_, tuple) else it_)()

            def flush_tag(tag):
                keep = []
                for it_ in side_o:
                    if it_[0] == tag:
                        it_[1]()
                    else:
                        keep.append(it_)
                side_o[:] = keep

            mscale = float((64 + 32) ** -0.5)
            WQ0, WQ1, WK, WV = 0, 1024, 2048, 2560
            nt6 = nt + (1 if sample else 0)

            if not sample:
                fo_ = o
                PB = []
                for h_ in range(8):
                    PB.append(dict(k=arena[:, fo_:fo_ + 512], v=arena[:, fo_ + 512:fo_ + 1024].rearrange("p (t c) -> p t c", c=128),
                                   q=arena[:, fo_ + 1024:fo_ + 1536], K=Res(f"pk{h_}"), V=Res(f"pv{h_}"), Q=Res(f"pq{h_}")))
                    fo_ += 1536
                obc4 = [arena[:, fo_ + i * 512:fo_ + (i + 1) * 512] for i in range(4)]
                OBC4 = [Res(f"obc4_{i}") for i in range(4)]
                fo_ += 2048
                assert fo_ <= ARENA, fo_

            def hbufs(h):
                if sample:
                    return dict(k=khb, v=vhb, q=qhbs[h % 2], K=KH, V=VH, Q=QHs[h % 2])
                return PB[h]

            def kv_tasks(h):
                hb = hbufs(h)
                khb, vhb, KH, VH = hb["k"], hb["v"], hb["K"], hb["V"]
                bi = h % 2
                vcol = 0 if bi == 0 else 64
                ocol = 64 if bi == 0 else 0
                tasks = []

                def t_init():
                    if not sample:
                        op("dve", lambda e: e.memset(khb[96:128, :], 0.0), writes=(KH,))
                    op("dve", lambda e: e.memset(vhb[:, :, ocol:ocol + 64], 1.0), writes=(VH,))
                    op("dve", lambda e: e.tensor_copy(out=khb[64:96, :], in_=krv[:, :]), reads=tuple(KR), writes=(KH,))
                tasks.append(t_init)
                for t6 in range(nt6):
                    def t_kv(t6=t6):
                        n = TS if t6 < nt else NCTX
                        k0 = t6 * TS
                        b0 = balloc()
                        op("pe", lambda e: e.matmul(banks[b0][:, 0:n], lhsT=g4[:, WK + h * 64:WK + h * 64 + 128],
                                                    rhs=ckvn[:, k0:k0 + n], start=True, stop=True),
                           reads=(G4, CKV[t6]), writes=(BANK[b0],))
                        op("dve", lambda e: e.tensor_copy(out=khb[0:64, k0:k0 + n], in_=banks[b0][0:64, 0:n]),
                           reads=(BANK[b0],), writes=(KH,))
                        bfree(b0)
                        b1 = balloc()
                        ntb = n // 128

                        def pe_v2(e):
                            ins = None
                            for tb in range(ntb):
                                ins = e.matmul(banks[b1][:, tb * 64:(tb + 1) * 64], lhsT=ckvn[:, k0 + tb * 128:k0 + (tb + 1) * 128],
                                               rhs=g4[:, WV + h * 64:WV + (h + 1) * 64], start=True, stop=True, skip_group_check=True)
                            return ins
                        op("pe", pe_v2, reads=(G4, CKV[t6]), writes=(BANK[b1],))
                        op("dve", lambda e: e.tensor_copy(
                            out=vhb[:, 4 * t6:4 * t6 + ntb, vcol:vcol + 64],
                            in_=banks[b1][:, 0:ntb * 64].rearrange("p (t c) -> p t c", c=64)),
                           reads=(BANK[b1],), writes=(VH,))
                        bfree(b1)
                    tasks.append(t_kv)
                return tasks

            def q_tasks(h):
                hb = hbufs(h)
                qhb, QH = hb["q"], hb["Q"]
                tasks = []
                for lt, t in enumerate(tiles):
                    def t_q(lt=lt, t=t):
                        tok = slice(lt * TS, (lt + 1) * TS)
                        b0 = balloc()

                        def pe_q(e):
                            e.matmul(banks[b0][:, :], lhsT=g4[:, WQ0 + h * 128:WQ0 + h * 128 + 128], rhs=cqn[:, 0, tok], start=True, stop=False)
                            return e.matmul(banks[b0][:, :], lhsT=g4[:, WQ1 + h * 128:WQ1 + h * 128 + 128], rhs=cqn[:, 1, tok], start=False, stop=True)
                        op("pe", pe_q, reads=(G4, CQ[lt], KR[lt]), writes=(BANK[b0],))
                        op("dve", lambda e: e.tensor_copy(out=qhb[0:64, tok], in_=banks[b0][0:64, :]),
                           reads=(BANK[b0],), writes=(QH,))
                        if sample:
                            gsl = slice(32 * (t - 1), 32 * (t - 1) + 32)
                            op("dve", lambda e: e.tensor_tensor(out=tmp[0][64:96, :], in0=banks[b0][64:96, :], in1=ropeB_all[gsl, 0, :], op=ALU.mult),
                               reads=(BANK[b0], ROPEB), writes=(TMP[0],))
                            op("dve", lambda e: e.tensor_tensor(out=tmp[1][64:96, :], in0=banks[b0][96:128, :], in1=ropeB_all[gsl, 1, :], op=ALU.mult),
                               reads=(BANK[b0], ROPEB), writes=(TMP[1],))
                            op("dve", lambda e: e.tensor_tensor(out=qhb[64:96, tok], in0=tmp[0][64:96, :], in1=tmp[1][64:96, :], op=ALU.add),
                               reads=(TMP[0], TMP[1]), writes=(QH,))
                        else:
                            op("dve", lambda e: e.tensor_copy(out=qhb[64:96, tok], in_=banks[b0][64:96, :]),
                               reads=(BANK[b0],), writes=(QH,))
                        bfree(b0)
                    tasks.append(t_q)
                return tasks

            def oproj_tasks(kcs, srcs, SRCS):
                tasks = []
                for lt, t in enumerate(tiles):
                    for j in range(8):
                        def t_o(lt=lt, t=t, j=j):
                            tok = slice(lt * TS, (lt + 1) * TS)
                            wt, WR = (wo0, WO0) if j < 4 else (wo1, WO1)
                            jj = j % 4
                            b0 = balloc()

                            def pe_o(e):
                                ins = None
                                for i, kc in enumerate(kcs):
                                    ins = e.matmul(banks[b0][:, :], lhsT=wt[:, kc * 512 + jj * 128:kc * 512 + jj * 128 + 128], rhs=srcs[i](tok),
                                                   start=(i == 0), stop=(i == len(kcs) - 1))
                                return ins
                            op("pe", pe_o, reads=(WR,) + tuple(SRCS(lt)), writes=(BANK[b0],))
                            resid_add(l, 0, j, t, b0)
                            bfree(b0)
                        tasks.append(t_o)
                return tasks

            reserve_side(True)
            trk.barrier()
            op("dve", lambda e: e.memset(arena[:, r3:r3 + 2 * NKg], 0.0), writes=(KZ,))
            op("dve", lambda e: e.tensor_copy(out=kaz[0][0:64, :], in_=ka[0:64, :]), reads=tuple(KA), writes=(KZ,))
            op("act", lambda e: e.activation(out=kaz[1][64:128, :], in_=ka[64:128, :], func=AF.Identity), reads=tuple(KA), writes=(KZ,))
            run_attn(jobs, 0.125, pop_side)
            side_o += [("A", f_) for f_ in oproj_tasks([0, 1, 2, 3], [(lambda tok, c=c: qa[:, c, tok]) for c in range(4)],
                                                       lambda lt: [QAH[c][hh][lt] for c in range(4) for hh in range(2)])]
            trk.barrier()
            if not sample:
                jobs = []
                for h in range(8):
                    hb = hbufs(h)
                    op("dve", lambda e, hb=hb: e.memset(hb["q"][96:128, :], 0.0), writes=(hb["Q"],))
                    flush(kv_tasks(h))
                    flush(q_tasks(h))
                for h in range(8):
                    hb = hbufs(h)
                    bi = h % 2
                    base = 64 * bi
                    cB = h // 2
                    for s_i in range(2):
                        q0 = s_i * 256
                        groups = []
                        for kb in range(2):
                            k0 = s_i * 256 + kb * 128
                            groups.append(dict(k=hb["k"][:, k0:k0 + 128], q=hb["q"][:, q0:q0 + 256], K=hb["K"], Q=hb["Q"],
                                               v=hb["v"][:, k0 // 128, :], V=hb["V"], c0=0, n=256, mask=None))
                        jobs.append(dict(groups=groups, finish=(lambda ob, base=base, cB=cB, q0=q0: normalize(
                            ob, base, 256, None, obc4[cB][base:base + 64, q0:q0 + 256], (OBC4[cB],)))))
                run_attn(jobs, mscale, pop_side)
                side_o += [("B", f_) for f_ in oproj_tasks([4, 5, 6, 7], [(lambda tok, c=c: obc4[c][:, tok]) for c in range(4)],
                                                       lambda lt: list(OBC4))]
            armed[0] = True
            for h in (range(8) if sample else []):
                bi = h % 2
                base = 64 * bi
                cB = h // 2
                if bi == 0 and cB >= 2:
                    flush_tag(cB - 2)
                flush(kv_tasks(h))
                if h == 0:
                    op("dve", lambda e: e.memset(khb[96:128, :], 0.0), writes=(KH,))
                    for i in range(2):
                        op("dve", lambda e, i=i: e.memset(qhbs[i][96:128, :], 0.0), writes=(QHs[i],))
                    side_q += q_tasks(0)
                flush(side_q)
                if h + 1 < 8:
                    side_q += q_tasks(h + 1)
                qhb, QH = qhbs[h % 2], QHs[h % 2]
                oc = obc[cB % 2]
                OC = OBC[cB % 2]
                jobs = []
                if not sample:
                    for s in range(2):
                        q0 = s * 256
                        groups = []
                        for kb in range(2):
                            k0 = s * 256 + kb * 128
                            groups.append(dict(k=khb[:, k0:k0 + 128], q=qhb[:, q0:q0 + 256], K=KH, Q=QH,
                                               v=vhb[:, k0 // 128, :], V=VH, c0=0, n=256, mask=None))
                        jobs.append(dict(groups=groups, finish=(lambda ob, base=base, oc=oc, OC=OC, q0=q0: normalize(
                            ob, base, 256, None, oc[base:base + 64, q0:q0 + 256], (OC,)))))
                else:
                    for T in range(4):
                        q0 = T * TS
                        groups = []
                        for kb in list(range(nlt, nkt)) + list(range(nlt)):
                            k0 = kb * 128
                            groups.append(dict(k=khb[:, k0:k0 + 128], q=qhb[:, q0:q0 + TS], K=KH, Q=QH,
                                               v=vhb[:, kb, :], V=VH, c0=0, n=TS, mask=None))
                        jobs.append(dict(groups=groups, finish=(lambda ob, base=base, oc=oc, OC=OC, q0=q0: normalize(
                            ob, base, TS, None, oc[base:base + 64, q0:q0 + TS], (OC,)))))
                run_attn(jobs, mscale, pop_side)
                if bi == 1:
                    side_o += [(cB, f_) for f_ in oproj_tasks([4 + cB], [(lambda tok, oc=oc: oc[:, tok])], lambda lt, OC=OC: [OC])]
            flush(side_q)
            flush(side_x)
            flush(side_o)
            for s_ in (s4, so0, so1):
                ring_done(s_)
            reserve_side(False)
            reserve_obanks(False)
            trk.barrier()

        def conv_layer(l):
            j_i = l // 2
            o = 0
            upad = arena[:, o:o + 4 * NPAD].rearrange("p (c n) -> p c n", c=4); o += 4 * NPAD
            zpad = arena[:, o:o + 4 * NPAD].rearrange("p (c n) -> p c n", c=4); o += 4 * NPAD
            hts = [arena[:, o + i * 4096:o + (i + 1) * 4096].rearrange("p (k n) -> p k n", k=8) for i in range(2)]
            o += 8192
            assert o <= ARENA, o
            HTs = [Res("ht0"), Res("ht1")]
            U = [Res(f"u{c}") for c in range(4)]
            Z = [Res(f"z{c}") for c in range(4)]
            for (so_, sn_), pb in zip(SEQS, PADB):
                for buf, RS in ((upad, U), (zpad, Z)):
                    op("dve", lambda e, buf=buf, pb=pb: e.memset(buf[:, :, pb:pb + 16], 0.0), writes=tuple(RS))
                    op("dve", lambda e, buf=buf, pb=pb, sn_=sn_: e.memset(buf[:, :, pb + 16 + sn_:pb + 32 + sn_], 0.0), writes=tuple(RS))
            ga, GA, sa = ring_get(("cin", j_i, 0))
            gg, GG, sg = ring_get(("cin", j_i, 1))
            gz, GZ, sz = ring_get(("cin", j_i, 2))
            load_cast(pw[:, :], d_poolw[j_i], 128, 512, (PW,))

            def segs_of_tile(t):
                if t == 0:
                    return [(0, 0, 256), (1, 256, 256)]
                return [(2, t * TS, TS)]

            norm_tile(l, 0, 0, hts[0], HTs[0])
            for t in range(NT):
                ht, HT = hts[t % 2], HTs[t % 2]
                ng = norm_gen(l, 0, t + 1, hts[(t + 1) % 2], HTs[(t + 1) % 2]) if t + 1 < NT else None
                for c in range(4):
                    if ng is not None:
                        for _ in range(2):
                            next(ng, None)
                    ba = balloc(); bg = balloc()
                    for (bk, wt, WR) in ((ba, ga, GA), (bg, gg, GG)):
                        def pe(e, bk=bk, wt=wt, c=c, ht=ht):
                            ins = None
                            for kc in range(8):
                                ins = e.matmul(banks[bk][:, :], lhsT=wt[:, kc * 512 + c * 128:kc * 512 + c * 128 + 128], rhs=ht[:, kc, :],
                                               start=(kc == 0), stop=(kc == 7))
                            return ins
                        op("pe", pe, reads=(WR, HT), writes=(BANK[bk],))
                    ti = c % 2
                    op("act", lambda e, bg=bg, ti=ti: e.activation(out=tmp[ti][:, :], in_=banks[bg][:, :], func=AF.Sigmoid), reads=(BANK[bg],), writes=(TMP[ti],))
                    for (s_, t0_, n) in segs_of_tile(t):
                        p0 = _padpos(t0_)
                        c0 = t0_ - t * TS
                        op("dve", lambda e, ba=ba, c=c, p0=p0, c0=c0, n=n, ti=ti: e.tensor_tensor(
                            out=upad[:, c, p0:p0 + n], in0=banks[ba][:, c0:c0 + n], in1=tmp[ti][:, c0:c0 + n], op=ALU.mult),
                           reads=(BANK[ba], TMP[ti]), writes=(U[c],))
                    bfree(ba); bfree(bg)
                    bz = balloc()

                    def pez(e, bz=bz, c=c, ht=ht):
                        ins = None
                        for kc in range(8):
                            ins = e.matmul(banks[bz][:, :], lhsT=gz[:, kc * 512 + c * 128:kc * 512 + c * 128 + 128], rhs=ht[:, kc, :],
                                           start=(kc == 0), stop=(kc == 7))
                        return ins
                    op("pe", pez, reads=(GZ, HT), writes=(BANK[bz],))
                    for (s_, t0_, n) in segs_of_tile(t):
                        p0 = _padpos(t0_)
                        c0 = t0_ - t * TS
                        op("act", lambda e, bz=bz, c=c, p0=p0, c0=c0, n=n: e.activation(out=zpad[:, c, p0:p0 + n], in_=banks[bz][:, c0:c0 + n], func=AF.Identity),
                           reads=(BANK[bz],), writes=(Z[c],))
                    bfree(bz)
                if ng is not None:
                    for _ in ng:
                        pass
            rst["limit"] = rst["got"] + 2
            for s_ in (sa, sg, sz):
                ring_done(s_)
            wo0, WO0, so0 = ring_get(("cout", j_i, 0))
            wo1, WO1, so1 = ring_get(("cout", j_i, 1))
            free_slots = [i for i in range(NRING) if not rst["held"][i]]
            assert len(free_slots) == 2, free_slots
            for i in free_slots:
                rst["held"][i] = True
            dgs = [ring[i] for i in free_slots]
            DGs = [RING[i] for i in free_slots]
            hcs, HCs = hts, [Res("hc0"), Res("hc1")]
            obdw = PM["bdw"][0] + j_i * 4
            olng = PM["lng"][0] + j_i * 4
            olnb = PM["lnb"][0] + j_i * 4
            odw = PM["dw"][0] + j_i * 4 * 31
            segs = []
            for t in range(NT):
                sl = segs_of_tile(t)
                for i, (s_, t0_, n) in enumerate(sl):
                    segs.append((s_, t0_, n, t, i == len(sl) - 1))
            dgi = [0]

            pending = []

            def build_dg(c):
                di = dgi[0] % 2
                dgi[0] += 1
                dg, DG = dgs[di], DGs[di]
                wc0 = odw + c * 31
                op("dve", lambda e: e.tensor_tensor(
                    out=dg[:, 0:31 * 128].rearrange("p (t m) -> p t m", t=31),
                    in0=ident[:, :].unsqueeze(1).broadcast_to([128, 31, 128]),
                    in1=params[:, wc0:wc0 + 31].unsqueeze(2).broadcast_to([128, 31, 128]), op=ALU.mult),
                   reads=(CONST, P_), writes=(DG,))
                return dg, DG

            def prebuild():
                pending.append(build_dg(0))
                pending.append(build_dg(1))

            def conv_taps(seg):
                s_, t0_, n, t, _ = seg
                p0 = _padpos(t0_)
                cb = [balloc() for _ in range(4)]
                for c in range(4):
                    dg, DG = pending.pop(0) if pending else build_dg(c)

                    def pe(e, c=c, dg=dg):
                        ins = None
                        for tap in range(31):
                            ins = e.matmul(banks[cb[c]][:, 0:n], lhsT=dg[:, tap * 128:(tap + 1) * 128],
                                           rhs=upad[:, c, p0 + tap - 15:p0 + tap - 15 + n], start=(tap == 0), stop=(tap == 30))
                        return ins
                    op("pe", pe, reads=(DG, U[c]), writes=(BANK[cb[c]],))
                return cb

            def evac(seg, cb):
                n = seg[2]
                for c in range(4):
                    op("act", lambda e, c=c: e.activation(out=cvt[c][:, 0:n], in_=banks[cb[c]][:, 0:n], func=AF.Identity,
                                                          bias=params[:, obdw + c:obdw + c + 1], scale=1.0),
                       reads=(BANK[cb[c]], P_), writes=(CVT[c],))
                    bfree(cb[c])

            def pool_seg(seg, hc, HC):
                s_, t0_, n, t, _ = seg
                p0 = _padpos(t0_)
                c0 = t0_ - t * TS
                seq_o, seq_n = SEQS[s_]
                at_start = (t0_ == seq_o)
                at_end = (t0_ + n == seq_o + seq_n)
                for gi, w in enumerate(POOL_W):
                    lo = w // 2
                    hi = w - lo - 1
                    bs_ = balloc(); bz = balloc()

                    def pe(e, gi=gi, lo=lo, hi=hi, bs_=bs_):
                        ins = None
                        for si, sft in enumerate(range(-lo, hi + 1)):
                            ins = e.matmul(banks[bs_][:, 0:n], lhsT=pw[:, gi * 128:(gi + 1) * 128], rhs=zpad[:, gi, p0 + sft:p0 + sft + n],
                                           start=(si == 0), stop=(sft == hi))
                        return ins
                    op("pe", pe, reads=(PW, Z[gi]), writes=(BANK[bs_],))
                    op("pe", lambda e, gi=gi, bz=bz: e.matmul(banks[bz][:, 0:n], lhsT=pw[:, gi * 128:(gi + 1) * 128], rhs=zpad[:, gi, p0:p0 + n], start=True, stop=True),
                       reads=(PW, Z[gi]), writes=(BANK[bz],))
                    pswc = o_psw + j_i * 4 + gi
                    npc = o_np + j_i * 4 + gi
                    op("act", lambda e, bs_=bs_, pswc=pswc: e.activation(out=tmp[2][:, 0:n], in_=banks[bs_][:, 0:n], func=AF.Identity, scale=der[:, pswc:pswc + 1]),
                       reads=(BANK[bs_], DER), writes=(TMP[2],))
                    if at_start and lo > 0:
                        op("dve", lambda e, gi=gi, lo=lo: e.tensor_tensor(out=tmp[2][:, 0:lo], in0=tmp[2][:, 0:lo], in1=pcorr[:, gi * 16:gi * 16 + lo], op=ALU.mult),
                           reads=(TMP[2], CONST), writes=(TMP[2],))
                    if at_end and hi > 0:
                        op("dve", lambda e, gi=gi, hi=hi: e.tensor_tensor(out=tmp[2][:, n - hi:n], in0=tmp[2][:, n - hi:n],
                                                                      in1=pcorr[:, gi * 16 + 16 - hi:gi * 16 + 16], op=ALU.mult),
                           reads=(TMP[2], CONST), writes=(TMP[2],))
                    op("dve", lambda e, gi=gi, bz=bz, npc=npc: e.scalar_tensor_tensor(out=hc[:, 4 + gi, c0:c0 + n], in0=banks[bz][:, 0:n], scalar=der[:, npc:npc + 1],
                                                                               in1=tmp[2][:, 0:n], op0=ALU.mult, op1=ALU.add),
                       reads=(BANK[bz], DER, TMP[2]), writes=(HC,))
                    bfree(bs_); bfree(bz)

            def ln_seg(seg, hc, HC):
                s_, t0_, n, t, _ = seg
                c0 = t0_ - t * TS
                bm = balloc(); bq = balloc()
                for c in range(4):
                    op("act", lambda e, c=c: e.activation(out=sq[0][:, 0:n], in_=cvt[c][:, 0:n], func=AF.Identity), reads=(CVT[c],), writes=(SQ[0],))
                    op("pe", lambda e, c=c: e.matmul(banks[bm][:, 0:n], lhsT=ones1[:, :], rhs=sq[0][:, 0:n], start=(c == 0), stop=(c == 3), skip_group_check=True),
                       reads=(SQ[0], CONST), writes=(BANK[bm],))
                    op("act", lambda e, c=c: e.activation(out=sq[1][:, 0:n], in_=cvt[c][:, 0:n], func=AF.Square), reads=(CVT[c],), writes=(SQ[1],))
                    op("pe", lambda e, c=c: e.matmul(banks[bq][:, 0:n], lhsT=ones1[:, :], rhs=sq[1][:, 0:n], start=(c == 0), stop=(c == 3), skip_group_check=True),
                       reads=(SQ[1], CONST), writes=(BANK[bq],))
                op("act", lambda e: e.activation(out=tmp[0][:, 0:n], in_=banks[bm][:, 0:n], func=AF.Identity, scale=1.0 / 512.0), reads=(BANK[bm],), writes=(TMP[0],))
                op("dve", lambda e: e.tensor_tensor(out=tmp[1][:, 0:n], in0=tmp[0][:, 0:n], in1=tmp[0][:, 0:n], op=ALU.mult), reads=(TMP[0],), writes=(TMP[1],))
                op("dve", lambda e: e.scalar_tensor_tensor(out=tmp[1][:, 0:n], in0=banks[bq][:, 0:n], scalar=1.0 / 512.0, in1=tmp[1][:, 0:n],
                                                           op0=ALU.mult, op1=ALU.subtract), reads=(BANK[bq], TMP[1]), writes=(TMP[1],))
                op("act", lambda e: e.activation(out=rstd[:, 0:n], in_=tmp[1][:, 0:n], func=AF.Ln, bias=epst[:, 0:1], scale=1.0), reads=(TMP[1], CONST), writes=(RSTD,))
                op("act", lambda e: e.activation(out=rstd[:, 0:n], in_=rstd[:, 0:n], func=AF.Exp, scale=-0.5), reads=(RSTD,), writes=(RSTD,))
                bfree(bm); bfree(bq)
                for c in range(4):
                    op("dve", lambda e, c=c: e.tensor_tensor(out=cvt[c][:, 0:n], in0=cvt[c][:, 0:n], in1=tmp[0][:, 0:n], op=ALU.subtract),
                       reads=(CVT[c], TMP[0]), writes=(CVT[c],))
                    op("dve", lambda e, c=c: e.tensor_tensor(out=cvt[c][:, 0:n], in0=cvt[c][:, 0:n], in1=rstd[:, 0:n], op=ALU.mult),
                       reads=(CVT[c], RSTD), writes=(CVT[c],))
                    op("act", lambda e, c=c: e.activation(out=hc[:, c, c0:c0 + n], in_=cvt[c][:, 0:n], func=AF.Silu,
                                                          bias=params[:, olnb + c:olnb + c + 1], scale=params[:, olng + c:olng + c + 1]),
                       reads=(CVT[c], P_), writes=(HC,))

            def outproj(t, hc, HC):
                for j in range(8):
                    wt, WR = (wo0, WO0) if j < 4 else (wo1, WO1)
                    jj = j % 4
                    b0 = balloc()

                    def pe_o(e, b0=b0, wt=wt, jj=jj):
                        ins = None
                        for kc in range(8):
                            ins = e.matmul(banks[b0][:, :], lhsT=wt[:, kc * 512 + jj * 128:kc * 512 + jj * 128 + 128], rhs=hc[:, kc, :],
                                           start=(kc == 0), stop=(kc == 7))
                        return ins
                    op("pe", pe_o, reads=(WR, HC), writes=(BANK[b0],))
                    resid_add(l, 0, j, t, b0)
                    bfree(b0)

            prebuild()
            cb = conv_taps(segs[0])
            prebuild()
            for i, seg in enumerate(segs):
                t = seg[3]
                hc, HC = hcs[t % 2], HCs[t % 2]
                evac(seg, cb)
                if i + 1 < len(segs):
                    cb = conv_taps(segs[i + 1])
                pool_seg(seg, hc, HC)
                ln_seg(seg, hc, HC)
                if i + 2 < len(segs):
                    prebuild()
                if seg[4]:
                    outproj(t, hc, HC)
            for i in free_slots:
                rst["held"][i] = False
            rst["limit"] = None
            ring_done(so0)
            ring_done(so1)
            trk.barrier()

        nsub = 2 * DEPTH if DEBUG_STOP < 0 else DEBUG_STOP
        for g in range(6):
            adaln_group(0, g)
        adaln_finish(0, halves=(0,))
        ada0_rest = [(lambda g=g: adaln_group(0, g)) for g in range(6, 12)] + [lambda: adaln_finish(0, halves=(1,))]
        sub = 0
        for l in range(DEPTH):
            if sub >= nsub:
                break
            if l % 2 == 0:
                attn_group(l, 0)
                attn_group(l, 1, extra=(ada0_rest if l == 0 else None))
            else:
                conv_layer(l)
            sub += 1
            if sub >= nsub:
                break
            nxt = [(l + 1, g) for g in range(12)] if l + 1 < DEPTH else []
            mlp(l, nxt)
            trk.barrier()
            sub += 1

        for t in range(NT):
            tok = slice(t * TS, (t + 1) * TS)
            if DEBUG_STOP >= 0:
                for k in range(8):
                    trk.dma("sp", out_sems.next(), o_yT[:, k, tok], x[:, k, tok], reads=(X[k][t],))
                continue
            b = balloc()
            for k in range(8):
                op("act", lambda e, k=k: e.activation(out=sq[k % 2][:, :], in_=x[:, k, tok], func=AF.Square), reads=(X[k][t],), writes=(SQ[k % 2],))
                op("pe", lambda e, k=k: e.matmul(banks[b][:, :], lhsT=onesm[:, :], rhs=sq[k % 2][:, :], start=(k == 0), stop=(k == 7), skip_group_check=True),
                   reads=(SQ[k % 2], CONST), writes=(BANK[b],))
            op("act", lambda e: e.activation(out=rstd[:, :], in_=banks[b][:, :], func=AF.Ln, bias=epst[:, 0:1], scale=1.0), reads=(BANK[b], CONST), writes=(RSTD,))
            bfree(b)
            op("act", lambda e: e.activation(out=rstd[:, :], in_=rstd[:, :], func=AF.Exp, scale=-0.5), reads=(RSTD,), writes=(RSTD,))
            fo_ = PM["finalg"][0]
            for k in range(8):
                ci = k % 4
                op("dve", lambda e, k=k, ci=ci: e.scalar_tensor_tensor(out=cvt[ci][:, :], in0=x[:, k, tok], scalar=params[:, fo_ + k:fo_ + k + 1],
                                                                     in1=rstd[:, :], op0=ALU.mult, op1=ALU.mult),
                   reads=(X[k][t], RSTD, P_), writes=(CVT[ci],))
                trk.dma("sp", out_sems.next(), o_yT[:, k, tok], cvt[ci][:, :], reads=(CVT[ci],))
        trk.final_wait("sp")
    return nc, wlist


def _count_images():
    n = 0
    for l in range(DEPTH):
        n += 12
        n += 14 if l % 2 == 0 else 5
        n += 16
    return n


N_IMAGES = _count_images()


def _img_k1024(w, col_idx):
    img = np.zeros((128, 8, 512), np.float32)
    col_idx = np.asarray(col_idx)
    valid = col_idx >= 0
    sel = w[:, col_idx[valid]]
    img[:, :, np.nonzero(valid)[0]] = sel.reshape(8, 128, -1).transpose(1, 0, 2)
    return img.reshape(128, SLOT)


def _build_images(wl, inp):
    imgs = np.zeros((N_IMAGES, 128, SLOT), np.float32)
    sw64 = lambda d: (d + 32) % 64
    for n, key in enumerate(wl):
        kind = key[0]
        if kind == "wmod":
            _, l, g = key
            imgs[n] = _img_k1024(inp["w_mod"][l], np.arange(g * 512, (g + 1) * 512))
        elif kind in ("w1",):
            _, l, g = key
            imgs[n] = _img_k1024(inp["mlp_w1"][l], np.arange(g * 512, (g + 1) * 512))
        elif kind == "w2":
            _, l, g = key
            w = inp["mlp_w2"][l][g * 512:(g + 1) * 512]
            imgs[n] = w.reshape(4, 128, 1024).transpose(1, 0, 2).reshape(128, SLOT)
        elif kind == "win":
            _, e, g = key
            w = inp["attn_w_in"][e]
            if g == 0:
                idx = -np.ones(512, np.int64)
                idx[0:192] = 768 + np.arange(192)
                idx[192:320] = 960 + np.arange(128)
                idx[320:352] = 1088 + np.arange(32)
                idx[352:384] = 1088 + (np.arange(32) + 16) % 32
            elif g in (1, 2):
                idx = np.zeros(512, np.int64)
                for c in range(4):
                    for p in range(128):
                        h = c if p < 64 else 4 + c
                        d = p % 64
                        if g == 2:
                            d = sw64(d)
                        idx[c * 128 + p] = h * 64 + d
            else:
                idx = -np.ones(512, np.int64)
                for p in range(128):
                    kh, d = p // 64, p % 64
                    idx[p] = 512 + kh * 64 + d
                    idx[128 + p] = 512 + kh * 64 + sw64(d)
                    idx[256 + p] = 640 + p
            imgs[n] = _img_k1024(w, idx)
        elif kind == "wqkv":
            _, e = key
            img = np.zeros((128, SLOT), np.float32)
            wq = inp["mla_w_qb"][e]
            qcols = np.zeros((8, 128), np.int64)
            for h in range(8):
                qcols[h, 0:64] = h * 96 + np.arange(64)
                qcols[h, 64:96] = h * 96 + 64 + np.arange(32)
                qcols[h, 96:128] = h * 96 + 64 + (np.arange(32) + 16) % 32
            wqa = wq[:, qcols.reshape(-1)]
            img[:, 0:1024] = wqa[0:128]
            img[0:64, 1024:2048] = wqa[128:192]
            wkv = inp["mla_w_kvb"][e]
            for h in range(8):
                img[:, 2048 + h * 64:2048 + (h + 1) * 64] = wkv[:, h * 128:h * 128 + 64]
                img[:, 2560 + h * 64:2560 + (h + 1) * 64] = wkv[:, h * 128 + 64:h * 128 + 128]
            imgs[n] = img
        elif kind == "wout":
            _, e, g = key
            w = inp["attn_w_out"][e]
            rows = np.zeros(1024, np.int64)
            for c in range(4):
                for p in range(128):
                    h = c if p < 64 else 4 + c
                    rows[c * 128 + p] = h * 64 + p % 64
            for c in range(4):
                for p in range(128):
                    h = 2 * c + (p // 64)
                    rows[512 + c * 128 + p] = 512 + h * 64 + p % 64
            wp = w[rows]
            imgs[n] = _img_k1024(wp, np.arange(g * 512, (g + 1) * 512))
        elif kind == "cin":
            _, j, g = key
            imgs[n] = _img_k1024(inp["conv_w_in"][j], np.arange(g * 512, (g + 1) * 512))
        elif kind == "cout":
            _, j, g = key
            imgs[n] = _img_k1024(inp["conv_w_out"][j], np.arange(g * 512, (g + 1) * 512))
        else:
            raise KeyError(key)
    return imgs


_CACHE = {}


def kernel(**inputs):
    inp = {k: np.asarray(v) for k, v in inputs.items()}
    if "prog" not in _CACHE:
        _CACHE["prog"] = build_program()
    nc, wl = _CACHE["prog"]
    assert len(wl) == N_IMAGES or DEBUG_STOP >= 0, (len(wl), N_IMAGES)
    consts = _consts()
    imgs = _build_images(wl, inp)

    def fm(v):
        return np.ascontiguousarray(v.reshape(8, 128).T)

    poolw = np.ascontiguousarray(inp["pool_w"].transpose(0, 2, 1, 3).reshape(2, 128, 512))
    dwp = np.zeros((128, 2, 4, 31), np.float32)
    for j in range(2):
        for c in range(4):
            dwp[:, j, c, :] = inp["conv_dw"][j][:, c * 128:(c + 1) * 128].T
    in_maps = []
    for i in range(8):
        toks = np.concatenate([inp["x_prompt"][2 * i], inp["x_prompt"][2 * i + 1], inp["x_sample"][i]], axis=0)
        xT = np.ascontiguousarray(toks.reshape(NTOK, 8, 128).transpose(2, 1, 0))
        P = np.zeros((128, PM["_n"]), np.float32)

        def put(name, arr):
            o, n = PM[name]
            P[:, o:o + n] = arr.reshape(128, n)
        put("bmod", np.stack([inp["b_mod"][l].reshape(48, 128).T for l in range(4)], 1))
        put("normg", np.stack([np.stack([fm(inp["norm_g"][l, w]) for w in range(2)], 1) for l in range(4)], 1))
        put("finalg", fm(inp["final_g"]))
        put("cT", np.stack([fm(inp["c_ctx"]), fm(inp["c"][i])], 2))
        qn = np.zeros((128, 2, 2), np.float32)
        for e in range(2):
            qn[:, e, 0] = inp["mla_q_norm"][e, 0:128]
            qn[0:64, e, 1] = inp["mla_q_norm"][e, 128:192]
        put("qnorm", qn)
        put("kvnorm", np.stack([inp["mla_kv_norm"][e] for e in range(2)], 1))
        put("sink", np.broadcast_to(inp["attn_sink"].reshape(1, 16), (128, 16)))
        for nm, src in (("bdw", "conv_dw_b"), ("lng", "conv_ln_g"), ("lnb", "conv_ln_b"), ("pscale", "pool_scale")):
            put(nm, np.stack([inp[src][j].reshape(4, 128).T for j in range(2)], 1))
        put("dw", dwp)
        m = {
            "xT": xT, "params": P, "wstream": imgs,
            "ropeA": consts["ropeA"], "ropeB": consts["ropeB"], "maskb": consts["maskb"], "ident": consts["ident"],
            "pcorr": consts["pcorr"],
            "ckT": np.ascontiguousarray(inp["cache_win_k"][i].reshape(2, NCTX, 128).transpose(0, 2, 1)),
            "cv": np.ascontiguousarray(inp["cache_win_v"][i].reshape(2, 2, 128, 128)),
            "cckvT": np.ascontiguousarray(inp["cache_mla_ckv"][i].transpose(0, 2, 1)),
            "ckrT": np.ascontiguousarray(inp["cache_mla_krope"][i].transpose(0, 2, 1)),
            "poolw": poolw,
        }
        in_maps.append(m)
    res = run_bass_kernel_spmd(nc, in_maps, core_ids=list(range(8)))
    R = res.results
    y_prompt = np.zeros((16, 256, D), np.float32)
    y_sample = np.zeros((8, 2048, D), np.float32)
    nk = np.zeros((16, 2, 256, 2, 64), np.float32)
    nv = np.zeros((16, 2, 256, 2, 64), np.float32)
    nckv = np.zeros((16, 2, 256, 128), np.float32)
    nkr = np.zeros((16, 2, 256, 32), np.float32)
    for i in range(8):
        r = R[i]
        y = np.asarray(r["yT"]).transpose(2, 1, 0).reshape(NTOK, D)
        y_prompt[2 * i] = y[0:256]
        y_prompt[2 * i + 1] = y[256:512]
        y_sample[i] = y[512:]
        kT = np.asarray(r["okT"])
        v = np.asarray(r["ov"])
        ck = np.asarray(r["ockvT"])
        kr = np.asarray(r["okrT"])
        for s in range(2):
            b = 2 * i + s
            for e in range(2):
                nk[b, e] = kT[e][:, s * 256:(s + 1) * 256].T.reshape(256, 2, 64)
                nv[b, e] = v[e][s * 256:(s + 1) * 256].reshape(256, 2, 64)
                nckv[b, e] = ck[e][:, s * 256:(s + 1) * 256].T
                nkr[b, e] = kr[e][:, s * 256:(s + 1) * 256].T
    return (y_prompt, y_sample, nk, nv, nckv, nkr)
```

```python
import contextlib
import os
import numpy as np
import concourse.bass as bass
import concourse.mybir as mybir
from concourse.bass_utils import run_bass_kernel_spmd

F32 = mybir.dt.float32
BF16 = mybir.dt.bfloat16
AF = mybir.ActivationFunctionType
ALU = mybir.AluOpType

D = 1024
DEPTH = 4
NTOK = 2560
NT = 5
TS = 512
NCTX = 256
EPS = 1e-6
NEG = -30000.0
SHIFT = 10.0
SLOT = 4096
NRING = 4
SEM_LIMIT = 30000

DEBUG_STOP = int(os.environ.get("KDEBUG_STOP", "-1"))


class Res:
    __slots__ = ("name", "w", "rd")

    def __init__(self, name):
        self.name = name
        self.w = None
        self.rd = {}


class Eng:
    def __init__(self, trk, name, handle, inc):
        self.trk = trk
        self.name = name
        self.h = handle
        self.inc = inc
        self.sem = None
        self.count = 0
        self.seen = {}
        self.sems = []

    def new_sem(self):
        self.sem = self.trk.alloc_sem(self.name)
        self.sems.append(self.sem)
        self.count = 0


class Tracker:
    def __init__(self, nc, es):
        self.nc = nc
        self.es = es
        self.nsem = 0
        self.E = {}
        for name, h, inc in (("pe", nc.tensor, 1), ("act", nc.scalar, 1), ("dve", nc.vector, 1),
                             ("pool", nc.gpsimd, 1)):
            e = Eng(self, name, h, inc)
            e.new_sem()
            self.E[name] = e
        self.Q = {"sp": Eng(self, "sp", nc.sync, 16), "poolq": self.E["pool"]}
        self.all_events = {}

    def alloc_sem(self, name):
        self.nsem += 1
        return self.es.enter_context(self.nc.semaphore(f"s_{name}_{self.nsem}"))

    def _wait(self, eng, ev):
        if ev is None:
            return
        sem, val, src = ev
        if src == "pe" and eng.name == "pe":
            return
        k = id(sem)
        if eng.seen.get(k, 0) >= val:
            return
        eng.seen[k] = val
        eng.h.wait_ge(sem, val)

    def _deps(self, eng, reads, writes, same_ok=False):
        for r in reads:
            if r.w is not None:
                self._wait(eng, r.w)
        for r in writes:
            if r.w is not None:
                self._wait(eng, r.w)
            for ev in r.rd.values():
                self._wait(eng, ev)

    def _record(self, ev, reads, writes):
        for r in reads:
            r.rd[id(ev[0])] = ev
        for r in writes:
            r.w = ev
            r.rd = {}
        self.all_events[id(ev[0])] = ev

    def op(self, ename, fn, reads=(), writes=()):
        eng = self.E[ename]
        if eng.count >= SEM_LIMIT:
            eng.new_sem()
        self._deps(eng, reads, writes)
        ins = fn(eng.h)
        eng.count += 1
        ins.then_inc(eng.sem, 1)
        ev = (eng.sem, eng.count, ename)
        self._record(ev, reads, writes)
        return ev

    def dma(self, qname, dsem, out, in_, reads=(), writes=()):
        q = self.Q[qname]
        self._deps(q, reads, writes)
        if dsem.last is not None:
            self._wait(q, dsem.last)
        if dsem.count + 16 > SEM_LIMIT:
            dsem.sem = self.alloc_sem("dma")
            dsem.count = 0
        ins = q.h.dma_start(out=out, in_=in_)
        dsem.count += 16
        ins.then_inc(dsem.sem, 16)
        ev = (dsem.sem, dsem.count, "dma")
        dsem.last = ev
        self._record(ev, reads, writes)
        return ev

    def barrier(self):
        evs = list(self.all_events.values())
        for e in list(self.E.values()) + [self.Q["sp"]]:
            for ev in evs:
                self._wait(e, ev)

    def final_wait(self, ename="sp"):
        q = self.Q[ename]
        for ev in list(self.all_events.values()):
            self._wait(q, ev)


class DmaSem:
    def __init__(self, trk, name):
        self.sem = trk.alloc_sem(name)
        self.count = 0
        self.last = None


class DmaSemPool:
    def __init__(self, trk, n, name):
        self.s = [DmaSem(trk, f"{name}{i}") for i in range(n)]
        self.i = 0

    def next(self):
        s = self.s[self.i % len(self.s)]
        self.i += 1
        return s


def _rope_tables(n, dim, grid_w=64, base=10000.0):
    rows = n // grid_w
    row = np.repeat(np.arange(rows), grid_w).astype(np.float32)
    col = np.tile(np.arange(grid_w), rows).astype(np.float32)
    quarter = dim // 4
    inv_freq = (base ** (-np.arange(quarter, dtype=np.float32) / quarter)).astype(np.float32)
    ang = np.concatenate([row[:, None] * inv_freq, col[:, None] * inv_freq], axis=-1).astype(np.float32)
    cos = np.cos(ang).astype(np.float32)
    sin = np.sin(ang).astype(np.float32)
    half = dim // 2
    cos2 = np.concatenate([cos, cos], axis=1).T
    sins = np.concatenate([-sin, sin], axis=1).T
    return np.ascontiguousarray(cos2), np.ascontiguousarray(sins)


def _consts():
    c = {}
    ca, sa = _rope_tables(2048, 64)
    c["ropeA"] = np.ascontiguousarray(np.stack([np.concatenate([ca, ca], 0), np.concatenate([sa, sa], 0)], 1))
    cb, sb = _rope_tables(2048, 32)
    c["ropeB"] = np.ascontiguousarray(np.stack([cb, sb], 1))
    b = np.arange(128)[:, None]
    a = np.arange(128)[None, :]
    m = np.zeros((128, 384), np.float32)
    m[:, 0:128] = np.where(b <= a, 0.0, NEG)
    m[:, 256:384] = np.where(a <= b, 0.0, NEG)
    c["maskb"] = m
    c["ident"] = np.eye(128, dtype=np.float32)
    corr = np.ones((128, 4, 2, 8), np.float32)
    for gi, w in enumerate((2, 4, 8, 16)):
        lo = w // 2
        hi = w - lo - 1
        for t in range(lo):
            corr[:, gi, 0, t] = w / float(t + hi + 1)
        for q in range(hi):
            corr[:, gi, 1, 7 - q] = w / float(lo + q + 1)
    c["pcorr"] = corr.reshape(128, 64)
    return c


def _pmap():
    m = {}
    o = 0

    def add(name, n):
        nonlocal o
        m[name] = (o, n)
        o += n
    add("bmod", 4 * 48)
    add("normg", 4 * 2 * 8)
    add("finalg", 8)
    add("cT", 16)
    add("qnorm", 4)
    add("kvnorm", 2)
    add("sink", 16)
    add("bdw", 8)
    add("lng", 8)
    add("lnb", 8)
    add("pscale", 8)
    add("dw", 2 * 4 * 31)
    m["_n"] = o
    return m


PM = _pmap()

POOL_W = (2, 4, 8, 16)
SEQS = [(0, 256), (256, 256), (512, 2048)]
PADB = [0, 288, 576]
NPAD = 2656


def _padpos(tok):
    for s, (o, n) in enumerate(SEQS):
        if o <= tok < o + n:
            return PADB[s] + 16 + (tok - o)
    raise ValueError


ARENA = 29568


def build_program():
    nc = bass.Bass("TRN2", target_bir_lowering=False)
    es = contextlib.ExitStack()
    wlist = []

    def dram(name, shape, kind="ExternalInput", dt=F32):
        return nc.dram_tensor(name, list(shape), dt, kind=kind).ap()

    d_xT = dram("xT", [128, 8, NTOK])
    d_params = dram("params", [128, PM["_n"]])
    d_ropeA = dram("ropeA", [128, 2, 2048])
    d_ropeB = dram("ropeB", [32, 2, 2048])
    d_maskb = dram("maskb", [128, 384])
    d_ident = dram("ident", [128, 128])
    d_pcorr = dram("pcorr", [128, 64])
    d_ckT = dram("ckT", [2, 128, NCTX])
    d_cv = dram("cv", [2, 2, 128, 128])
    d_cckvT = dram("cckvT", [2, 128, NCTX])
    d_ckrT = dram("ckrT", [2, 32, NCTX])
    d_poolw = dram("poolw", [2, 128, 4 * 128])
    d_w = dram("wstream", [N_IMAGES, 128, SLOT])
    o_yT = dram("yT", [128, 8, NTOK], kind="ExternalOutput")
    o_kT = dram("okT", [2, 128, 512], kind="ExternalOutput")
    o_v = dram("ov", [2, 512, 128], kind="ExternalOutput")
    o_ckvT = dram("ockvT", [2, 128, 512], kind="ExternalOutput")
    o_krT = dram("okrT", [2, 32, 512], kind="ExternalOutput")

    with es:
        trk = Tracker(nc, es)
        op = trk.op

        def sb(name, shape, dt):
            return es.enter_context(nc.sbuf_tensor(name, list(shape), dt))

        x = sb("x", [128, 8, NTOK], F32)
        X = [[Res(f"x{k}_{t}") for t in range(NT)] for k in range(8)]
        ring = [sb(f"ring{i}", [128, SLOT], BF16) for i in range(NRING)]
        RING = [Res(f"ring{i}") for i in range(NRING)]
        ring_sem = [DmaSem(trk, f"ring{i}") for i in range(NRING)]
        arena = sb("arena", [128, ARENA], BF16)
        params = sb("params_sb", [128, PM["_n"]], F32)
        P_ = Res("params")
        NDER = 4 * 6 * 8 * 2 + 64
        der = sb("der", [128, NDER], F32)
        DER = Res("der")
        mod = sb("mod", [128, 4, 48, 2], F32)
        MOD = [Res(f"mod{l}") for l in range(4)]
        scT = sb("scT", [128, 8, 2], BF16)
        SCT = Res("scT")
        ident = sb("ident_sb", [128, 128], BF16)
        onesm = sb("onesm", [128, 128], BF16)
        ones1 = sb("ones1", [128, 128], BF16)
        ones_lo = sb("ones_lo", [128, 128], BF16)
        maskb = sb("maskb_sb", [128, 384], BF16)
        pcorr = sb("pcorr_sb", [128, 64], F32)
        epst = sb("epst", [128, 1], F32)
        negc = sb("negc", [128, 1], F32)
        pw = sb("poolw_sb", [128, 512], BF16)
        PW = Res("pw")
        CONST = Res("const")
        rstd = sb("rstd", [128, 512], F32)
        RSTD = Res("rstd")
        sq = [sb(f"sq{i}", [128, 512], BF16) for i in range(2)]
        SQ = [Res(f"sq{i}") for i in range(2)]
        ft = [sb(f"ft{i}", [128, 512], F32) for i in range(7)]
        FT = [Res(f"ft{i}") for i in range(7)]
        tmp, TMP = ft[0:3], FT[0:3]
        cvt, CVT = ft[3:7], FT[3:7]
        rec, REC = ft[3], FT[3]
        ostage, OST = ft[4:6], FT[4:6]
        stage, STAGE = ft[6], FT[6]
        pt = [sb(f"pt{i}", [128, 512], BF16) for i in range(3)]
        PT = [Res(f"pt{i}") for i in range(3)]
        rope_t = sb("rope_t", [128, 2, TS], F32)
        ROPE = Res("rope")
        ropeB_all = sb("ropeB_all", [128, 2, TS], F32)
        ROPEB = Res("ropeB")
        ost_i = [0]

        banks = [es.enter_context(nc.psum_tensor(f"bank{i}", [128, 512], F32)) for i in range(8)]
        BANK = [Res(f"bank{i}") for i in range(8)]
        bank_free = list(range(8))

        side_free = []
        pool_sel = ["g"]

        def balloc():
            if pool_sel[0] == "side":
                assert side_free, "out of side PSUM banks"
                return side_free.pop(0)
            assert bank_free, "out of PSUM banks"
            return bank_free.pop(0)

        def bfree(b):
            if b in SIDE_POOL and pool_sel[0] == "side":
                side_free.append(b)
            else:
                bank_free.append(b)

        SIDE_POOL = [4, 5]

        def reserve_side(on):
            if on:
                for b in SIDE_POOL:
                    bank_free.remove(b)
                    side_free.append(b)
            else:
                for b in SIDE_POOL:
                    side_free.remove(b)
                    bank_free.append(b)

        sp_sems = DmaSemPool(trk, 6, "sp")
        out_sems = DmaSemPool(trk, 4, "out")

        rst = {"issued": 0, "got": 0, "held": [False] * NRING, "limit": None}

        def ring_pump():
            while rst["issued"] < N_IMAGES and rst["issued"] < rst["got"] + NRING:
                if rst["limit"] is not None and rst["issued"] >= rst["limit"]:
                    break
                i = rst["issued"]
                free = [k for k in range(NRING) if not rst["held"][k]]
                if not free:
                    break
                s_ = i % NRING if (i % NRING) in free else free[0]
                trk.dma("poolq", ring_sem[s_], ring[s_][:, :], d_w[i], reads=(), writes=(RING[s_],))
                rst["held"][s_] = True
                rst.setdefault("slot_of", {})[i] = s_
                rst["issued"] += 1

        def ring_get(key):
            i = rst["got"]
            wlist.append(key)
            ring_pump()
            assert rst["issued"] > i, ("ring stalled", key)
            rst["got"] += 1
            s_ = rst["slot_of"][i]
            return ring[s_], RING[s_], s_

        def ring_done(s_):
            rst["held"][s_] = False
            ring_pump()

        trk.dma("sp", sp_sems.next(), params[:, :], d_params[:, :], writes=(P_,))
        for k in range(8):
            trk.dma("sp", sp_sems.next(), x[:, k, :], d_xT[:, k, :], writes=tuple(X[k]))

        def load_cast(dst_ap, src_ap, nparts, ncols, wres):
            trk.dma("sp", sp_sems.next(), stage[0:nparts, 0:ncols], src_ap, writes=(STAGE,))
            op("dve", lambda e: e.tensor_copy(out=dst_ap, in_=stage[0:nparts, 0:ncols]), reads=(STAGE,), writes=wres)

        load_cast(ident[:, :], d_ident[:, :], 128, 128, (CONST,))
        load_cast(maskb[:, :], d_maskb[:, :], 128, 384, (CONST,))
        trk.dma("sp", sp_sems.next(), pcorr[:, :], d_pcorr[:, :], writes=(CONST,))
        op("dve", lambda e: e.memset(onesm[:, :], 1.0 / 1024.0), writes=(CONST,))
        op("dve", lambda e: e.memset(ones1[:, :], 1.0), writes=(CONST,))
        op("dve", lambda e: e.memset(ones_lo[:, :], 0.0), writes=(CONST,))
        op("dve", lambda e: e.memset(ones_lo[0:64, :], 1.0), writes=(CONST,))
        op("dve", lambda e: e.memset(epst[:, :], EPS), writes=(CONST,))
        op("dve", lambda e: e.memset(negc[:, :], -SHIFT), writes=(CONST,))

        o_cT = PM["cT"][0]
        op("act", lambda e: e.activation(out=scT[:, :, :], in_=params[:, o_cT:o_cT + 16].rearrange("p (k j) -> p k j", j=2),
                                         func=AF.Silu), reads=(P_,), writes=(SCT,))

        def dcol(l, which, k, g):
            return ((l * 6 + which) * 8 + k) * 2 + g

        o_es = 4 * 6 * 8 * 2
        o_np = o_es + 16
        o_psw = o_np + 8
        o_sink = PM["sink"][0]
        op("act", lambda e: e.activation(out=der[:, o_es:o_es + 16], in_=params[:, o_sink:o_sink + 16], func=AF.Exp, bias=negc[:, 0:1], scale=1.0),
           reads=(P_, CONST), writes=(DER,))
        o_ps = PM["pscale"][0]
        op("dve", lambda e: e.tensor_scalar(out=der[:, o_np:o_np + 8], in0=params[:, o_ps:o_ps + 8], scalar1=-1.0,
                                            scalar2=None, op0=ALU.mult), reads=(P_,), writes=(DER,))
        for j in range(2):
            for gi, w in enumerate(POOL_W):
                op("dve", lambda e, j=j, gi=gi, w=w: e.tensor_scalar(
                    out=der[:, o_psw + j * 4 + gi:o_psw + j * 4 + gi + 1],
                    in0=params[:, o_ps + j * 4 + gi:o_ps + j * 4 + gi + 1], scalar1=1.0 / w, scalar2=None,
                    op0=ALU.mult), reads=(P_,), writes=(DER,))

        def adaln_group(l, g):
            wt, WR, ws = ring_get(("wmod", l, g))
            b = balloc()

            def pe(e):
                ins = None
                for cc in range(4):
                    for kc in range(8):
                        ins = e.matmul(banks[b][:, cc * 2:cc * 2 + 2], lhsT=wt[:, kc * 512 + cc * 128:kc * 512 + cc * 128 + 128],
                                       rhs=scT[:, kc, :], start=(kc == 0), stop=(kc == 7), skip_group_check=True)
                return ins
            op("pe", pe, reads=(WR, SCT), writes=(BANK[b],))
            ring_done(ws)
            ob = PM["bmod"][0] + l * 48 + 4 * g
            for j in range(2):
                op("dve", lambda e, j=j: e.tensor_tensor(
                    out=mod[:, l, 4 * g:4 * g + 4, j],
                    in0=banks[b][:, 0:8].rearrange("p (c j) -> p c j", j=2)[:, :, j],
                    in1=params[:, ob:ob + 4], op=ALU.add), reads=(BANK[b], P_), writes=(MOD[l],))
            bfree(b)

        def adaln_finish(l, halves=(0, 1)):
            og = PM["normg"][0]
            for half in halves:
                for k in range(8):
                    gc = og + (l * 2 + half) * 8 + k
                    a0 = dcol(l, half * 3 + 0, k, 0)
                    op("dve", lambda e, k=k, half=half, gc=gc, a0=a0: e.tensor_scalar(
                        out=der[:, a0:a0 + 2], in0=mod[:, l, (half * 3 + 1) * 8 + k, :], scalar1=1.0, scalar2=params[:, gc:gc + 1],
                        op0=ALU.add, op1=ALU.mult), reads=(MOD[l], P_), writes=(DER,))
                    b0 = dcol(l, half * 3 + 1, k, 0)
                    op("dve", lambda e, k=k, half=half, b0=b0: e.tensor_copy(
                        out=der[:, b0:b0 + 2], in_=mod[:, l, (half * 3 + 0) * 8 + k, :]), reads=(MOD[l],), writes=(DER,))
                    g0 = dcol(l, half * 3 + 2, k, 0)
                    op("dve", lambda e, k=k, half=half, g0=g0: e.tensor_copy(
                        out=der[:, g0:g0 + 2], in_=mod[:, l, (half * 3 + 2) * 8 + k, :]), reads=(MOD[l],), writes=(DER,))

        def dv(l, which, k, g):
            c = dcol(l, which, k, g)
            return der[:, c:c + 1]

        def norm_tile(l, half, t, dest, DEST):
            grp = 0 if t == 0 else 1
            tok = slice(t * TS, (t + 1) * TS)
            b = balloc()
            for k in range(8):
                op("act", lambda e, k=k: e.activation(out=sq[k % 2][:, :], in_=x[:, k, tok], func=AF.Square),
                   reads=(X[k][t],), writes=(SQ[k % 2],))
                op("pe", lambda e, k=k: e.matmul(banks[b][:, :], lhsT=onesm[:, :], rhs=sq[k % 2][:, :], start=(k == 0),
                                                 stop=(k == 7), skip_group_check=True),
                   reads=(SQ[k % 2], CONST), writes=(BANK[b],))
            op("act", lambda e: e.activation(out=rstd[:, :], in_=banks[b][:, :], func=AF.Ln, bias=epst[:, 0:1], scale=1.0),
               reads=(BANK[b], CONST), writes=(RSTD,))
            bfree(b)
            op("act", lambda e: e.activation(out=rstd[:, :], in_=rstd[:, :], func=AF.Exp, scale=-0.5), reads=(RSTD,), writes=(RSTD,))
            for k in range(8):
                ti = k % 2
                op("dve", lambda e, k=k, ti=ti: e.tensor_tensor(out=tmp[ti][:, :], in0=x[:, k, tok], in1=rstd[:, :], op=ALU.mult),
                   reads=(X[k][t], RSTD), writes=(TMP[ti],))
                op("act", lambda e, k=k, ti=ti: e.activation(out=dest[:, k, :], in_=tmp[ti][:, :], func=AF.Identity,
                                                           bias=dv(l, half * 3 + 1, k, grp), scale=dv(l, half * 3 + 0, k, grp)),
                   reads=(TMP[ti], DER), writes=(DEST,))

        def norm_gen(l, half, t, dest, DEST):
            grp = 0 if t == 0 else 1
            tokx = slice(t * TS, (t + 1) * TS)
            b = balloc()

            def mm_(k):
                op("pe", lambda e: e.matmul(banks[b][:, :], lhsT=onesm[:, :], rhs=sq[k % 2][:, :], start=(k == 0),
                                            stop=(k == 7), skip_group_check=True),
                   reads=(SQ[k % 2], CONST), writes=(BANK[b],))
            for p_ in range(5):
                if p_ >= 1:
                    mm_(2 * p_ - 2)
                    mm_(2 * p_ - 1)
                if p_ < 4:
                    for k in (2 * p_, 2 * p_ + 1):
                        op("act", lambda e, k=k: e.activation(out=sq[k % 2][:, :], in_=x[:, k, tokx], func=AF.Square),
                           reads=(X[k][t],), writes=(SQ[k % 2],))
                    yield
            op("act", lambda e: e.activation(out=rstd[:, :], in_=banks[b][:, :], func=AF.Ln, bias=epst[:, 0:1], scale=1.0),
               reads=(BANK[b], CONST), writes=(RSTD,))
            bfree(b)
            op("act", lambda e: e.activation(out=rstd[:, :], in_=rstd[:, :], func=AF.Exp, scale=-0.5), reads=(RSTD,), writes=(RSTD,))
            yield
            for k in range(8):
                ti = k % 2
                op("dve", lambda e, k=k, ti=ti: e.tensor_tensor(out=tmp[ti][:, :], in0=x[:, k, tokx], in1=rstd[:, :], op=ALU.mult),
                   reads=(X[k][t], RSTD), writes=(TMP[ti],))
                op("act", lambda e, k=k, ti=ti: e.activation(out=dest[:, k, :], in_=tmp[ti][:, :], func=AF.Identity,
                                                           bias=dv(l, half * 3 + 1, k, grp), scale=dv(l, half * 3 + 0, k, grp)),
                   reads=(TMP[ti], DER), writes=(DEST,))
                if k % 4 == 3:
                    yield

        def drive(gens):
            gens = [g_ for g_ in gens if g_ is not None]
            while gens:
                for g_ in list(gens):
                    try:
                        next(g_)
                    except StopIteration:
                        gens.remove(g_)

        def resid_add(l, half, j, t, b):
            grp = 0 if t == 0 else 1
            xs = x[:, j, t * TS:(t + 1) * TS]
            op("dve", lambda e: e.scalar_tensor_tensor(out=xs, in0=banks[b][:, :], scalar=dv(l, half * 3 + 2, j, grp),
                                                       in1=xs, op0=ALU.mult, op1=ALU.add),
               reads=(BANK[b], DER, X[j][t]), writes=(X[j][t],))

        def mlp(l, next_adaln):
            hbuf = arena[:, 0:8 * NTOK].rearrange("p (k n) -> p k n", k=8)
            H = [Res(f"h{t}") for t in range(NT)]
            h1 = [arena[:, 8 * NTOK + i * 2048:8 * NTOK + (i + 1) * 2048].rearrange("p (c n) -> p c n", c=4) for i in range(2)]
            H1 = [Res("h1a"), Res("h1b")]
            norm_tile(l, 1, 0, hbuf[:, :, 0:TS], H[0])
            ada = list(next_adaln)
            seq = [(g, t) for g in range(8) for t in range(NT)]
            w1s, w2s = {}, {}

            def h1stage(k):
                g, t = seq[k]
                hi = k % 2
                if t == 0:
                    w1s[g] = ring_get(("w1", l, g))
                w1, W1, s1 = w1s[g]
                ng = None
                if g == 0 and t + 1 < NT:
                    ng = norm_gen(l, 1, t + 1, hbuf[:, :, (t + 1) * TS:(t + 2) * TS], H[t + 1])
                for c in range(4):
                    if ng is not None:
                        for _ in range(2):
                            next(ng, None)
                    b_ = balloc()

                    def pe(e, c=c, b_=b_):
                        ins = None
                        for kc in range(8):
                            ins = e.matmul(banks[b_][:, :], lhsT=w1[:, kc * 512 + c * 128:kc * 512 + c * 128 + 128],
                                           rhs=hbuf[:, kc, t * TS:(t + 1) * TS], start=(kc == 0), stop=(kc == 7))
                        return ins
                    op("pe", pe, reads=(W1, H[t]), writes=(BANK[b_],))
                    ti = c % 2
                    op("act", lambda e, b_=b_, ti=ti: e.activation(out=tmp[ti][:, :], in_=banks[b_][:, :], func=AF.Relu),
                       reads=(BANK[b_],), writes=(TMP[ti],))
                    bfree(b_)
                    op("dve", lambda e, c=c, ti=ti: e.tensor_tensor(out=h1[hi][:, c, :], in0=tmp[ti][:, :], in1=tmp[ti][:, :],
                                                                    op=ALU.mult), reads=(TMP[ti],), writes=(H1[hi],))
                if t == NT - 1:
                    ring_done(s1)
                if ng is not None:
                    for _ in ng:
                        pass

            def outstage(k):
                g, t = seq[k]
                hi = k % 2
                if t == 0:
                    w2s[g] = ring_get(("w2", l, g))
                w2, W2, s2 = w2s[g]
                for j in range(8):
                    b_ = balloc()

                    def pe2(e, j=j, b_=b_):
                        ins = None
                        for c in range(4):
                            ins = e.matmul(banks[b_][:, :], lhsT=w2[:, c * 1024 + j * 128:c * 1024 + j * 128 + 128],
                                           rhs=h1[hi][:, c, :], start=(c == 0), stop=(c == 3))
                        return ins
                    op("pe", pe2, reads=(W2, H1[hi]), writes=(BANK[b_],))
                    resid_add(l, 1, j, t, b_)
                    bfree(b_)
                if t == NT - 1:
                    ring_done(s2)
                    for _ in range(2):
                        if ada:
                            adaln_group(*ada.pop(0))
                            if not ada:
                                adaln_finish(l + 1)

            h1stage(0)
            for k in range(len(seq)):
                if k + 1 < len(seq):
                    h1stage(k + 1)
                outstage(k)
            assert not ada

        pti = [0]

        NPT = 3
        LA = 3
        OB_POOL = [6, 7]
        ob_i = [0]

        def reserve_obanks(on):
            if on:
                for b in OB_POOL:
                    bank_free.remove(b)
            else:
                bank_free.extend(OB_POOL)

        def run_attn(jobs, scale, side=None):
            items = []
            for ji, job in enumerate(jobs):
                for gi, g in enumerate(job["groups"]):
                    items.append((ji, gi, g))
            M = len(items)
            sbank = {}
            ptb = {}
            obank = {}
            for i in range(M + LA):
                if i < M:
                    ji, gi, g = items[i]
                    sbk = balloc()
                    sbank[i] = sbk
                    n = g["n"]

                    def pe(e, g=g, sbk=sbk, n=n):
                        ins = e.matmul(banks[sbk][:, 0:n], lhsT=g["k"], rhs=g["q"], start=True, stop=(g["mask"] is None), skip_group_check=True)
                        if g["mask"] is not None:
                            ins = e.matmul(banks[sbk][:, 0:n], lhsT=ident[:, :], rhs=g["mask"], start=False, stop=True, skip_group_check=True)
                        return ins
                    op("pe", pe, reads=(g["K"], g["Q"], CONST) + ((g["Q2"],) if "Q2" in g else ()), writes=(BANK[sbk],))
                j = i - (LA - 1)
                if 0 <= j < M:
                    ji, gi, g = items[j]
                    sbk = sbank.pop(j)
                    n = g["n"]
                    pi = pti[0] % NPT
                    pti[0] += 1
                    ptb[j] = pi
                    op("act", lambda e, sbk=sbk, n=n, pi=pi: e.activation(out=pt[pi][:, 0:n], in_=banks[sbk][:, 0:n], func=AF.Exp, bias=negc[:, 0:1], scale=scale),
                       reads=(BANK[sbk], CONST), writes=(PT[pi],))
                    bfree(sbk)
                k = i - LA
                if 0 <= k < M:
                    ji, gi, g = items[k]
                    if gi == 0:
                        obank[ji] = OB_POOL[ob_i[0] % 2]
                        ob_i[0] += 1
                    ob = obank[ji]
                    pi = ptb.pop(k)
                    n = g["n"]
                    c0 = g["c0"]
                    last = gi == len(jobs[ji]["groups"]) - 1
                    op("pe", lambda e, g=g, pi=pi, n=n, c0=c0, f=(gi == 0), la=last, ob=ob: e.matmul(
                        banks[ob][:, c0:c0 + n], lhsT=g["v"], rhs=pt[pi][:, 0:n], start=f, stop=la, skip_group_check=True),
                       reads=(g["V"], PT[pi]), writes=(BANK[ob],))
                    if last:
                        jobs[ji]["finish"](ob)
                        obank.pop(ji)
                if side is not None:
                    side()

        def normalize(obank, base, nq, sink_col, dst, wres, extra_reads=(), on_dve=False):
            dbase = 64 - base
            ds = slice(dbase, dbase + 64)
            if on_dve:
                op("dve", lambda e: e.tensor_scalar(out=rec[ds, 0:nq], in0=banks[obank][ds, 0:nq],
                                                    scalar1=der[ds, sink_col:sink_col + 1], scalar2=None, op0=ALU.add),
                   reads=(BANK[obank], DER), writes=(REC,))
                op("dve", lambda e: e.reciprocal(out=rec[ds, 0:nq], in_=rec[ds, 0:nq]), reads=(REC,), writes=(REC,))
            else:
                if sink_col is not None:
                    op("act", lambda e: e.activation(out=rec[ds, 0:nq], in_=banks[obank][ds, 0:nq], func=AF.Ln,
                                                     bias=der[ds, sink_col:sink_col + 1], scale=1.0),
                       reads=(BANK[obank], DER), writes=(REC,))
                else:
                    op("act", lambda e: e.activation(out=rec[ds, 0:nq], in_=banks[obank][ds, 0:nq], func=AF.Ln),
                       reads=(BANK[obank],), writes=(REC,))
                op("act", lambda e: e.activation(out=rec[ds, 0:nq], in_=rec[ds, 0:nq], func=AF.Exp, scale=-1.0), reads=(REC,), writes=(REC,))
            op("dve", lambda e: e.tensor_tensor(out=dst, in0=banks[obank][base:base + 64, 0:nq], in1=rec[ds, 0:nq], op=ALU.mult),
               reads=(BANK[obank], REC) + tuple(extra_reads), writes=wres)

        def rope_combine(b1, b2, nrows, dst, wres, p0=0, tab=None, TAB=None, tp0=None):
            ps = slice(p0, p0 + nrows)
            if tab is None:
                tab, TAB, tp0 = rope_t, ROPE, p0
            ts_ = slice(tp0, tp0 + nrows)
            op("dve", lambda e: e.tensor_tensor(out=tmp[0][ps, :], in0=banks[b1][ps, :], in1=tab[ts_, 0, :], op=ALU.mult),
               reads=(BANK[b1], TAB), writes=(TMP[0],))
            op("dve", lambda e: e.tensor_tensor(out=tmp[1][ps, :], in0=banks[b2][ps, :], in1=tab[ts_, 1, :], op=ALU.mult),
               reads=(BANK[b2], TAB), writes=(TMP[1],))
            op("dve", lambda e: e.tensor_tensor(out=dst, in0=tmp[0][ps, :], in1=tmp[1][ps, :], op=ALU.add),
               reads=(TMP[0], TMP[1]), writes=wres)

        def load_rope(which, T, p0=0):
            rsl = slice(T * TS, (T + 1) * TS)
            if which == "A":
                trk.dma("sp", sp_sems.next(), rope_t[:, :, :], d_ropeA[:, :, rsl], writes=(ROPE,))
            else:
                trk.dma("sp", sp_sems.next(), rope_t[p0:p0 + 32, :, :], d_ropeB[:, :, rsl], writes=(ROPE,))

        def attn_group(l, grp, extra=None):
            e_i = l // 2
            sample = grp == 1
            tiles = [1, 2, 3, 4] if sample else [0]
            t0 = tiles[0]
            ntok = TS * len(tiles)
            nctx = NCTX if sample else 0
            NKg = ntok + nctx
            nkt = NKg // 128
            nlt = ntok // 128
            nt = len(tiles)
            o = 0
            qa = arena[:, o:o + 4 * ntok].rearrange("p (c n) -> p c n", c=4); o += 4 * ntok
            cqn = arena[:, o:o + 2 * NKg].rearrange("p (c n) -> p c n", c=2); o += 2 * NKg
            ckvn = arena[:, o:o + NKg]; o += NKg
            r3 = o
            hts = [arena[:, o + i * 4096:o + (i + 1) * 4096].rearrange("p (k n) -> p k n", k=8) for i in range(2)]
            khb = arena[:, r3:r3 + NKg]
            vhb = arena[:, r3 + NKg:r3 + NKg + nkt * 128].rearrange("p (t c) -> p t c", c=128)
            qhbs = [arena[:, r3 + NKg + nkt * 128 + i * ntok:r3 + NKg + nkt * 128 + (i + 1) * ntok] for i in range(2)]
            o = r3 + max(8192, NKg + nkt * 128 + 2 * ntok)
            r4 = o
            ka = arena[:, o:o + NKg]
            va = arena[:, o + NKg:o + NKg + nkt * 192].rearrange("p (t c) -> p t c", c=192)
            obc = [arena[:, r4 + i * ntok:r4 + (i + 1) * ntok] for i in range(2)]
            o = r4 + max(NKg + nkt * 192, 2 * ntok)
            assert o <= ARENA, o
            HTs = [Res("ht0"), Res("ht1")]
            QAH = [[[Res(f"qah{c}_{hh}_{t}") for t in range(nt)] for hh in range(2)] for c in range(4)]
            CQ = [Res(f"cq{t}") for t in range(nt)]
            CKV = [Res(f"ckv{t}") for t in range(nt + 1)]
            KR = [Res(f"kr{t}") for t in range(nt + 1)]
            KA = [Res(f"ka{t}") for t in range(nt + 1)]
            VA = [Res(f"va{t}") for t in range(nt + 1)]
            OBC = [Res("obc0"), Res("obc1")]
            KH, VH = Res("kh"), Res("vh")
            QHs = [Res("qh0"), Res("qh1")]
            krv = cqn[64:96, 1, :]

            reserve_obanks(True)
            g0, G0, s0 = ring_get(("win", e_i, 0))
            g1, G1, s1 = ring_get(("win", e_i, 1))
            g2, G2, s2 = ring_get(("win", e_i, 2))
            g3, G3, s3 = ring_get(("win", e_i, 3))

            op("dve", lambda e: e.memset(va[:, :, 64:128], 1.0), writes=tuple(VA))
            op("dve", lambda e: e.memset(cqn[96:128, 1, :], 0.0), writes=tuple(CQ))
            if sample:
                for g_ in range(4):
                    trk.dma("sp", sp_sems.next(), ropeB_all[32 * g_:32 * g_ + 32, :, :], d_ropeB[:, :, g_ * TS:(g_ + 1) * TS], writes=(ROPEB,))
                load_cast(ka[:, ntok:NKg], d_ckT[e_i], 128, NCTX, (KA[nt],))
                load_cast(ckvn[:, ntok:NKg], d_cckvT[e_i], 128, NCTX, (CKV[nt],))
                load_cast(krv[:, ntok:NKg], d_ckrT[e_i], 32, NCTX, (KR[nt],))
                for kt in range(2):
                    trk.dma("sp", sp_sems.next(), stage[:, 0:128], d_cv[e_i, kt], writes=(STAGE,))
                    op("dve", lambda e, kt=kt: e.tensor_copy(out=va[:, nlt + kt, 0:64], in_=stage[:, 0:64]), reads=(STAGE,), writes=(VA[nt],))
                    op("dve", lambda e, kt=kt: e.tensor_copy(out=va[:, nlt + kt, 128:192], in_=stage[:, 64:128]), reads=(STAGE,), writes=(VA[nt],))

            oqn = PM["qnorm"][0] + e_i * 2
            okn = PM["kvnorm"][0] + e_i

            cur = {}

            def proj(bk, wt, WR, col0, ncol, ht, HT):

                def pe(e):
                    ins = None
                    for kc in range(8):
                        ins = e.matmul(banks[bk][0:ncol, :], lhsT=wt[:, kc * 512 + col0:kc * 512 + col0 + ncol], rhs=ht[:, kc, :],
                                       start=(kc == 0), stop=(kc == 7))
                    return ins
                op("pe", pe, reads=(WR, HT), writes=(BANK[bk],))

            def out_from(dst, src_ap, src_res, nrows, scale=None):
                i = ost_i[0] % 2
                ost_i[0] += 1
                if scale is None:
                    op("act", lambda e: e.activation(out=ostage[i][0:nrows, :], in_=src_ap, func=AF.Identity),
                       reads=src_res, writes=(OST[i],))
                else:
                    op("act", lambda e: e.activation(out=ostage[i][0:nrows, :], in_=src_ap, func=AF.Identity, scale=scale),
                       reads=src_res + (P_,), writes=(OST[i],))
                trk.dma("sp", out_sems.next(), dst, ostage[i][0:nrows, :], reads=(OST[i],))

            def norm_gen(t, dest, DEST):
                grp = 0 if t == 0 else 1
                tokx = slice(t * TS, (t + 1) * TS)
                b = balloc()

                def mm_(k):
                    op("pe", lambda e: e.matmul(banks[b][:, :], lhsT=onesm[:, :], rhs=sq[k % 2][:, :], start=(k == 0),
                                                stop=(k == 7), skip_group_check=True),
                       reads=(SQ[k % 2], CONST), writes=(BANK[b],))
                for p_ in range(5):
                    if p_ >= 1:
                        mm_(2 * p_ - 2)
                        mm_(2 * p_ - 1)
                    if p_ < 4:
                        for k in (2 * p_, 2 * p_ + 1):
                            op("act", lambda e, k=k: e.activation(out=sq[k % 2][:, :], in_=x[:, k, tokx], func=AF.Square),
                               reads=(X[k][t],), writes=(SQ[k % 2],))
                        yield
                op("act", lambda e: e.activation(out=rstd[:, :], in_=banks[b][:, :], func=AF.Ln, bias=epst[:, 0:1], scale=1.0),
                   reads=(BANK[b], CONST), writes=(RSTD,))
                bfree(b)
                op("act", lambda e: e.activation(out=rstd[:, :], in_=rstd[:, :], func=AF.Exp, scale=-0.5), reads=(RSTD,), writes=(RSTD,))
                yield
                for k in range(8):
                    ti = k % 2
                    op("dve", lambda e, k=k, ti=ti: e.tensor_tensor(out=tmp[ti][:, :], in0=x[:, k, tokx], in1=rstd[:, :], op=ALU.mult),
                       reads=(X[k][t], RSTD), writes=(TMP[ti],))
                    op("act", lambda e, k=k, ti=ti: e.activation(out=dest[:, k, :], in_=tmp[ti][:, :], func=AF.Identity,
                                                               bias=dv(l, 1, k, grp), scale=dv(l, 0, k, grp)),
                       reads=(TMP[ti], DER), writes=(DEST,))
                    if k % 4 == 3:
                        yield

            def stageA(lt):
                t = tiles[lt]
                tok = slice(lt * TS, (lt + 1) * TS)
                ht, HT = hts[lt % 2], HTs[lt % 2]
                cur["ht"], cur["HT"] = ht, HT
                yield from norm_gen(t, ht, HT)
                cur["ht"], cur["HT"] = ht, HT
                b0 = balloc(); b1 = balloc(); bs = balloc()
                proj(b0, g0, G0, 0, 128, ht, HT)
                proj(b1, g0, G0, 128, 128, ht, HT)
                op("act", lambda e, b0=b0: e.activation(out=sq[0][:, :], in_=banks[b0][:, :], func=AF.Square), reads=(BANK[b0],), writes=(SQ[0],))
                op("act", lambda e, b1=b1: e.activation(out=sq[1][:, :], in_=banks[b1][:, :], func=AF.Square), reads=(BANK[b1],), writes=(SQ[1],))

                def pe_ss(e, bs=bs):
                    e.matmul(banks[bs][:, :], lhsT=ones1[:, :], rhs=sq[0][:, :], start=True, stop=False, skip_group_check=True)
                    return e.matmul(banks[bs][:, :], lhsT=ones_lo[:, :], rhs=sq[1][:, :], start=False, stop=True, skip_group_check=True)
                op("pe", pe_ss, reads=(SQ[0], SQ[1], CONST), writes=(BANK[bs],))
                yield
                op("act", lambda e, bs=bs: e.activation(out=rstd[:, :], in_=banks[bs][:, :], func=AF.Ln, bias=epst[:, 0:1], scale=1.0 / 192.0),
                   reads=(BANK[bs], CONST), writes=(RSTD,))
                op("act", lambda e: e.activation(out=rstd[:, :], in_=rstd[:, :], func=AF.Exp, scale=-0.5), reads=(RSTD,), writes=(RSTD,))
                op("dve", lambda e, b0=b0: e.tensor_tensor(out=tmp[0][:, :], in0=banks[b0][:, :], in1=rstd[:, :], op=ALU.mult),
                   reads=(BANK[b0], RSTD), writes=(TMP[0],))
                op("act", lambda e, tok=tok: e.activation(out=cqn[:, 0, tok], in_=tmp[0][:, :], func=AF.Identity, scale=params[:, oqn:oqn + 1]),
                   reads=(TMP[0], P_), writes=(CQ[lt],))
                op("dve", lambda e, b1=b1: e.tensor_tensor(out=tmp[1][0:64, :], in0=banks[b1][0:64, :], in1=rstd[0:64, :], op=ALU.mult),
                   reads=(BANK[b1], RSTD), writes=(TMP[1],))
                op("act", lambda e, tok=tok: e.activation(out=cqn[0:64, 1, tok], in_=tmp[1][0:64, :], func=AF.Identity, scale=params[0:64, oqn + 1:oqn + 2]),
                   reads=(TMP[1], P_), writes=(CQ[lt],))
                bfree(b0); bfree(b1)
                yield
                b0 = balloc()
                proj(b0, g0, G0, 192, 128, ht, HT)
                op("act", lambda e, b0=b0: e.activation(out=sq[0][:, :], in_=banks[b0][:, :], func=AF.Square), reads=(BANK[b0],), writes=(SQ[0],))
                op("pe", lambda e, bs=bs: e.matmul(banks[bs][:, :], lhsT=ones1[:, :], rhs=sq[0][:, :], start=True, stop=True),
                   reads=(SQ[0], CONST), writes=(BANK[bs],))
                op("act", lambda e, bs=bs: e.activation(out=rstd[:, :], in_=banks[bs][:, :], func=AF.Ln, bias=epst[:, 0:1], scale=1.0 / 128.0),
                   reads=(BANK[bs], CONST), writes=(RSTD,))
                op("act", lambda e: e.activation(out=rstd[:, :], in_=rstd[:, :], func=AF.Exp, scale=-0.5), reads=(RSTD,), writes=(RSTD,))
                op("dve", lambda e, b0=b0: e.tensor_tensor(out=tmp[0][:, :], in0=banks[b0][:, :], in1=rstd[:, :], op=ALU.mult),
                   reads=(BANK[b0], RSTD), writes=(TMP[0],))
                op("act", lambda e, tok=tok: e.activation(out=ckvn[:, tok], in_=tmp[0][:, :], func=AF.Identity, scale=params[:, okn:okn + 1]),
                   reads=(TMP[0], P_), writes=(CKV[lt],))
                if not sample:
                    out_from(o_ckvT[e_i], tmp[0][:, :], (TMP[0],), 128, scale=params[:, okn:okn + 1])
                bfree(b0); bfree(bs)
                yield
                b0 = balloc()
                proj(b0, g0, G0, 320, 128, ht, HT)
                if sample:
                    b1 = balloc()
                    proj(b1, g0, G0, 352, 128, ht, HT)
                    rope_combine(b0, b1, 32, krv[:, tok], (KR[lt],), tab=ropeB_all, TAB=ROPEB, tp0=32 * (t - 1))
                    bfree(b1)
                else:
                    out_from(o_krT[e_i], banks[b0][0:32, :], (BANK[b0],), 32)
                    op("dve", lambda e, b0=b0, tok=tok: e.tensor_copy(out=krv[:, tok], in_=ostage[(ost_i[0] - 1) % 2][0:32, :]),
                       reads=(OST[(ost_i[0] - 1) % 2],), writes=(KR[lt],))
                bfree(b0)
                yield

            def stageB(lt):
                t = tiles[lt]
                tok = slice(lt * TS, (lt + 1) * TS)
                ht, HT = hts[lt % 2], HTs[lt % 2]
                cur["ht"], cur["HT"] = ht, HT
                if sample:
                    load_rope("A", t - 1)
                for c in range(4):
                    b0 = balloc()
                    proj(b0, g1, G1, c * 128, 128, ht, HT)
                    if sample:
                        b1 = balloc()
                        proj(b1, g2, G2, c * 128, 128, ht, HT)
                        rope_combine(b0, b1, 128, qa[:, c, tok], (QAH[c][0][lt], QAH[c][1][lt]))
                        bfree(b1)
                    else:
                        op("act", lambda e, c=c, b0=b0, tok=tok: e.activation(out=qa[:, c, tok], in_=banks[b0][:, :], func=AF.Identity),
                           reads=(BANK[b0],), writes=(QAH[c][0][lt], QAH[c][1][lt]))
                    bfree(b0)
                    yield
                b0 = balloc()
                proj(b0, g3, G3, 0, 128, ht, HT)
                if sample:
                    b1 = balloc()
                    proj(b1, g3, G3, 128, 128, ht, HT)
                    rope_combine(b0, b1, 128, ka[:, tok], (KA[lt],))
                    bfree(b1)
                else:
                    op("act", lambda e, b0=b0, tok=tok: e.activation(out=ka[:, tok], in_=banks[b0][:, :], func=AF.Identity), reads=(BANK[b0],), writes=(KA[lt],))
                    out_from(o_kT[e_i], banks[b0][:, :], (BANK[b0],), 128)
                bfree(b0)
                yield
                b0 = balloc()

                def pe_v(e, b0=b0, ht=ht):
                    ins = None
                    for tb in range(4):
                        for kc in range(8):
                            ins = e.matmul(banks[b0][:, tb * 128:(tb + 1) * 128], lhsT=ht[:, kc, tb * 128:(tb + 1) * 128],
                                           rhs=g3[:, kc * 512 + 256:kc * 512 + 384], start=(kc == 0), stop=(kc == 7), skip_group_check=True)
                    return ins
                op("pe", pe_v, reads=(G3, HT), writes=(BANK[b0],))
                bv = banks[b0][:, :].rearrange("p (t c) -> p t c", c=128)
                op("act", lambda e, bv=bv, lt=lt: e.activation(out=va[:, 4 * lt:4 * lt + 4, 0:64], in_=bv[:, :, 0:64], func=AF.Identity),
                   reads=(BANK[b0],), writes=(VA[lt],))
                op("act", lambda e, bv=bv, lt=lt: e.activation(out=va[:, 4 * lt:4 * lt + 4, 128:192], in_=bv[:, :, 64:128], func=AF.Identity),
                   reads=(BANK[b0],), writes=(VA[lt],))
                if not sample:
                    i = ost_i[0] % 2
                    ost_i[0] += 1
                    op("act", lambda e, i=i, b0=b0: e.activation(out=ostage[i][:, :], in_=banks[b0][:, :], func=AF.Identity),
                       reads=(BANK[b0],), writes=(OST[i],))
                    trk.dma("sp", out_sems.next(), o_v[e_i].rearrange("(t p) c -> p t c", p=128),
                            ostage[i][:, :].rearrange("p (t c) -> p t c", c=128), reads=(OST[i],))
                bfree(b0)
                yield

            def drive(gens):
                gens = [g_ for g_ in gens if g_ is not None]
                while gens:
                    for g_ in list(gens):
                        try:
                            next(g_)
                        except StopIteration:
                            gens.remove(g_)

            drive([stageA(0)])
            for lt in range(nt):
                drive([stageB(lt), stageA(lt + 1) if lt + 1 < nt else None])
            for s_ in (s0, s1, s2, s3):
                ring_done(s_)

            g4, G4, s4 = ring_get(("wqkv", e_i))
            wo0, WO0, so0 = ring_get(("wout", e_i, 0))
            wo1, WO1, so1 = ring_get(("wout", e_i, 1))

            kaz = [arena[:, r3 + i * NKg:r3 + (i + 1) * NKg] for i in range(2)]
            KZ = Res("kz")
            jobs = []
            for c in range(4):
                for hh in range(2):
                    h = c + 4 * hh
                    base = 64 * hh
                    vs = slice(0, 128) if hh == 0 else slice(64, 192)
                    sink_col = o_es + e_i * 8 + h
                    if not sample:
                        for s in range(2):
                            q0 = s * 256
                            groups = []
                            for kb in range(2):
                                k0 = s * 256 + kb * 128
                                groups.append(dict(k=kaz[hh][:, k0:k0 + 128], q=qa[:, c, q0:q0 + 256], K=KZ, Q=QAH[c][hh][0], Q2=QAH[c][1 - hh][0],
                                                   v=va[:, k0 // 128, vs], V=VA[0], c0=0, n=256, mask=None))
                            jobs.append(dict(groups=groups, finish=(lambda ob, base=base, sink_col=sink_col, c=c, q0=q0, hh=hh: normalize(
                                ob, base, 256, sink_col, qa[base:base + 64, c, q0:q0 + 256], (QAH[c][hh][0],)))))
                    else:
                        for T in range(4):
                            q0 = T * TS
                            groups = []
                            for kb in range(2):
                                groups.append(dict(k=kaz[hh][:, ntok + kb * 128:ntok + (kb + 1) * 128], q=qa[:, c, q0:q0 + TS],
                                                   K=KZ, Q=QAH[c][hh][T], Q2=QAH[c][1 - hh][T], v=va[:, nlt + kb, vs], V=VA[nt], c0=0, n=TS, mask=None))
                            for jb in range(4 * T - 1, 4 * T + 5):
                                if jb < 0 or jb > 15:
                                    continue
                                qlo = max(jb - 1, 4 * T)
                                qhi = min(jb + 1, 4 * T + 3)
                                n = (qhi - qlo + 1) * 128
                                c0 = (qlo - 4 * T) * 128
                                m0 = (qlo - (jb - 1)) * 128
                                k0 = jb * 128
                                groups.append(dict(k=kaz[hh][:, k0:k0 + 128], q=qa[:, c, q0 + c0:q0 + c0 + n],
                                                   K=KZ, Q=QAH[c][hh][T], Q2=QAH[c][1 - hh][T], v=va[:, jb, vs], V=VA[jb // 4],
                                                   c0=c0, n=n, mask=maskb[:, m0:m0 + n]))
                            jobs.append(dict(groups=groups, finish=(lambda ob, base=base, sink_col=sink_col, c=c, q0=q0, hh=hh, T=T: normalize(
                                ob, base, TS, sink_col, qa[base:base + 64, c, q0:q0 + TS], (QAH[c][hh][T],), on_dve=True))))
            side_q, side_o = [], []
            side_x = list(extra or [])
            s4_released = [False]

            sidestep = [0]
            armed = [False]

            def pop_side():
                pool_sel[0] = "side"
                sidestep[0] += 1
                if side_q:
                    side_q.pop(0)()
                elif side_x and sidestep[0] % 24 == 0 and armed[0]:
                    side_x.pop(0)()
                elif side_o and (sidestep[0] % 2 == 0 or not sample):
                    side_o.pop(0)[1]()
                pool_sel[0] = "g"

            def flush(lst):
                while lst:
                    it_ = lst.pop(0)
                    (it_[1] if isinstance(it_, tuple) else it_)()

            def flush_tag(tag):
                keep = []
                for it_ in side_o:
                    if it_[0] == tag:
                        it_[1]()
                    else:
                        keep.append(it_)
                side_o[:] = keep

            mscale = float((64 + 32) ** -0.5)
            WQ0, WQ1, WK, WV = 0, 1024, 2048, 2560
            nt6 = nt + (1 if sample else 0)

            if not sample:
                fo_ = o
                PB = []
                for h_ in range(8):
                    PB.append(dict(k=arena[:, fo_:fo_ + 512], v=arena[:, fo_ + 512:fo_ + 1024].rearrange("p (t c) -> p t c", c=128),
                                   q=arena[:, fo_ + 1024:fo_ + 1536], K=Res(f"pk{h_}"), V=Res(f"pv{h_}"), Q=Res(f"pq{h_}")))
                    fo_ += 1536
                obc4 = [arena[:, fo_ + i * 512:fo_ + (i + 1) * 512] for i in range(4)]
                OBC4 = [Res(f"obc4_{i}") for i in range(4)]
                fo_ += 2048
                assert fo_ <= ARENA, fo_

            def hbufs(h):
                if sample:
                    return dict(k=khb, v=vhb, q=qhbs[h % 2], K=KH, V=VH, Q=QHs[h % 2])
                return PB[h]

            def kv_tasks(h):
                hb = hbufs(h)
                khb, vhb, KH, VH = hb["k"], hb["v"], hb["K"], hb["V"]
                bi = h % 2
                vcol = 0 if bi == 0 else 64
                ocol = 64 if bi == 0 else 0
                tasks = []

                def t_init():
                    if not sample:
                        op("dve", lambda e: e.memset(khb[96:128, :], 0.0), writes=(KH,))
                    op("dve", lambda e: e.memset(vhb[:, :, ocol:ocol + 64], 1.0), writes=(VH,))
                    op("dve", lambda e: e.tensor_copy(out=khb[64:96, :], in_=krv[:, :]), reads=tuple(KR), writes=(KH,))
                tasks.append(t_init)
                for t6 in range(nt6):
                    def t_kv(t6=t6):
                        n = TS if t6 < nt else NCTX
                        k0 = t6 * TS
                        b0 = balloc()
                        op("pe", lambda e: e.matmul(banks[b0][:, 0:n], lhsT=g4[:, WK + h * 64:WK + h * 64 + 128],
                                                    rhs=ckvn[:, k0:k0 + n], start=True, stop=True),
                           reads=(G4, CKV[t6]), writes=(BANK[b0],))
                        op("dve", lambda e: e.tensor_copy(out=khb[0:64, k0:k0 + n], in_=banks[b0][0:64, 0:n]),
                           reads=(BANK[b0],), writes=(KH,))
                        bfree(b0)
                        b1 = balloc()
                        ntb = n // 128

                        def pe_v2(e):
                            ins = None
                            for tb in range(ntb):
                                ins = e.matmul(banks[b1][:, tb * 64:(tb + 1) * 64], lhsT=ckvn[:, k0 + tb * 128:k0 + (tb + 1) * 128],
                                               rhs=g4[:, WV + h * 64:WV + (h + 1) * 64], start=True, stop=True, skip_group_check=True)
                            return ins
                        op("pe", pe_v2, reads=(G4, CKV[t6]), writes=(BANK[b1],))
                        op("dve", lambda e: e.tensor_copy(
                            out=vhb[:, 4 * t6:4 * t6 + ntb, vcol:vcol + 64],
                            in_=banks[b1][:, 0:ntb * 64].rearrange("p (t c) -> p t c", c=64)),
                           reads=(BANK[b1],), writes=(VH,))
                        bfree(b1)
                    tasks.append(t_kv)
                return tasks

            def q_tasks(h):
                hb = hbufs(h)
                qhb, QH = hb["q"], hb["Q"]
                tasks = []
                for lt, t in enumerate(tiles):
                    def t_q(lt=lt, t=t):
                        tok = slice(lt * TS, (lt + 1) * TS)
                        b0 = balloc()

                        def pe_q(e):
                            e.matmul(banks[b0][:, :], lhsT=g4[:, WQ0 + h * 128:WQ0 + h * 128 + 128], rhs=cqn[:, 0, tok], start=True, stop=False)
                            return e.matmul(banks[b0][:, :], lhsT=g4[:, WQ1 + h * 128:WQ1 + h * 128 + 128], rhs=cqn[:, 1, tok], start=False, stop=True)
                        op("pe", pe_q, reads=(G4, CQ[lt], KR[lt]), writes=(BANK[b0],))
                        op("dve", lambda e: e.tensor_copy(out=qhb[0:64, tok], in_=banks[b0][0:64, :]),
                           reads=(BANK[b0],), writes=(QH,))
                        if sample:
                            gsl = slice(32 * (t - 1), 32 * (t - 1) + 32)
                            op("dve", lambda e: e.tensor_tensor(out=tmp[0][64:96, :], in0=banks[b0][64:96, :], in1=ropeB_all[gsl, 0, :], op=ALU.mult),
                               reads=(BANK[b0], ROPEB), writes=(TMP[0],))
                            op("dve", lambda e: e.tensor_tensor(out=tmp[1][64:96, :], in0=banks[b0][96:128, :], in1=ropeB_all[gsl, 1, :], op=ALU.mult),
                               reads=(BANK[b0], ROPEB), writes=(TMP[1],))
                            op("dve", lambda e: e.tensor_tensor(out=qhb[64:96, tok], in0=tmp[0][64:96, :], in1=tmp[1][64:96, :], op=ALU.add),
                               reads=(TMP[0], TMP[1]), writes=(QH,))
                        else:
                            op("dve", lambda e: e.tensor_copy(out=qhb[64:96, tok], in_=banks[b0][64:96, :]),
                               reads=(BANK[b0],), writes=(QH,))
                        bfree(b0)
                    tasks.append(t_q)
                return tasks

            def oproj_tasks(kcs, srcs, SRCS):
                tasks = []
                for lt, t in enumerate(tiles):
                    for j in range(8):
                        def t_o(lt=lt, t=t, j=j):
                            tok = slice(lt * TS, (lt + 1) * TS)
                            wt, WR = (wo0, WO0) if j < 4 else (wo1, WO1)
                            jj = j % 4
                            b0 = balloc()

                            def pe_o(e):
                                ins = None
                                for i, kc in enumerate(kcs):
                                    ins = e.matmul(banks[b0][:, :], lhsT=wt[:, kc * 512 + jj * 128:kc * 512 + jj * 128 + 128], rhs=srcs[i](tok),
                                                   start=(i == 0), stop=(i == len(kcs) - 1))
                                return ins
                            op("pe", pe_o, reads=(WR,) + tuple(SRCS(lt)), writes=(BANK[b0],))
                            resid_add(l, 0, j, t, b0)
                            bfree(b0)
                        tasks.append(t_o)
                return tasks

            reserve_side(True)
            trk.barrier()
            op("dve", lambda e: e.memset(arena[:, r3:r3 + 2 * NKg], 0.0), writes=(KZ,))
            op("dve", lambda e: e.tensor_copy(out=kaz[0][0:64, :], in_=ka[0:64, :]), reads=tuple(KA), writes=(KZ,))
            op("act", lambda e: e.activation(out=kaz[1][64:128, :], in_=ka[64:128, :], func=AF.Identity), reads=tuple(KA), writes=(KZ,))
            run_attn(jobs, 0.125, pop_side)
            side_o += [("A", f_) for f_ in oproj_tasks([0, 1, 2, 3], [(lambda tok, c=c: qa[:, c, tok]) for c in range(4)],
                                                       lambda lt: [QAH[c][hh][lt] for c in range(4) for hh in range(2)])]
            trk.barrier()
            if not sample:
                jobs = []
                for h in range(8):
                    hb = hbufs(h)
                    op("dve", lambda e, hb=hb: e.memset(hb["q"][96:128, :], 0.0), writes=(hb["Q"],))
                    flush(kv_tasks(h))
                    flush(q_tasks(h))
                for h in range(8):
                    hb = hbufs(h)
                    bi = h % 2
                    base = 64 * bi
                    cB = h // 2
                    for s_i in range(2):
                        q0 = s_i * 256
                        groups = []
                        for kb in range(2):
                            k0 = s_i * 256 + kb * 128
                            groups.append(dict(k=hb["k"][:, k0:k0 + 128], q=hb["q"][:, q0:q0 + 256], K=hb["K"], Q=hb["Q"],
                                               v=hb["v"][:, k0 // 128, :], V=hb["V"], c0=0, n=256, mask=None))
                        jobs.append(dict(groups=groups, finish=(lambda ob, base=base, cB=cB, q0=q0: normalize(
                            ob, base, 256, None, obc4[cB][base:base + 64, q0:q0 + 256], (OBC4[cB],)))))
                run_attn(jobs, mscale, pop_side)
                side_o += [("B", f_) for f_ in oproj_tasks([4, 5, 6, 7], [(lambda tok, c=c: obc4[c][:, tok]) for c in range(4)],
                                                       lambda lt: list(OBC4))]
            armed[0] = True
            for h in (range(8) if sample else []):
                bi = h % 2
                base = 64 * bi
                cB = h // 2
                if bi == 0 and cB >= 2:
                    flush_tag(cB - 2)
                flush(kv_tasks(h))
                if h == 0:
                    op("dve", lambda e: e.memset(khb[96:128, :], 0.0), writes=(KH,))
                    for i in range(2):
                        op("dve", lambda e, i=i: e.memset(qhbs[i][96:128, :], 0.0), writes=(QHs[i],))
                    side_q += q_tasks(0)
                flush(side_q)
                if h + 1 < 8:
                    side_q += q_tasks(h + 1)
                else:
                    ring_done(s4)
                    s4_released[0] = True
                qhb, QH = qhbs[h % 2], QHs[h % 2]
                oc = obc[cB % 2]
                OC = OBC[cB % 2]
                jobs = []
                if not sample:
                    for s in range(2):
                        q0 = s * 256
                        groups = []
                        for kb in range(2):
                            k0 = s * 256 + kb * 128
                            groups.append(dict(k=khb[:, k0:k0 + 128], q=qhb[:, q0:q0 + 256], K=KH, Q=QH,
                                               v=vhb[:, k0 // 128, :], V=VH, c0=0, n=256, mask=None))
                        jobs.append(dict(groups=groups, finish=(lambda ob, base=base, oc=oc, OC=OC, q0=q0: normalize(
                            ob, base, 256, None, oc[base:base + 64, q0:q0 + 256], (OC,)))))
                else:
                    for T in range(4):
                        q0 = T * TS
                        groups = []
                        for kb in list(range(nlt, nkt)) + list(range(nlt)):
                            k0 = kb * 128
                            groups.append(dict(k=khb[:, k0:k0 + 128], q=qhb[:, q0:q0 + TS], K=KH, Q=QH,
                                               v=vhb[:, kb, :], V=VH, c0=0, n=TS, mask=None))
                        jobs.append(dict(groups=groups, finish=(lambda ob, base=base, oc=oc, OC=OC, q0=q0: normalize(
                            ob, base, TS, None, oc[base:base + 64, q0:q0 + TS], (OC,)))))
                run_attn(jobs, mscale, pop_side)
                if bi == 1:
                    side_o += [(cB, f_) for f_ in oproj_tasks([4 + cB], [(lambda tok, oc=oc: oc[:, tok])], lambda lt, OC=OC: [OC])]
            flush(side_q)
            flush(side_x)
            flush(side_o)
            for s_ in ((so0, so1) if s4_released[0] else (s4, so0, so1)):
                ring_done(s_)
            reserve_side(False)
            reserve_obanks(False)
            trk.barrier()

        def conv_layer(l):
            j_i = l // 2
            o = 0
            upad = arena[:, o:o + 4 * NPAD].rearrange("p (c n) -> p c n", c=4); o += 4 * NPAD
            zpad = arena[:, o:o + 4 * NPAD].rearrange("p (c n) -> p c n", c=4); o += 4 * NPAD
            hts = [arena[:, o + i * 4096:o + (i + 1) * 4096].rearrange("p (k n) -> p k n", k=8) for i in range(2)]
            o += 8192
            assert o <= ARENA, o
            HTs = [Res("ht0"), Res("ht1")]
            U = [Res(f"u{c}") for c in range(4)]
            Z = [Res(f"z{c}") for c in range(4)]
            for (so_, sn_), pb in zip(SEQS, PADB):
                for buf, RS in ((upad, U), (zpad, Z)):
                    op("dve", lambda e, buf=buf, pb=pb: e.memset(buf[:, :, pb:pb + 16], 0.0), writes=tuple(RS))
                    op("dve", lambda e, buf=buf, pb=pb, sn_=sn_: e.memset(buf[:, :, pb + 16 + sn_:pb + 32 + sn_], 0.0), writes=tuple(RS))
            ga, GA, sa = ring_get(("cin", j_i, 0))
            gg, GG, sg = ring_get(("cin", j_i, 1))
            gz, GZ, sz = ring_get(("cin", j_i, 2))
            load_cast(pw[:, :], d_poolw[j_i], 128, 512, (PW,))

            def segs_of_tile(t):
                if t == 0:
                    return [(0, 0, 256), (1, 256, 256)]
                return [(2, t * TS, TS)]

            norm_tile(l, 0, 0, hts[0], HTs[0])
            for t in range(NT):
                ht, HT = hts[t % 2], HTs[t % 2]
                ng = norm_gen(l, 0, t + 1, hts[(t + 1) % 2], HTs[(t + 1) % 2]) if t + 1 < NT else None
                for c in range(4):
                    if ng is not None:
                        for _ in range(2):
                            next(ng, None)
                    ba = balloc(); bg = balloc()
                    for (bk, wt, WR) in ((ba, ga, GA), (bg, gg, GG)):
                        def pe(e, bk=bk, wt=wt, c=c, ht=ht):
                            ins = None
                            for kc in range(8):
                                ins = e.matmul(banks[bk][:, :], lhsT=wt[:, kc * 512 + c * 128:kc * 512 + c * 128 + 128], rhs=ht[:, kc, :],
                                               start=(kc == 0), stop=(kc == 7))
                            return ins
                        op("pe", pe, reads=(WR, HT), writes=(BANK[bk],))
                    ti = c % 2
                    op("act", lambda e, bg=bg, ti=ti: e.activation(out=tmp[ti][:, :], in_=banks[bg][:, :], func=AF.Sigmoid), reads=(BANK[bg],), writes=(TMP[ti],))
                    for (s_, t0_, n) in segs_of_tile(t):
                        p0 = _padpos(t0_)
                        c0 = t0_ - t * TS
                        op("dve", lambda e, ba=ba, c=c, p0=p0, c0=c0, n=n, ti=ti: e.tensor_tensor(
                            out=upad[:, c, p0:p0 + n], in0=banks[ba][:, c0:c0 + n], in1=tmp[ti][:, c0:c0 + n], op=ALU.mult),
                           reads=(BANK[ba], TMP[ti]), writes=(U[c],))
                    bfree(ba); bfree(bg)
                    bz = balloc()

                    def pez(e, bz=bz, c=c, ht=ht):
                        ins = None
                        for kc in range(8):
                            ins = e.matmul(banks[bz][:, :], lhsT=gz[:, kc * 512 + c * 128:kc * 512 + c * 128 + 128], rhs=ht[:, kc, :],
                                           start=(kc == 0), stop=(kc == 7))
                        return ins
                    op("pe", pez, reads=(GZ, HT), writes=(BANK[bz],))
                    for (s_, t0_, n) in segs_of_tile(t):
                        p0 = _padpos(t0_)
                        c0 = t0_ - t * TS
                        op("act", lambda e, bz=bz, c=c, p0=p0, c0=c0, n=n: e.activation(out=zpad[:, c, p0:p0 + n], in_=banks[bz][:, c0:c0 + n], func=AF.Identity),
                           reads=(BANK[bz],), writes=(Z[c],))
                    bfree(bz)
                if ng is not None:
                    for _ in ng:
                        pass
            rst["limit"] = rst["got"] + 2
            for s_ in (sa, sg, sz):
                ring_done(s_)
            wo0, WO0, so0 = ring_get(("cout", j_i, 0))
            wo1, WO1, so1 = ring_get(("cout", j_i, 1))
            free_slots = [i for i in range(NRING) if not rst["held"][i]]
            assert len(free_slots) == 2, free_slots
            for i in free_slots:
                rst["held"][i] = True
            dgs = [ring[i] for i in free_slots]
            DGs = [RING[i] for i in free_slots]
            hcs, HCs = hts, [Res("hc0"), Res("hc1")]
            obdw = PM["bdw"][0] + j_i * 4
            olng = PM["lng"][0] + j_i * 4
            olnb = PM["lnb"][0] + j_i * 4
            odw = PM["dw"][0] + j_i * 4 * 31
            segs = []
            for t in range(NT):
                sl = segs_of_tile(t)
                for i, (s_, t0_, n) in enumerate(sl):
                    segs.append((s_, t0_, n, t, i == len(sl) - 1))
            dgi = [0]

            pending = []

            def build_dg(c):
                di = dgi[0] % 2
                dgi[0] += 1
                dg, DG = dgs[di], DGs[di]
                wc0 = odw + c * 31
                op("dve", lambda e: e.tensor_tensor(
                    out=dg[:, 0:31 * 128].rearrange("p (t m) -> p t m", t=31),
                    in0=ident[:, :].unsqueeze(1).broadcast_to([128, 31, 128]),
                    in1=params[:, wc0:wc0 + 31].unsqueeze(2).broadcast_to([128, 31, 128]), op=ALU.mult),
                   reads=(CONST, P_), writes=(DG,))
                return dg, DG

            def prebuild():
                pending.append(build_dg(0))
                pending.append(build_dg(1))

            def conv_taps(seg):
                s_, t0_, n, t, _ = seg
                p0 = _padpos(t0_)
                cb = [balloc() for _ in range(4)]
                for c in range(4):
                    dg, DG = pending.pop(0) if pending else build_dg(c)

                    def pe(e, c=c, dg=dg):
                        ins = None
                        for tap in range(31):
                            ins = e.matmul(banks[cb[c]][:, 0:n], lhsT=dg[:, tap * 128:(tap + 1) * 128],
                                           rhs=upad[:, c, p0 + tap - 15:p0 + tap - 15 + n], start=(tap == 0), stop=(tap == 30))
                        return ins
                    op("pe", pe, reads=(DG, U[c]), writes=(BANK[cb[c]],))
                return cb

            def evac(seg, cb):
                n = seg[2]
                for c in range(4):
                    op("act", lambda e, c=c: e.activation(out=cvt[c][:, 0:n], in_=banks[cb[c]][:, 0:n], func=AF.Identity,
                                                          bias=params[:, obdw + c:obdw + c + 1], scale=1.0),
                       reads=(BANK[cb[c]], P_), writes=(CVT[c],))
                    bfree(cb[c])

            def pool_seg(seg, hc, HC):
                s_, t0_, n, t, _ = seg
                p0 = _padpos(t0_)
                c0 = t0_ - t * TS
                seq_o, seq_n = SEQS[s_]
                at_start = (t0_ == seq_o)
                at_end = (t0_ + n == seq_o + seq_n)
                for gi, w in enumerate(POOL_W):
                    lo = w // 2
                    hi = w - lo - 1
                    bs_ = balloc(); bz = balloc()

                    def pe(e, gi=gi, lo=lo, hi=hi, bs_=bs_):
                        ins = None
                        for si, sft in enumerate(range(-lo, hi + 1)):
                            ins = e.matmul(banks[bs_][:, 0:n], lhsT=pw[:, gi * 128:(gi + 1) * 128], rhs=zpad[:, gi, p0 + sft:p0 + sft + n],
                                           start=(si == 0), stop=(sft == hi))
                        return ins
                    op("pe", pe, reads=(PW, Z[gi]), writes=(BANK[bs_],))
                    op("pe", lambda e, gi=gi, bz=bz: e.matmul(banks[bz][:, 0:n], lhsT=pw[:, gi * 128:(gi + 1) * 128], rhs=zpad[:, gi, p0:p0 + n], start=True, stop=True),
                       reads=(PW, Z[gi]), writes=(BANK[bz],))
                    pswc = o_psw + j_i * 4 + gi
                    npc = o_np + j_i * 4 + gi
                    op("act", lambda e, bs_=bs_, pswc=pswc: e.activation(out=tmp[2][:, 0:n], in_=banks[bs_][:, 0:n], func=AF.Identity, scale=der[:, pswc:pswc + 1]),
                       reads=(BANK[bs_], DER), writes=(TMP[2],))
                    if at_start and lo > 0:
                        op("dve", lambda e, gi=gi, lo=lo: e.tensor_tensor(out=tmp[2][:, 0:lo], in0=tmp[2][:, 0:lo], in1=pcorr[:, gi * 16:gi * 16 + lo], op=ALU.mult),
                           reads=(TMP[2], CONST), writes=(TMP[2],))
                    if at_end and hi > 0:
                        op("dve", lambda e, gi=gi, hi=hi: e.tensor_tensor(out=tmp[2][:, n - hi:n], in0=tmp[2][:, n - hi:n],
                                                                      in1=pcorr[:, gi * 16 + 16 - hi:gi * 16 + 16], op=ALU.mult),
                           reads=(TMP[2], CONST), writes=(TMP[2],))
                    op("dve", lambda e, gi=gi, bz=bz, npc=npc: e.scalar_tensor_tensor(out=hc[:, 4 + gi, c0:c0 + n], in0=banks[bz][:, 0:n], scalar=der[:, npc:npc + 1],
                                                                               in1=tmp[2][:, 0:n], op0=ALU.mult, op1=ALU.add),
                       reads=(BANK[bz], DER, TMP[2]), writes=(HC,))
                    bfree(bs_); bfree(bz)

            def ln_seg(seg, hc, HC):
                s_, t0_, n, t, _ = seg
                c0 = t0_ - t * TS
                bm = balloc(); bq = balloc()
                for c in range(4):
                    op("act", lambda e, c=c: e.activation(out=sq[0][:, 0:n], in_=cvt[c][:, 0:n], func=AF.Identity), reads=(CVT[c],), writes=(SQ[0],))
                    op("pe", lambda e, c=c: e.matmul(banks[bm][:, 0:n], lhsT=ones1[:, :], rhs=sq[0][:, 0:n], start=(c == 0), stop=(c == 3), skip_group_check=True),
                       reads=(SQ[0], CONST), writes=(BANK[bm],))
                    op("act", lambda e, c=c: e.activation(out=sq[1][:, 0:n], in_=cvt[c][:, 0:n], func=AF.Square), reads=(CVT[c],), writes=(SQ[1],))
                    op("pe", lambda e, c=c: e.matmul(banks[bq][:, 0:n], lhsT=ones1[:, :], rhs=sq[1][:, 0:n], start=(c == 0), stop=(c == 3), skip_group_check=True),
                       reads=(SQ[1], CONST), writes=(BANK[bq],))
                op("act", lambda e: e.activation(out=tmp[0][:, 0:n], in_=banks[bm][:, 0:n], func=AF.Identity, scale=1.0 / 512.0), reads=(BANK[bm],), writes=(TMP[0],))
                op("dve", lambda e: e.tensor_tensor(out=tmp[1][:, 0:n], in0=tmp[0][:, 0:n], in1=tmp[0][:, 0:n], op=ALU.mult), reads=(TMP[0],), writes=(TMP[1],))
                op("dve", lambda e: e.scalar_tensor_tensor(out=tmp[1][:, 0:n], in0=banks[bq][:, 0:n], scalar=1.0 / 512.0, in1=tmp[1][:, 0:n],
                                                           op0=ALU.mult, op1=ALU.subtract), reads=(BANK[bq], TMP[1]), writes=(TMP[1],))
                op("act", lambda e: e.activation(out=rstd[:, 0:n], in_=tmp[1][:, 0:n], func=AF.Ln, bias=epst[:, 0:1], scale=1.0), reads=(TMP[1], CONST), writes=(RSTD,))
                op("act", lambda e: e.activation(out=rstd[:, 0:n], in_=rstd[:, 0:n], func=AF.Exp, scale=-0.5), reads=(RSTD,), writes=(RSTD,))
                bfree(bm); bfree(bq)
                for c in range(4):
                    op("dve", lambda e, c=c: e.tensor_tensor(out=cvt[c][:, 0:n], in0=cvt[c][:, 0:n], in1=tmp[0][:, 0:n], op=ALU.subtract),
                       reads=(CVT[c], TMP[0]), writes=(CVT[c],))
                    op("dve", lambda e, c=c: e.tensor_tensor(out=cvt[c][:, 0:n], in0=cvt[c][:, 0:n], in1=rstd[:, 0:n], op=ALU.mult),
                       reads=(CVT[c], RSTD), writes=(CVT[c],))
                    op("act", lambda e, c=c: e.activation(out=hc[:, c, c0:c0 + n], in_=cvt[c][:, 0:n], func=AF.Silu,
                                                          bias=params[:, olnb + c:olnb + c + 1], scale=params[:, olng + c:olng + c + 1]),
                       reads=(CVT[c], P_), writes=(HC,))

            def outproj(t, hc, HC):
                for j in range(8):
                    wt, WR = (wo0, WO0) if j < 4 else (wo1, WO1)
                    jj = j % 4
                    b0 = balloc()

                    def pe_o(e, b0=b0, wt=wt, jj=jj):
                        ins = None
                        for kc in range(8):
                            ins = e.matmul(banks[b0][:, :], lhsT=wt[:, kc * 512 + jj * 128:kc * 512 + jj * 128 + 128], rhs=hc[:, kc, :],
                                           start=(kc == 0), stop=(kc == 7))
                        return ins
                    op("pe", pe_o, reads=(WR, HC), writes=(BANK[b0],))
                    resid_add(l, 0, j, t, b0)
                    bfree(b0)

            prebuild()
            cb = conv_taps(segs[0])
            prebuild()
            for i, seg in enumerate(segs):
                t = seg[3]
                hc, HC = hcs[t % 2], HCs[t % 2]
                evac(seg, cb)
                if i + 1 < len(segs):
                    cb = conv_taps(segs[i + 1])
                pool_seg(seg, hc, HC)
                ln_seg(seg, hc, HC)
                if i + 2 < len(segs):
                    prebuild()
                if seg[4]:
                    outproj(t, hc, HC)
            for i in free_slots:
                rst["held"][i] = False
            rst["limit"] = None
            ring_done(so0)
            ring_done(so1)
            trk.barrier()

        nsub = 2 * DEPTH if DEBUG_STOP < 0 else DEBUG_STOP
        for g in range(6):
            adaln_group(0, g)
        adaln_finish(0, halves=(0,))
        ada0_rest = [(lambda g=g: adaln_group(0, g)) for g in range(6, 12)] + [lambda: adaln_finish(0, halves=(1,))]
        sub = 0
        for l in range(DEPTH):
            if sub >= nsub:
                break
            if l % 2 == 0:
                attn_group(l, 0)
                attn_group(l, 1, extra=(ada0_rest if l == 0 else None))
            else:
                conv_layer(l)
            sub += 1
            if sub >= nsub:
                break
            nxt = [(l + 1, g) for g in range(12)] if l + 1 < DEPTH else []
            mlp(l, nxt)
            trk.barrier()
            sub += 1

        for t in range(NT):
            tok = slice(t * TS, (t + 1) * TS)
            if DEBUG_STOP >= 0:
                for k in range(8):
                    trk.dma("sp", out_sems.next(), o_yT[:, k, tok], x[:, k, tok], reads=(X[k][t],))
                continue
            b = balloc()
            for k in range(8):
                op("act", lambda e, k=k: e.activation(out=sq[k % 2][:, :], in_=x[:, k, tok], func=AF.Square), reads=(X[k][t],), writes=(SQ[k % 2],))
                op("pe", lambda e, k=k: e.matmul(banks[b][:, :], lhsT=onesm[:, :], rhs=sq[k % 2][:, :], start=(k == 0), stop=(k == 7), skip_group_check=True),
                   reads=(SQ[k % 2], CONST), writes=(BANK[b],))
            op("act", lambda e: e.activation(out=rstd[:, :], in_=banks[b][:, :], func=AF.Ln, bias=epst[:, 0:1], scale=1.0), reads=(BANK[b], CONST), writes=(RSTD,))
            bfree(b)
            op("act", lambda e: e.activation(out=rstd[:, :], in_=rstd[:, :], func=AF.Exp, scale=-0.5), reads=(RSTD,), writes=(RSTD,))
            fo_ = PM["finalg"][0]
            for k in range(8):
                ci = k % 4
                op("dve", lambda e, k=k, ci=ci: e.scalar_tensor_tensor(out=cvt[ci][:, :], in0=x[:, k, tok], scalar=params[:, fo_ + k:fo_ + k + 1],
                                                                     in1=rstd[:, :], op0=ALU.mult, op1=ALU.mult),
                   reads=(X[k][t], RSTD, P_), writes=(CVT[ci],))
                trk.dma("sp", out_sems.next(), o_yT[:, k, tok], cvt[ci][:, :], reads=(CVT[ci],))
        trk.final_wait("sp")
    return nc, wlist


def _count_images():
    n = 0
    for l in range(DEPTH):
        n += 12
        n += 14 if l % 2 == 0 else 5
        n += 16
    return n


N_IMAGES = _count_images()


def _img_k1024(w, col_idx):
    img = np.zeros((128, 8, 512), np.float32)
    col_idx = np.asarray(col_idx)
    valid = col_idx >= 0
    sel = w[:, col_idx[valid]]
    img[:, :, np.nonzero(valid)[0]] = sel.reshape(8, 128, -1).transpose(1, 0, 2)
    return img.reshape(128, SLOT)


def _build_images(wl, inp):
    imgs = np.zeros((N_IMAGES, 128, SLOT), np.float32)
    sw64 = lambda d: (d + 32) % 64
    for n, key in enumerate(wl):
        kind = key[0]
        if kind == "wmod":
            _, l, g = key
            imgs[n] = _img_k1024(inp["w_mod"][l], np.arange(g * 512, (g + 1) * 512))
        elif kind in ("w1",):
            _, l, g = key
            imgs[n] = _img_k1024(inp["mlp_w1"][l], np.arange(g * 512, (g + 1) * 512))
        elif kind == "w2":
            _, l, g = key
            w = inp["mlp_w2"][l][g * 512:(g + 1) * 512]
            imgs[n] = w.reshape(4, 128, 1024).transpose(1, 0, 2).reshape(128, SLOT)
        elif kind == "win":
            _, e, g = key
            w = inp["attn_w_in"][e]
            if g == 0:
                idx = -np.ones(512, np.int64)
                idx[0:192] = 768 + np.arange(192)
                idx[192:320] = 960 + np.arange(128)
                idx[320:352] = 1088 + np.arange(32)
                idx[352:384] = 1088 + (np.arange(32) + 16) % 32
            elif g in (1, 2):
                idx = np.zeros(512, np.int64)
                for c in range(4):
                    for p in range(128):
                        h = c if p < 64 else 4 + c
                        d = p % 64
                        if g == 2:
                            d = sw64(d)
                        idx[c * 128 + p] = h * 64 + d
            else:
                idx = -np.ones(512, np.int64)
                for p in range(128):
                    kh, d = p // 64, p % 64
                    idx[p] = 512 + kh * 64 + d
                    idx[128 + p] = 512 + kh * 64 + sw64(d)
                    idx[256 + p] = 640 + p
            imgs[n] = _img_k1024(w, idx)
        elif kind == "wqkv":
            _, e = key
            img = np.zeros((128, SLOT), np.float32)
            wq = inp["mla_w_qb"][e]
            qcols = np.zeros((8, 128), np.int64)
            for h in range(8):
                qcols[h, 0:64] = h * 96 + np.arange(64)
                qcols[h, 64:96] = h * 96 + 64 + np.arange(32)
                qcols[h, 96:128] = h * 96 + 64 + (np.arange(32) + 16) % 32
            wqa = wq[:, qcols.reshape(-1)]
            img[:, 0:1024] = wqa[0:128]
            img[0:64, 1024:2048] = wqa[128:192]
            wkv = inp["mla_w_kvb"][e]
            for h in range(8):
                img[:, 2048 + h * 64:2048 + (h + 1) * 64] = wkv[:, h * 128:h * 128 + 64]
                img[:, 2560 + h * 64:2560 + (h + 1) * 64] = wkv[:, h * 128 + 64:h * 128 + 128]
            imgs[n] = img
        elif kind == "wout":
            _, e, g = key
            w = inp["attn_w_out"][e]
            rows = np.zeros(1024, np.int64)
            for c in range(4):
                for p in range(128):
                    h = c if p < 64 else 4 + c
                    rows[c * 128 + p] = h * 64 + p % 64
            for c in range(4):
                for p in range(128):
                    h = 2 * c + (p // 64)
                    rows[512 + c * 128 + p] = 512 + h * 64 + p % 64
            wp = w[rows]
            imgs[n] = _img_k1024(wp, np.arange(g * 512, (g + 1) * 512))
        elif kind == "cin":
            _, j, g = key
            imgs[n] = _img_k1024(inp["conv_w_in"][j], np.arange(g * 512, (g + 1) * 512))
        elif kind == "cout":
            _, j, g = key
            imgs[n] = _img_k1024(inp["conv_w_out"][j], np.arange(g * 512, (g + 1) * 512))
        else:
            raise KeyError(key)
    return imgs


_CACHE = {}


def kernel(**inputs):
    inp = {k: np.asarray(v) for k, v in inputs.items()}
    if "prog" not in _CACHE:
        _CACHE["prog"] = build_program()
    nc, wl = _CACHE["prog"]
    assert len(wl) == N_IMAGES or DEBUG_STOP >= 0, (len(wl), N_IMAGES)
    consts = _consts()
    imgs = _build_images(wl, inp)

    def fm(v):
        return np.ascontiguousarray(v.reshape(8, 128).T)

    poolw = np.ascontiguousarray(inp["pool_w"].transpose(0, 2, 1, 3).reshape(2, 128, 512))
    dwp = np.zeros((128, 2, 4, 31), np.float32)
    for j in range(2):
        for c in range(4):
            dwp[:, j, c, :] = inp["conv_dw"][j][:, c * 128:(c + 1) * 128].T
    in_maps = []
    for i in range(8):
        toks = np.concatenate([inp["x_prompt"][2 * i], inp["x_prompt"][2 * i + 1], inp["x_sample"][i]], axis=0)
        xT = np.ascontiguousarray(toks.reshape(NTOK, 8, 128).transpose(2, 1, 0))
        P = np.zeros((128, PM["_n"]), np.float32)

        def put(name, arr):
            o, n = PM[name]
            P[:, o:o + n] = arr.reshape(128, n)
        put("bmod", np.stack([inp["b_mod"][l].reshape(48, 128).T for l in range(4)], 1))
        put("normg", np.stack([np.stack([fm(inp["norm_g"][l, w]) for w in range(2)], 1) for l in range(4)], 1))
        put("finalg", fm(inp["final_g"]))
        put("cT", np.stack([fm(inp["c_ctx"]), fm(inp["c"][i])], 2))
        qn = np.zeros((128, 2, 2), np.float32)
        for e in range(2):
            qn[:, e, 0] = inp["mla_q_norm"][e, 0:128]
            qn[0:64, e, 1] = inp["mla_q_norm"][e, 128:192]
        put("qnorm", qn)
        put("kvnorm", np.stack([inp["mla_kv_norm"][e] for e in range(2)], 1))
        put("sink", np.broadcast_to(inp["attn_sink"].reshape(1, 16), (128, 16)))
        for nm, src in (("bdw", "conv_dw_b"), ("lng", "conv_ln_g"), ("lnb", "conv_ln_b"), ("pscale", "pool_scale")):
            put(nm, np.stack([inp[src][j].reshape(4, 128).T for j in range(2)], 1))
        put("dw", dwp)
        m = {
            "xT": xT, "params": P, "wstream": imgs,
            "ropeA": consts["ropeA"], "ropeB": consts["ropeB"], "maskb": consts["maskb"], "ident": consts["ident"],
            "pcorr": consts["pcorr"],
            "ckT": np.ascontiguousarray(inp["cache_win_k"][i].reshape(2, NCTX, 128).transpose(0, 2, 1)),
            "cv": np.ascontiguousarray(inp["cache_win_v"][i].reshape(2, 2, 128, 128)),
            "cckvT": np.ascontiguousarray(inp["cache_mla_ckv"][i].transpose(0, 2, 1)),
            "ckrT": np.ascontiguousarray(inp["cache_mla_krope"][i].transpose(0, 2, 1)),
            "poolw": poolw,
        }
        in_maps.append(m)
    res = run_bass_kernel_spmd(nc, in_maps, core_ids=list(range(8)))
    R = res.results
    y_prompt = np.zeros((16, 256, D), np.float32)
    y_sample = np.zeros((8, 2048, D), np.float32)
    nk = np.zeros((16, 2, 256, 2, 64), np.float32)
    nv = np.zeros((16, 2, 256, 2, 64), np.float32)
    nckv = np.zeros((16, 2, 256, 128), np.float32)
    nkr = np.zeros((16, 2, 256, 32), np.float32)
    for i in range(8):
        r = R[i]
        y = np.asarray(r["yT"]).transpose(2, 1, 0).reshape(NTOK, D)
        y_prompt[2 * i] = y[0:256]
        y_prompt[2 * i + 1] = y[256:512]
        y_sample[i] = y[512:]
        kT = np.asarray(r["okT"])
        v = np.asarray(r["ov"])
        ck = np.asarray(r["ockvT"])
        kr = np.asarray(r["okrT"])
        for s in range(2):
            b = 2 * i + s
            for e in range(2):
                nk[b, e] = kT[e][:, s * 256:(s + 1) * 256].T.reshape(256, 2, 64)
                nv[b, e] = v[e][s * 256:(s + 1) * 256].reshape(256, 2, 64)
                nckv[b, e] = ck[e][:, s * 256:(s + 1) * 256].T
                nkr[b, e] = kr[e][:, s * 256:(s + 1) * 256].T
    return (y_prompt, y_sample, nk, nv, nckv, nkr)
```

```python
import contextlib
import os
import numpy as np
import concourse.bass as bass
import concourse.mybir as mybir
from concourse.bass_utils import run_bass_kernel_spmd

F32 = mybir.dt.float32
BF16 = mybir.dt.bfloat16
AF = mybir.ActivationFunctionType
ALU = mybir.AluOpType

D = 1024
DEPTH = 4
NTOK = 2560
NT = 5
TS = 512
NCTX = 256
EPS = 1e-6
NEG = -30000.0
SHIFT = 10.0
SLOT = 4096
NRING = 4
SEM_LIMIT = 30000

DEBUG_STOP = int(os.environ.get("KDEBUG_STOP", "-1"))


class Res:
    __slots__ = ("name", "w", "rd")

    def __init__(self, name):
        self.name = name
        self.w = None
        self.rd = {}


class Eng:
    def __init__(self, trk, name, handle, inc):
        self.trk = trk
        self.name = name
        self.h = handle
        self.inc = inc
        self.sem = None
        self.count = 0
        self.seen = {}
        self.sems = []

    def new_sem(self):
        self.sem = self.trk.alloc_sem(self.name)
        self.sems.append(self.sem)
        self.count = 0


class Tracker:
    def __init__(self, nc, es):
        self.nc = nc
        self.es = es
        self.nsem = 0
        self.E = {}
        for name, h, inc in (("pe", nc.tensor, 1), ("act", nc.scalar, 1), ("dve", nc.vector, 1),
                             ("pool", nc.gpsimd, 1)):
            e = Eng(self, name, h, inc)
            e.new_sem()
            self.E[name] = e
        self.Q = {"sp": Eng(self, "sp", nc.sync, 16), "poolq": self.E["pool"]}
        self.all_events = {}

    def alloc_sem(self, name):
        self.nsem += 1
        return self.es.enter_context(self.nc.semaphore(f"s_{name}_{self.nsem}"))

    def _wait(self, eng, ev):
        if ev is None:
            return
        sem, val, src = ev
        if src == "pe" and eng.name == "pe":
            return
        k = id(sem)
        if eng.seen.get(k, 0) >= val:
            return
        eng.seen[k] = val
        eng.h.wait_ge(sem, val)

    def _deps(self, eng, reads, writes, same_ok=False):
        for r in reads:
            if r.w is not None:
                self._wait(eng, r.w)
        for r in writes:
            if r.w is not None:
                self._wait(eng, r.w)
            for ev in r.rd.values():
                self._wait(eng, ev)

    def _record(self, ev, reads, writes):
        for r in reads:
            r.rd[id(ev[0])] = ev
        for r in writes:
            r.w = ev
            r.rd = {}
        self.all_events[id(ev[0])] = ev

    def op(self, ename, fn, reads=(), writes=()):
        eng = self.E[ename]
        if eng.count >= SEM_LIMIT:
            eng.new_sem()
        self._deps(eng, reads, writes)
        ins = fn(eng.h)
        eng.count += 1
        ins.then_inc(eng.sem, 1)
        ev = (eng.sem, eng.count, ename)
        self._record(ev, reads, writes)
        return ev

    def dma(self, qname, dsem, out, in_, reads=(), writes=()):
        q = self.Q[qname]
        self._deps(q, reads, writes)
        if dsem.last is not None:
            self._wait(q, dsem.last)
        if dsem.count + 16 > SEM_LIMIT:
            dsem.sem = self.alloc_sem("dma")
            dsem.count = 0
        ins = q.h.dma_start(out=out, in_=in_)
        dsem.count += 16
        ins.then_inc(dsem.sem, 16)
        ev = (dsem.sem, dsem.count, "dma")
        dsem.last = ev
        self._record(ev, reads, writes)
        return ev

    def barrier(self):
        evs = list(self.all_events.values())
        for e in list(self.E.values()) + [self.Q["sp"]]:
            for ev in evs:
                self._wait(e, ev)

    def final_wait(self, ename="sp"):
        q = self.Q[ename]
        for ev in list(self.all_events.values()):
            self._wait(q, ev)


class DmaSem:
    def __init__(self, trk, name):
        self.sem = trk.alloc_sem(name)
        self.count = 0
        self.last = None


class DmaSemPool:
    def __init__(self, trk, n, name):
        self.s = [DmaSem(trk, f"{name}{i}") for i in range(n)]
        self.i = 0

    def next(self):
        s = self.s[self.i % len(self.s)]
        self.i += 1
        return s


def _rope_tables(n, dim, grid_w=64, base=10000.0):
    rows = n // grid_w
    row = np.repeat(np.arange(rows), grid_w).astype(np.float32)
    col = np.tile(np.arange(grid_w), rows).astype(np.float32)
    quarter = dim // 4
    inv_freq = (base ** (-np.arange(quarter, dtype=np.float32) / quarter)).astype(np.float32)
    ang = np.concatenate([row[:, None] * inv_freq, col[:, None] * inv_freq], axis=-1).astype(np.float32)
    cos = np.cos(ang).astype(np.float32)
    sin = np.sin(ang).astype(np.float32)
    half = dim // 2
    cos2 = np.concatenate([cos, cos], axis=1).T
    sins = np.concatenate([-sin, sin], axis=1).T
    return np.ascontiguousarray(cos2), np.ascontiguousarray(sins)


def _consts():
    c = {}
    ca, sa = _rope_tables(2048, 64)
    c["ropeA"] = np.ascontiguousarray(np.stack([np.concatenate([ca, ca], 0), np.concatenate([sa, sa], 0)], 1))
    cb, sb = _rope_tables(2048, 32)
    c["ropeB"] = np.ascontiguousarray(np.stack([cb, sb], 1))
    b = np.arange(128)[:, None]
    a = np.arange(128)[None, :]
    m = np.zeros((128, 384), np.float32)
    m[:, 0:128] = np.where(b <= a, 0.0, NEG)
    m[:, 256:384] = np.where(a <= b, 0.0, NEG)
    c["maskb"] = m
    c["ident"] = np.eye(128, dtype=np.float32)
    corr = np.ones((128, 4, 2, 8), np.float32)
    for gi, w in enumerate((2, 4, 8, 16)):
        lo = w // 2
        hi = w - lo - 1
        for t in range(lo):
            corr[:, gi, 0, t] = w / float(t + hi + 1)
        for q in range(hi):
            corr[:, gi, 1, 7 - q] = w / float(lo + q + 1)
    c["pcorr"] = corr.reshape(128, 64)
    return c


def _pmap():
    m = {}
    o = 0

    def add(name, n):
        nonlocal o
        m[name] = (o, n)
        o += n
    add("bmod", 4 * 48)
    add("normg", 4 * 2 * 8)
    add("finalg", 8)
    add("cT", 16)
    add("qnorm", 4)
    add("kvnorm", 2)
    add("sink", 16)
    add("bdw", 8)
    add("lng", 8)
    add("lnb", 8)
    add("pscale", 8)
    add("dw", 2 * 4 * 31)
    m["_n"] = o
    return m


PM = _pmap()

POOL_W = (2, 4, 8, 16)
SEQS = [(0, 256), (256, 256), (512, 2048)]
PADB = [0, 288, 576]
NPAD = 2656


def _padpos(tok):
    for s, (o, n) in enumerate(SEQS):
        if o <= tok < o + n:
            return PADB[s] + 16 + (tok - o)
    raise ValueError


ARENA = 29568


def build_program():
    nc = bass.Bass("TRN2", target_bir_lowering=False)
    es = contextlib.ExitStack()
    wlist = []

    def dram(name, shape, kind="ExternalInput", dt=F32):
        return nc.dram_tensor(name, list(shape), dt, kind=kind).ap()

    d_xT = dram("xT", [128, 8, NTOK])
    d_params = dram("params", [128, PM["_n"]])
    d_ropeA = dram("ropeA", [128, 2, 2048])
    d_ropeB = dram("ropeB", [32, 2, 2048])
    d_maskb = dram("maskb", [128, 384])
    d_ident = dram("ident", [128, 128])
    d_pcorr = dram("pcorr", [128, 64])
    d_ckT = dram("ckT", [2, 128, NCTX])
    d_cv = dram("cv", [2, 2, 128, 128])
    d_cckvT = dram("cckvT", [2, 128, NCTX])
    d_ckrT = dram("ckrT", [2, 32, NCTX])
    d_poolw = dram("poolw", [2, 128, 4 * 128])
    d_w = dram("wstream", [N_IMAGES, 128, SLOT])
    o_yT = dram("yT", [128, 8, NTOK], kind="ExternalOutput")
    o_kT = dram("okT", [2, 128, 512], kind="ExternalOutput")
    o_v = dram("ov", [2, 512, 128], kind="ExternalOutput")
    o_ckvT = dram("ockvT", [2, 128, 512], kind="ExternalOutput")
    o_krT = dram("okrT", [2, 32, 512], kind="ExternalOutput")

    with es:
        trk = Tracker(nc, es)
        op = trk.op

        def sb(name, shape, dt):
            return es.enter_context(nc.sbuf_tensor(name, list(shape), dt))

        x = sb("x", [128, 8, NTOK], F32)
        X = [[Res(f"x{k}_{t}") for t in range(NT)] for k in range(8)]
        ring = [sb(f"ring{i}", [128, SLOT], BF16) for i in range(NRING)]
        RING = [Res(f"ring{i}") for i in range(NRING)]
        ring_sem = [DmaSem(trk, f"ring{i}") for i in range(NRING)]
        arena = sb("arena", [128, ARENA], BF16)
        params = sb("params_sb", [128, PM["_n"]], F32)
        P_ = Res("params")
        NDER = 4 * 6 * 8 * 2 + 64
        der = sb("der", [128, NDER], F32)
        DER = Res("der")
        mod = sb("mod", [128, 4, 48, 2], F32)
        MOD = [Res(f"mod{l}") for l in range(4)]
        scT = sb("scT", [128, 8, 2], BF16)
        SCT = Res("scT")
        ident = sb("ident_sb", [128, 128], BF16)
        onesm = sb("onesm", [128, 128], BF16)
        ones1 = sb("ones1", [128, 128], BF16)
        ones_lo = sb("ones_lo", [128, 128], BF16)
        maskb = sb("maskb_sb", [128, 384], BF16)
        pcorr = sb("pcorr_sb", [128, 64], F32)
        epst = sb("epst", [128, 1], F32)
        negc = sb("negc", [128, 1], F32)
        pw = sb("poolw_sb", [128, 512], BF16)
        PW = Res("pw")
        CONST = Res("const")
        rstd = sb("rstd", [128, 512], F32)
        RSTD = Res("rstd")
        sq = [sb(f"sq{i}", [128, 512], BF16) for i in range(2)]
        SQ = [Res(f"sq{i}") for i in range(2)]
        ft = [sb(f"ft{i}", [128, 512], F32) for i in range(7)]
        FT = [Res(f"ft{i}") for i in range(7)]
        tmp, TMP = ft[0:3], FT[0:3]
        cvt, CVT = ft[3:7], FT[3:7]
        rec, REC = ft[3], FT[3]
        ostage, OST = ft[4:6], FT[4:6]
        stage, STAGE = ft[6], FT[6]
        pt = [sb(f"pt{i}", [128, 512], BF16) for i in range(3)]
        PT = [Res(f"pt{i}") for i in range(3)]
        rope_t = sb("rope_t", [128, 2, TS], F32)
        ROPE = Res("rope")
        ropeB_all = sb("ropeB_all", [128, 2, TS], F32)
        ROPEB = Res("ropeB")
        ost_i = [0]

        banks = [es.enter_context(nc.psum_tensor(f"bank{i}", [128, 512], F32)) for i in range(8)]
        BANK = [Res(f"bank{i}") for i in range(8)]
        bank_free = list(range(8))

        side_free = []
        pool_sel = ["g"]

        def balloc():
            if pool_sel[0] == "side":
                assert side_free, "out of side PSUM banks"
                return side_free.pop(0)
            assert bank_free, "out of PSUM banks"
            return bank_free.pop(0)

        def bfree(b):
            if b in SIDE_POOL and pool_sel[0] == "side":
                side_free.append(b)
            else:
                bank_free.append(b)

        SIDE_POOL = [4, 5]

        def reserve_side(on):
            if on:
                for b in SIDE_POOL:
                    bank_free.remove(b)
                    side_free.append(b)
            else:
                for b in SIDE_POOL:
                    side_free.remove(b)
                    bank_free.append(b)

        sp_sems = DmaSemPool(trk, 6, "sp")
        out_sems = DmaSemPool(trk, 4, "out")

        rst = {"issued": 0, "got": 0, "held": [False] * NRING, "limit": None}

        def ring_pump():
            while rst["issued"] < N_IMAGES and rst["issued"] < rst["got"] + NRING:
                if rst["limit"] is not None and rst["issued"] >= rst["limit"]:
                    break
                i = rst["issued"]
                free = [k for k in range(NRING) if not rst["held"][k]]
                if not free:
                    break
                s_ = i % NRING if (i % NRING) in free else free[0]
                trk.dma("poolq", ring_sem[s_], ring[s_][:, :], d_w[i], reads=(), writes=(RING[s_],))
                rst["held"][s_] = True
                rst.setdefault("slot_of", {})[i] = s_
                rst["issued"] += 1

        def ring_get(key):
            i = rst["got"]
            wlist.append(key)
            ring_pump()
            assert rst["issued"] > i, ("ring stalled", key)
            rst["got"] += 1
            s_ = rst["slot_of"][i]
            return ring[s_], RING[s_], s_

        def ring_done(s_):
            rst["held"][s_] = False
            ring_pump()

        trk.dma("sp", sp_sems.next(), params[:, :], d_params[:, :], writes=(P_,))
        for k in range(8):
            trk.dma("sp", sp_sems.next(), x[:, k, :], d_xT[:, k, :], writes=tuple(X[k]))

        def load_cast(dst_ap, src_ap, nparts, ncols, wres):
            trk.dma("sp", sp_sems.next(), stage[0:nparts, 0:ncols], src_ap, writes=(STAGE,))
            op("dve", lambda e: e.tensor_copy(out=dst_ap, in_=stage[0:nparts, 0:ncols]), reads=(STAGE,), writes=wres)

        load_cast(ident[:, :], d_ident[:, :], 128, 128, (CONST,))
        load_cast(maskb[:, :], d_maskb[:, :], 128, 384, (CONST,))
        trk.dma("sp", sp_sems.next(), pcorr[:, :], d_pcorr[:, :], writes=(CONST,))
        op("dve", lambda e: e.memset(onesm[:, :], 1.0 / 1024.0), writes=(CONST,))
        op("dve", lambda e: e.memset(ones1[:, :], 1.0), writes=(CONST,))
        op("dve", lambda e: e.memset(ones_lo[:, :], 0.0), writes=(CONST,))
        op("dve", lambda e: e.memset(ones_lo[0:64, :], 1.0), writes=(CONST,))
        op("dve", lambda e: e.memset(epst[:, :], EPS), writes=(CONST,))
        op("dve", lambda e: e.memset(negc[:, :], -SHIFT), writes=(CONST,))

        o_cT = PM["cT"][0]
        op("act", lambda e: e.activation(out=scT[:, :, :], in_=params[:, o_cT:o_cT + 16].rearrange("p (k j) -> p k j", j=2),
                                         func=AF.Silu), reads=(P_,), writes=(SCT,))

        def dcol(l, which, k, g):
            return ((l * 6 + which) * 8 + k) * 2 + g

        o_es = 4 * 6 * 8 * 2
        o_np = o_es + 16
        o_psw = o_np + 8
        o_sink = PM["sink"][0]
        op("act", lambda e: e.activation(out=der[:, o_es:o_es + 16], in_=params[:, o_sink:o_sink + 16], func=AF.Exp, bias=negc[:, 0:1], scale=1.0),
           reads=(P_, CONST), writes=(DER,))
        o_ps = PM["pscale"][0]
        op("dve", lambda e: e.tensor_scalar(out=der[:, o_np:o_np + 8], in0=params[:, o_ps:o_ps + 8], scalar1=-1.0,
                                            scalar2=None, op0=ALU.mult), reads=(P_,), writes=(DER,))
        for j in range(2):
            for gi, w in enumerate(POOL_W):
                op("dve", lambda e, j=j, gi=gi, w=w: e.tensor_scalar(
                    out=der[:, o_psw + j * 4 + gi:o_psw + j * 4 + gi + 1],
                    in0=params[:, o_ps + j * 4 + gi:o_ps + j * 4 + gi + 1], scalar1=1.0 / w, scalar2=None,
                    op0=ALU.mult), reads=(P_,), writes=(DER,))

        def adaln_group(l, g):
            wt, WR, ws = ring_get(("wmod", l, g))
            b = balloc()

            def pe(e):
                ins = None
                for cc in range(4):
                    for kc in range(8):
                        ins = e.matmul(banks[b][:, cc * 2:cc * 2 + 2], lhsT=wt[:, kc * 512 + cc * 128:kc * 512 + cc * 128 + 128],
                                       rhs=scT[:, kc, :], start=(kc == 0), stop=(kc == 7), skip_group_check=True)
                return ins
            op("pe", pe, reads=(WR, SCT), writes=(BANK[b],))
            ring_done(ws)
            ob = PM["bmod"][0] + l * 48 + 4 * g
            for j in range(2):
                op("dve", lambda e, j=j: e.tensor_tensor(
                    out=mod[:, l, 4 * g:4 * g + 4, j],
                    in0=banks[b][:, 0:8].rearrange("p (c j) -> p c j", j=2)[:, :, j],
                    in1=params[:, ob:ob + 4], op=ALU.add), reads=(BANK[b], P_), writes=(MOD[l],))
            bfree(b)

        def adaln_finish(l, halves=(0, 1)):
            og = PM["normg"][0]
            for half in halves:
                for k in range(8):
                    gc = og + (l * 2 + half) * 8 + k
                    a0 = dcol(l, half * 3 + 0, k, 0)
                    op("dve", lambda e, k=k, half=half, gc=gc, a0=a0: e.tensor_scalar(
                        out=der[:, a0:a0 + 2], in0=mod[:, l, (half * 3 + 1) * 8 + k, :], scalar1=1.0, scalar2=params[:, gc:gc + 1],
                        op0=ALU.add, op1=ALU.mult), reads=(MOD[l], P_), writes=(DER,))
                    b0 = dcol(l, half * 3 + 1, k, 0)
                    op("dve", lambda e, k=k, half=half, b0=b0: e.tensor_copy(
                        out=der[:, b0:b0 + 2], in_=mod[:, l, (half * 3 + 0) * 8 + k, :]), reads=(MOD[l],), writes=(DER,))
                    g0 = dcol(l, half * 3 + 2, k, 0)
                    op("dve", lambda e, k=k, half=half, g0=g0: e.tensor_copy(
                        out=der[:, g0:g0 + 2], in_=mod[:, l, (half * 3 + 2) * 8 + k, :]), reads=(MOD[l],), writes=(DER,))

        def dv(l, which, k, g):
            c = dcol(l, which, k, g)
            return der[:, c:c + 1]

        def norm_tile(l, half, t, dest, DEST):
            grp = 0 if t == 0 else 1
            tok = slice(t * TS, (t + 1) * TS)
            b = balloc()
            for k in range(8):
                op("act", lambda e, k=k: e.activation(out=sq[k % 2][:, :], in_=x[:, k, tok], func=AF.Square),
                   reads=(X[k][t],), writes=(SQ[k % 2],))
                op("pe", lambda e, k=k: e.matmul(banks[b][:, :], lhsT=onesm[:, :], rhs=sq[k % 2][:, :], start=(k == 0),
                                                 stop=(k == 7), skip_group_check=True),
                   reads=(SQ[k % 2], CONST), writes=(BANK[b],))
            op("act", lambda e: e.activation(out=rstd[:, :], in_=banks[b][:, :], func=AF.Ln, bias=epst[:, 0:1], scale=1.0),
               reads=(BANK[b], CONST), writes=(RSTD,))
            bfree(b)
            op("act", lambda e: e.activation(out=rstd[:, :], in_=rstd[:, :], func=AF.Exp, scale=-0.5), reads=(RSTD,), writes=(RSTD,))
            for k in range(8):
                ti = k % 2
                op("dve", lambda e, k=k, ti=ti: e.tensor_tensor(out=tmp[ti][:, :], in0=x[:, k, tok], in1=rstd[:, :], op=ALU.mult),
                   reads=(X[k][t], RSTD), writes=(TMP[ti],))
                op("act", lambda e, k=k, ti=ti: e.activation(out=dest[:, k, :], in_=tmp[ti][:, :], func=AF.Identity,
                                                           bias=dv(l, half * 3 + 1, k, grp), scale=dv(l, half * 3 + 0, k, grp)),
                   reads=(TMP[ti], DER), writes=(DEST,))

        def norm_gen(l, half, t, dest, DEST):
            grp = 0 if t == 0 else 1
            tokx = slice(t * TS, (t + 1) * TS)
            b = balloc()

            def mm_(k):
                op("pe", lambda e: e.matmul(banks[b][:, :], lhsT=onesm[:, :], rhs=sq[k % 2][:, :], start=(k == 0),
                                            stop=(k == 7), skip_group_check=True),
                   reads=(SQ[k % 2], CONST), writes=(BANK[b],))
            for p_ in range(5):
                if p_ >= 1:
                    mm_(2 * p_ - 2)
                    mm_(2 * p_ - 1)
                if p_ < 4:
                    for k in (2 * p_, 2 * p_ + 1):
                        op("act", lambda e, k=k: e.activation(out=sq[k % 2][:, :], in_=x[:, k, tokx], func=AF.Square),
                           reads=(X[k][t],), writes=(SQ[k % 2],))
                    yield
            op("act", lambda e: e.activation(out=rstd[:, :], in_=banks[b][:, :], func=AF.Ln, bias=epst[:, 0:1], scale=1.0),
               reads=(BANK[b], CONST), writes=(RSTD,))
            bfree(b)
            op("act", lambda e: e.activation(out=rstd[:, :], in_=rstd[:, :], func=AF.Exp, scale=-0.5), reads=(RSTD,), writes=(RSTD,))
            yield
            for k in range(8):
                ti = k % 2
                op("dve", lambda e, k=k, ti=ti: e.tensor_tensor(out=tmp[ti][:, :], in0=x[:, k, tokx], in1=rstd[:, :], op=ALU.mult),
                   reads=(X[k][t], RSTD), writes=(TMP[ti],))
                op("act", lambda e, k=k, ti=ti: e.activation(out=dest[:, k, :], in_=tmp[ti][:, :], func=AF.Identity,
                                                           bias=dv(l, half * 3 + 1, k, grp), scale=dv(l, half * 3 + 0, k, grp)),
                   reads=(TMP[ti], DER), writes=(DEST,))
                if k % 4 == 3:
                    yield

        def drive(gens):
            gens = [g_ for g_ in gens if g_ is not None]
            while gens:
                for g_ in list(gens):
                    try:
                        next(g_)
                    except StopIteration:
                        gens.remove(g_)

        def resid_add(l, half, j, t, b):
            grp = 0 if t == 0 else 1
            xs = x[:, j, t * TS:(t + 1) * TS]
            op("dve", lambda e: e.scalar_tensor_tensor(out=xs, in0=banks[b][:, :], scalar=dv(l, half * 3 + 2, j, grp),
                                                       in1=xs, op0=ALU.mult, op1=ALU.add),
               reads=(BANK[b], DER, X[j][t]), writes=(X[j][t],))

        def mlp(l, next_adaln):
            hbuf = arena[:, 0:8 * NTOK].rearrange("p (k n) -> p k n", k=8)
            H = [Res(f"h{t}") for t in range(NT)]
            h1 = [arena[:, 8 * NTOK + i * 2048:8 * NTOK + (i + 1) * 2048].rearrange("p (c n) -> p c n", c=4) for i in range(2)]
            H1 = [Res("h1a"), Res("h1b")]
            norm_tile(l, 1, 0, hbuf[:, :, 0:TS], H[0])
            ada = list(next_adaln)
            seq = [(g, t) for g in range(8) for t in range(NT)]
            w1s, w2s = {}, {}

            def h1stage(k):
                g, t = seq[k]
                hi = k % 2
                if t == 0:
                    w1s[g] = ring_get(("w1", l, g))
                w1, W1, s1 = w1s[g]
                ng = None
                if g == 0 and t + 1 < NT:
                    ng = norm_gen(l, 1, t + 1, hbuf[:, :, (t + 1) * TS:(t + 2) * TS], H[t + 1])
                for c in range(4):
                    if ng is not None:
                        for _ in range(2):
                            next(ng, None)
                    b_ = balloc()

                    def pe(e, c=c, b_=b_):
                        ins = None
                        for kc in range(8):
                            ins = e.matmul(banks[b_][:, :], lhsT=w1[:, kc * 512 + c * 128:kc * 512 + c * 128 + 128],
                                           rhs=hbuf[:, kc, t * TS:(t + 1) * TS], start=(kc == 0), stop=(kc == 7))
                        return ins
                    op("pe", pe, reads=(W1, H[t]), writes=(BANK[b_],))
                    ti = c % 2
                    op("act", lambda e, b_=b_, ti=ti: e.activation(out=tmp[ti][:, :], in_=banks[b_][:, :], func=AF.Relu),
                       reads=(BANK[b_],), writes=(TMP[ti],))
                    bfree(b_)
                    op("dve", lambda e, c=c, ti=ti: e.tensor_tensor(out=h1[hi][:, c, :], in0=tmp[ti][:, :], in1=tmp[ti][:, :],
                                                                    op=ALU.mult), reads=(TMP[ti],), writes=(H1[hi],))
                if t == NT - 1:
                    ring_done(s1)
                if ng is not None:
                    for _ in ng:
                        pass

            def outstage(k):
                g, t = seq[k]
                hi = k % 2
                if t == 0:
                    w2s[g] = ring_get(("w2", l, g))
                w2, W2, s2 = w2s[g]
                for j in range(8):
                    b_ = balloc()

                    def pe2(e, j=j, b_=b_):
                        ins = None
                        for c in range(4):
                            ins = e.matmul(banks[b_][:, :], lhsT=w2[:, c * 1024 + j * 128:c * 1024 + j * 128 + 128],
                                           rhs=h1[hi][:, c, :], start=(c == 0), stop=(c == 3))
                        return ins
                    op("pe", pe2, reads=(W2, H1[hi]), writes=(BANK[b_],))
                    resid_add(l, 1, j, t, b_)
                    bfree(b_)
                if t == NT - 1:
                    ring_done(s2)
                    for _ in range(2):
                        if ada:
                            adaln_group(*ada.pop(0))
                            if not ada:
                                adaln_finish(l + 1)

            h1stage(0)
            for k in range(len(seq)):
                if k + 1 < len(seq):
                    h1stage(k + 1)
                outstage(k)
            assert not ada

        pti = [0]

        NPT = 3
        LA = 3
        OB_POOL = [6, 7]
        ob_i = [0]

        def reserve_obanks(on):
            if on:
                for b in OB_POOL:
                    bank_free.remove(b)
            else:
                bank_free.extend(OB_POOL)

        def run_attn(jobs, scale, side=None):
            items = []
            for ji, job in enumerate(jobs):
                for gi, g in enumerate(job["groups"]):
                    items.append((ji, gi, g))
            M = len(items)
            sbank = {}
            ptb = {}
            obank = {}
            for i in range(M + LA):
                if i < M:
                    ji, gi, g = items[i]
                    sbk = balloc()
                    sbank[i] = sbk
                    n = g["n"]

                    def pe(e, g=g, sbk=sbk, n=n):
                        ins = e.matmul(banks[sbk][:, 0:n], lhsT=g["k"], rhs=g["q"], start=True, stop=(g["mask"] is None), skip_group_check=True)
                        if g["mask"] is not None:
                            ins = e.matmul(banks[sbk][:, 0:n], lhsT=ident[:, :], rhs=g["mask"], start=False, stop=True, skip_group_check=True)
                        return ins
                    op("pe", pe, reads=(g["K"], g["Q"], CONST) + ((g["Q2"],) if "Q2" in g else ()), writes=(BANK[sbk],))
                j = i - (LA - 1)
                if 0 <= j < M:
                    ji, gi, g = items[j]
                    sbk = sbank.pop(j)
                    n = g["n"]
                    pi = pti[0] % NPT
                    pti[0] += 1
                    ptb[j] = pi
                    op("act", lambda e, sbk=sbk, n=n, pi=pi: e.activation(out=pt[pi][:, 0:n], in_=banks[sbk][:, 0:n], func=AF.Exp, bias=negc[:, 0:1], scale=scale),
                       reads=(BANK[sbk], CONST), writes=(PT[pi],))
                    bfree(sbk)
                k = i - LA
                if 0 <= k < M:
                    ji, gi, g = items[k]
                    if gi == 0:
                        obank[ji] = OB_POOL[ob_i[0] % 2]
                        ob_i[0] += 1
                    ob = obank[ji]
                    pi = ptb.pop(k)
                    n = g["n"]
                    c0 = g["c0"]
                    last = gi == len(jobs[ji]["groups"]) - 1
                    op("pe", lambda e, g=g, pi=pi, n=n, c0=c0, f=(gi == 0), la=last, ob=ob: e.matmul(
                        banks[ob][:, c0:c0 + n], lhsT=g["v"], rhs=pt[pi][:, 0:n], start=f, stop=la, skip_group_check=True),
                       reads=(g["V"], PT[pi]), writes=(BANK[ob],))
                    if last:
                        jobs[ji]["finish"](ob)
                        obank.pop(ji)
                if side is not None:
                    side()

        def normalize(obank, base, nq, sink_col, dst, wres, extra_reads=(), on_dve=False):
            dbase = 64 - base
            ds = slice(dbase, dbase + 64)
            if on_dve:
                op("dve", lambda e: e.tensor_scalar(out=rec[ds, 0:nq], in0=banks[obank][ds, 0:nq],
                                                    scalar1=der[ds, sink_col:sink_col + 1], scalar2=None, op0=ALU.add),
                   reads=(BANK[obank], DER), writes=(REC,))
                op("dve", lambda e: e.reciprocal(out=rec[ds, 0:nq], in_=rec[ds, 0:nq]), reads=(REC,), writes=(REC,))
            else:
                if sink_col is not None:
                    op("act", lambda e: e.activation(out=rec[ds, 0:nq], in_=banks[obank][ds, 0:nq], func=AF.Ln,
                                                     bias=der[ds, sink_col:sink_col + 1], scale=1.0),
                       reads=(BANK[obank], DER), writes=(REC,))
                else:
                    op("act", lambda e: e.activation(out=rec[ds, 0:nq], in_=banks[obank][ds, 0:nq], func=AF.Ln),
                       reads=(BANK[obank],), writes=(REC,))
                op("act", lambda e: e.activation(out=rec[ds, 0:nq], in_=rec[ds, 0:nq], func=AF.Exp, scale=-1.0), reads=(REC,), writes=(REC,))
            op("dve", lambda e: e.tensor_tensor(out=dst, in0=banks[obank][base:base + 64, 0:nq], in1=rec[ds, 0:nq], op=ALU.mult),
               reads=(BANK[obank], REC) + tuple(extra_reads), writes=wres)

        def rope_combine(b1, b2, nrows, dst, wres, p0=0, tab=None, TAB=None, tp0=None):
            ps = slice(p0, p0 + nrows)
            if tab is None:
                tab, TAB, tp0 = rope_t, ROPE, p0
            ts_ = slice(tp0, tp0 + nrows)
            op("dve", lambda e: e.tensor_tensor(out=tmp[0][ps, :], in0=banks[b1][ps, :], in1=tab[ts_, 0, :], op=ALU.mult),
               reads=(BANK[b1], TAB), writes=(TMP[0],))
            op("dve", lambda e: e.tensor_tensor(out=tmp[1][ps, :], in0=banks[b2][ps, :], in1=tab[ts_, 1, :], op=ALU.mult),
               reads=(BANK[b2], TAB), writes=(TMP[1],))
            op("dve", lambda e: e.tensor_tensor(out=dst, in0=tmp[0][ps, :], in1=tmp[1][ps, :], op=ALU.add),
               reads=(TMP[0], TMP[1]), writes=wres)

        def load_rope(which, T, p0=0):
            rsl = slice(T * TS, (T + 1) * TS)
            if which == "A":
                trk.dma("sp", sp_sems.next(), rope_t[:, :, :], d_ropeA[:, :, rsl], writes=(ROPE,))
            else:
                trk.dma("sp", sp_sems.next(), rope_t[p0:p0 + 32, :, :], d_ropeB[:, :, rsl], writes=(ROPE,))

        def attn_group(l, grp, extra=None):
            e_i = l // 2
            sample = grp == 1
            tiles = [1, 2, 3, 4] if sample else [0]
            t0 = tiles[0]
            ntok = TS * len(tiles)
            nctx = NCTX if sample else 0
            NKg = ntok + nctx
            nkt = NKg // 128
            nlt = ntok // 128
            nt = len(tiles)
            o = 0
            qa = arena[:, o:o + 4 * ntok].rearrange("p (c n) -> p c n", c=4); o += 4 * ntok
            cqn = arena[:, o:o + 2 * NKg].rearrange("p (c n) -> p c n", c=2); o += 2 * NKg
            ckvn = arena[:, o:o + NKg]; o += NKg
            r3 = o
            hts = [arena[:, o + i * 4096:o + (i + 1) * 4096].rearrange("p (k n) -> p k n", k=8) for i in range(2)]
            khb = arena[:, r3:r3 + NKg]
            vhb = arena[:, r3 + NKg:r3 + NKg + nkt * 128].rearrange("p (t c) -> p t c", c=128)
            qhbs = [arena[:, r3 + NKg + nkt * 128 + i * ntok:r3 + NKg + nkt * 128 + (i + 1) * ntok] for i in range(2)]
            o = r3 + max(8192, NKg + nkt * 128 + 2 * ntok)
            r4 = o
            ka = arena[:, o:o + NKg]
            va = arena[:, o + NKg:o + NKg + nkt * 192].rearrange("p (t c) -> p t c", c=192)
            obc = [arena[:, r4 + i * ntok:r4 + (i + 1) * ntok] for i in range(2)]
            o = r4 + max(NKg + nkt * 192, 2 * ntok)
            assert o <= ARENA, o
            HTs = [Res("ht0"), Res("ht1")]
            QAH = [[[Res(f"qah{c}_{hh}_{t}") for t in range(nt)] for hh in range(2)] for c in range(4)]
            CQ = [Res(f"cq{t}") for t in range(nt)]
            CKV = [Res(f"ckv{t}") for t in range(nt + 1)]
            KR = [Res(f"kr{t}") for t in range(nt + 1)]
            KA = [Res(f"ka{t}") for t in range(nt + 1)]
            VA = [Res(f"va{t}") for t in range(nt + 1)]
            OBC = [Res("obc0"), Res("obc1")]
            KH, VH = Res("kh"), Res("vh")
            QHs = [Res("qh0"), Res("qh1")]
            krv = cqn[64:96, 1, :]

            reserve_obanks(True)
            g0, G0, s0 = ring_get(("win", e_i, 0))
            g1, G1, s1 = ring_get(("win", e_i, 1))
            g2, G2, s2 = ring_get(("win", e_i, 2))
            g3, G3, s3 = ring_get(("win", e_i, 3))

            op("dve", lambda e: e.memset(va[:, :, 64:128], 1.0), writes=tuple(VA))
            op("dve", lambda e: e.memset(cqn[96:128, 1, :], 0.0), writes=tuple(CQ))
            if sample:
                for g_ in range(4):
                    trk.dma("sp", sp_sems.next(), ropeB_all[32 * g_:32 * g_ + 32, :, :], d_ropeB[:, :, g_ * TS:(g_ + 1) * TS], writes=(ROPEB,))
                load_cast(ka[:, ntok:NKg], d_ckT[e_i], 128, NCTX, (KA[nt],))
                load_cast(ckvn[:, ntok:NKg], d_cckvT[e_i], 128, NCTX, (CKV[nt],))
                load_cast(krv[:, ntok:NKg], d_ckrT[e_i], 32, NCTX, (KR[nt],))
                for kt in range(2):
                    trk.dma("sp", sp_sems.next(), stage[:, 0:128], d_cv[e_i, kt], writes=(STAGE,))
                    op("dve", lambda e, kt=kt: e.tensor_copy(out=va[:, nlt + kt, 0:64], in_=stage[:, 0:64]), reads=(STAGE,), writes=(VA[nt],))
                    op("dve", lambda e, kt=kt: e.tensor_copy(out=va[:, nlt + kt, 128:192], in_=stage[:, 64:128]), reads=(STAGE,), writes=(VA[nt],))

            oqn = PM["qnorm"][0] + e_i * 2
            okn = PM["kvnorm"][0] + e_i

            cur = {}

            def proj(bk, wt, WR, col0, ncol, ht, HT):

                def pe(e):
                    ins = None
                    for kc in range(8):
                        ins = e.matmul(banks[bk][0:ncol, :], lhsT=wt[:, kc * 512 + col0:kc * 512 + col0 + ncol], rhs=ht[:, kc, :],
                                       start=(kc == 0), stop=(kc == 7))
                    return ins
                op("pe", pe, reads=(WR, HT), writes=(BANK[bk],))

            def out_from(dst, src_ap, src_res, nrows, scale=None):
                i = ost_i[0] % 2
                ost_i[0] += 1
                if scale is None:
                    op("act", lambda e: e.activation(out=ostage[i][0:nrows, :], in_=src_ap, func=AF.Identity),
                       reads=src_res, writes=(OST[i],))
                else:
                    op("act", lambda e: e.activation(out=ostage[i][0:nrows, :], in_=src_ap, func=AF.Identity, scale=scale),
                       reads=src_res + (P_,), writes=(OST[i],))
                trk.dma("sp", out_sems.next(), dst, ostage[i][0:nrows, :], reads=(OST[i],))

            def norm_gen(t, dest, DEST):
                grp = 0 if t == 0 else 1
                tokx = slice(t * TS, (t + 1) * TS)
                b = balloc()

                def mm_(k):
                    op("pe", lambda e: e.matmul(banks[b][:, :], lhsT=onesm[:, :], rhs=sq[k % 2][:, :], start=(k == 0),
                                                stop=(k == 7), skip_group_check=True),
                       reads=(SQ[k % 2], CONST), writes=(BANK[b],))
                for p_ in range(5):
                    if p_ >= 1:
                        mm_(2 * p_ - 2)
                        mm_(2 * p_ - 1)
                    if p_ < 4:
                        for k in (2 * p_, 2 * p_ + 1):
                            op("act", lambda e, k=k: e.activation(out=sq[k % 2][:, :], in_=x[:, k, tokx], func=AF.Square),
                               reads=(X[k][t],), writes=(SQ[k % 2],))
                        yield
                op("act", lambda e: e.activation(out=rstd[:, :], in_=banks[b][:, :], func=AF.Ln, bias=epst[:, 0:1], scale=1.0),
                   reads=(BANK[b], CONST), writes=(RSTD,))
                bfree(b)
                op("act", lambda e: e.activation(out=rstd[:, :], in_=rstd[:, :], func=AF.Exp, scale=-0.5), reads=(RSTD,), writes=(RSTD,))
                yield
                for k in range(8):
                    ti = k % 2
                    op("dve", lambda e, k=k, ti=ti: e.tensor_tensor(out=tmp[ti][:, :], in0=x[:, k, tokx], in1=rstd[:, :], op=ALU.mult),
                       reads=(X[k][t], RSTD), writes=(TMP[ti],))
                    op("act", lambda e, k=k, ti=ti: e.activation(out=dest[:, k, :], in_=tmp[ti][:, :], func=AF.Identity,
                                                               bias=dv(l, 1, k, grp), scale=dv(l, 0, k, grp)),
                       reads=(TMP[ti], DER), writes=(DEST,))
                    if k % 4 == 3:
                        yield

            def stageA(lt):
                t = tiles[lt]
                tok = slice(lt * TS, (lt + 1) * TS)
                ht, HT = hts[lt % 2], HTs[lt % 2]
                cur["ht"], cur["HT"] = ht, HT
                yield from norm_gen(t, ht, HT)
                cur["ht"], cur["HT"] = ht, HT
                b0 = balloc(); b1 = balloc(); bs = balloc()
                proj(b0, g0, G0, 0, 128, ht, HT)
                proj(b1, g0, G0, 128, 128, ht, HT)
                op("act", lambda e, b0=b0: e.activation(out=sq[0][:, :], in_=banks[b0][:, :], func=AF.Square), reads=(BANK[b0],), writes=(SQ[0],))
                op("act", lambda e, b1=b1: e.activation(out=sq[1][:, :], in_=banks[b1][:, :], func=AF.Square), reads=(BANK[b1],), writes=(SQ[1],))

                def pe_ss(e, bs=bs):
                    e.matmul(banks[bs][:, :], lhsT=ones1[:, :], rhs=sq[0][:, :], start=True, stop=False, skip_group_check=True)
                    return e.matmul(banks[bs][:, :], lhsT=ones_lo[:, :], rhs=sq[1][:, :], start=False, stop=True, skip_group_check=True)
                op("pe", pe_ss, reads=(SQ[0], SQ[1], CONST), writes=(BANK[bs],))
                yield
                op("act", lambda e, bs=bs: e.activation(out=rstd[:, :], in_=banks[bs][:, :], func=AF.Ln, bias=epst[:, 0:1], scale=1.0 / 192.0),
                   reads=(BANK[bs], CONST), writes=(RSTD,))
                op("act", lambda e: e.activation(out=rstd[:, :], in_=rstd[:, :], func=AF.Exp, scale=-0.5), reads=(RSTD,), writes=(RSTD,))
                op("dve", lambda e, b0=b0: e.tensor_tensor(out=tmp[0][:, :], in0=banks[b0][:, :], in1=rstd[:, :], op=ALU.mult),
                   reads=(BANK[b0], RSTD), writes=(TMP[0],))
                op("act", lambda e, tok=tok: e.activation(out=cqn[:, 0, tok], in_=tmp[0][:, :], func=AF.Identity, scale=params[:, oqn:oqn + 1]),
                   reads=(TMP[0], P_), writes=(CQ[lt],))
                op("dve", lambda e, b1=b1: e.tensor_tensor(out=tmp[1][0:64, :], in0=banks[b1][0:64, :], in1=rstd[0:64, :], op=ALU.mult),
                   reads=(BANK[b1], RSTD), writes=(TMP[1],))
                op("act", lambda e, tok=tok: e.activation(out=cqn[0:64, 1, tok], in_=tmp[1][0:64, :], func=AF.Identity, scale=params[0:64, oqn + 1:oqn + 2]),
                   reads=(TMP[1], P_), writes=(CQ[lt],))
                bfree(b0); bfree(b1)
                yield
                b0 = balloc()
                proj(b0, g0, G0, 192, 128, ht, HT)
                op("act", lambda e, b0=b0: e.activation(out=sq[0][:, :], in_=banks[b0][:, :], func=AF.Square), reads=(BANK[b0],), writes=(SQ[0],))
                op("pe", lambda e, bs=bs: e.matmul(banks[bs][:, :], lhsT=ones1[:, :], rhs=sq[0][:, :], start=True, stop=True),
                   reads=(SQ[0], CONST), writes=(BANK[bs],))
                op("act", lambda e, bs=bs: e.activation(out=rstd[:, :], in_=banks[bs][:, :], func=AF.Ln, bias=epst[:, 0:1], scale=1.0 / 128.0),
                   reads=(BANK[bs], CONST), writes=(RSTD,))
                op("act", lambda e: e.activation(out=rstd[:, :], in_=rstd[:, :], func=AF.Exp, scale=-0.5), reads=(RSTD,), writes=(RSTD,))
                op("dve", lambda e, b0=b0: e.tensor_tensor(out=tmp[0][:, :], in0=banks[b0][:, :], in1=rstd[:, :], op=ALU.mult),
                   reads=(BANK[b0], RSTD), writes=(TMP[0],))
                op("act", lambda e, tok=tok: e.activation(out=ckvn[:, tok], in_=tmp[0][:, :], func=AF.Identity, scale=params[:, okn:okn + 1]),
                   reads=(TMP[0], P_), writes=(CKV[lt],))
                if not sample:
                    out_from(o_ckvT[e_i], tmp[0][:, :], (TMP[0],), 128, scale=params[:, okn:okn + 1])
                bfree(b0); bfree(bs)
                yield
                b0 = balloc()
                proj(b0, g0, G0, 320, 128, ht, HT)
                if sample:
                    b1 = balloc()
                    proj(b1, g0, G0, 352, 128, ht, HT)
                    rope_combine(b0, b1, 32, krv[:, tok], (KR[lt],), tab=ropeB_all, TAB=ROPEB, tp0=32 * (t - 1))
                    bfree(b1)
                else:
                    out_from(o_krT[e_i], banks[b0][0:32, :], (BANK[b0],), 32)
                    op("dve", lambda e, b0=b0, tok=tok: e.tensor_copy(out=krv[:, tok], in_=ostage[(ost_i[0] - 1) % 2][0:32, :]),
                       reads=(OST[(ost_i[0] - 1) % 2],), writes=(KR[lt],))
                bfree(b0)
                yield

            def stageB(lt):
                t = tiles[lt]
                tok = slice(lt * TS, (lt + 1) * TS)
                ht, HT = hts[lt % 2], HTs[lt % 2]
                cur["ht"], cur["HT"] = ht, HT
                if sample:
                    load_rope("A", t - 1)
                for c in range(4):
                    b0 = balloc()
                    proj(b0, g1, G1, c * 128, 128, ht, HT)
                    if sample:
                        b1 = balloc()
                        proj(b1, g2, G2, c * 128, 128, ht, HT)
                        rope_combine(b0, b1, 128, qa[:, c, tok], (QAH[c][0][lt], QAH[c][1][lt]))
                        bfree(b1)
                    else:
                        op("act", lambda e, c=c, b0=b0, tok=tok: e.activation(out=qa[:, c, tok], in_=banks[b0][:, :], func=AF.Identity),
                           reads=(BANK[b0],), writes=(QAH[c][0][lt], QAH[c][1][lt]))
                    bfree(b0)
                    yield
                b0 = balloc()
                proj(b0, g3, G3, 0, 128, ht, HT)
                if sample:
                    b1 = balloc()
                    proj(b1, g3, G3, 128, 128, ht, HT)
                    rope_combine(b0, b1, 128, ka[:, tok], (KA[lt],))
                    bfree(b1)
                else:
                    op("act", lambda e, b0=b0, tok=tok: e.activation(out=ka[:, tok], in_=banks[b0][:, :], func=AF.Identity), reads=(BANK[b0],), writes=(KA[lt],))
                    out_from(o_kT[e_i], banks[b0][:, :], (BANK[b0],), 128)
                bfree(b0)
                yield
                b0 = balloc()

                def pe_v(e, b0=b0, ht=ht):
                    ins = None
                    for tb in range(4):
                        for kc in range(8):
                            ins = e.matmul(banks[b0][:, tb * 128:(tb + 1) * 128], lhsT=ht[:, kc, tb * 128:(tb + 1) * 128],
                                           rhs=g3[:, kc * 512 + 256:kc * 512 + 384], start=(kc == 0), stop=(kc == 7), skip_group_check=True)
                    return ins
                op("pe", pe_v, reads=(G3, HT), writes=(BANK[b0],))
                bv = banks[b0][:, :].rearrange("p (t c) -> p t c", c=128)
                op("act", lambda e, bv=bv, lt=lt: e.activation(out=va[:, 4 * lt:4 * lt + 4, 0:64], in_=bv[:, :, 0:64], func=AF.Identity),
                   reads=(BANK[b0],), writes=(VA[lt],))
                op("act", lambda e, bv=bv, lt=lt: e.activation(out=va[:, 4 * lt:4 * lt + 4, 128:192], in_=bv[:, :, 64:128], func=AF.Identity),
                   reads=(BANK[b0],), writes=(VA[lt],))
                if not sample:
                    i = ost_i[0] % 2
                    ost_i[0] += 1
                    op("act", lambda e, i=i, b0=b0: e.activation(out=ostage[i][:, :], in_=banks[b0][:, :], func=AF.Identity),
                       reads=(BANK[b0],), writes=(OST[i],))
                    trk.dma("sp", out_sems.next(), o_v[e_i].rearrange("(t p) c -> p t c", p=128),
                            ostage[i][:, :].rearrange("p (t c) -> p t c", c=128), reads=(OST[i],))
                bfree(b0)
                yield

            def drive(gens):
                gens = [g_ for g_ in gens if g_ is not None]
                while gens:
                    for g_ in list(gens):
                        try:
                            next(g_)
                        except StopIteration:
                            gens.remove(g_)

            drive([stageA(0)])
            for lt in range(nt):
                drive([stageB(lt), stageA(lt + 1) if lt + 1 < nt else None])
            for s_ in (s0, s1, s2, s3):
                ring_done(s_)

            g4, G4, s4 = ring_get(("wqkv", e_i))
            wo0, WO0, so0 = ring_get(("wout", e_i, 0))
            wo1, WO1, so1 = ring_get(("wout", e_i, 1))

            kaz = [arena[:, r3 + i * NKg:r3 + (i + 1) * NKg] for i in range(2)]
            KZ = Res("kz")
            jobs = []
            for c in range(4):
                for hh in range(2):
                    h = c + 4 * hh
                    base = 64 * hh
                    vs = slice(0, 128) if hh == 0 else slice(64, 192)
                    sink_col = o_es + e_i * 8 + h
                    if not sample:
                        for s in range(2):
                            q0 = s * 256
                            groups = []
                            for kb in range(2):
                                k0 = s * 256 + kb * 128
                                groups.append(dict(k=kaz[hh][:, k0:k0 + 128], q=qa[:, c, q0:q0 + 256], K=KZ, Q=QAH[c][hh][0], Q2=QAH[c][1 - hh][0],
                                                   v=va[:, k0 // 128, vs], V=VA[0], c0=0, n=256, mask=None))
                            jobs.append(dict(groups=groups, finish=(lambda ob, base=base, sink_col=sink_col, c=c, q0=q0, hh=hh: normalize(
                                ob, base, 256, sink_col, qa[base:base + 64, c, q0:q0 + 256], (QAH[c][hh][0],)))))
                    else:
                        for T in range(4):
                            q0 = T * TS
                            groups = []
                            for kb in range(2):
                                groups.append(dict(k=kaz[hh][:, ntok + kb * 128:ntok + (kb + 1) * 128], q=qa[:, c, q0:q0 + TS],
                                                   K=KZ, Q=QAH[c][hh][T], Q2=QAH[c][1 - hh][T], v=va[:, nlt + kb, vs], V=VA[nt], c0=0, n=TS, mask=None))
                            for jb in range(4 * T - 1, 4 * T + 5):
                                if jb < 0 or jb > 15:
                                    continue
                                qlo = max(jb - 1, 4 * T)
                                qhi = min(jb + 1, 4 * T + 3)
                                n = (qhi - qlo + 1) * 128
                                c0 = (qlo - 4 * T) * 128
                                m0 = (qlo - (jb - 1)) * 128
                                k0 = jb * 128
                                groups.append(dict(k=kaz[hh][:, k0:k0 + 128], q=qa[:, c, q0 + c0:q0 + c0 + n],
                                                   K=KZ, Q=QAH[c][hh][T], Q2=QAH[c][1 - hh][T], v=va[:, jb, vs], V=VA[jb // 4],
                                                   c0=c0, n=n, mask=maskb[:, m0:m0 + n]))
                            jobs.append(dict(groups=groups, finish=(lambda ob, base=base, sink_col=sink_col, c=c, q0=q0, hh=hh, T=T: normalize(
                                ob, base, TS, sink_col, qa[base:base + 64, c, q0:q0 + TS], (QAH[c][hh][T],), on_dve=(T % 2 == 0)))))
            side_q, side_o = [], []
            side_x = list(extra or [])
            s4_released = [False]

            sidestep = [0]
            armed = [False]

            def pop_side():
                pool_sel[0] = "side"
                sidestep[0] += 1
                if side_q:
                    side_q.pop(0)()
                elif side_x and sidestep[0] % 24 == 0 and armed[0]:
                    side_x.pop(0)()
                elif side_o and (sidestep[0] % 2 == 0 or not sample):
                    side_o.pop(0)[1]()
                pool_sel[0] = "g"

            def flush(lst):
                while lst:
                    it_ = lst.pop(0)
                    (it_[1] if isinstance(it_, tuple) else it_)()

            def flush_tag(tag):
                keep = []
                for it_ in side_o:
                    if it_[0] == tag:
                        it_[1]()
                    else:
                        keep.append(it_)
                side_o[:] = keep

            mscale = float((64 + 32) ** -0.5)
            WQ0, WQ1, WK, WV = 0, 1024, 2048, 2560
            nt6 = nt + (1 if sample else 0)

            if not sample:
                fo_ = o
                PB = []
                for h_ in range(8):
                    PB.append(dict(k=arena[:, fo_:fo_ + 512], v=arena[:, fo_ + 512:fo_ + 1024].rearrange("p (t c) -> p t c", c=128),
                                   q=arena[:, fo_ + 1024:fo_ + 1536], K=Res(f"pk{h_}"), V=Res(f"pv{h_}"), Q=Res(f"pq{h_}")))
                    fo_ += 1536
                obc4 = [arena[:, fo_ + i * 512:fo_ + (i + 1) * 512] for i in range(4)]
                OBC4 = [Res(f"obc4_{i}") for i in range(4)]
                fo_ += 2048
                assert fo_ <= ARENA, fo_

            def hbufs(h):
                if sample:
                    return dict(k=khb, v=vhb, q=qhbs[h % 2], K=KH, V=VH, Q=QHs[h % 2])
                return PB[h]

            def kv_tasks(h):
                hb = hbufs(h)
                khb, vhb, KH, VH = hb["k"], hb["v"], hb["K"], hb["V"]
                bi = h % 2
                vcol = 0 if bi == 0 else 64
                ocol = 64 if bi == 0 else 0
                tasks = []

                def t_init():
                    if not sample:
                        op("dve", lambda e: e.memset(khb[96:128, :], 0.0), writes=(KH,))
                    op("dve", lambda e: e.memset(vhb[:, :, ocol:ocol + 64], 1.0), writes=(VH,))
                    op("dve", lambda e: e.tensor_copy(out=khb[64:96, :], in_=krv[:, :]), reads=tuple(KR), writes=(KH,))
                tasks.append(t_init)
                for t6 in range(nt6):
                    def t_kv(t6=t6):
                        n = TS if t6 < nt else NCTX
                        k0 = t6 * TS
                        b0 = balloc()
                        op("pe", lambda e: e.matmul(banks[b0][:, 0:n], lhsT=g4[:, WK + h * 64:WK + h * 64 + 128],
                                                    rhs=ckvn[:, k0:k0 + n], start=True, stop=True),
                           reads=(G4, CKV[t6]), writes=(BANK[b0],))
                        op("dve", lambda e: e.tensor_copy(out=khb[0:64, k0:k0 + n], in_=banks[b0][0:64, 0:n]),
                           reads=(BANK[b0],), writes=(KH,))
                        bfree(b0)
                        b1 = balloc()
                        ntb = n // 128

                        def pe_v2(e):
                            ins = None
                            for tb in range(ntb):
                                ins = e.matmul(banks[b1][:, tb * 64:(tb + 1) * 64], lhsT=ckvn[:, k0 + tb * 128:k0 + (tb + 1) * 128],
                                               rhs=g4[:, WV + h * 64:WV + (h + 1) * 64], start=True, stop=True, skip_group_check=True)
                            return ins
                        op("pe", pe_v2, reads=(G4, CKV[t6]), writes=(BANK[b1],))
                        op("dve", lambda e: e.tensor_copy(
                            out=vhb[:, 4 * t6:4 * t6 + ntb, vcol:vcol + 64],
                            in_=banks[b1][:, 0:ntb * 64].rearrange("p (t c) -> p t c", c=64)),
                           reads=(BANK[b1],), writes=(VH,))
                        bfree(b1)
                    tasks.append(t_kv)
                return tasks

            def q_tasks(h):
                hb = hbufs(h)
                qhb, QH = hb["q"], hb["Q"]
                tasks = []
                for lt, t in enumerate(tiles):
                    def t_q(lt=lt, t=t):
                        tok = slice(lt * TS, (lt + 1) * TS)
                        b0 = balloc()

                        def pe_q(e):
                            e.matmul(banks[b0][:, :], lhsT=g4[:, WQ0 + h * 128:WQ0 + h * 128 + 128], rhs=cqn[:, 0, tok], start=True, stop=False)
                            return e.matmul(banks[b0][:, :], lhsT=g4[:, WQ1 + h * 128:WQ1 + h * 128 + 128], rhs=cqn[:, 1, tok], start=False, stop=True)
                        op("pe", pe_q, reads=(G4, CQ[lt], KR[lt]), writes=(BANK[b0],))
                        op("dve", lambda e: e.tensor_copy(out=qhb[0:64, tok], in_=banks[b0][0:64, :]),
                           reads=(BANK[b0],), writes=(QH,))
                        if sample:
                            gsl = slice(32 * (t - 1), 32 * (t - 1) + 32)
                            op("dve", lambda e: e.tensor_tensor(out=tmp[0][64:96, :], in0=banks[b0][64:96, :], in1=ropeB_all[gsl, 0, :], op=ALU.mult),
                               reads=(BANK[b0], ROPEB), writes=(TMP[0],))
                            op("dve", lambda e: e.tensor_tensor(out=tmp[1][64:96, :], in0=banks[b0][96:128, :], in1=ropeB_all[gsl, 1, :], op=ALU.mult),
                               reads=(BANK[b0], ROPEB), writes=(TMP[1],))
                            op("dve", lambda e: e.tensor_tensor(out=qhb[64:96, tok], in0=tmp[0][64:96, :], in1=tmp[1][64:96, :], op=ALU.add),
                               reads=(TMP[0], TMP[1]), writes=(QH,))
                        else:
                            op("dve", lambda e: e.tensor_copy(out=qhb[64:96, tok], in_=banks[b0][64:96, :]),
                               reads=(BANK[b0],), writes=(QH,))
                        bfree(b0)
                    tasks.append(t_q)
                return tasks

            def oproj_tasks(kcs, srcs, SRCS):
                tasks = []
                for lt, t in enumerate(tiles):
                    for j in range(8):
                        def t_o(lt=lt, t=t, j=j):
                            tok = slice(lt * TS, (lt + 1) * TS)
                            wt, WR = (wo0, WO0) if j < 4 else (wo1, WO1)
                            jj = j % 4
                            b0 = balloc()

                            def pe_o(e):
                                ins = None
                                for i, kc in enumerate(kcs):
                                    ins = e.matmul(banks[b0][:, :], lhsT=wt[:, kc * 512 + jj * 128:kc * 512 + jj * 128 + 128], rhs=srcs[i](tok),
                                                   start=(i == 0), stop=(i == len(kcs) - 1))
                                return ins
                            op("pe", pe_o, reads=(WR,) + tuple(SRCS(lt)), writes=(BANK[b0],))
                            resid_add(l, 0, j, t, b0)
                            bfree(b0)
                        tasks.append(t_o)
                return tasks

            reserve_side(True)
            trk.barrier()
            op("dve", lambda e: e.memset(arena[:, r3:r3 + 2 * NKg], 0.0), writes=(KZ,))
            op("dve", lambda e: e.tensor_copy(out=kaz[0][0:64, :], in_=ka[0:64, :]), reads=tuple(KA), writes=(KZ,))
            op("act", lambda e: e.activation(out=kaz[1][64:128, :], in_=ka[64:128, :], func=AF.Identity), reads=tuple(KA), writes=(KZ,))
            run_attn(jobs, 0.125, pop_side)
            side_o += [("A", f_) for f_ in oproj_tasks([0, 1, 2, 3], [(lambda tok, c=c: qa[:, c, tok]) for c in range(4)],
                                                       lambda lt: [QAH[c][hh][lt] for c in range(4) for hh in range(2)])]
            trk.barrier()
            if not sample:
                jobs = []
                for h in range(8):
                    hb = hbufs(h)
                    op("dve", lambda e, hb=hb: e.memset(hb["q"][96:128, :], 0.0), writes=(hb["Q"],))
                    flush(kv_tasks(h))
                    flush(q_tasks(h))
                for h in range(8):
                    hb = hbufs(h)
                    bi = h % 2
                    base = 64 * bi
                    cB = h // 2
                    for s_i in range(2):
                        q0 = s_i * 256
                        groups = []
                        for kb in range(2):
                            k0 = s_i * 256 + kb * 128
                            groups.append(dict(k=hb["k"][:, k0:k0 + 128], q=hb["q"][:, q0:q0 + 256], K=hb["K"], Q=hb["Q"],
                                               v=hb["v"][:, k0 // 128, :], V=hb["V"], c0=0, n=256, mask=None))
                        jobs.append(dict(groups=groups, finish=(lambda ob, base=base, cB=cB, q0=q0: normalize(
                            ob, base, 256, None, obc4[cB][base:base + 64, q0:q0 + 256], (OBC4[cB],)))))
                run_attn(jobs, mscale, pop_side)
                side_o += [("B", f_) for f_ in oproj_tasks([4, 5, 6, 7], [(lambda tok, c=c: obc4[c][:, tok]) for c in range(4)],
                                                       lambda lt: list(OBC4))]
            armed[0] = True
            for h in (range(8) if sample else []):
                bi = h % 2
                base = 64 * bi
                cB = h // 2
                if bi == 0 and cB >= 2:
                    flush_tag(cB - 2)
                flush(kv_tasks(h))
                if h == 0:
                    op("dve", lambda e: e.memset(khb[96:128, :], 0.0), writes=(KH,))
                    for i in range(2):
                        op("dve", lambda e, i=i: e.memset(qhbs[i][96:128, :], 0.0), writes=(QHs[i],))
                    side_q += q_tasks(0)
                flush(side_q)
                if h + 1 < 8:
                    side_q += q_tasks(h + 1)
                else:
                    ring_done(s4)
                    s4_released[0] = True
                qhb, QH = qhbs[h % 2], QHs[h % 2]
                oc = obc[cB % 2]
                OC = OBC[cB % 2]
                jobs = []
                if not sample:
                    for s in range(2):
                        q0 = s * 256
                        groups = []
                        for kb in range(2):
                            k0 = s * 256 + kb * 128
                            groups.append(dict(k=khb[:, k0:k0 + 128], q=qhb[:, q0:q0 + 256], K=KH, Q=QH,
                                               v=vhb[:, k0 // 128, :], V=VH, c0=0, n=256, mask=None))
                        jobs.append(dict(groups=groups, finish=(lambda ob, base=base, oc=oc, OC=OC, q0=q0: normalize(
                            ob, base, 256, None, oc[base:base + 64, q0:q0 + 256], (OC,)))))
                else:
                    for T in range(4):
                        q0 = T * TS
                        groups = []
                        for kb in list(range(nlt, nkt)) + list(range(nlt)):
                            k0 = kb * 128
                            groups.append(dict(k=khb[:, k0:k0 + 128], q=qhb[:, q0:q0 + TS], K=KH, Q=QH,
                                               v=vhb[:, kb, :], V=VH, c0=0, n=TS, mask=None))
                        jobs.append(dict(groups=groups, finish=(lambda ob, base=base, oc=oc, OC=OC, q0=q0: normalize(
                            ob, base, TS, None, oc[base:base + 64, q0:q0 + TS], (OC,)))))
                run_attn(jobs, mscale, pop_side)
                if bi == 1:
                    side_o += [(cB, f_) for f_ in oproj_tasks([4 + cB], [(lambda tok, oc=oc: oc[:, tok])], lambda lt, OC=OC: [OC])]
            flush(side_q)
            flush(side_x)
            flush(side_o)
            for s_ in ((so0, so1) if s4_released[0] else (s4, so0, so1)):
                ring_done(s_)
            reserve_side(False)
            reserve_obanks(False)
            trk.barrier()

        def conv_layer(l):
            j_i = l // 2
            o = 0
            upad = arena[:, o:o + 4 * NPAD].rearrange("p (c n) -> p c n", c=4); o += 4 * NPAD
            zpad = arena[:, o:o + 4 * NPAD].rearrange("p (c n) -> p c n", c=4); o += 4 * NPAD
            hts = [arena[:, o + i * 4096:o + (i + 1) * 4096].rearrange("p (k n) -> p k n", k=8) for i in range(2)]
            o += 8192
            assert o <= ARENA, o
            HTs = [Res("ht0"), Res("ht1")]
            U = [Res(f"u{c}") for c in range(4)]
            Z = [Res(f"z{c}") for c in range(4)]
            for (so_, sn_), pb in zip(SEQS, PADB):
                for buf, RS in ((upad, U), (zpad, Z)):
                    op("dve", lambda e, buf=buf, pb=pb: e.memset(buf[:, :, pb:pb + 16], 0.0), writes=tuple(RS))
                    op("dve", lambda e, buf=buf, pb=pb, sn_=sn_: e.memset(buf[:, :, pb + 16 + sn_:pb + 32 + sn_], 0.0), writes=tuple(RS))
            ga, GA, sa = ring_get(("cin", j_i, 0))
            gg, GG, sg = ring_get(("cin", j_i, 1))
            gz, GZ, sz = ring_get(("cin", j_i, 2))
            load_cast(pw[:, :], d_poolw[j_i], 128, 512, (PW,))

            def segs_of_tile(t):
                if t == 0:
                    return [(0, 0, 256), (1, 256, 256)]
                return [(2, t * TS, TS)]

            norm_tile(l, 0, 0, hts[0], HTs[0])
            for t in range(NT):
                ht, HT = hts[t % 2], HTs[t % 2]
                ng = norm_gen(l, 0, t + 1, hts[(t + 1) % 2], HTs[(t + 1) % 2]) if t + 1 < NT else None
                for c in range(4):
                    if ng is not None:
                        for _ in range(2):
                            next(ng, None)
                    ba = balloc(); bg = balloc()
                    for (bk, wt, WR) in ((ba, ga, GA), (bg, gg, GG)):
                        def pe(e, bk=bk, wt=wt, c=c, ht=ht):
                            ins = None
                            for kc in range(8):
                                ins = e.matmul(banks[bk][:, :], lhsT=wt[:, kc * 512 + c * 128:kc * 512 + c * 128 + 128], rhs=ht[:, kc, :],
                                               start=(kc == 0), stop=(kc == 7))
                            return ins
                        op("pe", pe, reads=(WR, HT), writes=(BANK[bk],))
                    ti = c % 2
                    op("act", lambda e, bg=bg, ti=ti: e.activation(out=tmp[ti][:, :], in_=banks[bg][:, :], func=AF.Sigmoid), reads=(BANK[bg],), writes=(TMP[ti],))
                    for (s_, t0_, n) in segs_of_tile(t):
                        p0 = _padpos(t0_)
                        c0 = t0_ - t * TS
                        op("dve", lambda e, ba=ba, c=c, p0=p0, c0=c0, n=n, ti=ti: e.tensor_tensor(
                            out=upad[:, c, p0:p0 + n], in0=banks[ba][:, c0:c0 + n], in1=tmp[ti][:, c0:c0 + n], op=ALU.mult),
                           reads=(BANK[ba], TMP[ti]), writes=(U[c],))
                    bfree(ba); bfree(bg)
                    bz = balloc()

                    def pez(e, bz=bz, c=c, ht=ht):
                        ins = None
                        for kc in range(8):
                            ins = e.matmul(banks[bz][:, :], lhsT=gz[:, kc * 512 + c * 128:kc * 512 + c * 128 + 128], rhs=ht[:, kc, :],
                                           start=(kc == 0), stop=(kc == 7))
                        return ins
                    op("pe", pez, reads=(GZ, HT), writes=(BANK[bz],))
                    for (s_, t0_, n) in segs_of_tile(t):
                        p0 = _padpos(t0_)
                        c0 = t0_ - t * TS
                        op("act", lambda e, bz=bz, c=c, p0=p0, c0=c0, n=n: e.activation(out=zpad[:, c, p0:p0 + n], in_=banks[bz][:, c0:c0 + n], func=AF.Identity),
                           reads=(BANK[bz],), writes=(Z[c],))
                    bfree(bz)
                if ng is not None:
                    for _ in ng:
                        pass
            rst["limit"] = rst["got"] + 2
            for s_ in (sa, sg, sz):
                ring_done(s_)
            wo0, WO0, so0 = ring_get(("cout", j_i, 0))
            wo1, WO1, so1 = ring_get(("cout", j_i, 1))
            free_slots = [i for i in range(NRING) if not rst["held"][i]]
            assert len(free_slots) == 2, free_slots
            for i in free_slots:
                rst["held"][i] = True
            dgs = [ring[i] for i in free_slots]
            DGs = [RING[i] for i in free_slots]
            hcs, HCs = hts, [Res("hc0"), Res("hc1")]
            obdw = PM["bdw"][0] + j_i * 4
            olng = PM["lng"][0] + j_i * 4
            olnb = PM["lnb"][0] + j_i * 4
            odw = PM["dw"][0] + j_i * 4 * 31
            segs = []
            for t in range(NT):
                sl = segs_of_tile(t)
                for i, (s_, t0_, n) in enumerate(sl):
                    segs.append((s_, t0_, n, t, i == len(sl) - 1))
            dgi = [0]

            pending = []

            def build_dg(c):
                di = dgi[0] % 2
                dgi[0] += 1
                dg, DG = dgs[di], DGs[di]
                wc0 = odw + c * 31
                op("dve", lambda e: e.tensor_tensor(
                    out=dg[:, 0:31 * 128].rearrange("p (t m) -> p t m", t=31),
                    in0=ident[:, :].unsqueeze(1).broadcast_to([128, 31, 128]),
                    in1=params[:, wc0:wc0 + 31].unsqueeze(2).broadcast_to([128, 31, 128]), op=ALU.mult),
                   reads=(CONST, P_), writes=(DG,))
                return dg, DG

            def prebuild():
                pending.append(build_dg(0))
                pending.append(build_dg(1))

            def conv_taps(seg):
                s_, t0_, n, t, _ = seg
                p0 = _padpos(t0_)
                cb = [balloc() for _ in range(4)]
                for c in range(4):
                    dg, DG = pending.pop(0) if pending else build_dg(c)

                    def pe(e, c=c, dg=dg):
                        ins = None
                        for tap in range(31):
                            ins = e.matmul(banks[cb[c]][:, 0:n], lhsT=dg[:, tap * 128:(tap + 1) * 128],
                                           rhs=upad[:, c, p0 + tap - 15:p0 + tap - 15 + n], start=(tap == 0), stop=(tap == 30))
                        return ins
                    op("pe", pe, reads=(DG, U[c]), writes=(BANK[cb[c]],))
                return cb

            def evac(seg, cb):
                n = seg[2]
                for c in range(4):
                    op("act", lambda e, c=c: e.activation(out=cvt[c][:, 0:n], in_=banks[cb[c]][:, 0:n], func=AF.Identity,
                                                          bias=params[:, obdw + c:obdw + c + 1], scale=1.0),
                       reads=(BANK[cb[c]], P_), writes=(CVT[c],))
                    bfree(cb[c])

            def pool_seg(seg, hc, HC):
                s_, t0_, n, t, _ = seg
                p0 = _padpos(t0_)
                c0 = t0_ - t * TS
                seq_o, seq_n = SEQS[s_]
                at_start = (t0_ == seq_o)
                at_end = (t0_ + n == seq_o + seq_n)
                for gi, w in enumerate(POOL_W):
                    lo = w // 2
                    hi = w - lo - 1
                    bs_ = balloc(); bz = balloc()

                    def pe(e, gi=gi, lo=lo, hi=hi, bs_=bs_):
                        ins = None
                        for si, sft in enumerate(range(-lo, hi + 1)):
                            ins = e.matmul(banks[bs_][:, 0:n], lhsT=pw[:, gi * 128:(gi + 1) * 128], rhs=zpad[:, gi, p0 + sft:p0 + sft + n],
                                           start=(si == 0), stop=(sft == hi))
                        return ins
                    op("pe", pe, reads=(PW, Z[gi]), writes=(BANK[bs_],))
                    op("pe", lambda e, gi=gi, bz=bz: e.matmul(banks[bz][:, 0:n], lhsT=pw[:, gi * 128:(gi + 1) * 128], rhs=zpad[:, gi, p0:p0 + n], start=True, stop=True),
                       reads=(PW, Z[gi]), writes=(BANK[bz],))
                    pswc = o_psw + j_i * 4 + gi
                    npc = o_np + j_i * 4 + gi
                    op("act", lambda e, bs_=bs_, pswc=pswc: e.activation(out=tmp[2][:, 0:n], in_=banks[bs_][:, 0:n], func=AF.Identity, scale=der[:, pswc:pswc + 1]),
                       reads=(BANK[bs_], DER), writes=(TMP[2],))
                    if at_start and lo > 0:
                        op("dve", lambda e, gi=gi, lo=lo: e.tensor_tensor(out=tmp[2][:, 0:lo], in0=tmp[2][:, 0:lo], in1=pcorr[:, gi * 16:gi * 16 + lo], op=ALU.mult),
                           reads=(TMP[2], CONST), writes=(TMP[2],))
                    if at_end and hi > 0:
                        op("dve", lambda e, gi=gi, hi=hi: e.tensor_tensor(out=tmp[2][:, n - hi:n], in0=tmp[2][:, n - hi:n],
                                                                      in1=pcorr[:, gi * 16 + 16 - hi:gi * 16 + 16], op=ALU.mult),
                           reads=(TMP[2], CONST), writes=(TMP[2],))
                    op("dve", lambda e, gi=gi, bz=bz, npc=npc: e.scalar_tensor_tensor(out=hc[:, 4 + gi, c0:c0 + n], in0=banks[bz][:, 0:n], scalar=der[:, npc:npc + 1],
                                                                               in1=tmp[2][:, 0:n], op0=ALU.mult, op1=ALU.add),
                       reads=(BANK[bz], DER, TMP[2]), writes=(HC,))
                    bfree(bs_); bfree(bz)

            def ln_seg(seg, hc, HC):
                s_, t0_, n, t, _ = seg
                c0 = t0_ - t * TS
                bm = balloc(); bq = balloc()
                for c in range(4):
                    op("act", lambda e, c=c: e.activation(out=sq[0][:, 0:n], in_=cvt[c][:, 0:n], func=AF.Identity), reads=(CVT[c],), writes=(SQ[0],))
                    op("pe", lambda e, c=c: e.matmul(banks[bm][:, 0:n], lhsT=ones1[:, :], rhs=sq[0][:, 0:n], start=(c == 0), stop=(c == 3), skip_group_check=True),
                       reads=(SQ[0], CONST), writes=(BANK[bm],))
                    op("act", lambda e, c=c: e.activation(out=sq[1][:, 0:n], in_=cvt[c][:, 0:n], func=AF.Square), reads=(CVT[c],), writes=(SQ[1],))
                    op("pe", lambda e, c=c: e.matmul(banks[bq][:, 0:n], lhsT=ones1[:, :], rhs=sq[1][:, 0:n], start=(c == 0), stop=(c == 3), skip_group_check=True),
                       reads=(SQ[1], CONST), writes=(BANK[bq],))
                op("act", lambda e: e.activation(out=tmp[0][:, 0:n], in_=banks[bm][:, 0:n], func=AF.Identity, scale=1.0 / 512.0), reads=(BANK[bm],), writes=(TMP[0],))
                op("dve", lambda e: e.tensor_tensor(out=tmp[1][:, 0:n], in0=tmp[0][:, 0:n], in1=tmp[0][:, 0:n], op=ALU.mult), reads=(TMP[0],), writes=(TMP[1],))
                op("dve", lambda e: e.scalar_tensor_tensor(out=tmp[1][:, 0:n], in0=banks[bq][:, 0:n], scalar=1.0 / 512.0, in1=tmp[1][:, 0:n],
                                                           op0=ALU.mult, op1=ALU.subtract), reads=(BANK[bq], TMP[1]), writes=(TMP[1],))
                op("act", lambda e: e.activation(out=rstd[:, 0:n], in_=tmp[1][:, 0:n], func=AF.Ln, bias=epst[:, 0:1], scale=1.0), reads=(TMP[1], CONST), writes=(RSTD,))
                op("act", lambda e: e.activation(out=rstd[:, 0:n], in_=rstd[:, 0:n], func=AF.Exp, scale=-0.5), reads=(RSTD,), writes=(RSTD,))
                bfree(bm); bfree(bq)
                for c in range(4):
                    op("dve", lambda e, c=c: e.tensor_tensor(out=cvt[c][:, 0:n], in0=cvt[c][:, 0:n], in1=tmp[0][:, 0:n], op=ALU.subtract),
                       reads=(CVT[c], TMP[0]), writes=(CVT[c],))
                    op("dve", lambda e, c=c: e.tensor_tensor(out=cvt[c][:, 0:n], in0=cvt[c][:, 0:n], in1=rstd[:, 0:n], op=ALU.mult),
                       reads=(CVT[c], RSTD), writes=(CVT[c],))
                    op("act", lambda e, c=c: e.activation(out=hc[:, c, c0:c0 + n], in_=cvt[c][:, 0:n], func=AF.Silu,
                                                          bias=params[:, olnb + c:olnb + c + 1], scale=params[:, olng + c:olng + c + 1]),
                       reads=(CVT[c], P_), writes=(HC,))

            def outproj(t, hc, HC):
                for j in range(8):
                    wt, WR = (wo0, WO0) if j < 4 else (wo1, WO1)
                    jj = j % 4
                    b0 = balloc()

                    def pe_o(e, b0=b0, wt=wt, jj=jj):
                        ins = None
                        for kc in range(8):
                            ins = e.matmul(banks[b0][:, :], lhsT=wt[:, kc * 512 + jj * 128:kc * 512 + jj * 128 + 128], rhs=hc[:, kc, :],
                                           start=(kc == 0), stop=(kc == 7))
                        return ins
                    op("pe", pe_o, reads=(WR, HC), writes=(BANK[b0],))
                    resid_add(l, 0, j, t, b0)
                    bfree(b0)

            prebuild()
            cb = conv_taps(segs[0])
            prebuild()
            for i, seg in enumerate(segs):
                t = seg[3]
                hc, HC = hcs[t % 2], HCs[t % 2]
                evac(seg, cb)
                if i + 1 < len(segs):
                    cb = conv_taps(segs[i + 1])
                pool_seg(seg, hc, HC)
                ln_seg(seg, hc, HC)
                if i + 2 < len(segs):
                    prebuild()
                if seg[4]:
                    outproj(t, hc, HC)
            for i in free_slots:
                rst["held"][i] = False
            rst["limit"] = None
            ring_done(so0)
            ring_done(so1)
            trk.barrier()

        nsub = 2 * DEPTH if DEBUG_STOP < 0 else DEBUG_STOP
        for g in range(6):
            adaln_group(0, g)
        adaln_finish(0, halves=(0,))
        ada0_rest = [(lambda g=g: adaln_group(0, g)) for g in range(6, 12)] + [lambda: adaln_finish(0, halves=(1,))]
        sub = 0
        for l in range(DEPTH):
            if sub >= nsub:
                break
            if l % 2 == 0:
                attn_group(l, 0)
                attn_group(l, 1, extra=(ada0_rest if l == 0 else None))
            else:
                conv_layer(l)
            sub += 1
            if sub >= nsub:
                break
            nxt = [(l + 1, g) for g in range(12)] if l + 1 < DEPTH else []
            mlp(l, nxt)
            trk.barrier()
            sub += 1

        for t in range(NT):
            tok = slice(t * TS, (t + 1) * TS)
            if DEBUG_STOP >= 0:
                for k in range(8):
                    trk.dma("sp", out_sems.next(), o_yT[:, k, tok], x[:, k, tok], reads=(X[k][t],))
                continue
            b = balloc()
            for k in range(8):
                op("act", lambda e, k=k: e.activation(out=sq[k % 2][:, :], in_=x[:, k, tok], func=AF.Square), reads=(X[k][t],), writes=(SQ[k % 2],))
                op("pe", lambda e, k=k: e.matmul(banks[b][:, :], lhsT=onesm[:, :], rhs=sq[k % 2][:, :], start=(k == 0), stop=(k == 7), skip_group_check=True),
                   reads=(SQ[k % 2], CONST), writes=(BANK[b],))
            op("act", lambda e: e.activation(out=rstd[:, :], in_=banks[b][:, :], func=AF.Ln, bias=epst[:, 0:1], scale=1.0), reads=(BANK[b], CONST), writes=(RSTD,))
            bfree(b)
            op("act", lambda e: e.activation(out=rstd[:, :], in_=rstd[:, :], func=AF.Exp, scale=-0.5), reads=(RSTD,), writes=(RSTD,))
            fo_ = PM["finalg"][0]
            for k in range(8):
                ci = k % 4
                op("dve", lambda e, k=k, ci=ci: e.scalar_tensor_tensor(out=cvt[ci][:, :], in0=x[:, k, tok], scalar=params[:, fo_ + k:fo_ + k + 1],
                                                                     in1=rstd[:, :], op0=ALU.mult, op1=ALU.mult),
                   reads=(X[k][t], RSTD, P_), writes=(CVT[ci],))
                trk.dma("sp", out_sems.next(), o_yT[:, k, tok], cvt[ci][:, :], reads=(CVT[ci],))
        trk.final_wait("sp")
    return nc, wlist


def _count_images():
    n = 0
    for l in range(DEPTH):
        n += 12
        n += 14 if l % 2 == 0 else 5
        n += 16
    return n


N_IMAGES = _count_images()


def _img_k1024(w, col_idx):
    img = np.zeros((128, 8, 512), np.float32)
    col_idx = np.asarray(col_idx)
    valid = col_idx >= 0
    sel = w[:, col_idx[valid]]
    img[:, :, np.nonzero(valid)[0]] = sel.reshape(8, 128, -1).transpose(1, 0, 2)
    return img.reshape(128, SLOT)


def _build_images(wl, inp):
    imgs = np.zeros((N_IMAGES, 128, SLOT), np.float32)
    sw64 = lambda d: (d + 32) % 64
    for n, key in enumerate(wl):
        kind = key[0]
        if kind == "wmod":
            _, l, g = key
            imgs[n] = _img_k1024(inp["w_mod"][l], np.arange(g * 512, (g + 1) * 512))
        elif kind in ("w1",):
            _, l, g = key
            imgs[n] = _img_k1024(inp["mlp_w1"][l], np.arange(g * 512, (g + 1) * 512))
        elif kind == "w2":
            _, l, g = key
            w = inp["mlp_w2"][l][g * 512:(g + 1) * 512]
            imgs[n] = w.reshape(4, 128, 1024).transpose(1, 0, 2).reshape(128, SLOT)
        elif kind == "win":
            _, e, g = key
            w = inp["attn_w_in"][e]
            if g == 0:
                idx = -np.ones(512, np.int64)
                idx[0:192] = 768 + np.arange(192)
                idx[192:320] = 960 + np.arange(128)
                idx[320:352] = 1088 + np.arange(32)
                idx[352:384] = 1088 + (np.arange(32) + 16) % 32
            elif g in (1, 2):
                idx = np.zeros(512, np.int64)
                for c in range(4):
                    for p in range(128):
                        h = c if p < 64 else 4 + c
                        d = p % 64
                        if g == 2:
                            d = sw64(d)
                        idx[c * 128 + p] = h * 64 + d
            else:
                idx = -np.ones(512, np.int64)
                for p in range(128):
                    kh, d = p // 64, p % 64
                    idx[p] = 512 + kh * 64 + d
                    idx[128 + p] = 512 + kh * 64 + sw64(d)
                    idx[256 + p] = 640 + p
            imgs[n] = _img_k1024(w, idx)
        elif kind == "wqkv":
            _, e = key
            img = np.zeros((128, SLOT), np.float32)
            wq = inp["mla_w_qb"][e]
            qcols = np.zeros((8, 128), np.int64)
            for h in range(8):
                qcols[h, 0:64] = h * 96 + np.arange(64)
                qcols[h, 64:96] = h * 96 + 64 + np.arange(32)
                qcols[h, 96:128] = h * 96 + 64 + (np.arange(32) + 16) % 32
            wqa = wq[:, qcols.reshape(-1)]
            img[:, 0:1024] = wqa[0:128]
            img[0:64, 1024:2048] = wqa[128:192]
            wkv = inp["mla_w_kvb"][e]
            for h in range(8):
                img[:, 2048 + h * 64:2048 + (h + 1) * 64] = wkv[:, h * 128:h * 128 + 64]
                img[:, 2560 + h * 64:2560 + (h + 1) * 64] = wkv[:, h * 128 + 64:h * 128 + 128]
            imgs[n] = img
        elif kind == "wout":
            _, e, g = key
            w = inp["attn_w_out"][e]
            rows = np.zeros(1024, np.int64)
            for c in range(4):
                for p in range(128):
                    h = c if p < 64 else 4 + c
                    rows[c * 128 + p] = h * 64 + p % 64
            for c in range(4):
                for p in range(128):
                    h = 2 * c + (p // 64)
                    rows[512 + c * 128 + p] = 512 + h * 64 + p % 64
            wp = w[rows]
            imgs[n] = _img_k1024(wp, np.arange(g * 512, (g + 1) * 512))
        elif kind == "cin":
            _, j, g = key
            imgs[n] = _img_k1024(inp["conv_w_in"][j], np.arange(g * 512, (g + 1) * 512))
        elif kind == "cout":
            _, j, g = key
            imgs[n] = _img_k1024(inp["conv_w_out"][j], np.arange(g * 512, (g + 1) * 512))
        else:
            raise KeyError(key)
    return imgs


_CACHE = {}


def kernel(**inputs):
    inp = {k: np.asarray(v) for k, v in inputs.items()}
    if "prog" not in _CACHE:
        _CACHE["prog"] = build_program()
    nc, wl = _CACHE["prog"]
    assert len(wl) == N_IMAGES or DEBUG_STOP >= 0, (len(wl), N_IMAGES)
    consts = _consts()
    imgs = _build_images(wl, inp)

    def fm(v):
        return np.ascontiguousarray(v.reshape(8, 128).T)

    poolw = np.ascontiguousarray(inp["pool_w"].transpose(0, 2, 1, 3).reshape(2, 128, 512))
    dwp = np.zeros((128, 2, 4, 31), np.float32)
    for j in range(2):
        for c in range(4):
            dwp[:, j, c, :] = inp["conv_dw"][j][:, c * 128:(c + 1) * 128].T
    in_maps = []
    for i in range(8):
        toks = np.concatenate([inp["x_prompt"][2 * i], inp["x_prompt"][2 * i + 1], inp["x_sample"][i]], axis=0)
        xT = np.ascontiguousarray(toks.reshape(NTOK, 8, 128).transpose(2, 1, 0))
        P = np.zeros((128, PM["_n"]), np.float32)

        def put(name, arr):
            o, n = PM[name]
            P[:, o:o + n] = arr.reshape(128, n)
        put("bmod", np.stack([inp["b_mod"][l].reshape(48, 128).T for l in range(4)], 1))
        put("normg", np.stack([np.stack([fm(inp["norm_g"][l, w]) for w in range(2)], 1) for l in range(4)], 1))
        put("finalg", fm(inp["final_g"]))
        put("cT", np.stack([fm(inp["c_ctx"]), fm(inp["c"][i])], 2))
        qn = np.zeros((128, 2, 2), np.float32)
        for e in range(2):
            qn[:, e, 0] = inp["mla_q_norm"][e, 0:128]
            qn[0:64, e, 1] = inp["mla_q_norm"][e, 128:192]
        put("qnorm", qn)
        put("kvnorm", np.stack([inp["mla_kv_norm"][e] for e in range(2)], 1))
        put("sink", np.broadcast_to(inp["attn_sink"].reshape(1, 16), (128, 16)))
        for nm, src in (("bdw", "conv_dw_b"), ("lng", "conv_ln_g"), ("lnb", "conv_ln_b"), ("pscale", "pool_scale")):
            put(nm, np.stack([inp[src][j].reshape(4, 128).T for j in range(2)], 1))
        put("dw", dwp)
        m = {
            "xT": xT, "params": P, "wstream": imgs,
            "ropeA": consts["ropeA"], "ropeB": consts["ropeB"], "maskb": consts["maskb"], "ident": consts["ident"],
            "pcorr": consts["pcorr"],
            "ckT": np.ascontiguousarray(inp["cache_win_k"][i].reshape(2, NCTX, 128).transpose(0, 2, 1)),
            "cv": np.ascontiguousarray(inp["cache_win_v"][i].reshape(2, 2, 128, 128)),
            "cckvT": np.ascontiguousarray(inp["cache_mla_ckv"][i].transpose(0, 2, 1)),
            "ckrT": np.ascontiguousarray(inp["cache_mla_krope"][i].transpose(0, 2, 1)),
            "poolw": poolw,
        }
        in_maps.append(m)
    res = run_bass_kernel_spmd(nc, in_maps, core_ids=list(range(8)))
    R = res.results
    y_prompt = np.zeros((16, 256, D), np.float32)
    y_sample = np.zeros((8, 2048, D), np.float32)
    nk = np.zeros((16, 2, 256, 2, 64), np.float32)
    nv = np.zeros((16, 2, 256, 2, 64), np.float32)
    nckv = np.zeros((16, 2, 256, 128), np.float32)
    nkr = np.zeros((16, 2, 256, 32), np.float32)
    for i in range(8):
        r = R[i]
        y = np.asarray(r["yT"]).transpose(2, 1, 0).reshape(NTOK, D)
        y_prompt[2 * i] = y[0:256]
        y_prompt[2 * i + 1] = y[256:512]
        y_sample[i] = y[512:]
        kT = np.asarray(r["okT"])
        v = np.asarray(r["ov"])
        ck = np.asarray(r["ockvT"])
        kr = np.asarray(r["okrT"])
        for s in range(2):
            b = 2 * i + s
            for e in range(2):
                nk[b, e] = kT[e][:, s * 256:(s + 1) * 256].T.reshape(256, 2, 64)
                nv[b, e] = v[e][s * 256:(s + 1) * 256].reshape(256, 2, 64)
                nckv[b, e] = ck[e][:, s * 256:(s + 1) * 256].T
                nkr[b, e] = kr[e][:, s * 256:(s + 1) * 256].T
    return (y_prompt, y_sample, nk, nv, nckv, nkr)
```

```python
import contextlib
import os
import numpy as np
import concourse.bass as bass
import concourse.mybir as mybir
from concourse.bass_utils import run_bass_kernel_spmd

F32 = mybir.dt.float32
BF16 = mybir.dt.bfloat16
AF = mybir.ActivationFunctionType
ALU = mybir.AluOpType

D = 1024
DEPTH = 4
NTOK = 2560
NT = 5
TS = 512
NCTX = 256
EPS = 1e-6
NEG = -30000.0
SHIFT = 10.0
SLOT = 4096
NRING = 4
SEM_LIMIT = 30000

DEBUG_STOP = int(os.environ.get("KDEBUG_STOP", "-1"))


class Res:
    __slots__ = ("name", "w", "rd")

    def __init__(self, name):
        self.name = name
        self.w = None
        self.rd = {}


class Eng:
    def __init__(self, trk, name, handle, inc):
        self.trk = trk
        self.name = name
        self.h = handle
        self.inc = inc
        self.sem = None
        self.count = 0
        self.seen = {}
        self.sems = []

    def new_sem(self):
        self.sem = self.trk.alloc_sem(self.name)
        self.sems.append(self.sem)
        self.count = 0


class Tracker:
    def __init__(self, nc, es):
        self.nc = nc
        self.es = es
        self.nsem = 0
        self.E = {}
        for name, h, inc in (("pe", nc.tensor, 1), ("act", nc.scalar, 1), ("dve", nc.vector, 1),
                             ("pool", nc.gpsimd, 1)):
            e = Eng(self, name, h, inc)
            e.new_sem()
            self.E[name] = e
        self.Q = {"sp": Eng(self, "sp", nc.sync, 16), "poolq": self.E["pool"]}
        self.all_events = {}

    def alloc_sem(self, name):
        self.nsem += 1
        return self.es.enter_context(self.nc.semaphore(f"s_{name}_{self.nsem}"))

    def _wait(self, eng, ev):
        if ev is None:
            return
        sem, val, src = ev
        if src == "pe" and eng.name == "pe":
            return
        k = id(sem)
        if eng.seen.get(k, 0) >= val:
            return
        eng.seen[k] = val
        eng.h.wait_ge(sem, val)

    def _deps(self, eng, reads, writes, same_ok=False):
        for r in reads:
            if r.w is not None:
                self._wait(eng, r.w)
        for r in writes:
            if r.w is not None:
                self._wait(eng, r.w)
            for ev in r.rd.values():
                self._wait(eng, ev)

    def _record(self, ev, reads, writes):
        for r in reads:
            r.rd[id(ev[0])] = ev
        for r in writes:
            r.w = ev
            r.rd = {}
        self.all_events[id(ev[0])] = ev

    def op(self, ename, fn, reads=(), writes=()):
        eng = self.E[ename]
        if eng.count >= SEM_LIMIT:
            eng.new_sem()
        self._deps(eng, reads, writes)
        ins = fn(eng.h)
        eng.count += 1
        ins.then_inc(eng.sem, 1)
        ev = (eng.sem, eng.count, ename)
        self._record(ev, reads, writes)
        return ev

    def dma(self, qname, dsem, out, in_, reads=(), writes=()):
        q = self.Q[qname]
        self._deps(q, reads, writes)
        if dsem.last is not None:
            self._wait(q, dsem.last)
        if dsem.count + 16 > SEM_LIMIT:
            dsem.sem = self.alloc_sem("dma")
            dsem.count = 0
        ins = q.h.dma_start(out=out, in_=in_)
        dsem.count += 16
        ins.then_inc(dsem.sem, 16)
        ev = (dsem.sem, dsem.count, "dma")
        dsem.last = ev
        self._record(ev, reads, writes)
        return ev

    def barrier(self):
        evs = list(self.all_events.values())
        for e in list(self.E.values()) + [self.Q["sp"]]:
            for ev in evs:
                self._wait(e, ev)

    def final_wait(self, ename="sp"):
        q = self.Q[ename]
        for ev in list(self.all_events.values()):
            self._wait(q, ev)


class DmaSem:
    def __init__(self, trk, name):
        self.sem = trk.alloc_sem(name)
        self.count = 0
        self.last = None


class DmaSemPool:
    def __init__(self, trk, n, name):
        self.s = [DmaSem(trk, f"{name}{i}") for i in range(n)]
        self.i = 0

    def next(self):
        s = self.s[self.i % len(self.s)]
        self.i += 1
        return s


def _rope_tables(n, dim, grid_w=64, base=10000.0):
    rows = n // grid_w
    row = np.repeat(np.arange(rows), grid_w).astype(np.float32)
    col = np.tile(np.arange(grid_w), rows).astype(np.float32)
    quarter = dim // 4
    inv_freq = (base ** (-np.arange(quarter, dtype=np.float32) / quarter)).astype(np.float32)
    ang = np.concatenate([row[:, None] * inv_freq, col[:, None] * inv_freq], axis=-1).astype(np.float32)
    cos = np.cos(ang).astype(np.float32)
    sin = np.sin(ang).astype(np.float32)
    half = dim // 2
    cos2 = np.concatenate([cos, cos], axis=1).T
    sins = np.concatenate([-sin, sin], axis=1).T
    return np.ascontiguousarray(cos2), np.ascontiguousarray(sins)


def _consts():
    c = {}
    ca, sa = _rope_tables(2048, 64)
    c["ropeA"] = np.ascontiguousarray(np.stack([np.concatenate([ca, ca], 0), np.concatenate([sa, sa], 0)], 1))
    cb, sb = _rope_tables(2048, 32)
    c["ropeB"] = np.ascontiguousarray(np.stack([cb, sb], 1))
    b = np.arange(128)[:, None]
    a = np.arange(128)[None, :]
    m = np.zeros((128, 384), np.float32)
    m[:, 0:128] = np.where(b <= a, 0.0, NEG)
    m[:, 256:384] = np.where(a <= b, 0.0, NEG)
    c["maskb"] = m
    c["ident"] = np.eye(128, dtype=np.float32)
    corr = np.ones((128, 4, 2, 8), np.float32)
    for gi, w in enumerate((2, 4, 8, 16)):
        lo = w // 2
        hi = w - lo - 1
        for t in range(lo):
            corr[:, gi, 0, t] = w / float(t + hi + 1)
        for q in range(hi):
            corr[:, gi, 1, 7 - q] = w / float(lo + q + 1)
    c["pcorr"] = corr.reshape(128, 64)
    return c


def _pmap():
    m = {}
    o = 0

    def add(name, n):
        nonlocal o
        m[name] = (o, n)
        o += n
    add("bmod", 4 * 48)
    add("normg", 4 * 2 * 8)
    add("finalg", 8)
    add("cT", 16)
    add("qnorm", 4)
    add("kvnorm", 2)
    add("sink", 16)
    add("bdw", 8)
    add("lng", 8)
    add("lnb", 8)
    add("pscale", 8)
    add("dw", 2 * 4 * 31)
    m["_n"] = o
    return m


PM = _pmap()

POOL_W = (2, 4, 8, 16)
SEQS = [(0, 256), (256, 256), (512, 2048)]
PADB = [0, 288, 576]
NPAD = 2656


def _padpos(tok):
    for s, (o, n) in enumerate(SEQS):
        if o <= tok < o + n:
            return PADB[s] + 16 + (tok - o)
    raise ValueError


ARENA = 29568


def build_program():
    nc = bass.Bass("TRN2", target_bir_lowering=False)
    es = contextlib.ExitStack()
    wlist = []

    def dram(name, shape, kind="ExternalInput", dt=F32):
        return nc.dram_tensor(name, list(shape), dt, kind=kind).ap()

    d_xT = dram("xT", [128, 8, NTOK])
    d_params = dram("params", [128, PM["_n"]])
    d_ropeA = dram("ropeA", [128, 2, 2048])
    d_ropeB = dram("ropeB", [32, 2, 2048])
    d_maskb = dram("maskb", [128, 384])
    d_ident = dram("ident", [128, 128])
    d_pcorr = dram("pcorr", [128, 64])
    d_ckT = dram("ckT", [2, 128, NCTX])
    d_cv = dram("cv", [2, 2, 128, 128])
    d_cckvT = dram("cckvT", [2, 128, NCTX])
    d_ckrT = dram("ckrT", [2, 32, NCTX])
    d_poolw = dram("poolw", [2, 128, 4 * 128])
    d_w = dram("wstream", [N_IMAGES, 128, SLOT])
    o_yT = dram("yT", [128, 8, NTOK], kind="ExternalOutput")
    o_kT = dram("okT", [2, 128, 512], kind="ExternalOutput")
    o_v = dram("ov", [2, 512, 128], kind="ExternalOutput")
    o_ckvT = dram("ockvT", [2, 128, 512], kind="ExternalOutput")
    o_krT = dram("okrT", [2, 32, 512], kind="ExternalOutput")

    with es:
        trk = Tracker(nc, es)
        op = trk.op

        def sb(name, shape, dt):
            return es.enter_context(nc.sbuf_tensor(name, list(shape), dt))

        x = sb("x", [128, 8, NTOK], F32)
        X = [[Res(f"x{k}_{t}") for t in range(NT)] for k in range(8)]
        ring = [sb(f"ring{i}", [128, SLOT], BF16) for i in range(NRING)]
        RING = [Res(f"ring{i}") for i in range(NRING)]
        ring_sem = [DmaSem(trk, f"ring{i}") for i in range(NRING)]
        arena = sb("arena", [128, ARENA], BF16)
        params = sb("params_sb", [128, PM["_n"]], F32)
        P_ = Res("params")
        NDER = 4 * 6 * 8 * 2 + 64
        der = sb("der", [128, NDER], F32)
        DER = Res("der")
        mod = sb("mod", [128, 4, 48, 2], F32)
        MOD = [Res(f"mod{l}") for l in range(4)]
        scT = sb("scT", [128, 8, 2], BF16)
        SCT = Res("scT")
        ident = sb("ident_sb", [128, 128], BF16)
        onesm = sb("onesm", [128, 128], BF16)
        ones1 = sb("ones1", [128, 128], BF16)
        ones_lo = sb("ones_lo", [128, 128], BF16)
        maskb = sb("maskb_sb", [128, 384], BF16)
        pcorr = sb("pcorr_sb", [128, 64], F32)
        epst = sb("epst", [128, 1], F32)
        negc = sb("negc", [128, 1], F32)
        pw = sb("poolw_sb", [128, 512], BF16)
        PW = Res("pw")
        CONST = Res("const")
        rstd = sb("rstd", [128, 512], F32)
        RSTD = Res("rstd")
        sq = [sb(f"sq{i}", [128, 512], BF16) for i in range(2)]
        SQ = [Res(f"sq{i}") for i in range(2)]
        ft = [sb(f"ft{i}", [128, 512], F32) for i in range(7)]
        FT = [Res(f"ft{i}") for i in range(7)]
        tmp, TMP = ft[0:3], FT[0:3]
        cvt, CVT = ft[3:7], FT[3:7]
        rec, REC = ft[3], FT[3]
        ostage, OST = ft[4:6], FT[4:6]
        stage, STAGE = ft[6], FT[6]
        pt = [sb(f"pt{i}", [128, 512], BF16) for i in range(3)]
        PT = [Res(f"pt{i}") for i in range(3)]
        rope_t = sb("rope_t", [128, 2, TS], F32)
        ROPE = Res("rope")
        ropeB_all = sb("ropeB_all", [128, 2, TS], F32)
        ROPEB = Res("ropeB")
        ost_i = [0]

        banks = [es.enter_context(nc.psum_tensor(f"bank{i}", [128, 512], F32)) for i in range(8)]
        BANK = [Res(f"bank{i}") for i in range(8)]
        bank_free = list(range(8))

        side_free = []
        pool_sel = ["g"]

        def balloc():
            if pool_sel[0] == "side":
                assert side_free, "out of side PSUM banks"
                return side_free.pop(0)
            assert bank_free, "out of PSUM banks"
            return bank_free.pop(0)

        def bfree(b):
            if b in SIDE_POOL and pool_sel[0] == "side":
                side_free.append(b)
            else:
                bank_free.append(b)

        SIDE_POOL = [4, 5]

        def reserve_side(on):
            if on:
                for b in SIDE_POOL:
                    bank_free.remove(b)
                    side_free.append(b)
            else:
                for b in SIDE_POOL:
                    side_free.remove(b)
                    bank_free.append(b)

        sp_sems = DmaSemPool(trk, 6, "sp")
        out_sems = DmaSemPool(trk, 4, "out")

        rst = {"issued": 0, "got": 0, "held": [False] * NRING, "limit": None}

        def ring_pump():
            while rst["issued"] < N_IMAGES and rst["issued"] < rst["got"] + NRING:
                if rst["limit"] is not None and rst["issued"] >= rst["limit"]:
                    break
                i = rst["issued"]
                free = [k for k in range(NRING) if not rst["held"][k]]
                if not free:
                    break
                s_ = i % NRING if (i % NRING) in free else free[0]
                trk.dma("poolq", ring_sem[s_], ring[s_][:, :], d_w[i], reads=(), writes=(RING[s_],))
                rst["held"][s_] = True
                rst.setdefault("slot_of", {})[i] = s_
                rst["issued"] += 1

        def ring_get(key):
            i = rst["got"]
            wlist.append(key)
            ring_pump()
            assert rst["issued"] > i, ("ring stalled", key)
            rst["got"] += 1
            s_ = rst["slot_of"][i]
            return ring[s_], RING[s_], s_

        def ring_done(s_):
            rst["held"][s_] = False
            ring_pump()

        trk.dma("sp", sp_sems.next(), params[:, :], d_params[:, :], writes=(P_,))
        for k in range(8):
            trk.dma("sp", sp_sems.next(), x[:, k, :], d_xT[:, k, :], writes=tuple(X[k]))

        def load_cast(dst_ap, src_ap, nparts, ncols, wres):
            trk.dma("sp", sp_sems.next(), stage[0:nparts, 0:ncols], src_ap, writes=(STAGE,))
            op("dve", lambda e: e.tensor_copy(out=dst_ap, in_=stage[0:nparts, 0:ncols]), reads=(STAGE,), writes=wres)

        load_cast(ident[:, :], d_ident[:, :], 128, 128, (CONST,))
        load_cast(maskb[:, :], d_maskb[:, :], 128, 384, (CONST,))
        trk.dma("sp", sp_sems.next(), pcorr[:, :], d_pcorr[:, :], writes=(CONST,))
        op("dve", lambda e: e.memset(onesm[:, :], 1.0 / 1024.0), writes=(CONST,))
        op("dve", lambda e: e.memset(ones1[:, :], 1.0), writes=(CONST,))
        op("dve", lambda e: e.memset(ones_lo[:, :], 0.0), writes=(CONST,))
        op("dve", lambda e: e.memset(ones_lo[0:64, :], 1.0), writes=(CONST,))
        op("dve", lambda e: e.memset(epst[:, :], EPS), writes=(CONST,))
        op("dve", lambda e: e.memset(negc[:, :], -SHIFT), writes=(CONST,))

        o_cT = PM["cT"][0]
        op("act", lambda e: e.activation(out=scT[:, :, :], in_=params[:, o_cT:o_cT + 16].rearrange("p (k j) -> p k j", j=2),
                                         func=AF.Silu), reads=(P_,), writes=(SCT,))

        def dcol(l, which, k, g):
            return ((l * 6 + which) * 8 + k) * 2 + g

        o_es = 4 * 6 * 8 * 2
        o_np = o_es + 16
        o_psw = o_np + 8
        o_sink = PM["sink"][0]
        op("act", lambda e: e.activation(out=der[:, o_es:o_es + 16], in_=params[:, o_sink:o_sink + 16], func=AF.Exp, bias=negc[:, 0:1], scale=1.0),
           reads=(P_, CONST), writes=(DER,))
        o_ps = PM["pscale"][0]
        op("dve", lambda e: e.tensor_scalar(out=der[:, o_np:o_np + 8], in0=params[:, o_ps:o_ps + 8], scalar1=-1.0,
                                            scalar2=None, op0=ALU.mult), reads=(P_,), writes=(DER,))
        for j in range(2):
            for gi, w in enumerate(POOL_W):
                op("dve", lambda e, j=j, gi=gi, w=w: e.tensor_scalar(
                    out=der[:, o_psw + j * 4 + gi:o_psw + j * 4 + gi + 1],
                    in0=params[:, o_ps + j * 4 + gi:o_ps + j * 4 + gi + 1], scalar1=1.0 / w, scalar2=None,
                    op0=ALU.mult), reads=(P_,), writes=(DER,))

        def adaln_group(l, g):
            wt, WR, ws = ring_get(("wmod", l, g))
            b = balloc()

            def pe(e):
                ins = None
                for cc in range(4):
                    for kc in range(8):
                        ins = e.matmul(banks[b][:, cc * 2:cc * 2 + 2], lhsT=wt[:, kc * 512 + cc * 128:kc * 512 + cc * 128 + 128],
                                       rhs=scT[:, kc, :], start=(kc == 0), stop=(kc == 7), skip_group_check=True)
                return ins
            op("pe", pe, reads=(WR, SCT), writes=(BANK[b],))
            ring_done(ws)
            ob = PM["bmod"][0] + l * 48 + 4 * g
            for j in range(2):
                op("dve", lambda e, j=j: e.tensor_tensor(
                    out=mod[:, l, 4 * g:4 * g + 4, j],
                    in0=banks[b][:, 0:8].rearrange("p (c j) -> p c j", j=2)[:, :, j],
                    in1=params[:, ob:ob + 4], op=ALU.add), reads=(BANK[b], P_), writes=(MOD[l],))
            bfree(b)

        def adaln_finish(l, halves=(0, 1)):
            og = PM["normg"][0]
            for half in halves:
                for k in range(8):
                    gc = og + (l * 2 + half) * 8 + k
                    a0 = dcol(l, half * 3 + 0, k, 0)
                    op("dve", lambda e, k=k, half=half, gc=gc, a0=a0: e.tensor_scalar(
                        out=der[:, a0:a0 + 2], in0=mod[:, l, (half * 3 + 1) * 8 + k, :], scalar1=1.0, scalar2=params[:, gc:gc + 1],
                        op0=ALU.add, op1=ALU.mult), reads=(MOD[l], P_), writes=(DER,))
                    b0 = dcol(l, half * 3 + 1, k, 0)
                    op("dve", lambda e, k=k, half=half, b0=b0: e.tensor_copy(
                        out=der[:, b0:b0 + 2], in_=mod[:, l, (half * 3 + 0) * 8 + k, :]), reads=(MOD[l],), writes=(DER,))
                    g0 = dcol(l, half * 3 + 2, k, 0)
                    op("dve", lambda e, k=k, half=half, g0=g0: e.tensor_copy(
                        out=der[:, g0:g0 + 2], in_=mod[:, l, (half * 3 + 2) * 8 + k, :]), reads=(MOD[l],), writes=(DER,))

        def dv(l, which, k, g):
            c = dcol(l, which, k, g)
            return der[:, c:c + 1]

        def norm_tile(l, half, t, dest, DEST):
            grp = 0 if t == 0 else 1
            tok = slice(t * TS, (t + 1) * TS)
            b = balloc()
            for k in range(8):
                op("act", lambda e, k=k: e.activation(out=sq[k % 2][:, :], in_=x[:, k, tok], func=AF.Square),
                   reads=(X[k][t],), writes=(SQ[k % 2],))
                op("pe", lambda e, k=k: e.matmul(banks[b][:, :], lhsT=onesm[:, :], rhs=sq[k % 2][:, :], start=(k == 0),
                                                 stop=(k == 7), skip_group_check=True),
                   reads=(SQ[k % 2], CONST), writes=(BANK[b],))
            op("act", lambda e: e.activation(out=rstd[:, :], in_=banks[b][:, :], func=AF.Ln, bias=epst[:, 0:1], scale=1.0),
               reads=(BANK[b], CONST), writes=(RSTD,))
            bfree(b)
            op("act", lambda e: e.activation(out=rstd[:, :], in_=rstd[:, :], func=AF.Exp, scale=-0.5), reads=(RSTD,), writes=(RSTD,))
            for k in range(8):
                ti = k % 2
                op("dve", lambda e, k=k, ti=ti: e.tensor_tensor(out=tmp[ti][:, :], in0=x[:, k, tok], in1=rstd[:, :], op=ALU.mult),
                   reads=(X[k][t], RSTD), writes=(TMP[ti],))
                op("act", lambda e, k=k, ti=ti: e.activation(out=dest[:, k, :], in_=tmp[ti][:, :], func=AF.Identity,
                                                           bias=dv(l, half * 3 + 1, k, grp), scale=dv(l, half * 3 + 0, k, grp)),
                   reads=(TMP[ti], DER), writes=(DEST,))

        def norm_gen(l, half, t, dest, DEST):
            grp = 0 if t == 0 else 1
            tokx = slice(t * TS, (t + 1) * TS)
            b = balloc()

            def mm_(k):
                op("pe", lambda e: e.matmul(banks[b][:, :], lhsT=onesm[:, :], rhs=sq[k % 2][:, :], start=(k == 0),
                                            stop=(k == 7), skip_group_check=True),
                   reads=(SQ[k % 2], CONST), writes=(BANK[b],))
            for p_ in range(5):
                if p_ >= 1:
                    mm_(2 * p_ - 2)
                    mm_(2 * p_ - 1)
                if p_ < 4:
                    for k in (2 * p_, 2 * p_ + 1):
                        op("act", lambda e, k=k: e.activation(out=sq[k % 2][:, :], in_=x[:, k, tokx], func=AF.Square),
                           reads=(X[k][t],), writes=(SQ[k % 2],))
                    yield
            op("act", lambda e: e.activation(out=rstd[:, :], in_=banks[b][:, :], func=AF.Ln, bias=epst[:, 0:1], scale=1.0),
               reads=(BANK[b], CONST), writes=(RSTD,))
            bfree(b)
            op("act", lambda e: e.activation(out=rstd[:, :], in_=rstd[:, :], func=AF.Exp, scale=-0.5), reads=(RSTD,), writes=(RSTD,))
            yield
            for k in range(8):
                ti = k % 2
                op("dve", lambda e, k=k, ti=ti: e.tensor_tensor(out=tmp[ti][:, :], in0=x[:, k, tokx], in1=rstd[:, :], op=ALU.mult),
                   reads=(X[k][t], RSTD), writes=(TMP[ti],))
                op("act", lambda e, k=k, ti=ti: e.activation(out=dest[:, k, :], in_=tmp[ti][:, :], func=AF.Identity,
                                                           bias=dv(l, half * 3 + 1, k, grp), scale=dv(l, half * 3 + 0, k, grp)),
                   reads=(TMP[ti], DER), writes=(DEST,))
                if k % 4 == 3:
                    yield

        def drive(gens):
            gens = [g_ for g_ in gens if g_ is not None]
            while gens:
                for g_ in list(gens):
                    try:
                        next(g_)
                    except StopIteration:
                        gens.remove(g_)

        def resid_add(l, half, j, t, b):
            grp = 0 if t == 0 else 1
            xs = x[:, j, t * TS:(t + 1) * TS]
            op("dve", lambda e: e.scalar_tensor_tensor(out=xs, in0=banks[b][:, :], scalar=dv(l, half * 3 + 2, j, grp),
                                                       in1=xs, op0=ALU.mult, op1=ALU.add),
               reads=(BANK[b], DER, X[j][t]), writes=(X[j][t],))

        def mlp(l, next_adaln):
            hbuf = arena[:, 0:8 * NTOK].rearrange("p (k n) -> p k n", k=8)
            H = [Res(f"h{t}") for t in range(NT)]
            h1 = [arena[:, 8 * NTOK + i * 2048:8 * NTOK + (i + 1) * 2048].rearrange("p (c n) -> p c n", c=4) for i in range(2)]
            H1 = [Res("h1a"), Res("h1b")]
            norm_tile(l, 1, 0, hbuf[:, :, 0:TS], H[0])
            ada = list(next_adaln)
            seq = [(g, t) for g in range(8) for t in range(NT)]
            w1s, w2s = {}, {}

            def h1stage(k):
                g, t = seq[k]
                hi = k % 2
                if t == 0:
                    w1s[g] = ring_get(("w1", l, g))
                w1, W1, s1 = w1s[g]
                ng = None
                if g == 0 and t + 1 < NT:
                    ng = norm_gen(l, 1, t + 1, hbuf[:, :, (t + 1) * TS:(t + 2) * TS], H[t + 1])
                for c in range(4):
                    if ng is not None:
                        for _ in range(2):
                            next(ng, None)
                    b_ = balloc()

                    def pe(e, c=c, b_=b_):
                        ins = None
                        for kc in range(8):
                            ins = e.matmul(banks[b_][:, :], lhsT=w1[:, kc * 512 + c * 128:kc * 512 + c * 128 + 128],
                                           rhs=hbuf[:, kc, t * TS:(t + 1) * TS], start=(kc == 0), stop=(kc == 7))
                        return ins
                    op("pe", pe, reads=(W1, H[t]), writes=(BANK[b_],))
                    ti = c % 2
                    op("act", lambda e, b_=b_, ti=ti: e.activation(out=tmp[ti][:, :], in_=banks[b_][:, :], func=AF.Relu),
                       reads=(BANK[b_],), writes=(TMP[ti],))
                    bfree(b_)
                    op("dve", lambda e, c=c, ti=ti: e.tensor_tensor(out=h1[hi][:, c, :], in0=tmp[ti][:, :], in1=tmp[ti][:, :],
                                                                    op=ALU.mult), reads=(TMP[ti],), writes=(H1[hi],))
                if t == NT - 1:
                    ring_done(s1)
                if ng is not None:
                    for _ in ng:
                        pass

            def outstage(k):
                g, t = seq[k]
                hi = k % 2
                if t == 0:
                    w2s[g] = ring_get(("w2", l, g))
                w2, W2, s2 = w2s[g]
                for j in range(8):
                    b_ = balloc()

                    def pe2(e, j=j, b_=b_):
                        ins = None
                        for c in range(4):
                            ins = e.matmul(banks[b_][:, :], lhsT=w2[:, c * 1024 + j * 128:c * 1024 + j * 128 + 128],
                                           rhs=h1[hi][:, c, :], start=(c == 0), stop=(c == 3))
                        return ins
                    op("pe", pe2, reads=(W2, H1[hi]), writes=(BANK[b_],))
                    resid_add(l, 1, j, t, b_)
                    bfree(b_)
                if t == NT - 1:
                    ring_done(s2)
                    for _ in range(2):
                        if ada:
                            adaln_group(*ada.pop(0))
                            if not ada:
                                adaln_finish(l + 1)

            h1stage(0)
            for k in range(len(seq)):
                if k + 1 < len(seq):
                    h1stage(k + 1)
                outstage(k)
            assert not ada

        pti = [0]

        NPT = 3
        LA = 3
        OB_POOL = [6, 7]
        ob_i = [0]

        def reserve_obanks(on):
            if on:
                for b in OB_POOL:
                    bank_free.remove(b)
            else:
                bank_free.extend(OB_POOL)

        def run_attn(jobs, scale, side=None):
            items = []
            for ji, job in enumerate(jobs):
                for gi, g in enumerate(job["groups"]):
                    items.append((ji, gi, g))
            M = len(items)
            sbank = {}
            ptb = {}
            obank = {}
            for i in range(M + LA):
                if i < M:
                    ji, gi, g = items[i]
                    sbk = balloc()
                    sbank[i] = sbk
                    n = g["n"]

                    def pe(e, g=g, sbk=sbk, n=n):
                        ins = e.matmul(banks[sbk][:, 0:n], lhsT=g["k"], rhs=g["q"], start=True, stop=(g["mask"] is None), skip_group_check=True)
                        if g["mask"] is not None:
                            ins = e.matmul(banks[sbk][:, 0:n], lhsT=ident[:, :], rhs=g["mask"], start=False, stop=True, skip_group_check=True)
                        return ins
                    op("pe", pe, reads=(g["K"], g["Q"], CONST) + ((g["Q2"],) if "Q2" in g else ()), writes=(BANK[sbk],))
                j = i - (LA - 1)
                if 0 <= j < M:
                    ji, gi, g = items[j]
                    sbk = sbank.pop(j)
                    n = g["n"]
                    pi = pti[0] % NPT
                    pti[0] += 1
                    ptb[j] = pi
                    op("act", lambda e, sbk=sbk, n=n, pi=pi: e.activation(out=pt[pi][:, 0:n], in_=banks[sbk][:, 0:n], func=AF.Exp, bias=negc[:, 0:1], scale=scale),
                       reads=(BANK[sbk], CONST), writes=(PT[pi],))
                    bfree(sbk)
                k = i - LA
                if 0 <= k < M:
                    ji, gi, g = items[k]
                    if gi == 0:
                        obank[ji] = OB_POOL[ob_i[0] % 2]
                        ob_i[0] += 1
                    ob = obank[ji]
                    pi = ptb.pop(k)
                    n = g["n"]
                    c0 = g["c0"]
                    last = gi == len(jobs[ji]["groups"]) - 1
                    op("pe", lambda e, g=g, pi=pi, n=n, c0=c0, f=(gi == 0), la=last, ob=ob: e.matmul(
                        banks[ob][:, c0:c0 + n], lhsT=g["v"], rhs=pt[pi][:, 0:n], start=f, stop=la, skip_group_check=True),
                       reads=(g["V"], PT[pi]), writes=(BANK[ob],))
                    if last:
                        jobs[ji]["finish"](ob)
                        obank.pop(ji)
                if side is not None:
                    side()

        def normalize(obank, base, nq, sink_col, dst, wres, extra_reads=(), on_dve=False):
            dbase = 64 - base
            ds = slice(dbase, dbase + 64)
            if on_dve:
                op("dve", lambda e: e.tensor_scalar(out=rec[ds, 0:nq], in0=banks[obank][ds, 0:nq],
                                                    scalar1=der[ds, sink_col:sink_col + 1], scalar2=None, op0=ALU.add),
                   reads=(BANK[obank], DER), writes=(REC,))
                op("dve", lambda e: e.reciprocal(out=rec[ds, 0:nq], in_=rec[ds, 0:nq]), reads=(REC,), writes=(REC,))
            else:
                if sink_col is not None:
                    op("act", lambda e: e.activation(out=rec[ds, 0:nq], in_=banks[obank][ds, 0:nq], func=AF.Ln,
                                                     bias=der[ds, sink_col:sink_col + 1], scale=1.0),
                       reads=(BANK[obank], DER), writes=(REC,))
                else:
                    op("act", lambda e: e.activation(out=rec[ds, 0:nq], in_=banks[obank][ds, 0:nq], func=AF.Ln),
                       reads=(BANK[obank],), writes=(REC,))
                op("act", lambda e: e.activation(out=rec[ds, 0:nq], in_=rec[ds, 0:nq], func=AF.Exp, scale=-1.0), reads=(REC,), writes=(REC,))
            op("dve", lambda e: e.tensor_tensor(out=dst, in0=banks[obank][base:base + 64, 0:nq], in1=rec[ds, 0:nq], op=ALU.mult),
               reads=(BANK[obank], REC) + tuple(extra_reads), writes=wres)

        def rope_combine(b1, b2, nrows, dst, wres, p0=0, tab=None, TAB=None, tp0=None):
            ps = slice(p0, p0 + nrows)
            if tab is None:
                tab, TAB, tp0 = rope_t, ROPE, p0
            ts_ = slice(tp0, tp0 + nrows)
            op("dve", lambda e: e.tensor_tensor(out=tmp[0][ps, :], in0=banks[b1][ps, :], in1=tab[ts_, 0, :], op=ALU.mult),
               reads=(BANK[b1], TAB), writes=(TMP[0],))
            op("dve", lambda e: e.tensor_tensor(out=tmp[1][ps, :], in0=banks[b2][ps, :], in1=tab[ts_, 1, :], op=ALU.mult),
               reads=(BANK[b2], TAB), writes=(TMP[1],))
            op("dve", lambda e: e.tensor_tensor(out=dst, in0=tmp[0][ps, :], in1=tmp[1][ps, :], op=ALU.add),
               reads=(TMP[0], TMP[1]), writes=wres)

        def load_rope(which, T, p0=0):
            rsl = slice(T * TS, (T + 1) * TS)
            if which == "A":
                trk.dma("sp", sp_sems.next(), rope_t[:, :, :], d_ropeA[:, :, rsl], writes=(ROPE,))
            else:
                trk.dma("sp", sp_sems.next(), rope_t[p0:p0 + 32, :, :], d_ropeB[:, :, rsl], writes=(ROPE,))

        def attn_group(l, grp, extra=None):
            e_i = l // 2
            sample = grp == 1
            tiles = [1, 2, 3, 4] if sample else [0]
            t0 = tiles[0]
            ntok = TS * len(tiles)
            nctx = NCTX if sample else 0
            NKg = ntok + nctx
            nkt = NKg // 128
            nlt = ntok // 128
            nt = len(tiles)
            o = 0
            qa = arena[:, o:o + 4 * ntok].rearrange("p (c n) -> p c n", c=4); o += 4 * ntok
            cqn = arena[:, o:o + 2 * NKg].rearrange("p (c n) -> p c n", c=2); o += 2 * NKg
            ckvn = arena[:, o:o + NKg]; o += NKg
            r3 = o
            hts = [arena[:, o + i * 4096:o + (i + 1) * 4096].rearrange("p (k n) -> p k n", k=8) for i in range(2)]
            khb = arena[:, r3:r3 + NKg]
            vhb = arena[:, r3 + NKg:r3 + NKg + nkt * 128].rearrange("p (t c) -> p t c", c=128)
            qhbs = [arena[:, r3 + NKg + nkt * 128 + i * ntok:r3 + NKg + nkt * 128 + (i + 1) * ntok] for i in range(2)]
            o = r3 + max(8192, NKg + nkt * 128 + 2 * ntok)
            r4 = o
            ka = arena[:, o:o + NKg]
            va = arena[:, o + NKg:o + NKg + nkt * 192].rearrange("p (t c) -> p t c", c=192)
            obc = [arena[:, r4 + i * ntok:r4 + (i + 1) * ntok] for i in range(2)]
            o = r4 + max(NKg + nkt * 192, 2 * ntok)
            assert o <= ARENA, o
            HTs = [Res("ht0"), Res("ht1")]
            QAH = [[[Res(f"qah{c}_{hh}_{t}") for t in range(nt)] for hh in range(2)] for c in range(4)]
            CQ = [Res(f"cq{t}") for t in range(nt)]
            CKV = [Res(f"ckv{t}") for t in range(nt + 1)]
            KR = [Res(f"kr{t}") for t in range(nt + 1)]
            KA = [Res(f"ka{t}") for t in range(nt + 1)]
            VA = [Res(f"va{t}") for t in range(nt + 1)]
            OBC = [Res("obc0"), Res("obc1")]
            KH, VH = Res("kh"), Res("vh")
            QHs = [Res("qh0"), Res("qh1")]
            krv = cqn[64:96, 1, :]

            reserve_obanks(True)
            g0, G0, s0 = ring_get(("win", e_i, 0))
            g1, G1, s1 = ring_get(("win", e_i, 1))
            g2, G2, s2 = ring_get(("win", e_i, 2))
            g3, G3, s3 = ring_get(("win", e_i, 3))

            op("dve", lambda e: e.memset(va[:, :, 64:128], 1.0), writes=tuple(VA))
            op("dve", lambda e: e.memset(cqn[96:128, 1, :], 0.0), writes=tuple(CQ))
            if sample:
                for g_ in range(4):
                    trk.dma("sp", sp_sems.next(), ropeB_all[32 * g_:32 * g_ + 32, :, :], d_ropeB[:, :, g_ * TS:(g_ + 1) * TS], writes=(ROPEB,))
                load_cast(ka[:, ntok:NKg], d_ckT[e_i], 128, NCTX, (KA[nt],))
                load_cast(ckvn[:, ntok:NKg], d_cckvT[e_i], 128, NCTX, (CKV[nt],))
                load_cast(krv[:, ntok:NKg], d_ckrT[e_i], 32, NCTX, (KR[nt],))
                for kt in range(2):
                    trk.dma("sp", sp_sems.next(), stage[:, 0:128], d_cv[e_i, kt], writes=(STAGE,))
                    op("dve", lambda e, kt=kt: e.tensor_copy(out=va[:, nlt + kt, 0:64], in_=stage[:, 0:64]), reads=(STAGE,), writes=(VA[nt],))
                    op("dve", lambda e, kt=kt: e.tensor_copy(out=va[:, nlt + kt, 128:192], in_=stage[:, 64:128]), reads=(STAGE,), writes=(VA[nt],))

            oqn = PM["qnorm"][0] + e_i * 2
            okn = PM["kvnorm"][0] + e_i

            cur = {}

            def proj(bk, wt, WR, col0, ncol, ht, HT):

                def pe(e):
                    ins = None
                    for kc in range(8):
                        ins = e.matmul(banks[bk][0:ncol, :], lhsT=wt[:, kc * 512 + col0:kc * 512 + col0 + ncol], rhs=ht[:, kc, :],
                                       start=(kc == 0), stop=(kc == 7))
                    return ins
                op("pe", pe, reads=(WR, HT), writes=(BANK[bk],))

            def out_from(dst, src_ap, src_res, nrows, scale=None):
                i = ost_i[0] % 2
                ost_i[0] += 1
                if scale is None:
                    op("act", lambda e: e.activation(out=ostage[i][0:nrows, :], in_=src_ap, func=AF.Identity),
                       reads=src_res, writes=(OST[i],))
                else:
                    op("act", lambda e: e.activation(out=ostage[i][0:nrows, :], in_=src_ap, func=AF.Identity, scale=scale),
                       reads=src_res + (P_,), writes=(OST[i],))
                trk.dma("sp", out_sems.next(), dst, ostage[i][0:nrows, :], reads=(OST[i],))

            def norm_gen(t, dest, DEST):
                grp = 0 if t == 0 else 1
                tokx = slice(t * TS, (t + 1) * TS)
                b = balloc()

                def mm_(k):
                    op("pe", lambda e: e.matmul(banks[b][:, :], lhsT=onesm[:, :], rhs=sq[k % 2][:, :], start=(k == 0),
                                                stop=(k == 7), skip_group_check=True),
                       reads=(SQ[k % 2], CONST), writes=(BANK[b],))
                for p_ in range(5):
                    if p_ >= 1:
                        mm_(2 * p_ - 2)
                        mm_(2 * p_ - 1)
                    if p_ < 4:
                        for k in (2 * p_, 2 * p_ + 1):
                            op("act", lambda e, k=k: e.activation(out=sq[k % 2][:, :], in_=x[:, k, tokx], func=AF.Square),
                               reads=(X[k][t],), writes=(SQ[k % 2],))
                        yield
                op("act", lambda e: e.activation(out=rstd[:, :], in_=banks[b][:, :], func=AF.Ln, bias=epst[:, 0:1], scale=1.0),
                   reads=(BANK[b], CONST), writes=(RSTD,))
                bfree(b)
                op("act", lambda e: e.activation(out=rstd[:, :], in_=rstd[:, :], func=AF.Exp, scale=-0.5), reads=(RSTD,), writes=(RSTD,))
                yield
                for k in range(8):
                    ti = k % 2
                    op("dve", lambda e, k=k, ti=ti: e.tensor_tensor(out=tmp[ti][:, :], in0=x[:, k, tokx], in1=rstd[:, :], op=ALU.mult),
                       reads=(X[k][t], RSTD), writes=(TMP[ti],))
                    op("act", lambda e, k=k, ti=ti: e.activation(out=dest[:, k, :], in_=tmp[ti][:, :], func=AF.Identity,
                                                               bias=dv(l, 1, k, grp), scale=dv(l, 0, k, grp)),
                       reads=(TMP[ti], DER), writes=(DEST,))
                    if k % 4 == 3:
                        yield

            def stageA(lt):
                t = tiles[lt]
                tok = slice(lt * TS, (lt + 1) * TS)
                ht, HT = hts[lt % 2], HTs[lt % 2]
                cur["ht"], cur["HT"] = ht, HT
                yield from norm_gen(t, ht, HT)
                cur["ht"], cur["HT"] = ht, HT
                b0 = balloc(); b1 = balloc(); bs = balloc()
                proj(b0, g0, G0, 0, 128, ht, HT)
                proj(b1, g0, G0, 128, 128, ht, HT)
                op("act", lambda e, b0=b0: e.activation(out=sq[0][:, :], in_=banks[b0][:, :], func=AF.Square), reads=(BANK[b0],), writes=(SQ[0],))
                op("act", lambda e, b1=b1: e.activation(out=sq[1][:, :], in_=banks[b1][:, :], func=AF.Square), reads=(BANK[b1],), writes=(SQ[1],))

                def pe_ss(e, bs=bs):
                    e.matmul(banks[bs][:, :], lhsT=ones1[:, :], rhs=sq[0][:, :], start=True, stop=False, skip_group_check=True)
                    return e.matmul(banks[bs][:, :], lhsT=ones_lo[:, :], rhs=sq[1][:, :], start=False, stop=True, skip_group_check=True)
                op("pe", pe_ss, reads=(SQ[0], SQ[1], CONST), writes=(BANK[bs],))
                yield
                op("act", lambda e, bs=bs: e.activation(out=rstd[:, :], in_=banks[bs][:, :], func=AF.Ln, bias=epst[:, 0:1], scale=1.0 / 192.0),
                   reads=(BANK[bs], CONST), writes=(RSTD,))
                op("act", lambda e: e.activation(out=rstd[:, :], in_=rstd[:, :], func=AF.Exp, scale=-0.5), reads=(RSTD,), writes=(RSTD,))
                op("dve", lambda e, b0=b0: e.tensor_tensor(out=tmp[0][:, :], in0=banks[b0][:, :], in1=rstd[:, :], op=ALU.mult),
                   reads=(BANK[b0], RSTD), writes=(TMP[0],))
                op("act", lambda e, tok=tok: e.activation(out=cqn[:, 0, tok], in_=tmp[0][:, :], func=AF.Identity, scale=params[:, oqn:oqn + 1]),
                   reads=(TMP[0], P_), writes=(CQ[lt],))
                op("dve", lambda e, b1=b1: e.tensor_tensor(out=tmp[1][0:64, :], in0=banks[b1][0:64, :], in1=rstd[0:64, :], op=ALU.mult),
                   reads=(BANK[b1], RSTD), writes=(TMP[1],))
                op("act", lambda e, tok=tok: e.activation(out=cqn[0:64, 1, tok], in_=tmp[1][0:64, :], func=AF.Identity, scale=params[0:64, oqn + 1:oqn + 2]),
                   reads=(TMP[1], P_), writes=(CQ[lt],))
                bfree(b0); bfree(b1)
                yield
                b0 = balloc()
                proj(b0, g0, G0, 192, 128, ht, HT)
                op("act", lambda e, b0=b0: e.activation(out=sq[0][:, :], in_=banks[b0][:, :], func=AF.Square), reads=(BANK[b0],), writes=(SQ[0],))
                op("pe", lambda e, bs=bs: e.matmul(banks[bs][:, :], lhsT=ones1[:, :], rhs=sq[0][:, :], start=True, stop=True),
                   reads=(SQ[0], CONST), writes=(BANK[bs],))
                op("act", lambda e, bs=bs: e.activation(out=rstd[:, :], in_=banks[bs][:, :], func=AF.Ln, bias=epst[:, 0:1], scale=1.0 / 128.0),
                   reads=(BANK[bs], CONST), writes=(RSTD,))
                op("act", lambda e: e.activation(out=rstd[:, :], in_=rstd[:, :], func=AF.Exp, scale=-0.5), reads=(RSTD,), writes=(RSTD,))
                op("dve", lambda e, b0=b0: e.tensor_tensor(out=tmp[0][:, :], in0=banks[b0][:, :], in1=rstd[:, :], op=ALU.mult),
                   reads=(BANK[b0], RSTD), writes=(TMP[0],))
                op("act", lambda e, tok=tok: e.activation(out=ckvn[:, tok], in_=tmp[0][:, :], func=AF.Identity, scale=params[:, okn:okn + 1]),
                   reads=(TMP[0], P_), writes=(CKV[lt],))
                if not sample:
                    out_from(o_ckvT[e_i], tmp[0][:, :], (TMP[0],), 128, scale=params[:, okn:okn + 1])
                bfree(b0); bfree(bs)
                yield
                b0 = balloc()
                proj(b0, g0, G0, 320, 128, ht, HT)
                if sample:
                    b1 = balloc()
                    proj(b1, g0, G0, 352, 128, ht, HT)
                    rope_combine(b0, b1, 32, krv[:, tok], (KR[lt],), tab=ropeB_all, TAB=ROPEB, tp0=32 * (t - 1))
                    bfree(b1)
                else:
                    out_from(o_krT[e_i], banks[b0][0:32, :], (BANK[b0],), 32)
                    op("dve", lambda e, b0=b0, tok=tok: e.tensor_copy(out=krv[:, tok], in_=ostage[(ost_i[0] - 1) % 2][0:32, :]),
                       reads=(OST[(ost_i[0] - 1) % 2],), writes=(KR[lt],))
                bfree(b0)
                yield

            def stageB(lt):
                t = tiles[lt]
                tok = slice(lt * TS, (lt + 1) * TS)
                ht, HT = hts[lt % 2], HTs[lt % 2]
                cur["ht"], cur["HT"] = ht, HT
                if sample:
                    load_rope("A", t - 1)
                for c in range(4):
                    b0 = balloc()
                    proj(b0, g1, G1, c * 128, 128, ht, HT)
                    if sample:
                        b1 = balloc()
                        proj(b1, g2, G2, c * 128, 128, ht, HT)
                        rope_combine(b0, b1, 128, qa[:, c, tok], (QAH[c][0][lt], QAH[c][1][lt]))
                        bfree(b1)
                    else:
                        op("act", lambda e, c=c, b0=b0, tok=tok: e.activation(out=qa[:, c, tok], in_=banks[b0][:, :], func=AF.Identity),
                           reads=(BANK[b0],), writes=(QAH[c][0][lt], QAH[c][1][lt]))
                    bfree(b0)
                    yield
                b0 = balloc()
                proj(b0, g3, G3, 0, 128, ht, HT)
                if sample:
                    b1 = balloc()
                    proj(b1, g3, G3, 128, 128, ht, HT)
                    rope_combine(b0, b1, 128, ka[:, tok], (KA[lt],))
                    bfree(b1)
                else:
                    op("act", lambda e, b0=b0, tok=tok: e.activation(out=ka[:, tok], in_=banks[b0][:, :], func=AF.Identity), reads=(BANK[b0],), writes=(KA[lt],))
                    out_from(o_kT[e_i], banks[b0][:, :], (BANK[b0],), 128)
                bfree(b0)
                yield
                b0 = balloc()

                def pe_v(e, b0=b0, ht=ht):
                    ins = None
                    for tb in range(4):
                        for kc in range(8):
                            ins = e.matmul(banks[b0][:, tb * 128:(tb + 1) * 128], lhsT=ht[:, kc, tb * 128:(tb + 1) * 128],
                                           rhs=g3[:, kc * 512 + 256:kc * 512 + 384], start=(kc == 0), stop=(kc == 7), skip_group_check=True)
                    return ins
                op("pe", pe_v, reads=(G3, HT), writes=(BANK[b0],))
                bv = banks[b0][:, :].rearrange("p (t c) -> p t c", c=128)
                op("act", lambda e, bv=bv, lt=lt: e.activation(out=va[:, 4 * lt:4 * lt + 4, 0:64], in_=bv[:, :, 0:64], func=AF.Identity),
                   reads=(BANK[b0],), writes=(VA[lt],))
                op("act", lambda e, bv=bv, lt=lt: e.activation(out=va[:, 4 * lt:4 * lt + 4, 128:192], in_=bv[:, :, 64:128], func=AF.Identity),
                   reads=(BANK[b0],), writes=(VA[lt],))
                if not sample:
                    i = ost_i[0] % 2
                    ost_i[0] += 1
                    op("act", lambda e, i=i, b0=b0: e.activation(out=ostage[i][:, :], in_=banks[b0][:, :], func=AF.Identity),
                       reads=(BANK[b0],), writes=(OST[i],))
                    trk.dma("sp", out_sems.next(), o_v[e_i].rearrange("(t p) c -> p t c", p=128),
                            ostage[i][:, :].rearrange("p (t c) -> p t c", c=128), reads=(OST[i],))
                bfree(b0)
                yield

            def drive(gens):
                gens = [g_ for g_ in gens if g_ is not None]
                while gens:
                    for g_ in list(gens):
                        try:
                            next(g_)
                        except StopIteration:
                            gens.remove(g_)

            drive([stageA(0)])
            for lt in range(nt):
                drive([stageB(lt), stageA(lt + 1) if lt + 1 < nt else None])
            for s_ in (s0, s1, s2, s3):
                ring_done(s_)

            g4, G4, s4 = ring_get(("wqkv", e_i))
            wo0, WO0, so0 = ring_get(("wout", e_i, 0))
            wo1, WO1, so1 = ring_get(("wout", e_i, 1))

            kaz = [arena[:, r3 + i * NKg:r3 + (i + 1) * NKg] for i in range(2)]
            KZ = Res("kz")
            jobs = []
            for c in range(4):
                for hh in range(2):
                    h = c + 4 * hh
                    base = 64 * hh
                    vs = slice(0, 128) if hh == 0 else slice(64, 192)
                    sink_col = o_es + e_i * 8 + h
                    if not sample:
                        for s in range(2):
                            q0 = s * 256
                            groups = []
                            for kb in range(2):
                                k0 = s * 256 + kb * 128
                                groups.append(dict(k=kaz[hh][:, k0:k0 + 128], q=qa[:, c, q0:q0 + 256], K=KZ, Q=QAH[c][hh][0], Q2=QAH[c][1 - hh][0],
                                                   v=va[:, k0 // 128, vs], V=VA[0], c0=0, n=256, mask=None))
                            jobs.append(dict(groups=groups, finish=(lambda ob, base=base, sink_col=sink_col, c=c, q0=q0, hh=hh: normalize(
                                ob, base, 256, sink_col, qa[base:base + 64, c, q0:q0 + 256], (QAH[c][hh][0],)))))
                    else:
                        for T in range(4):
                            q0 = T * TS
                            groups = []
                            for kb in range(2):
                                groups.append(dict(k=kaz[hh][:, ntok + kb * 128:ntok + (kb + 1) * 128], q=qa[:, c, q0:q0 + TS],
                                                   K=KZ, Q=QAH[c][hh][T], Q2=QAH[c][1 - hh][T], v=va[:, nlt + kb, vs], V=VA[nt], c0=0, n=TS, mask=None))
                            for jb in range(4 * T - 1, 4 * T + 5):
                                if jb < 0 or jb > 15:
                                    continue
                                qlo = max(jb - 1, 4 * T)
                                qhi = min(jb + 1, 4 * T + 3)
                                n = (qhi - qlo + 1) * 128
                                c0 = (qlo - 4 * T) * 128
                                m0 = (qlo - (jb - 1)) * 128
                                k0 = jb * 128
                                groups.append(dict(k=kaz[hh][:, k0:k0 + 128], q=qa[:, c, q0 + c0:q0 + c0 + n],
                                                   K=KZ, Q=QAH[c][hh][T], Q2=QAH[c][1 - hh][T], v=va[:, jb, vs], V=VA[jb // 4],
                                                   c0=c0, n=n, mask=maskb[:, m0:m0 + n]))
                            jobs.append(dict(groups=groups, finish=(lambda ob, base=base, sink_col=sink_col, c=c, q0=q0, hh=hh, T=T: normalize(
                                ob, base, TS, sink_col, qa[base:base + 64, c, q0:q0 + TS], (QAH[c][hh][T],)))))
            side_q, side_o = [], []
            side_x = list(extra or [])
            s4_released = [False]

            sidestep = [0]
            armed = [False]

            def pop_side():
                pool_sel[0] = "side"
                sidestep[0] += 1
                if side_q:
                    side_q.pop(0)()
                elif side_x and sidestep[0] % 24 == 0 and armed[0]:
                    side_x.pop(0)()
                elif side_o and (sidestep[0] % 2 == 0 or not sample):
                    side_o.pop(0)[1]()
                pool_sel[0] = "g"

            def flush(lst):
                while lst:
                    it_ = lst.pop(0)
                    (it_[1] if isinstance(it_, tuple) else it_)()

            def flush_tag(tag):
                keep = []
                for it_ in side_o:
                    if it_[0] == tag:
                        it_[1]()
                    else:
                        keep.append(it_)
                side_o[:] = keep

            mscale = float((64 + 32) ** -0.5)
            WQ0, WQ1, WK, WV = 0, 1024, 2048, 2560
            nt6 = nt + (1 if sample else 0)

            if not sample:
                fo_ = o
                PB = []
                for h_ in range(8):
                    PB.append(dict(k=arena[:, fo_:fo_ + 512], v=arena[:, fo_ + 512:fo_ + 1024].rearrange("p (t c) -> p t c", c=128),
                                   q=arena[:, fo_ + 1024:fo_ + 1536], K=Res(f"pk{h_}"), V=Res(f"pv{h_}"), Q=Res(f"pq{h_}")))
                    fo_ += 1536
                obc4 = [arena[:, fo_ + i * 512:fo_ + (i + 1) * 512] for i in range(4)]
                OBC4 = [Res(f"obc4_{i}") for i in range(4)]
                fo_ += 2048
                assert fo_ <= ARENA, fo_

            def hbufs(h):
                if sample:
                    return dict(k=khb, v=vhb, q=qhbs[h % 2], K=KH, V=VH, Q=QHs[h % 2])
                return PB[h]

            def kv_tasks(h):
                hb = hbufs(h)
                khb, vhb, KH, VH = hb["k"], hb["v"], hb["K"], hb["V"]
                bi = h % 2
                vcol = 0 if bi == 0 else 64
                ocol = 64 if bi == 0 else 0
                tasks = []

                def t_init():
                    if not sample:
                        op("dve", lambda e: e.memset(khb[96:128, :], 0.0), writes=(KH,))
                    op("dve", lambda e: e.memset(vhb[:, :, ocol:ocol + 64], 1.0), writes=(VH,))
                    op("dve", lambda e: e.tensor_copy(out=khb[64:96, :], in_=krv[:, :]), reads=tuple(KR), writes=(KH,))
                tasks.append(t_init)
                for t6 in range(nt6):
                    def t_kv(t6=t6):
                        n = TS if t6 < nt else NCTX
                        k0 = t6 * TS
                        b0 = balloc()
                        op("pe", lambda e: e.matmul(banks[b0][:, 0:n], lhsT=g4[:, WK + h * 64:WK + h * 64 + 128],
                                                    rhs=ckvn[:, k0:k0 + n], start=True, stop=True),
                           reads=(G4, CKV[t6]), writes=(BANK[b0],))
                        op("dve", lambda e: e.tensor_copy(out=khb[0:64, k0:k0 + n], in_=banks[b0][0:64, 0:n]),
                           reads=(BANK[b0],), writes=(KH,))
                        bfree(b0)
                        b1 = balloc()
                        ntb = n // 128

                        def pe_v2(e):
                            ins = None
                            for tb in range(ntb):
                                ins = e.matmul(banks[b1][:, tb * 64:(tb + 1) * 64], lhsT=ckvn[:, k0 + tb * 128:k0 + (tb + 1) * 128],
                                               rhs=g4[:, WV + h * 64:WV + (h + 1) * 64], start=True, stop=True, skip_group_check=True)
                            return ins
                        op("pe", pe_v2, reads=(G4, CKV[t6]), writes=(BANK[b1],))
                        op("dve", lambda e: e.tensor_copy(
                            out=vhb[:, 4 * t6:4 * t6 + ntb, vcol:vcol + 64],
                            in_=banks[b1][:, 0:ntb * 64].rearrange("p (t c) -> p t c", c=64)),
                           reads=(BANK[b1],), writes=(VH,))
                        bfree(b1)
                    tasks.append(t_kv)
                return tasks

            def q_tasks(h):
                hb = hbufs(h)
                qhb, QH = hb["q"], hb["Q"]
                tasks = []
                for lt, t in enumerate(tiles):
                    def t_q(lt=lt, t=t):
                        tok = slice(lt * TS, (lt + 1) * TS)
                        b0 = balloc()

                        def pe_q(e):
                            e.matmul(banks[b0][:, :], lhsT=g4[:, WQ0 + h * 128:WQ0 + h * 128 + 128], rhs=cqn[:, 0, tok], start=True, stop=False)
                            return e.matmul(banks[b0][:, :], lhsT=g4[:, WQ1 + h * 128:WQ1 + h * 128 + 128], rhs=cqn[:, 1, tok], start=False, stop=True)
                        op("pe", pe_q, reads=(G4, CQ[lt], KR[lt]), writes=(BANK[b0],))
                        op("dve", lambda e: e.tensor_copy(out=qhb[0:64, tok], in_=banks[b0][0:64, :]),
                           reads=(BANK[b0],), writes=(QH,))
                        if sample:
                            gsl = slice(32 * (t - 1), 32 * (t - 1) + 32)
                            op("dve", lambda e: e.tensor_tensor(out=tmp[0][64:96, :], in0=banks[b0][64:96, :], in1=ropeB_all[gsl, 0, :], op=ALU.mult),
                               reads=(BANK[b0], ROPEB), writes=(TMP[0],))
                            op("dve", lambda e: e.tensor_tensor(out=tmp[1][64:96, :], in0=banks[b0][96:128, :], in1=ropeB_all[gsl, 1, :], op=ALU.mult),
                               reads=(BANK[b0], ROPEB), writes=(TMP[1],))
                            op("dve", lambda e: e.tensor_tensor(out=qhb[64:96, tok], in0=tmp[0][64:96, :], in1=tmp[1][64:96, :], op=ALU.add),
                               reads=(TMP[0], TMP[1]), writes=(QH,))
                        else:
                            op("dve", lambda e: e.tensor_copy(out=qhb[64:96, tok], in_=banks[b0][64:96, :]),
                               reads=(BANK[b0],), writes=(QH,))
                        bfree(b0)
                    tasks.append(t_q)
                return tasks

            def oproj_tasks(kcs, srcs, SRCS):
                tasks = []
                for lt, t in enumerate(tiles):
                    for j in range(8):
                        def t_o(lt=lt, t=t, j=j):
                            tok = slice(lt * TS, (lt + 1) * TS)
                            wt, WR = (wo0, WO0) if j < 4 else (wo1, WO1)
                            jj = j % 4
                            b0 = balloc()

                            def pe_o(e):
                                ins = None
                                for i, kc in enumerate(kcs):
                                    ins = e.matmul(banks[b0][:, :], lhsT=wt[:, kc * 512 + jj * 128:kc * 512 + jj * 128 + 128], rhs=srcs[i](tok),
                                                   start=(i == 0), stop=(i == len(kcs) - 1))
                                return ins
                            op("pe", pe_o, reads=(WR,) + tuple(SRCS(lt)), writes=(BANK[b0],))
                            resid_add(l, 0, j, t, b0)
                            bfree(b0)
                        tasks.append(t_o)
                return tasks

            reserve_side(True)
            trk.barrier()
            op("dve", lambda e: e.memset(arena[:, r3:r3 + 2 * NKg], 0.0), writes=(KZ,))
            op("dve", lambda e: e.tensor_copy(out=kaz[0][0:64, :], in_=ka[0:64, :]), reads=tuple(KA), writes=(KZ,))
            op("act", lambda e: e.activation(out=kaz[1][64:128, :], in_=ka[64:128, :], func=AF.Identity), reads=tuple(KA), writes=(KZ,))
            run_attn(jobs, 0.125, pop_side)
            side_o += [("A", f_) for f_ in oproj_tasks([0, 1, 2, 3], [(lambda tok, c=c: qa[:, c, tok]) for c in range(4)],
                                                       lambda lt: [QAH[c][hh][lt] for c in range(4) for hh in range(2)])]
            trk.barrier()
            if not sample:
                jobs = []
                for h in range(8):
                    hb = hbufs(h)
                    op("dve", lambda e, hb=hb: e.memset(hb["q"][96:128, :], 0.0), writes=(hb["Q"],))
                    flush(kv_tasks(h))
                    flush(q_tasks(h))
                ring_done(s4)
                s4_released[0] = True
                for h in range(8):
                    hb = hbufs(h)
                    bi = h % 2
                    base = 64 * bi
                    cB = h // 2
                    for s_i in range(2):
                        q0 = s_i * 256
                        groups = []
                        for kb in range(2):
                            k0 = s_i * 256 + kb * 128
                            groups.append(dict(k=hb["k"][:, k0:k0 + 128], q=hb["q"][:, q0:q0 + 256], K=hb["K"], Q=hb["Q"],
                                               v=hb["v"][:, k0 // 128, :], V=hb["V"], c0=0, n=256, mask=None))
                        jobs.append(dict(groups=groups, finish=(lambda ob, base=base, cB=cB, q0=q0: normalize(
                            ob, base, 256, None, obc4[cB][base:base + 64, q0:q0 + 256], (OBC4[cB],)))))
                run_attn(jobs, mscale, pop_side)
                side_o += [("B", f_) for f_ in oproj_tasks([4, 5, 6, 7], [(lambda tok, c=c: obc4[c][:, tok]) for c in range(4)],
                                                       lambda lt: list(OBC4))]
            armed[0] = True
            for h in (range(8) if sample else []):
                bi = h % 2
                base = 64 * bi
                cB = h // 2
                if bi == 0 and cB >= 2:
                    flush_tag(cB - 2)
                flush(kv_tasks(h))
                if h == 0:
                    op("dve", lambda e: e.memset(khb[96:128, :], 0.0), writes=(KH,))
                    for i in range(2):
                        op("dve", lambda e, i=i: e.memset(qhbs[i][96:128, :], 0.0), writes=(QHs[i],))
                    side_q += q_tasks(0)
                flush(side_q)
                if h + 1 < 8:
                    side_q += q_tasks(h + 1)
                else:
                    ring_done(s4)
                    s4_released[0] = True
                qhb, QH = qhbs[h % 2], QHs[h % 2]
                oc = obc[cB % 2]
                OC = OBC[cB % 2]
                jobs = []
                if not sample:
                    for s in range(2):
                        q0 = s * 256
                        groups = []
                        for kb in range(2):
                            k0 = s * 256 + kb * 128
                            groups.append(dict(k=khb[:, k0:k0 + 128], q=qhb[:, q0:q0 + 256], K=KH, Q=QH,
                                               v=vhb[:, k0 // 128, :], V=VH, c0=0, n=256, mask=None))
                        jobs.append(dict(groups=groups, finish=(lambda ob, base=base, oc=oc, OC=OC, q0=q0: normalize(
                            ob, base, 256, None, oc[base:base + 64, q0:q0 + 256], (OC,)))))
                else:
                    for T in range(4):
                        q0 = T * TS
                        groups = []
                        for kb in list(range(nlt, nkt)) + list(range(nlt)):
                            k0 = kb * 128
                            groups.append(dict(k=khb[:, k0:k0 + 128], q=qhb[:, q0:q0 + TS], K=KH, Q=QH,
                                               v=vhb[:, kb, :], V=VH, c0=0, n=TS, mask=None))
                        jobs.append(dict(groups=groups, finish=(lambda ob, base=base, oc=oc, OC=OC, q0=q0: normalize(
                            ob, base, TS, None, oc[base:base + 64, q0:q0 + TS], (OC,)))))
                run_attn(jobs, mscale, pop_side)
                if bi == 1:
                    side_o += [(cB, f_) for f_ in oproj_tasks([4 + cB], [(lambda tok, oc=oc: oc[:, tok])], lambda lt, OC=OC: [OC])]
            flush(side_q)
            flush(side_x)
            flush(side_o)
            for s_ in ((so0, so1) if s4_released[0] else (s4, so0, so1)):
                ring_done(s_)
            reserve_side(False)
            reserve_obanks(False)
            trk.barrier()

        def conv_layer(l):
            j_i = l // 2
            o = 0
            upad = arena[:, o:o + 4 * NPAD].rearrange("p (c n) -> p c n", c=4); o += 4 * NPAD
            zpad = arena[:, o:o + 4 * NPAD].rearrange("p (c n) -> p c n", c=4); o += 4 * NPAD
            hts = [arena[:, o + i * 4096:o + (i + 1) * 4096].rearrange("p (k n) -> p k n", k=8) for i in range(2)]
            o += 8192
            assert o <= ARENA, o
            HTs = [Res("ht0"), Res("ht1")]
            U = [Res(f"u{c}") for c in range(4)]
            Z = [Res(f"z{c}") for c in range(4)]
            for (so_, sn_), pb in zip(SEQS, PADB):
                for buf, RS in ((upad, U), (zpad, Z)):
                    op("dve", lambda e, buf=buf, pb=pb: e.memset(buf[:, :, pb:pb + 16], 0.0), writes=tuple(RS))
                    op("dve", lambda e, buf=buf, pb=pb, sn_=sn_: e.memset(buf[:, :, pb + 16 + sn_:pb + 32 + sn_], 0.0), writes=tuple(RS))
            ga, GA, sa = ring_get(("cin", j_i, 0))
            gg, GG, sg = ring_get(("cin", j_i, 1))
            gz, GZ, sz = ring_get(("cin", j_i, 2))
            load_cast(pw[:, :], d_poolw[j_i], 128, 512, (PW,))

            def segs_of_tile(t):
                if t == 0:
                    return [(0, 0, 256), (1, 256, 256)]
                return [(2, t * TS, TS)]

            norm_tile(l, 0, 0, hts[0], HTs[0])
            for t in range(NT):
                ht, HT = hts[t % 2], HTs[t % 2]
                ng = norm_gen(l, 0, t + 1, hts[(t + 1) % 2], HTs[(t + 1) % 2]) if t + 1 < NT else None
                for c in range(4):
                    if ng is not None:
                        for _ in range(2):
                            next(ng, None)
                    ba = balloc(); bg = balloc()
                    for (bk, wt, WR) in ((ba, ga, GA), (bg, gg, GG)):
                        def pe(e, bk=bk, wt=wt, c=c, ht=ht):
                            ins = None
                            for kc in range(8):
                                ins = e.matmul(banks[bk][:, :], lhsT=wt[:, kc * 512 + c * 128:kc * 512 + c * 128 + 128], rhs=ht[:, kc, :],
                                               start=(kc == 0), stop=(kc == 7))
                            return ins
                        op("pe", pe, reads=(WR, HT), writes=(BANK[bk],))
                    ti = c % 2
                    op("act", lambda e, bg=bg, ti=ti: e.activation(out=tmp[ti][:, :], in_=banks[bg][:, :], func=AF.Sigmoid), reads=(BANK[bg],), writes=(TMP[ti],))
                    for (s_, t0_, n) in segs_of_tile(t):
                        p0 = _padpos(t0_)
                        c0 = t0_ - t * TS
                        op("dve", lambda e, ba=ba, c=c, p0=p0, c0=c0, n=n, ti=ti: e.tensor_tensor(
                            out=upad[:, c, p0:p0 + n], in0=banks[ba][:, c0:c0 + n], in1=tmp[ti][:, c0:c0 + n], op=ALU.mult),
                           reads=(BANK[ba], TMP[ti]), writes=(U[c],))
                    bfree(ba); bfree(bg)
                    bz = balloc()

                    def pez(e, bz=bz, c=c, ht=ht):
                        ins = None
                        for kc in range(8):
                            ins = e.matmul(banks[bz][:, :], lhsT=gz[:, kc * 512 + c * 128:kc * 512 + c * 128 + 128], rhs=ht[:, kc, :],
                                           start=(kc == 0), stop=(kc == 7))
                        return ins
                    op("pe", pez, reads=(GZ, HT), writes=(BANK[bz],))
                    for (s_, t0_, n) in segs_of_tile(t):
                        p0 = _padpos(t0_)
                        c0 = t0_ - t * TS
                        op("act", lambda e, bz=bz, c=c, p0=p0, c0=c0, n=n: e.activation(out=zpad[:, c, p0:p0 + n], in_=banks[bz][:, c0:c0 + n], func=AF.Identity),
                           reads=(BANK[bz],), writes=(Z[c],))
                    bfree(bz)
                if ng is not None:
                    for _ in ng:
                        pass
            rst["limit"] = rst["got"] + 2
            for s_ in (sa, sg, sz):
                ring_done(s_)
            wo0, WO0, so0 = ring_get(("cout", j_i, 0))
            wo1, WO1, so1 = ring_get(("cout", j_i, 1))
            free_slots = [i for i in range(NRING) if not rst["held"][i]]
            assert len(free_slots) == 2, free_slots
            for i in free_slots:
                rst["held"][i] = True
            dgs = [ring[i] for i in free_slots]
            DGs = [RING[i] for i in free_slots]
            hcs, HCs = hts, [Res("hc0"), Res("hc1")]
            obdw = PM["bdw"][0] + j_i * 4
            olng = PM["lng"][0] + j_i * 4
            olnb = PM["lnb"][0] + j_i * 4
            odw = PM["dw"][0] + j_i * 4 * 31
            segs = []
            for t in range(NT):
                sl = segs_of_tile(t)
                for i, (s_, t0_, n) in enumerate(sl):
                    segs.append((s_, t0_, n, t, i == len(sl) - 1))
            dgi = [0]

            pending = []

            def build_dg(c):
                di = dgi[0] % 2
                dgi[0] += 1
                dg, DG = dgs[di], DGs[di]
                wc0 = odw + c * 31
                op("dve", lambda e: e.tensor_tensor(
                    out=dg[:, 0:31 * 128].rearrange("p (t m) -> p t m", t=31),
                    in0=ident[:, :].unsqueeze(1).broadcast_to([128, 31, 128]),
                    in1=params[:, wc0:wc0 + 31].unsqueeze(2).broadcast_to([128, 31, 128]), op=ALU.mult),
                   reads=(CONST, P_), writes=(DG,))
                return dg, DG

            def prebuild():
                pending.append(build_dg(0))
                pending.append(build_dg(1))

            def conv_taps(seg):
                s_, t0_, n, t, _ = seg
                p0 = _padpos(t0_)
                cb = [balloc() for _ in range(4)]
                for c in range(4):
                    dg, DG = pending.pop(0) if pending else build_dg(c)

                    def pe(e, c=c, dg=dg):
                        ins = None
                        for tap in range(31):
                            ins = e.matmul(banks[cb[c]][:, 0:n], lhsT=dg[:, tap * 128:(tap + 1) * 128],
                                           rhs=upad[:, c, p0 + tap - 15:p0 + tap - 15 + n], start=(tap == 0), stop=(tap == 30))
                        return ins
                    op("pe", pe, reads=(DG, U[c]), writes=(BANK[cb[c]],))
                return cb

            def evac(seg, cb):
                n = seg[2]
                for c in range(4):
                    op("act", lambda e, c=c: e.activation(out=cvt[c][:, 0:n], in_=banks[cb[c]][:, 0:n], func=AF.Identity,
                                                          bias=params[:, obdw + c:obdw + c + 1], scale=1.0),
                       reads=(BANK[cb[c]], P_), writes=(CVT[c],))
                    bfree(cb[c])

            def pool_seg(seg, hc, HC):
                s_, t0_, n, t, _ = seg
                p0 = _padpos(t0_)
                c0 = t0_ - t * TS
                seq_o, seq_n = SEQS[s_]
                at_start = (t0_ == seq_o)
                at_end = (t0_ + n == seq_o + seq_n)
                for gi, w in enumerate(POOL_W):
                    lo = w // 2
                    hi = w - lo - 1
                    bs_ = balloc(); bz = balloc()

                    def pe(e, gi=gi, lo=lo, hi=hi, bs_=bs_):
                        ins = None
                        for si, sft in enumerate(range(-lo, hi + 1)):
                            ins = e.matmul(banks[bs_][:, 0:n], lhsT=pw[:, gi * 128:(gi + 1) * 128], rhs=zpad[:, gi, p0 + sft:p0 + sft + n],
                                           start=(si == 0), stop=(sft == hi))
                        return ins
                    op("pe", pe, reads=(PW, Z[gi]), writes=(BANK[bs_],))
                    op("pe", lambda e, gi=gi, bz=bz: e.matmul(banks[bz][:, 0:n], lhsT=pw[:, gi * 128:(gi + 1) * 128], rhs=zpad[:, gi, p0:p0 + n], start=True, stop=True),
                       reads=(PW, Z[gi]), writes=(BANK[bz],))
                    pswc = o_psw + j_i * 4 + gi
                    npc = o_np + j_i * 4 + gi
                    op("act", lambda e, bs_=bs_, pswc=pswc: e.activation(out=tmp[2][:, 0:n], in_=banks[bs_][:, 0:n], func=AF.Identity, scale=der[:, pswc:pswc + 1]),
                       reads=(BANK[bs_], DER), writes=(TMP[2],))
                    if at_start and lo > 0:
                        op("dve", lambda e, gi=gi, lo=lo: e.tensor_tensor(out=tmp[2][:, 0:lo], in0=tmp[2][:, 0:lo], in1=pcorr[:, gi * 16:gi * 16 + lo], op=ALU.mult),
                           reads=(TMP[2], CONST), writes=(TMP[2],))
                    if at_end and hi > 0:
                        op("dve", lambda e, gi=gi, hi=hi: e.tensor_tensor(out=tmp[2][:, n - hi:n], in0=tmp[2][:, n - hi:n],
                                                                      in1=pcorr[:, gi * 16 + 16 - hi:gi * 16 + 16], op=ALU.mult),
                           reads=(TMP[2], CONST), writes=(TMP[2],))
                    op("dve", lambda e, gi=gi, bz=bz, npc=npc: e.scalar_tensor_tensor(out=hc[:, 4 + gi, c0:c0 + n], in0=banks[bz][:, 0:n], scalar=der[:, npc:npc + 1],
                                                                               in1=tmp[2][:, 0:n], op0=ALU.mult, op1=ALU.add),
                       reads=(BANK[bz], DER, TMP[2]), writes=(HC,))
                    bfree(bs_); bfree(bz)

            def ln_seg(seg, hc, HC):
                s_, t0_, n, t, _ = seg
                c0 = t0_ - t * TS
                bm = balloc(); bq = balloc()
                for c in range(4):
                    op("act", lambda e, c=c: e.activation(out=sq[0][:, 0:n], in_=cvt[c][:, 0:n], func=AF.Identity), reads=(CVT[c],), writes=(SQ[0],))
                    op("pe", lambda e, c=c: e.matmul(banks[bm][:, 0:n], lhsT=ones1[:, :], rhs=sq[0][:, 0:n], start=(c == 0), stop=(c == 3), skip_group_check=True),
                       reads=(SQ[0], CONST), writes=(BANK[bm],))
                    op("act", lambda e, c=c: e.activation(out=sq[1][:, 0:n], in_=cvt[c][:, 0:n], func=AF.Square), reads=(CVT[c],), writes=(SQ[1],))
                    op("pe", lambda e, c=c: e.matmul(banks[bq][:, 0:n], lhsT=ones1[:, :], rhs=sq[1][:, 0:n], start=(c == 0), stop=(c == 3), skip_group_check=True),
                       reads=(SQ[1], CONST), writes=(BANK[bq],))
                op("act", lambda e: e.activation(out=tmp[0][:, 0:n], in_=banks[bm][:, 0:n], func=AF.Identity, scale=1.0 / 512.0), reads=(BANK[bm],), writes=(TMP[0],))
                op("dve", lambda e: e.tensor_tensor(out=tmp[1][:, 0:n], in0=tmp[0][:, 0:n], in1=tmp[0][:, 0:n], op=ALU.mult), reads=(TMP[0],), writes=(TMP[1],))
                op("dve", lambda e: e.scalar_tensor_tensor(out=tmp[1][:, 0:n], in0=banks[bq][:, 0:n], scalar=1.0 / 512.0, in1=tmp[1][:, 0:n],
                                                           op0=ALU.mult, op1=ALU.subtract), reads=(BANK[bq], TMP[1]), writes=(TMP[1],))
                op("act", lambda e: e.activation(out=rstd[:, 0:n], in_=tmp[1][:, 0:n], func=AF.Ln, bias=epst[:, 0:1], scale=1.0), reads=(TMP[1], CONST), writes=(RSTD,))
                op("act", lambda e: e.activation(out=rstd[:, 0:n], in_=rstd[:, 0:n], func=AF.Exp, scale=-0.5), reads=(RSTD,), writes=(RSTD,))
                bfree(bm); bfree(bq)
                for c in range(4):
                    op("dve", lambda e, c=c: e.tensor_tensor(out=cvt[c][:, 0:n], in0=cvt[c][:, 0:n], in1=tmp[0][:, 0:n], op=ALU.subtract),
                       reads=(CVT[c], TMP[0]), writes=(CVT[c],))
                    op("dve", lambda e, c=c: e.tensor_tensor(out=cvt[c][:, 0:n], in0=cvt[c][:, 0:n], in1=rstd[:, 0:n], op=ALU.mult),
                       reads=(CVT[c], RSTD), writes=(CVT[c],))
                    op("act", lambda e, c=c: e.activation(out=hc[:, c, c0:c0 + n], in_=cvt[c][:, 0:n], func=AF.Silu,
                                                          bias=params[:, olnb + c:olnb + c + 1], scale=params[:, olng + c:olng + c + 1]),
                       reads=(CVT[c], P_), writes=(HC,))

            def outproj(t, hc, HC):
                for j in range(8):
                    wt, WR = (wo0, WO0) if j < 4 else (wo1, WO1)
                    jj = j % 4
                    b0 = balloc()

                    def pe_o(e, b0=b0, wt=wt, jj=jj):
                        ins = None
                        for kc in range(8):
                            ins = e.matmul(banks[b0][:, :], lhsT=wt[:, kc * 512 + jj * 128:kc * 512 + jj * 128 + 128], rhs=hc[:, kc, :],
                                           start=(kc == 0), stop=(kc == 7))
                        return ins
                    op("pe", pe_o, reads=(WR, HC), writes=(BANK[b0],))
                    resid_add(l, 0, j, t, b0)
                    bfree(b0)

            prebuild()
            cb = conv_taps(segs[0])
            prebuild()
            for i, seg in enumerate(segs):
                t = seg[3]
                hc, HC = hcs[t % 2], HCs[t % 2]
                evac(seg, cb)
                if i + 1 < len(segs):
                    cb = conv_taps(segs[i + 1])
                pool_seg(seg, hc, HC)
                ln_seg(seg, hc, HC)
                if i + 2 < len(segs):
                    prebuild()
                if seg[4]:
                    outproj(t, hc, HC)
            for i in free_slots:
                rst["held"][i] = False
            rst["limit"] = None
            ring_done(so0)
            ring_done(so1)
            trk.barrier()

        nsub = 2 * DEPTH if DEBUG_STOP < 0 else DEBUG_STOP
        for g in range(6):
            adaln_group(0, g)
        adaln_finish(0, halves=(0,))
        ada0_rest = [(lambda g=g: adaln_group(0, g)) for g in range(6, 12)] + [lambda: adaln_finish(0, halves=(1,))]
        sub = 0
        for l in range(DEPTH):
            if sub >= nsub:
                break
            if l % 2 == 0:
                attn_group(l, 0)
                attn_group(l, 1, extra=(ada0_rest if l == 0 else None))
            else:
                conv_layer(l)
            sub += 1
            if sub >= nsub:
                break
            nxt = [(l + 1, g) for g in range(12)] if l + 1 < DEPTH else []
            mlp(l, nxt)
            trk.barrier()
            sub += 1

        for t in range(NT):
            tok = slice(t * TS, (t + 1) * TS)
            if DEBUG_STOP >= 0:
                for k in range(8):
                    trk.dma("sp", out_sems.next(), o_yT[:, k, tok], x[:, k, tok], reads=(X[k][t],))
                continue
            b = balloc()
            for k in range(8):
                op("act", lambda e, k=k: e.activation(out=sq[k % 2][:, :], in_=x[:, k, tok], func=AF.Square), reads=(X[k][t],), writes=(SQ[k % 2],))
                op("pe", lambda e, k=k: e.matmul(banks[b][:, :], lhsT=onesm[:, :], rhs=sq[k % 2][:, :], start=(k == 0), stop=(k == 7), skip_group_check=True),
                   reads=(SQ[k % 2], CONST), writes=(BANK[b],))
            op("act", lambda e: e.activation(out=rstd[:, :], in_=banks[b][:, :], func=AF.Ln, bias=epst[:, 0:1], scale=1.0), reads=(BANK[b], CONST), writes=(RSTD,))
            bfree(b)
            op("act", lambda e: e.activation(out=rstd[:, :], in_=rstd[:, :], func=AF.Exp, scale=-0.5), reads=(RSTD,), writes=(RSTD,))
            fo_ = PM["finalg"][0]
            for k in range(8):
                ci = k % 4
                op("dve", lambda e, k=k, ci=ci: e.scalar_tensor_tensor(out=cvt[ci][:, :], in0=x[:, k, tok], scalar=params[:, fo_ + k:fo_ + k + 1],
                                                                     in1=rstd[:, :], op0=ALU.mult, op1=ALU.mult),
                   reads=(X[k][t], RSTD, P_), writes=(CVT[ci],))
                trk.dma("sp", out_sems.next(), o_yT[:, k, tok], cvt[ci][:, :], reads=(CVT[ci],))
        trk.final_wait("sp")
    return nc, wlist


def _count_images():
    n = 0
    for l in range(DEPTH):
        n += 12
        n += 14 if l % 2 == 0 else 5
        n += 16
    return n


N_IMAGES = _count_images()


def _img_k1024(w, col_idx):
    img = np.zeros((128, 8, 512), np.float32)
    col_idx = np.asarray(col_idx)
    valid = col_idx >= 0
    sel = w[:, col_idx[valid]]
    img[:, :, np.nonzero(valid)[0]] = sel.reshape(8, 128, -1).transpose(1, 0, 2)
    return img.reshape(128, SLOT)


def _build_images(wl, inp):
    imgs = np.zeros((N_IMAGES, 128, SLOT), np.float32)
    sw64 = lambda d: (d + 32) % 64
    for n, key in enumerate(wl):
        kind = key[0]
        if kind == "wmod":
            _, l, g = key
            imgs[n] = _img_k1024(inp["w_mod"][l], np.arange(g * 512, (g + 1) * 512))
        elif kind in ("w1",):
            _, l, g = key
            imgs[n] = _img_k1024(inp["mlp_w1"][l], np.arange(g * 512, (g + 1) * 512))
        elif kind == "w2":
            _, l, g = key
            w = inp["mlp_w2"][l][g * 512:(g + 1) * 512]
            imgs[n] = w.reshape(4, 128, 1024).transpose(1, 0, 2).reshape(128, SLOT)
        elif kind == "win":
            _, e, g = key
            w = inp["attn_w_in"][e]
            if g == 0:
                idx = -np.ones(512, np.int64)
                idx[0:192] = 768 + np.arange(192)
                idx[192:320] = 960 + np.arange(128)
                idx[320:352] = 1088 + np.arange(32)
                idx[352:384] = 1088 + (np.arange(32) + 16) % 32
            elif g in (1, 2):
                idx = np.zeros(512, np.int64)
                for c in range(4):
                    for p in range(128):
                        h = c if p < 64 else 4 + c
                        d = p % 64
                        if g == 2:
                            d = sw64(d)
                        idx[c * 128 + p] = h * 64 + d
            else:
                idx = -np.ones(512, np.int64)
                for p in range(128):
                    kh, d = p // 64, p % 64
                    idx[p] = 512 + kh * 64 + d
                    idx[128 + p] = 512 + kh * 64 + sw64(d)
                    idx[256 + p] = 640 + p
            imgs[n] = _img_k1024(w, idx)
        elif kind == "wqkv":
            _, e = key
            img = np.zeros((128, SLOT), np.float32)
            wq = inp["mla_w_qb"][e]
            qcols = np.zeros((8, 128), np.int64)
            for h in range(8):
                qcols[h, 0:64] = h * 96 + np.arange(64)
                qcols[h, 64:96] = h * 96 + 64 + np.arange(32)
                qcols[h, 96:128] = h * 96 + 64 + (np.arange(32) + 16) % 32
            wqa = wq[:, qcols.reshape(-1)]
            img[:, 0:1024] = wqa[0:128]
            img[0:64, 1024:2048] = wqa[128:192]
            wkv = inp["mla_w_kvb"][e]
            for h in range(8):
                img[:, 2048 + h * 64:2048 + (h + 1) * 64] = wkv[:, h * 128:h * 128 + 64]
                img[:, 2560 + h * 64:2560 + (h + 1) * 64] = wkv[:, h * 128 + 64:h * 128 + 128]
            imgs[n] = img
        elif kind == "wout":
            _, e, g = key
            w = inp["attn_w_out"][e]
            rows = np.zeros(1024, np.int64)
            for c in range(4):
                for p in range(128):
                    h = c if p < 64 else 4 + c
                    rows[c * 128 + p] = h * 64 + p % 64
            for c in range(4):
                for p in range(128):
                    h = 2 * c + (p // 64)
                    rows[512 + c * 128 + p] = 512 + h * 64 + p % 64
            wp = w[rows]
            imgs[n] = _img_k1024(wp, np.arange(g * 512, (g + 1) * 512))
        elif kind == "cin":
            _, j, g = key
            imgs[n] = _img_k1024(inp["conv_w_in"][j], np.arange(g * 512, (g + 1) * 512))
        elif kind == "cout":
            _, j, g = key
            imgs[n] = _img_k1024(inp["conv_w_out"][j], np.arange(g * 512, (g + 1) * 512))
        else:
            raise KeyError(key)
    return imgs


_CACHE = {}


def kernel(**inputs):
    inp = {k: np.asarray(v) for k, v in inputs.items()}
    if "prog" not in _CACHE:
        _CACHE["prog"] = build_program()
    nc, wl = _CACHE["prog"]
    assert len(wl) == N_IMAGES or DEBUG_STOP >= 0, (len(wl), N_IMAGES)
    consts = _consts()
    imgs = _build_images(wl, inp)

    def fm(v):
        return np.ascontiguousarray(v.reshape(8, 128).T)

    poolw = np.ascontiguousarray(inp["pool_w"].transpose(0, 2, 1, 3).reshape(2, 128, 512))
    dwp = np.zeros((128, 2, 4, 31), np.float32)
    for j in range(2):
        for c in range(4):
            dwp[:, j, c, :] = inp["conv_dw"][j][:, c * 128:(c + 1) * 128].T
    in_maps = []
    for i in range(8):
        toks = np.concatenate([inp["x_prompt"][2 * i], inp["x_prompt"][2 * i + 1], inp["x_sample"][i]], axis=0)
        xT = np.ascontiguousarray(toks.reshape(NTOK, 8, 128).transpose(2, 1, 0))
        P = np.zeros((128, PM["_n"]), np.float32)

        def put(name, arr):
            o, n = PM[name]
            P[:, o:o + n] = arr.reshape(128, n)
        put("bmod", np.stack([inp["b_mod"][l].reshape(48, 128).T for l in range(4)], 1))
        put("normg", np.stack([np.stack([fm(inp["norm_g"][l, w]) for w in range(2)], 1) for l in range(4)], 1))
        put("finalg", fm(inp["final_g"]))
        put("cT", np.stack([fm(inp["c_ctx"]), fm(inp["c"][i])], 2))
        qn = np.zeros((128, 2, 2), np.float32)
        for e in range(2):
            qn[:, e, 0] = inp["mla_q_norm"][e, 0:128]
            qn[0:64, e, 1] = inp["mla_q_norm"][e, 128:192]
        put("qnorm", qn)
        put("kvnorm", np.stack([inp["mla_kv_norm"][e] for e in range(2)], 1))
        put("sink", np.broadcast_to(inp["attn_sink"].reshape(1, 16), (128, 16)))
        for nm, src in (("bdw", "conv_dw_b"), ("lng", "conv_ln_g"), ("lnb", "conv_ln_b"), ("pscale", "pool_scale")):
            put(nm, np.stack([inp[src][j].reshape(4, 128).T for j in range(2)], 1))
        put("dw", dwp)
        m = {
            "xT": xT, "params": P, "wstream": imgs,
            "ropeA": consts["ropeA"], "ropeB": consts["ropeB"], "maskb": consts["maskb"], "ident": consts["ident"],
            "pcorr": consts["pcorr"],
            "ckT": np.ascontiguousarray(inp["cache_win_k"][i].reshape(2, NCTX, 128).transpose(0, 2, 1)),
            "cv": np.ascontiguousarray(inp["cache_win_v"][i].reshape(2, 2, 128, 128)),
            "cckvT": np.ascontiguousarray(inp["cache_mla_ckv"][i].transpose(0, 2, 1)),
            "ckrT": np.ascontiguousarray(inp["cache_mla_krope"][i].transpose(0, 2, 1)),
            "poolw": poolw,
        }
        in_maps.append(m)
    res = run_bass_kernel_spmd(nc, in_maps, core_ids=list(range(8)))
    R = res.results
    y_prompt = np.zeros((16, 256, D), np.float32)
    y_sample = np.zeros((8, 2048, D), np.float32)
    nk = np.zeros((16, 2, 256, 2, 64), np.float32)
    nv = np.zeros((16, 2, 256, 2, 64), np.float32)
    nckv = np.zeros((16, 2, 256, 128), np.float32)
    nkr = np.zeros((16, 2, 256, 32), np.float32)
    for i in range(8):
        r = R[i]
        y = np.asarray(r["yT"]).transpose(2, 1, 0).reshape(NTOK, D)
        y_prompt[2 * i] = y[0:256]
        y_prompt[2 * i + 1] = y[256:512]
        y_sample[i] = y[512:]
        kT = np.asarray(r["okT"])
        v = np.asarray(r["ov"])
        ck = np.asarray(r["ockvT"])
        kr = np.asarray(r["okrT"])
        for s in range(2):
            b = 2 * i + s
            for e in range(2):
                nk[b, e] = kT[e][:, s * 256:(s + 1) * 256].T.reshape(256, 2, 64)
                nv[b, e] = v[e][s * 256:(s + 1) * 256].reshape(256, 2, 64)
                nckv[b, e] = ck[e][:, s * 256:(s + 1) * 256].T
                nkr[b, e] = kr[e][:, s * 256:(s + 1) * 256].T
    return (y_prompt, y_sample, nk, nv, nckv, nkr)
```

```python
import contextlib
import os
import numpy as np
import concourse.bass as bass
import concourse.mybir as mybir
from concourse.bass_utils import run_bass_kernel_spmd

F32 = mybir.dt.float32
BF16 = mybir.dt.bfloat16
AF = mybir.ActivationFunctionType
ALU = mybir.AluOpType

D = 1024
DEPTH = 4
NTOK = 2560
NT = 5
TS = 512
NCTX = 256
EPS = 1e-6
NEG = -30000.0
SHIFT = 10.0
SLOT = 4096
NRING = 4
SEM_LIMIT = 30000

DEBUG_STOP = int(os.environ.get("KDEBUG_STOP", "-1"))


class Res:
    __slots__ = ("name", "w", "rd")

    def __init__(self, name):
        self.name = name
        self.w = None
        self.rd = {}


class Eng:
    def __init__(self, trk, name, handle, inc):
        self.trk = trk
        self.name = name
        self.h = handle
        self.inc = inc
        self.sem = None
        self.count = 0
        self.seen = {}
        self.sems = []

    def new_sem(self):
        self.sem = self.trk.alloc_sem(self.name)
        self.sems.append(self.sem)
        self.count = 0


class Tracker:
    def __init__(self, nc, es):
        self.nc = nc
        self.es = es
        self.nsem = 0
        self.E = {}
        for name, h, inc in (("pe", nc.tensor, 1), ("act", nc.scalar, 1), ("dve", nc.vector, 1),
                             ("pool", nc.gpsimd, 1)):
            e = Eng(self, name, h, inc)
            e.new_sem()
            self.E[name] = e
        self.Q = {"sp": Eng(self, "sp", nc.sync, 16), "poolq": self.E["pool"]}
        self.all_events = {}

    def alloc_sem(self, name):
        self.nsem += 1
        return self.es.enter_context(self.nc.semaphore(f"s_{name}_{self.nsem}"))

    def _wait(self, eng, ev):
        if ev is None:
            return
        sem, val, src = ev
        if src == "pe" and eng.name == "pe":
            return
        k = id(sem)
        if eng.seen.get(k, 0) >= val:
            return
        eng.seen[k] = val
        eng.h.wait_ge(sem, val)

    def _deps(self, eng, reads, writes, same_ok=False):
        for r in reads:
            if r.w is not None:
                self._wait(eng, r.w)
        for r in writes:
            if r.w is not None:
                self._wait(eng, r.w)
            for ev in r.rd.values():
                self._wait(eng, ev)

    def _record(self, ev, reads, writes):
        for r in reads:
            r.rd[id(ev[0])] = ev
        for r in writes:
            r.w = ev
            r.rd = {}
        self.all_events[id(ev[0])] = ev

    def op(self, ename, fn, reads=(), writes=()):
        eng = self.E[ename]
        if eng.count >= SEM_LIMIT:
            eng.new_sem()
        self._deps(eng, reads, writes)
        ins = fn(eng.h)
        eng.count += 1
        ins.then_inc(eng.sem, 1)
        ev = (eng.sem, eng.count, ename)
        self._record(ev, reads, writes)
        return ev

    def dma(self, qname, dsem, out, in_, reads=(), writes=()):
        q = self.Q[qname]
        self._deps(q, reads, writes)
        if dsem.last is not None:
            self._wait(q, dsem.last)
        if dsem.count + 16 > SEM_LIMIT:
            dsem.sem = self.alloc_sem("dma")
            dsem.count = 0
        ins = q.h.dma_start(out=out, in_=in_)
        dsem.count += 16
        ins.then_inc(dsem.sem, 16)
        ev = (dsem.sem, dsem.count, "dma")
        dsem.last = ev
        self._record(ev, reads, writes)
        return ev

    def barrier(self):
        evs = list(self.all_events.values())
        for e in list(self.E.values()) + [self.Q["sp"]]:
            for ev in evs:
                self._wait(e, ev)

    def final_wait(self, ename="sp"):
        q = self.Q[ename]
        for ev in list(self.all_events.values()):
            self._wait(q, ev)


class DmaSem:
    def __init__(self, trk, name):
        self.sem = trk.alloc_sem(name)
        self.count = 0
        self.last = None


class DmaSemPool:
    def __init__(self, trk, n, name):
        self.s = [DmaSem(trk, f"{name}{i}") for i in range(n)]
        self.i = 0

    def next(self):
        s = self.s[self.i % len(self.s)]
        self.i += 1
        return s


def _rope_tables(n, dim, grid_w=64, base=10000.0):
    rows = n // grid_w
    row = np.repeat(np.arange(rows), grid_w).astype(np.float32)
    col = np.tile(np.arange(grid_w), rows).astype(np.float32)
    quarter = dim // 4
    inv_freq = (base ** (-np.arange(quarter, dtype=np.float32) / quarter)).astype(np.float32)
    ang = np.concatenate([row[:, None] * inv_freq, col[:, None] * inv_freq], axis=-1).astype(np.float32)
    cos = np.cos(ang).astype(np.float32)
    sin = np.sin(ang).astype(np.float32)
    half = dim // 2
    cos2 = np.concatenate([cos, cos], axis=1).T
    sins = np.concatenate([-sin, sin], axis=1).T
    return np.ascontiguousarray(cos2), np.ascontiguousarray(sins)


def _consts():
    c = {}
    ca, sa = _rope_tables(2048, 64)
    c["ropeA"] = np.ascontiguousarray(np.stack([np.concatenate([ca, ca], 0), np.concatenate([sa, sa], 0)], 1))
    cb, sb = _rope_tables(2048, 32)
    c["ropeB"] = np.ascontiguousarray(np.stack([cb, sb], 1))
    b = np.arange(128)[:, None]
    a = np.arange(128)[None, :]
    m = np.zeros((128, 384), np.float32)
    m[:, 0:128] = np.where(b <= a, 0.0, NEG)
    m[:, 256:384] = np.where(a <= b, 0.0, NEG)
    c["maskb"] = m
    c["ident"] = np.eye(128, dtype=np.float32)
    corr = np.ones((128, 4, 2, 8), np.float32)
    for gi, w in enumerate((2, 4, 8, 16)):
        lo = w // 2
        hi = w - lo - 1
        for t in range(lo):
            corr[:, gi, 0, t] = w / float(t + hi + 1)
        for q in range(hi):
            corr[:, gi, 1, 7 - q] = w / float(lo + q + 1)
    c["pcorr"] = corr.reshape(128, 64)
    return c


def _pmap():
    m = {}
    o = 0

    def add(name, n):
        nonlocal o
        m[name] = (o, n)
        o += n
    add("bmod", 4 * 48)
    add("normg", 4 * 2 * 8)
    add("finalg", 8)
    add("cT", 16)
    add("qnorm", 4)
    add("kvnorm", 2)
    add("sink", 16)
    add("bdw", 8)
    add("lng", 8)
    add("lnb", 8)
    add("pscale", 8)
    add("dw", 2 * 4 * 31)
    m["_n"] = o
    return m


PM = _pmap()

POOL_W = (2, 4, 8, 16)
SEQS = [(0, 256), (256, 256), (512, 2048)]
PADB = [0, 288, 576]
NPAD = 2656


def _padpos(tok):
    for s, (o, n) in enumerate(SEQS):
        if o <= tok < o + n:
            return PADB[s] + 16 + (tok - o)
    raise ValueError


ARENA = 29568


def build_program():
    nc = bass.Bass("TRN2", target_bir_lowering=False)
    es = contextlib.ExitStack()
    wlist = []

    def dram(name, shape, kind="ExternalInput", dt=F32):
        return nc.dram_tensor(name, list(shape), dt, kind=kind).ap()

    d_xT = dram("xT", [128, 8, NTOK])
    d_params = dram("params", [128, PM["_n"]])
    d_ropeA = dram("ropeA", [128, 2, 2048])
    d_ropeB = dram("ropeB", [32, 2, 2048])
    d_maskb = dram("maskb", [128, 384])
    d_ident = dram("ident", [128, 128])
    d_pcorr = dram("pcorr", [128, 64])
    d_ckT = dram("ckT", [2, 128, NCTX])
    d_cv = dram("cv", [2, 2, 128, 128])
    d_cckvT = dram("cckvT", [2, 128, NCTX])
    d_ckrT = dram("ckrT", [2, 32, NCTX])
    d_poolw = dram("poolw", [2, 128, 4 * 128])
    d_w = dram("wstream", [N_IMAGES, 128, SLOT])
    o_yT = dram("yT", [128, 8, NTOK], kind="ExternalOutput")
    o_kT = dram("okT", [2, 128, 512], kind="ExternalOutput")
    o_v = dram("ov", [2, 512, 128], kind="ExternalOutput")
    o_ckvT = dram("ockvT", [2, 128, 512], kind="ExternalOutput")
    o_krT = dram("okrT", [2, 32, 512], kind="ExternalOutput")

    with es:
        trk = Tracker(nc, es)
        op = trk.op

        def sb(name, shape, dt):
            return es.enter_context(nc.sbuf_tensor(name, list(shape), dt))

        x = sb("x", [128, 8, NTOK], F32)
        X = [[Res(f"x{k}_{t}") for t in range(NT)] for k in range(8)]
        ring = [sb(f"ring{i}", [128, SLOT], BF16) for i in range(NRING)]
        RING = [Res(f"ring{i}") for i in range(NRING)]
        ring_sem = [DmaSem(trk, f"ring{i}") for i in range(NRING)]
        arena = sb("arena", [128, ARENA], BF16)
        params = sb("params_sb", [128, PM["_n"]], F32)
        P_ = Res("params")
        NDER = 4 * 6 * 8 * 2 + 64
        der = sb("der", [128, NDER], F32)
        DER = Res("der")
        mod = sb("mod", [128, 4, 48, 2], F32)
        MOD = [Res(f"mod{l}") for l in range(4)]
        scT = sb("scT", [128, 8, 2], BF16)
        SCT = Res("scT")
        ident = sb("ident_sb", [128, 128], BF16)
        onesm = sb("onesm", [128, 128], BF16)
        ones1 = sb("ones1", [128, 128], BF16)
        ones_lo = sb("ones_lo", [128, 128], BF16)
        maskb = sb("maskb_sb", [128, 384], BF16)
        pcorr = sb("pcorr_sb", [128, 64], F32)
        epst = sb("epst", [128, 1], F32)
        negc = sb("negc", [128, 1], F32)
        pw = sb("poolw_sb", [128, 512], BF16)
        PW = Res("pw")
        CONST = Res("const")
        rstd = sb("rstd", [128, 512], F32)
        RSTD = Res("rstd")
        sq = [sb(f"sq{i}", [128, 512], BF16) for i in range(2)]
        SQ = [Res(f"sq{i}") for i in range(2)]
        ft = [sb(f"ft{i}", [128, 512], F32) for i in range(7)]
        FT = [Res(f"ft{i}") for i in range(7)]
        tmp, TMP = ft[0:3], FT[0:3]
        cvt, CVT = ft[3:7], FT[3:7]
        rec, REC = ft[3], FT[3]
        ostage, OST = ft[4:6], FT[4:6]
        stage, STAGE = ft[6], FT[6]
        pt = [sb(f"pt{i}", [128, 512], BF16) for i in range(3)]
        PT = [Res(f"pt{i}") for i in range(3)]
        rope_t = sb("rope_t", [128, 2, TS], F32)
        ROPE = Res("rope")
        ropeB_all = sb("ropeB_all", [128, 2, TS], F32)
        ROPEB = Res("ropeB")
        ost_i = [0]

        banks = [es.enter_context(nc.psum_tensor(f"bank{i}", [128, 512], F32)) for i in range(8)]
        BANK = [Res(f"bank{i}") for i in range(8)]
        bank_free = list(range(8))

        side_free = []
        pool_sel = ["g"]

        def balloc():
            if pool_sel[0] == "side":
                assert side_free, "out of side PSUM banks"
                return side_free.pop(0)
            assert bank_free, "out of PSUM banks"
            return bank_free.pop(0)

        def bfree(b):
            if b in SIDE_POOL and pool_sel[0] == "side":
                side_free.append(b)
            else:
                bank_free.append(b)

        SIDE_POOL = [4, 5]

        def reserve_side(on):
            if on:
                for b in SIDE_POOL:
                    bank_free.remove(b)
                    side_free.append(b)
            else:
                for b in SIDE_POOL:
                    side_free.remove(b)
                    bank_free.append(b)

        sp_sems = DmaSemPool(trk, 6, "sp")
        out_sems = DmaSemPool(trk, 4, "out")

        rst = {"issued": 0, "got": 0, "held": [False] * NRING, "limit": None}

        def ring_pump():
            while rst["issued"] < N_IMAGES and rst["issued"] < rst["got"] + NRING:
                if rst["limit"] is not None and rst["issued"] >= rst["limit"]:
                    break
                i = rst["issued"]
                free = [k for k in range(NRING) if not rst["held"][k]]
                if not free:
                    break
                s_ = i % NRING if (i % NRING) in free else free[0]
                trk.dma("poolq", ring_sem[s_], ring[s_][:, :], d_w[i], reads=(), writes=(RING[s_],))
                rst["held"][s_] = True
                rst.setdefault("slot_of", {})[i] = s_
                rst["issued"] += 1

        def ring_get(key):
            i = rst["got"]
            wlist.append(key)
            ring_pump()
            assert rst["issued"] > i, ("ring stalled", key)
            rst["got"] += 1
            s_ = rst["slot_of"][i]
            return ring[s_], RING[s_], s_

        def ring_done(s_):
            rst["held"][s_] = False
            ring_pump()

        trk.dma("sp", sp_sems.next(), params[:, :], d_params[:, :], writes=(P_,))
        for k in range(8):
            trk.dma("sp", sp_sems.next(), x[:, k, :], d_xT[:, k, :], writes=tuple(X[k]))

        def load_cast(dst_ap, src_ap, nparts, ncols, wres):
            trk.dma("sp", sp_sems.next(), stage[0:nparts, 0:ncols], src_ap, writes=(STAGE,))
            op("dve", lambda e: e.tensor_copy(out=dst_ap, in_=stage[0:nparts, 0:ncols]), reads=(STAGE,), writes=wres)

        load_cast(ident[:, :], d_ident[:, :], 128, 128, (CONST,))
        load_cast(maskb[:, :], d_maskb[:, :], 128, 384, (CONST,))
        trk.dma("sp", sp_sems.next(), pcorr[:, :], d_pcorr[:, :], writes=(CONST,))
        op("dve", lambda e: e.memset(onesm[:, :], 1.0 / 1024.0), writes=(CONST,))
        op("dve", lambda e: e.memset(ones1[:, :], 1.0), writes=(CONST,))
        op("dve", lambda e: e.memset(ones_lo[:, :], 0.0), writes=(CONST,))
        op("dve", lambda e: e.memset(ones_lo[0:64, :], 1.0), writes=(CONST,))
        op("dve", lambda e: e.memset(epst[:, :], EPS), writes=(CONST,))
        op("dve", lambda e: e.memset(negc[:, :], -SHIFT), writes=(CONST,))

        o_cT = PM["cT"][0]
        op("act", lambda e: e.activation(out=scT[:, :, :], in_=params[:, o_cT:o_cT + 16].rearrange("p (k j) -> p k j", j=2),
                                         func=AF.Silu), reads=(P_,), writes=(SCT,))

        def dcol(l, which, k, g):
            return ((l * 6 + which) * 8 + k) * 2 + g

        o_es = 4 * 6 * 8 * 2
        o_np = o_es + 16
        o_psw = o_np + 8
        o_sink = PM["sink"][0]
        op("act", lambda e: e.activation(out=der[:, o_es:o_es + 16], in_=params[:, o_sink:o_sink + 16], func=AF.Exp, bias=negc[:, 0:1], scale=1.0),
           reads=(P_, CONST), writes=(DER,))
        o_ps = PM["pscale"][0]
        op("dve", lambda e: e.tensor_scalar(out=der[:, o_np:o_np + 8], in0=params[:, o_ps:o_ps + 8], scalar1=-1.0,
                                            scalar2=None, op0=ALU.mult), reads=(P_,), writes=(DER,))
        for j in range(2):
            for gi, w in enumerate(POOL_W):
                op("dve", lambda e, j=j, gi=gi, w=w: e.tensor_scalar(
                    out=der[:, o_psw + j * 4 + gi:o_psw + j * 4 + gi + 1],
                    in0=params[:, o_ps + j * 4 + gi:o_ps + j * 4 + gi + 1], scalar1=1.0 / w, scalar2=None,
                    op0=ALU.mult), reads=(P_,), writes=(DER,))

        def adaln_group(l, g):
            wt, WR, ws = ring_get(("wmod", l, g))
            b = balloc()

            def pe(e):
                ins = None
                for cc in range(4):
                    for kc in range(8):
                        ins = e.matmul(banks[b][:, cc * 2:cc * 2 + 2], lhsT=wt[:, kc * 512 + cc * 128:kc * 512 + cc * 128 + 128],
                                       rhs=scT[:, kc, :], start=(kc == 0), stop=(kc == 7), skip_group_check=True)
                return ins
            op("pe", pe, reads=(WR, SCT), writes=(BANK[b],))
            ring_done(ws)
            ob = PM["bmod"][0] + l * 48 + 4 * g
            for j in range(2):
                op("dve", lambda e, j=j: e.tensor_tensor(
                    out=mod[:, l, 4 * g:4 * g + 4, j],
                    in0=banks[b][:, 0:8].rearrange("p (c j) -> p c j", j=2)[:, :, j],
                    in1=params[:, ob:ob + 4], op=ALU.add), reads=(BANK[b], P_), writes=(MOD[l],))
            bfree(b)

        def adaln_finish(l, halves=(0, 1)):
            og = PM["normg"][0]
            for half in halves:
                for k in range(8):
                    gc = og + (l * 2 + half) * 8 + k
                    a0 = dcol(l, half * 3 + 0, k, 0)
                    op("dve", lambda e, k=k, half=half, gc=gc, a0=a0: e.tensor_scalar(
                        out=der[:, a0:a0 + 2], in0=mod[:, l, (half * 3 + 1) * 8 + k, :], scalar1=1.0, scalar2=params[:, gc:gc + 1],
                        op0=ALU.add, op1=ALU.mult), reads=(MOD[l], P_), writes=(DER,))
                    b0 = dcol(l, half * 3 + 1, k, 0)
                    op("dve", lambda e, k=k, half=half, b0=b0: e.tensor_copy(
                        out=der[:, b0:b0 + 2], in_=mod[:, l, (half * 3 + 0) * 8 + k, :]), reads=(MOD[l],), writes=(DER,))
                    g0 = dcol(l, half * 3 + 2, k, 0)
                    op("dve", lambda e, k=k, half=half, g0=g0: e.tensor_copy(
                        out=der[:, g0:g0 + 2], in_=mod[:, l, (half * 3 + 2) * 8 + k, :]), reads=(MOD[l],), writes=(DER,))

        def dv(l, which, k, g):
            c = dcol(l, which, k, g)
            return der[:, c:c + 1]

        def norm_tile(l, half, t, dest, DEST):
            grp = 0 if t == 0 else 1
            tok = slice(t * TS, (t + 1) * TS)
            b = balloc()
            for k in range(8):
                op("act", lambda e, k=k: e.activation(out=sq[k % 2][:, :], in_=x[:, k, tok], func=AF.Square),
                   reads=(X[k][t],), writes=(SQ[k % 2],))
                op("pe", lambda e, k=k: e.matmul(banks[b][:, :], lhsT=onesm[:, :], rhs=sq[k % 2][:, :], start=(k == 0),
                                                 stop=(k == 7), skip_group_check=True),
                   reads=(SQ[k % 2], CONST), writes=(BANK[b],))
            op("act", lambda e: e.activation(out=rstd[:, :], in_=banks[b][:, :], func=AF.Ln, bias=epst[:, 0:1], scale=1.0),
               reads=(BANK[b], CONST), writes=(RSTD,))
            bfree(b)
            op("act", lambda e: e.activation(out=rstd[:, :], in_=rstd[:, :], func=AF.Exp, scale=-0.5), reads=(RSTD,), writes=(RSTD,))
            for k in range(8):
                ti = k % 2
                op("dve", lambda e, k=k, ti=ti: e.tensor_tensor(out=tmp[ti][:, :], in0=x[:, k, tok], in1=rstd[:, :], op=ALU.mult),
                   reads=(X[k][t], RSTD), writes=(TMP[ti],))
                op("act", lambda e, k=k, ti=ti: e.activation(out=dest[:, k, :], in_=tmp[ti][:, :], func=AF.Identity,
                                                           bias=dv(l, half * 3 + 1, k, grp), scale=dv(l, half * 3 + 0, k, grp)),
                   reads=(TMP[ti], DER), writes=(DEST,))

        def norm_gen(l, half, t, dest, DEST):
            grp = 0 if t == 0 else 1
            tokx = slice(t * TS, (t + 1) * TS)
            b = balloc()

            def mm_(k):
                op("pe", lambda e: e.matmul(banks[b][:, :], lhsT=onesm[:, :], rhs=sq[k % 2][:, :], start=(k == 0),
                                            stop=(k == 7), skip_group_check=True),
                   reads=(SQ[k % 2], CONST), writes=(BANK[b],))
            for p_ in range(5):
                if p_ >= 1:
                    mm_(2 * p_ - 2)
                    mm_(2 * p_ - 1)
                if p_ < 4:
                    for k in (2 * p_, 2 * p_ + 1):
                        op("act", lambda e, k=k: e.activation(out=sq[k % 2][:, :], in_=x[:, k, tokx], func=AF.Square),
                           reads=(X[k][t],), writes=(SQ[k % 2],))
                    yield
            op("act", lambda e: e.activation(out=rstd[:, :], in_=banks[b][:, :], func=AF.Ln, bias=epst[:, 0:1], scale=1.0),
               reads=(BANK[b], CONST), writes=(RSTD,))
            bfree(b)
            op("act", lambda e: e.activation(out=rstd[:, :], in_=rstd[:, :], func=AF.Exp, scale=-0.5), reads=(RSTD,), writes=(RSTD,))
            yield
            for k in range(8):
                ti = k % 2
                op("dve", lambda e, k=k, ti=ti: e.tensor_tensor(out=tmp[ti][:, :], in0=x[:, k, tokx], in1=rstd[:, :], op=ALU.mult),
                   reads=(X[k][t], RSTD), writes=(TMP[ti],))
                op("act", lambda e, k=k, ti=ti: e.activation(out=dest[:, k, :], in_=tmp[ti][:, :], func=AF.Identity,
                                                           bias=dv(l, half * 3 + 1, k, grp), scale=dv(l, half * 3 + 0, k, grp)),
                   reads=(TMP[ti], DER), writes=(DEST,))
                if k % 4 == 3:
                    yield

        def drive(gens):
            gens = [g_ for g_ in gens if g_ is not None]
            while gens:
                for g_ in list(gens):
                    try:
                        next(g_)
                    except StopIteration:
                        gens.remove(g_)

        def resid_add(l, half, j, t, b):
            grp = 0 if t == 0 else 1
            xs = x[:, j, t * TS:(t + 1) * TS]
            op("dve", lambda e: e.scalar_tensor_tensor(out=xs, in0=banks[b][:, :], scalar=dv(l, half * 3 + 2, j, grp),
                                                       in1=xs, op0=ALU.mult, op1=ALU.add),
               reads=(BANK[b], DER, X[j][t]), writes=(X[j][t],))

        def mlp(l, next_adaln):
            hbuf = arena[:, 0:8 * NTOK].rearrange("p (k n) -> p k n", k=8)
            H = [Res(f"h{t}") for t in range(NT)]
            h1 = [arena[:, 8 * NTOK + i * 2048:8 * NTOK + (i + 1) * 2048].rearrange("p (c n) -> p c n", c=4) for i in range(2)]
            H1 = [Res("h1a"), Res("h1b")]
            norm_tile(l, 1, 0, hbuf[:, :, 0:TS], H[0])
            ada = list(next_adaln)
            seq = [(g, t) for g in range(8) for t in range(NT)]
            w1s, w2s = {}, {}

            def h1stage(k):
                g, t = seq[k]
                hi = k % 2
                if t == 0:
                    w1s[g] = ring_get(("w1", l, g))
                w1, W1, s1 = w1s[g]
                ng = None
                if g == 0 and t + 1 < NT:
                    ng = norm_gen(l, 1, t + 1, hbuf[:, :, (t + 1) * TS:(t + 2) * TS], H[t + 1])
                for c in range(4):
                    if ng is not None:
                        for _ in range(2):
                            next(ng, None)
                    b_ = balloc()

                    def pe(e, c=c, b_=b_):
                        ins = None
                        for kc in range(8):
                            ins = e.matmul(banks[b_][:, :], lhsT=w1[:, kc * 512 + c * 128:kc * 512 + c * 128 + 128],
                                           rhs=hbuf[:, kc, t * TS:(t + 1) * TS], start=(kc == 0), stop=(kc == 7))
                        return ins
                    op("pe", pe, reads=(W1, H[t]), writes=(BANK[b_],))
                    ti = c % 2
                    op("act", lambda e, b_=b_, ti=ti: e.activation(out=tmp[ti][:, :], in_=banks[b_][:, :], func=AF.Relu),
                       reads=(BANK[b_],), writes=(TMP[ti],))
                    bfree(b_)
                    op("dve", lambda e, c=c, ti=ti: e.tensor_tensor(out=h1[hi][:, c, :], in0=tmp[ti][:, :], in1=tmp[ti][:, :],
                                                                    op=ALU.mult), reads=(TMP[ti],), writes=(H1[hi],))
                if t == NT - 1:
                    ring_done(s1)
                if ng is not None:
                    for _ in ng:
                        pass

            def outstage(k):
                g, t = seq[k]
                hi = k % 2
                if t == 0:
                    w2s[g] = ring_get(("w2", l, g))
                w2, W2, s2 = w2s[g]
                for j in range(8):
                    b_ = balloc()

                    def pe2(e, j=j, b_=b_):
                        ins = None
                        for c in range(4):
                            ins = e.matmul(banks[b_][:, :], lhsT=w2[:, c * 1024 + j * 128:c * 1024 + j * 128 + 128],
                                           rhs=h1[hi][:, c, :], start=(c == 0), stop=(c == 3))
                        return ins
                    op("pe", pe2, reads=(W2, H1[hi]), writes=(BANK[b_],))
                    resid_add(l, 1, j, t, b_)
                    bfree(b_)
                if t == NT - 1:
                    ring_done(s2)
                    for _ in range(2):
                        if ada:
                            adaln_group(*ada.pop(0))
                            if not ada:
                                adaln_finish(l + 1)

            h1stage(0)
            for k in range(len(seq)):
                if k + 1 < len(seq):
                    h1stage(k + 1)
                outstage(k)
            assert not ada

        pti = [0]

        NPT = 3
        LA = 3
        OB_POOL = [6, 7]
        ob_i = [0]

        def reserve_obanks(on):
            if on:
                for b in OB_POOL:
                    bank_free.remove(b)
            else:
                bank_free.extend(OB_POOL)

        def run_attn(jobs, scale, side=None):
            items = []
            for ji, job in enumerate(jobs):
                for gi, g in enumerate(job["groups"]):
                    items.append((ji, gi, g))
            M = len(items)
            sbank = {}
            ptb = {}
            obank = {}
            for i in range(M + LA):
                if i < M:
                    ji, gi, g = items[i]
                    sbk = balloc()
                    sbank[i] = sbk
                    n = g["n"]

                    def pe(e, g=g, sbk=sbk, n=n):
                        ins = e.matmul(banks[sbk][:, 0:n], lhsT=g["k"], rhs=g["q"], start=True, stop=(g["mask"] is None), skip_group_check=True)
                        if g["mask"] is not None:
                            ins = e.matmul(banks[sbk][:, 0:n], lhsT=ident[:, :], rhs=g["mask"], start=False, stop=True, skip_group_check=True)
                        return ins
                    op("pe", pe, reads=(g["K"], g["Q"], CONST) + ((g["Q2"],) if "Q2" in g else ()), writes=(BANK[sbk],))
                j = i - (LA - 1)
                if 0 <= j < M:
                    ji, gi, g = items[j]
                    sbk = sbank.pop(j)
                    n = g["n"]
                    pi = pti[0] % NPT
                    pti[0] += 1
                    ptb[j] = pi
                    op("act", lambda e, sbk=sbk, n=n, pi=pi: e.activation(out=pt[pi][:, 0:n], in_=banks[sbk][:, 0:n], func=AF.Exp, bias=negc[:, 0:1], scale=scale),
                       reads=(BANK[sbk], CONST), writes=(PT[pi],))
                    bfree(sbk)
                k = i - LA
                if 0 <= k < M:
                    ji, gi, g = items[k]
                    if gi == 0:
                        obank[ji] = OB_POOL[ob_i[0] % 2]
                        ob_i[0] += 1
                    ob = obank[ji]
                    pi = ptb.pop(k)
                    n = g["n"]
                    c0 = g["c0"]
                    last = gi == len(jobs[ji]["groups"]) - 1
                    op("pe", lambda e, g=g, pi=pi, n=n, c0=c0, f=(gi == 0), la=last, ob=ob: e.matmul(
                        banks[ob][:, c0:c0 + n], lhsT=g["v"], rhs=pt[pi][:, 0:n], start=f, stop=la, skip_group_check=True),
                       reads=(g["V"], PT[pi]), writes=(BANK[ob],))
                    if last:
                        jobs[ji]["finish"](ob)
                        obank.pop(ji)
                if side is not None:
                    side()

        def normalize(obank, base, nq, sink_col, dst, wres, extra_reads=(), on_dve=False):
            dbase = 64 - base
            ds = slice(dbase, dbase + 64)
            if on_dve:
                op("dve", lambda e: e.tensor_scalar(out=rec[ds, 0:nq], in0=banks[obank][ds, 0:nq],
                                                    scalar1=der[ds, sink_col:sink_col + 1], scalar2=None, op0=ALU.add),
                   reads=(BANK[obank], DER), writes=(REC,))
                op("dve", lambda e: e.reciprocal(out=rec[ds, 0:nq], in_=rec[ds, 0:nq]), reads=(REC,), writes=(REC,))
            else:
                if sink_col is not None:
                    op("act", lambda e: e.activation(out=rec[ds, 0:nq], in_=banks[obank][ds, 0:nq], func=AF.Ln,
                                                     bias=der[ds, sink_col:sink_col + 1], scale=1.0),
                       reads=(BANK[obank], DER), writes=(REC,))
                else:
                    op("act", lambda e: e.activation(out=rec[ds, 0:nq], in_=banks[obank][ds, 0:nq], func=AF.Ln),
                       reads=(BANK[obank],), writes=(REC,))
                op("act", lambda e: e.activation(out=rec[ds, 0:nq], in_=rec[ds, 0:nq], func=AF.Exp, scale=-1.0), reads=(REC,), writes=(REC,))
            op("dve", lambda e: e.tensor_tensor(out=dst, in0=banks[obank][base:base + 64, 0:nq], in1=rec[ds, 0:nq], op=ALU.mult),
               reads=(BANK[obank], REC) + tuple(extra_reads), writes=wres)

        def rope_combine(b1, b2, nrows, dst, wres, p0=0, tab=None, TAB=None, tp0=None):
            ps = slice(p0, p0 + nrows)
            if tab is None:
                tab, TAB, tp0 = rope_t, ROPE, p0
            ts_ = slice(tp0, tp0 + nrows)
            op("dve", lambda e: e.tensor_tensor(out=tmp[0][ps, :], in0=banks[b1][ps, :], in1=tab[ts_, 0, :], op=ALU.mult),
               reads=(BANK[b1], TAB), writes=(TMP[0],))
            op("dve", lambda e: e.tensor_tensor(out=tmp[1][ps, :], in0=banks[b2][ps, :], in1=tab[ts_, 1, :], op=ALU.mult),
               reads=(BANK[b2], TAB), writes=(TMP[1],))
            op("dve", lambda e: e.tensor_tensor(out=dst, in0=tmp[0][ps, :], in1=tmp[1][ps, :], op=ALU.add),
               reads=(TMP[0], TMP[1]), writes=wres)

        def load_rope(which, T, p0=0):
            rsl = slice(T * TS, (T + 1) * TS)
            if which == "A":
                trk.dma("sp", sp_sems.next(), rope_t[:, :, :], d_ropeA[:, :, rsl], writes=(ROPE,))
            else:
                trk.dma("sp", sp_sems.next(), rope_t[p0:p0 + 32, :, :], d_ropeB[:, :, rsl], writes=(ROPE,))

        def attn_group(l, grp, extra=None):
            e_i = l // 2
            sample = grp == 1
            tiles = [1, 2, 3, 4] if sample else [0]
            t0 = tiles[0]
            ntok = TS * len(tiles)
            nctx = NCTX if sample else 0
            NKg = ntok + nctx
            nkt = NKg // 128
            nlt = ntok // 128
            nt = len(tiles)
            o = 0
            qa = arena[:, o:o + 4 * ntok].rearrange("p (c n) -> p c n", c=4); o += 4 * ntok
            cqn = arena[:, o:o + 2 * NKg].rearrange("p (c n) -> p c n", c=2); o += 2 * NKg
            ckvn = arena[:, o:o + NKg]; o += NKg
            r3 = o
            hts = [arena[:, o + i * 4096:o + (i + 1) * 4096].rearrange("p (k n) -> p k n", k=8) for i in range(2)]
            khb = arena[:, r3:r3 + NKg]
            vhb = arena[:, r3 + NKg:r3 + NKg + nkt * 128].rearrange("p (t c) -> p t c", c=128)
            qhbs = [arena[:, r3 + NKg + nkt * 128 + i * ntok:r3 + NKg + nkt * 128 + (i + 1) * ntok] for i in range(2)]
            o = r3 + max(8192, NKg + nkt * 128 + 2 * ntok)
            r4 = o
            ka = arena[:, o:o + NKg]
            va = arena[:, o + NKg:o + NKg + nkt * 192].rearrange("p (t c) -> p t c", c=192)
            obc = [arena[:, r4 + i * ntok:r4 + (i + 1) * ntok] for i in range(2)]
            o = r4 + max(NKg + nkt * 192, 2 * ntok)
            assert o <= ARENA, o
            HTs = [Res("ht0"), Res("ht1")]
            QAH = [[[Res(f"qah{c}_{hh}_{t}") for t in range(nt)] for hh in range(2)] for c in range(4)]
            CQ = [Res(f"cq{t}") for t in range(nt)]
            CKV = [Res(f"ckv{t}") for t in range(nt + 1)]
            KR = [Res(f"kr{t}") for t in range(nt + 1)]
            KA = [Res(f"ka{t}") for t in range(nt + 1)]
            VA = [Res(f"va{t}") for t in range(nt + 1)]
            OBC = [Res("obc0"), Res("obc1")]
            KH, VH = Res("kh"), Res("vh")
            QHs = [Res("qh0"), Res("qh1")]
            krv = cqn[64:96, 1, :]

            reserve_obanks(True)
            g0, G0, s0 = ring_get(("win", e_i, 0))
            g1, G1, s1 = ring_get(("win", e_i, 1))
            g2, G2, s2 = ring_get(("win", e_i, 2))
            g3, G3, s3 = ring_get(("win", e_i, 3))

            op("dve", lambda e: e.memset(va[:, :, 64:128], 1.0), writes=tuple(VA))
            op("dve", lambda e: e.memset(cqn[96:128, 1, :], 0.0), writes=tuple(CQ))
            if sample:
                for g_ in range(4):
                    trk.dma("sp", sp_sems.next(), ropeB_all[32 * g_:32 * g_ + 32, :, :], d_ropeB[:, :, g_ * TS:(g_ + 1) * TS], writes=(ROPEB,))
                load_cast(ka[:, ntok:NKg], d_ckT[e_i], 128, NCTX, (KA[nt],))
                load_cast(ckvn[:, ntok:NKg], d_cckvT[e_i], 128, NCTX, (CKV[nt],))
                load_cast(krv[:, ntok:NKg], d_ckrT[e_i], 32, NCTX, (KR[nt],))
                for kt in range(2):
                    trk.dma("sp", sp_sems.next(), stage[:, 0:128], d_cv[e_i, kt], writes=(STAGE,))
                    op("dve", lambda e, kt=kt: e.tensor_copy(out=va[:, nlt + kt, 0:64], in_=stage[:, 0:64]), reads=(STAGE,), writes=(VA[nt],))
                    op("dve", lambda e, kt=kt: e.tensor_copy(out=va[:, nlt + kt, 128:192], in_=stage[:, 64:128]), reads=(STAGE,), writes=(VA[nt],))

            oqn = PM["qnorm"][0] + e_i * 2
            okn = PM["kvnorm"][0] + e_i

            cur = {}

            def proj(bk, wt, WR, col0, ncol, ht, HT):

                def pe(e):
                    ins = None
                    for kc in range(8):
                        ins = e.matmul(banks[bk][0:ncol, :], lhsT=wt[:, kc * 512 + col0:kc * 512 + col0 + ncol], rhs=ht[:, kc, :],
                                       start=(kc == 0), stop=(kc == 7))
                    return ins
                op("pe", pe, reads=(WR, HT), writes=(BANK[bk],))

            def out_from(dst, src_ap, src_res, nrows, scale=None):
                i = ost_i[0] % 2
                ost_i[0] += 1
                if scale is None:
                    op("act", lambda e: e.activation(out=ostage[i][0:nrows, :], in_=src_ap, func=AF.Identity),
                       reads=src_res, writes=(OST[i],))
                else:
                    op("act", lambda e: e.activation(out=ostage[i][0:nrows, :], in_=src_ap, func=AF.Identity, scale=scale),
                       reads=src_res + (P_,), writes=(OST[i],))
                trk.dma("sp", out_sems.next(), dst, ostage[i][0:nrows, :], reads=(OST[i],))

            def norm_gen(t, dest, DEST):
                grp = 0 if t == 0 else 1
                tokx = slice(t * TS, (t + 1) * TS)
                b = balloc()

                def mm_(k):
                    op("pe", lambda e: e.matmul(banks[b][:, :], lhsT=onesm[:, :], rhs=sq[k % 2][:, :], start=(k == 0),
                                                stop=(k == 7), skip_group_check=True),
                       reads=(SQ[k % 2], CONST), writes=(BANK[b],))
                for p_ in range(5):
                    if p_ >= 1:
                        mm_(2 * p_ - 2)
                        mm_(2 * p_ - 1)
                    if p_ < 4:
                        for k in (2 * p_, 2 * p_ + 1):
                            op("act", lambda e, k=k: e.activation(out=sq[k % 2][:, :], in_=x[:, k, tokx], func=AF.Square),
                               reads=(X[k][t],), writes=(SQ[k % 2],))
                        yield
                op("act", lambda e: e.activation(out=rstd[:, :], in_=banks[b][:, :], func=AF.Ln, bias=epst[:, 0:1], scale=1.0),
                   reads=(BANK[b], CONST), writes=(RSTD,))
                bfree(b)
                op("act", lambda e: e.activation(out=rstd[:, :], in_=rstd[:, :], func=AF.Exp, scale=-0.5), reads=(RSTD,), writes=(RSTD,))
                yield
                for k in range(8):
                    ti = k % 2
                    op("dve", lambda e, k=k, ti=ti: e.tensor_tensor(out=tmp[ti][:, :], in0=x[:, k, tokx], in1=rstd[:, :], op=ALU.mult),
                       reads=(X[k][t], RSTD), writes=(TMP[ti],))
                    op("act", lambda e, k=k, ti=ti: e.activation(out=dest[:, k, :], in_=tmp[ti][:, :], func=AF.Identity,
                                                               bias=dv(l, 1, k, grp), scale=dv(l, 0, k, grp)),
                       reads=(TMP[ti], DER), writes=(DEST,))
                    if k % 4 == 3:
                        yield

            def stageA(lt):
                t = tiles[lt]
                tok = slice(lt * TS, (lt + 1) * TS)
                ht, HT = hts[lt % 2], HTs[lt % 2]
                cur["ht"], cur["HT"] = ht, HT
                yield from norm_gen(t, ht, HT)
                cur["ht"], cur["HT"] = ht, HT
                b0 = balloc(); b1 = balloc(); bs = balloc()
                proj(b0, g0, G0, 0, 128, ht, HT)
                proj(b1, g0, G0, 128, 128, ht, HT)
                op("act", lambda e, b0=b0: e.activation(out=sq[0][:, :], in_=banks[b0][:, :], func=AF.Square), reads=(BANK[b0],), writes=(SQ[0],))
                op("act", lambda e, b1=b1: e.activation(out=sq[1][:, :], in_=banks[b1][:, :], func=AF.Square), reads=(BANK[b1],), writes=(SQ[1],))

                def pe_ss(e, bs=bs):
                    e.matmul(banks[bs][:, :], lhsT=ones1[:, :], rhs=sq[0][:, :], start=True, stop=False, skip_group_check=True)
                    return e.matmul(banks[bs][:, :], lhsT=ones_lo[:, :], rhs=sq[1][:, :], start=False, stop=True, skip_group_check=True)
                op("pe", pe_ss, reads=(SQ[0], SQ[1], CONST), writes=(BANK[bs],))
                yield
                op("act", lambda e, bs=bs: e.activation(out=rstd[:, :], in_=banks[bs][:, :], func=AF.Ln, bias=epst[:, 0:1], scale=1.0 / 192.0),
                   reads=(BANK[bs], CONST), writes=(RSTD,))
                op("act", lambda e: e.activation(out=rstd[:, :], in_=rstd[:, :], func=AF.Exp, scale=-0.5), reads=(RSTD,), writes=(RSTD,))
                op("dve", lambda e, b0=b0: e.tensor_tensor(out=tmp[0][:, :], in0=banks[b0][:, :], in1=rstd[:, :], op=ALU.mult),
                   reads=(BANK[b0], RSTD), writes=(TMP[0],))
                op("act", lambda e, tok=tok: e.activation(out=cqn[:, 0, tok], in_=tmp[0][:, :], func=AF.Identity, scale=params[:, oqn:oqn + 1]),
                   reads=(TMP[0], P_), writes=(CQ[lt],))
                op("dve", lambda e, b1=b1: e.tensor_tensor(out=tmp[1][0:64, :], in0=banks[b1][0:64, :], in1=rstd[0:64, :], op=ALU.mult),
                   reads=(BANK[b1], RSTD), writes=(TMP[1],))
                op("act", lambda e, tok=tok: e.activation(out=cqn[0:64, 1, tok], in_=tmp[1][0:64, :], func=AF.Identity, scale=params[0:64, oqn + 1:oqn + 2]),
                   reads=(TMP[1], P_), writes=(CQ[lt],))
                bfree(b0); bfree(b1)
                yield
                b0 = balloc()
                proj(b0, g0, G0, 192, 128, ht, HT)
                op("act", lambda e, b0=b0: e.activation(out=sq[0][:, :], in_=banks[b0][:, :], func=AF.Square), reads=(BANK[b0],), writes=(SQ[0],))
                op("pe", lambda e, bs=bs: e.matmul(banks[bs][:, :], lhsT=ones1[:, :], rhs=sq[0][:, :], start=True, stop=True),
                   reads=(SQ[0], CONST), writes=(BANK[bs],))
                op("act", lambda e, bs=bs: e.activation(out=rstd[:, :], in_=banks[bs][:, :], func=AF.Ln, bias=epst[:, 0:1], scale=1.0 / 128.0),
                   reads=(BANK[bs], CONST), writes=(RSTD,))
                op("act", lambda e: e.activation(out=rstd[:, :], in_=rstd[:, :], func=AF.Exp, scale=-0.5), reads=(RSTD,), writes=(RSTD,))
                op("dve", lambda e, b0=b0: e.tensor_tensor(out=tmp[0][:, :], in0=banks[b0][:, :], in1=rstd[:, :], op=ALU.mult),
                   reads=(BANK[b0], RSTD), writes=(TMP[0],))
                op("act", lambda e, tok=tok: e.activation(out=ckvn[:, tok], in_=tmp[0][:, :], func=AF.Identity, scale=params[:, okn:okn + 1]),
                   reads=(TMP[0], P_), writes=(CKV[lt],))
                if not sample:
                    out_from(o_ckvT[e_i], tmp[0][:, :], (TMP[0],), 128, scale=params[:, okn:okn + 1])
                bfree(b0); bfree(bs)
                yield
                b0 = balloc()
                proj(b0, g0, G0, 320, 128, ht, HT)
                if sample:
                    b1 = balloc()
                    proj(b1, g0, G0, 352, 128, ht, HT)
                    rope_combine(b0, b1, 32, krv[:, tok], (KR[lt],), tab=ropeB_all, TAB=ROPEB, tp0=32 * (t - 1))
                    bfree(b1)
                else:
                    out_from(o_krT[e_i], banks[b0][0:32, :], (BANK[b0],), 32)
                    op("dve", lambda e, b0=b0, tok=tok: e.tensor_copy(out=krv[:, tok], in_=ostage[(ost_i[0] - 1) % 2][0:32, :]),
                       reads=(OST[(ost_i[0] - 1) % 2],), writes=(KR[lt],))
                bfree(b0)
                yield

            def stageB(lt):
                t = tiles[lt]
                tok = slice(lt * TS, (lt + 1) * TS)
                ht, HT = hts[lt % 2], HTs[lt % 2]
                cur["ht"], cur["HT"] = ht, HT
                if sample:
                    load_rope("A", t - 1)
                for c in range(4):
                    b0 = balloc()
                    proj(b0, g1, G1, c * 128, 128, ht, HT)
                    if sample:
                        b1 = balloc()
                        proj(b1, g2, G2, c * 128, 128, ht, HT)
                        rope_combine(b0, b1, 128, qa[:, c, tok], (QAH[c][0][lt], QAH[c][1][lt]))
                        bfree(b1)
                    else:
                        op("act", lambda e, c=c, b0=b0, tok=tok: e.activation(out=qa[:, c, tok], in_=banks[b0][:, :], func=AF.Identity),
                           reads=(BANK[b0],), writes=(QAH[c][0][lt], QAH[c][1][lt]))
                    bfree(b0)
                    yield
                b0 = balloc()
                proj(b0, g3, G3, 0, 128, ht, HT)
                if sample:
                    b1 = balloc()
                    proj(b1, g3, G3, 128, 128, ht, HT)
                    rope_combine(b0, b1, 128, ka[:, tok], (KA[lt],))
                    bfree(b1)
                else:
                    op("act", lambda e, b0=b0, tok=tok: e.activation(out=ka[:, tok], in_=banks[b0][:, :], func=AF.Identity), reads=(BANK[b0],), writes=(KA[lt],))
                    out_from(o_kT[e_i], banks[b0][:, :], (BANK[b0],), 128)
                bfree(b0)
                yield
                b0 = balloc()

                def pe_v(e, b0=b0, ht=ht):
                    ins = None
                    for tb in range(4):
                        for kc in range(8):
                            ins = e.matmul(banks[b0][:, tb * 128:(tb + 1) * 128], lhsT=ht[:, kc, tb * 128:(tb + 1) * 128],
                                           rhs=g3[:, kc * 512 + 256:kc * 512 + 384], start=(kc == 0), stop=(kc == 7), skip_group_check=True)
                    return ins
                op("pe", pe_v, reads=(G3, HT), writes=(BANK[b0],))
                bv = banks[b0][:, :].rearrange("p (t c) -> p t c", c=128)
                op("act", lambda e, bv=bv, lt=lt: e.activation(out=va[:, 4 * lt:4 * lt + 4, 0:64], in_=bv[:, :, 0:64], func=AF.Identity),
                   reads=(BANK[b0],), writes=(VA[lt],))
                op("act", lambda e, bv=bv, lt=lt: e.activation(out=va[:, 4 * lt:4 * lt + 4, 128:192], in_=bv[:, :, 64:128], func=AF.Identity),
                   reads=(BANK[b0],), writes=(VA[lt],))
                if not sample:
                    i = ost_i[0] % 2
                    ost_i[0] += 1
                    op("act", lambda e, i=i, b0=b0: e.activation(out=ostage[i][:, :], in_=banks[b0][:, :], func=AF.Identity),
                       reads=(BANK[b0],), writes=(OST[i],))
                    trk.dma("sp", out_sems.next(), o_v[e_i].rearrange("(t p) c -> p t c", p=128),
                            ostage[i][:, :].rearrange("p (t c) -> p t c", c=128), reads=(OST[i],))
                bfree(b0)
                yield

            def drive(gens):
                gens = [g_ for g_ in gens if g_ is not None]
                while gens:
                    for g_ in list(gens):
                        try:
                            next(g_)
                        except StopIteration:
                            gens.remove(g_)

            drive([stageA(0)])
            for lt in range(nt):
                drive([stageB(lt), stageA(lt + 1) if lt + 1 < nt else None])
            for s_ in (s0, s1, s2, s3):
                ring_done(s_)

            g4, G4, s4 = ring_get(("wqkv", e_i))
            wo0, WO0, so0 = ring_get(("wout", e_i, 0))
            wo1, WO1, so1 = ring_get(("wout", e_i, 1))

            kaz = [arena[:, r3 + i * NKg:r3 + (i + 1) * NKg] for i in range(2)]
            KZ = Res("kz")
            jobs = []
            for c in range(4):
                for hh in range(2):
                    h = c + 4 * hh
                    base = 64 * hh
                    vs = slice(0, 128) if hh == 0 else slice(64, 192)
                    sink_col = o_es + e_i * 8 + h
                    if not sample:
                        for s in range(2):
                            q0 = s * 256
                            groups = []
                            for kb in range(2):
                                k0 = s * 256 + kb * 128
                                groups.append(dict(k=kaz[hh][:, k0:k0 + 128], q=qa[:, c, q0:q0 + 256], K=KZ, Q=QAH[c][hh][0], Q2=QAH[c][1 - hh][0],
                                                   v=va[:, k0 // 128, vs], V=VA[0], c0=0, n=256, mask=None))
                            jobs.append(dict(groups=groups, finish=(lambda ob, base=base, sink_col=sink_col, c=c, q0=q0, hh=hh: normalize(
                                ob, base, 256, sink_col, qa[base:base + 64, c, q0:q0 + 256], (QAH[c][hh][0],)))))
                    else:
                        for T in range(4):
                            q0 = T * TS
                            groups = []
                            for kb in range(2):
                                groups.append(dict(k=kaz[hh][:, ntok + kb * 128:ntok + (kb + 1) * 128], q=qa[:, c, q0:q0 + TS],
                                                   K=KZ, Q=QAH[c][hh][T], Q2=QAH[c][1 - hh][T], v=va[:, nlt + kb, vs], V=VA[nt], c0=0, n=TS, mask=None))
                            for jb in range(4 * T - 1, 4 * T + 5):
                                if jb < 0 or jb > 15:
                                    continue
                                qlo = max(jb - 1, 4 * T)
                                qhi = min(jb + 1, 4 * T + 3)
                                n = (qhi - qlo + 1) * 128
                                c0 = (qlo - 4 * T) * 128
                                m0 = (qlo - (jb - 1)) * 128
                                k0 = jb * 128
                                groups.append(dict(k=kaz[hh][:, k0:k0 + 128], q=qa[:, c, q0 + c0:q0 + c0 + n],
                                                   K=KZ, Q=QAH[c][hh][T], Q2=QAH[c][1 - hh][T], v=va[:, jb, vs], V=VA[jb // 4],
                                                   c0=c0, n=n, mask=maskb[:, m0:m0 + n]))
                            jobs.append(dict(groups=groups, finish=(lambda ob, base=base, sink_col=sink_col, c=c, q0=q0, hh=hh, T=T: normalize(
                                ob, base, TS, sink_col, qa[base:base + 64, c, q0:q0 + TS], (QAH[c][hh][T],)))))
            side_q, side_o = [], []
            side_x = list(extra or [])
            s4_released = [False]

            sidestep = [0]
            armed = [False]

            def pop_side():
                pool_sel[0] = "side"
                sidestep[0] += 1
                if side_q:
                    side_q.pop(0)()
                elif side_x and sidestep[0] % 24 == 0 and armed[0]:
                    side_x.pop(0)()
                elif side_o and (sidestep[0] % 2 == 0 or not sample):
                    side_o.pop(0)[1]()
                pool_sel[0] = "g"

            def flush(lst):
                while lst:
                    it_ = lst.pop(0)
                    (it_[1] if isinstance(it_, tuple) else it_)()

            def flush_tag(tag):
                keep = []
                for it_ in side_o:
                    if it_[0] == tag:
                        it_[1]()
                    else:
                        keep.append(it_)
                side_o[:] = keep

            mscale = float((64 + 32) ** -0.5)
            WQ0, WQ1, WK, WV = 0, 1024, 2048, 2560
            nt6 = nt + (1 if sample else 0)

            if not sample:
                fo_ = o
                PB = []
                for h_ in range(8):
                    PB.append(dict(k=arena[:, fo_:fo_ + 512], v=arena[:, fo_ + 512:fo_ + 1024].rearrange("p (t c) -> p t c", c=128),
                                   q=arena[:, fo_ + 1024:fo_ + 1536], K=Res(f"pk{h_}"), V=Res(f"pv{h_}"), Q=Res(f"pq{h_}")))
                    fo_ += 1536
                obc4 = [arena[:, fo_ + i * 512:fo_ + (i + 1) * 512] for i in range(4)]
                OBC4 = [Res(f"obc4_{i}") for i in range(4)]
                fo_ += 2048
                assert fo_ <= ARENA, fo_

            def hbufs(h):
                if sample:
                    return dict(k=khb, v=vhb, q=qhbs[h % 2], K=KH, V=VH, Q=QHs[h % 2])
                return PB[h]

            def kv_tasks(h):
                hb = hbufs(h)
                khb, vhb, KH, VH = hb["k"], hb["v"], hb["K"], hb["V"]
                bi = h % 2
                vcol = 0 if bi == 0 else 64
                ocol = 64 if bi == 0 else 0
                tasks = []

                def t_init():
                    if not sample:
                        op("dve", lambda e: e.memset(khb[96:128, :], 0.0), writes=(KH,))
                    op("dve", lambda e: e.memset(vhb[:, :, ocol:ocol + 64], 1.0), writes=(VH,))
                    op("dve", lambda e: e.tensor_copy(out=khb[64:96, :], in_=krv[:, :]), reads=tuple(KR), writes=(KH,))
                tasks.append(t_init)
                for t6 in range(nt6):
                    def t_kv(t6=t6):
                        n = TS if t6 < nt else NCTX
                        k0 = t6 * TS
                        b0 = balloc()
                        op("pe", lambda e: e.matmul(banks[b0][:, 0:n], lhsT=g4[:, WK + h * 64:WK + h * 64 + 128],
                                                    rhs=ckvn[:, k0:k0 + n], start=True, stop=True),
                           reads=(G4, CKV[t6]), writes=(BANK[b0],))
                        op("dve", lambda e: e.tensor_copy(out=khb[0:64, k0:k0 + n], in_=banks[b0][0:64, 0:n]),
                           reads=(BANK[b0],), writes=(KH,))
                        bfree(b0)
                        b1 = balloc()
                        ntb = n // 128

                        def pe_v2(e):
                            ins = None
                            for tb in range(ntb):
                                ins = e.matmul(banks[b1][:, tb * 64:(tb + 1) * 64], lhsT=ckvn[:, k0 + tb * 128:k0 + (tb + 1) * 128],
                                               rhs=g4[:, WV + h * 64:WV + (h + 1) * 64], start=True, stop=True, skip_group_check=True)
                            return ins
                        op("pe", pe_v2, reads=(G4, CKV[t6]), writes=(BANK[b1],))
                        op("dve", lambda e: e.tensor_copy(
                            out=vhb[:, 4 * t6:4 * t6 + ntb, vcol:vcol + 64],
                            in_=banks[b1][:, 0:ntb * 64].rearrange("p (t c) -> p t c", c=64)),
                           reads=(BANK[b1],), writes=(VH,))
                        bfree(b1)
                    tasks.append(t_kv)
                return tasks

            def q_tasks(h):
                hb = hbufs(h)
                qhb, QH = hb["q"], hb["Q"]
                tasks = []
                for lt, t in enumerate(tiles):
                    def t_q(lt=lt, t=t):
                        tok = slice(lt * TS, (lt + 1) * TS)
                        b0 = balloc()

                        def pe_q(e):
                            e.matmul(banks[b0][:, :], lhsT=g4[:, WQ0 + h * 128:WQ0 + h * 128 + 128], rhs=cqn[:, 0, tok], start=True, stop=False)
                            return e.matmul(banks[b0][:, :], lhsT=g4[:, WQ1 + h * 128:WQ1 + h * 128 + 128], rhs=cqn[:, 1, tok], start=False, stop=True)
                        op("pe", pe_q, reads=(G4, CQ[lt], KR[lt]), writes=(BANK[b0],))
                        op("dve", lambda e: e.tensor_copy(out=qhb[0:64, tok], in_=banks[b0][0:64, :]),
                           reads=(BANK[b0],), writes=(QH,))
                        if sample:
                            gsl = slice(32 * (t - 1), 32 * (t - 1) + 32)
                            op("dve", lambda e: e.tensor_tensor(out=tmp[0][64:96, :], in0=banks[b0][64:96, :], in1=ropeB_all[gsl, 0, :], op=ALU.mult),
                               reads=(BANK[b0], ROPEB), writes=(TMP[0],))
                            op("dve", lambda e: e.tensor_tensor(out=tmp[1][64:96, :], in0=banks[b0][96:128, :], in1=ropeB_all[gsl, 1, :], op=ALU.mult),
                               reads=(BANK[b0], ROPEB), writes=(TMP[1],))
                            op("dve", lambda e: e.tensor_tensor(out=qhb[64:96, tok], in0=tmp[0][64:96, :], in1=tmp[1][64:96, :], op=ALU.add),
                               reads=(TMP[0], TMP[1]), writes=(QH,))
                        else:
                            op("dve", lambda e: e.tensor_copy(out=qhb[64:96, tok], in_=banks[b0][64:96, :]),
                               reads=(BANK[b0],), writes=(QH,))
                        bfree(b0)
                    tasks.append(t_q)
                return tasks

            def oproj_tasks(kcs, srcs, SRCS):
                tasks = []
                for lt, t in enumerate(tiles):
                    for j in range(8):
                        def t_o(lt=lt, t=t, j=j):
                            tok = slice(lt * TS, (lt + 1) * TS)
                            wt, WR = (wo0, WO0) if j < 4 else (wo1, WO1)
                            jj = j % 4
                            b0 = balloc()

                            def pe_o(e):
                                ins = None
                                for i, kc in enumerate(kcs):
                                    ins = e.matmul(banks[b0][:, :], lhsT=wt[:, kc * 512 + jj * 128:kc * 512 + jj * 128 + 128], rhs=srcs[i](tok),
                                                   start=(i == 0), stop=(i == len(kcs) - 1))
                                return ins
                            op("pe", pe_o, reads=(WR,) + tuple(SRCS(lt)), writes=(BANK[b0],))
                            resid_add(l, 0, j, t, b0)
                            bfree(b0)
                        tasks.append(t_o)
                return tasks

            reserve_side(True)
            op("dve", lambda e: e.memset(arena[:, r3:r3 + 2 * NKg], 0.0), writes=(KZ, HTs[0], HTs[1]))
            op("dve", lambda e: e.tensor_copy(out=kaz[0][0:64, :], in_=ka[0:64, :]), reads=tuple(KA), writes=(KZ,))
            op("act", lambda e: e.activation(out=kaz[1][64:128, :], in_=ka[64:128, :], func=AF.Identity), reads=tuple(KA), writes=(KZ,))
            run_attn(jobs, 0.125, pop_side)
            side_o += [("A", f_) for f_ in oproj_tasks([0, 1, 2, 3], [(lambda tok, c=c: qa[:, c, tok]) for c in range(4)],
                                                       lambda lt: [QAH[c][hh][lt] for c in range(4) for hh in range(2)])]
            if sample:
                op("dve", lambda e: e.memset(obc[0][0:1, 0:1], 0.0),
                   writes=(OBC[0], OBC[1], KZ, KH, VH, QHs[0], QHs[1]) + tuple(KA) + tuple(VA))
            if not sample:
                jobs = []
                for h in range(8):
                    hb = hbufs(h)
                    op("dve", lambda e, hb=hb: e.memset(hb["q"][96:128, :], 0.0), writes=(hb["Q"],))
                    flush(kv_tasks(h))
                    flush(q_tasks(h))
                ring_done(s4)
                s4_released[0] = True
                for h in range(8):
                    hb = hbufs(h)
                    bi = h % 2
                    base = 64 * bi
                    cB = h // 2
                    for s_i in range(2):
                        q0 = s_i * 256
                        groups = []
                        for kb in range(2):
                            k0 = s_i * 256 + kb * 128
                            groups.append(dict(k=hb["k"][:, k0:k0 + 128], q=hb["q"][:, q0:q0 + 256], K=hb["K"], Q=hb["Q"],
                                               v=hb["v"][:, k0 // 128, :], V=hb["V"], c0=0, n=256, mask=None))
                        jobs.append(dict(groups=groups, finish=(lambda ob, base=base, cB=cB, q0=q0: normalize(
                            ob, base, 256, None, obc4[cB][base:base + 64, q0:q0 + 256], (OBC4[cB],)))))
                run_attn(jobs, mscale, pop_side)
                side_o += [("B", f_) for f_ in oproj_tasks([4, 5, 6, 7], [(lambda tok, c=c: obc4[c][:, tok]) for c in range(4)],
                                                       lambda lt: list(OBC4))]
            armed[0] = True
            for h in (range(8) if sample else []):
                bi = h % 2
                base = 64 * bi
                cB = h // 2
                if bi == 0 and cB >= 2:
                    flush_tag(cB - 2)
                flush(kv_tasks(h))
                if h == 0:
                    op("dve", lambda e: e.memset(khb[96:128, :], 0.0), writes=(KH,))
                    for i in range(2):
                        op("dve", lambda e, i=i: e.memset(qhbs[i][96:128, :], 0.0), writes=(QHs[i],))
                    side_q += q_tasks(0)
                flush(side_q)
                if h + 1 < 8:
                    side_q += q_tasks(h + 1)
                else:
                    ring_done(s4)
                    s4_released[0] = True
                qhb, QH = qhbs[h % 2], QHs[h % 2]
                oc = obc[cB % 2]
                OC = OBC[cB % 2]
                jobs = []
                if not sample:
                    for s in range(2):
                        q0 = s * 256
                        groups = []
                        for kb in range(2):
                            k0 = s * 256 + kb * 128
                            groups.append(dict(k=khb[:, k0:k0 + 128], q=qhb[:, q0:q0 + 256], K=KH, Q=QH,
                                               v=vhb[:, k0 // 128, :], V=VH, c0=0, n=256, mask=None))
                        jobs.append(dict(groups=groups, finish=(lambda ob, base=base, oc=oc, OC=OC, q0=q0: normalize(
                            ob, base, 256, None, oc[base:base + 64, q0:q0 + 256], (OC,)))))
                else:
                    for T in range(4):
                        q0 = T * TS
                        groups = []
                        for kb in list(range(nlt, nkt)) + list(range(nlt)):
                            k0 = kb * 128
                            groups.append(dict(k=khb[:, k0:k0 + 128], q=qhb[:, q0:q0 + TS], K=KH, Q=QH,
                                               v=vhb[:, kb, :], V=VH, c0=0, n=TS, mask=None))
                        jobs.append(dict(groups=groups, finish=(lambda ob, base=base, oc=oc, OC=OC, q0=q0: normalize(
                            ob, base, TS, None, oc[base:base + 64, q0:q0 + TS], (OC,)))))
                run_attn(jobs, mscale, pop_side)
                if bi == 1:
                    side_o += [(cB, f_) for f_ in oproj_tasks([4 + cB], [(lambda tok, oc=oc: oc[:, tok])], lambda lt, OC=OC: [OC])]
            flush(side_q)
            flush(side_x)
            flush(side_o)
            for s_ in ((so0, so1) if s4_released[0] else (s4, so0, so1)):
                ring_done(s_)
            reserve_side(False)
            reserve_obanks(False)
            trk.barrier()

        def conv_layer(l):
            j_i = l // 2
            o = 0
            upad = arena[:, o:o + 4 * NPAD].rearrange("p (c n) -> p c n", c=4); o += 4 * NPAD
            zpad = arena[:, o:o + 4 * NPAD].rearrange("p (c n) -> p c n", c=4); o += 4 * NPAD
            hts = [arena[:, o + i * 4096:o + (i + 1) * 4096].rearrange("p (k n) -> p k n", k=8) for i in range(2)]
            o += 8192
            assert o <= ARENA, o
            HTs = [Res("ht0"), Res("ht1")]
            U = [Res(f"u{c}") for c in range(4)]
            Z = [Res(f"z{c}") for c in range(4)]
            for (so_, sn_), pb in zip(SEQS, PADB):
                for buf, RS in ((upad, U), (zpad, Z)):
                    op("dve", lambda e, buf=buf, pb=pb: e.memset(buf[:, :, pb:pb + 16], 0.0), writes=tuple(RS))
                    op("dve", lambda e, buf=buf, pb=pb, sn_=sn_: e.memset(buf[:, :, pb + 16 + sn_:pb + 32 + sn_], 0.0), writes=tuple(RS))
            ga, GA, sa = ring_get(("cin", j_i, 0))
            gg, GG, sg = ring_get(("cin", j_i, 1))
            gz, GZ, sz = ring_get(("cin", j_i, 2))
            load_cast(pw[:, :], d_poolw[j_i], 128, 512, (PW,))

            def segs_of_tile(t):
                if t == 0:
                    return [(0, 0, 256), (1, 256, 256)]
                return [(2, t * TS, TS)]

            norm_tile(l, 0, 0, hts[0], HTs[0])
            for t in range(NT):
                ht, HT = hts[t % 2], HTs[t % 2]
                ng = norm_gen(l, 0, t + 1, hts[(t + 1) % 2], HTs[(t + 1) % 2]) if t + 1 < NT else None
                for c in range(4):
                    if ng is not None:
                        for _ in range(2):
                            next(ng, None)
                    ba = balloc(); bg = balloc()
                    for (bk, wt, WR) in ((ba, ga, GA), (bg, gg, GG)):
                        def pe(e, bk=bk, wt=wt, c=c, ht=ht):
                            ins = None
                            for kc in range(8):
                                ins = e.matmul(banks[bk][:, :], lhsT=wt[:, kc * 512 + c * 128:kc * 512 + c * 128 + 128], rhs=ht[:, kc, :],
                                               start=(kc == 0), stop=(kc == 7))
                            return ins
                        op("pe", pe, reads=(WR, HT), writes=(BANK[bk],))
                    ti = c % 2
                    op("act", lambda e, bg=bg, ti=ti: e.activation(out=tmp[ti][:, :], in_=banks[bg][:, :], func=AF.Sigmoid), reads=(BANK[bg],), writes=(TMP[ti],))
                    for (s_, t0_, n) in segs_of_tile(t):
                        p0 = _padpos(t0_)
                        c0 = t0_ - t * TS
                        op("dve", lambda e, ba=ba, c=c, p0=p0, c0=c0, n=n, ti=ti: e.tensor_tensor(
                            out=upad[:, c, p0:p0 + n], in0=banks[ba][:, c0:c0 + n], in1=tmp[ti][:, c0:c0 + n], op=ALU.mult),
                           reads=(BANK[ba], TMP[ti]), writes=(U[c],))
                    bfree(ba); bfree(bg)
                    bz = balloc()

                    def pez(e, bz=bz, c=c, ht=ht):
                        ins = None
                        for kc in range(8):
                            ins = e.matmul(banks[bz][:, :], lhsT=gz[:, kc * 512 + c * 128:kc * 512 + c * 128 + 128], rhs=ht[:, kc, :],
                                           start=(kc == 0), stop=(kc == 7))
                        return ins
                    op("pe", pez, reads=(GZ, HT), writes=(BANK[bz],))
                    for (s_, t0_, n) in segs_of_tile(t):
                        p0 = _padpos(t0_)
                        c0 = t0_ - t * TS
                        op("act", lambda e, bz=bz, c=c, p0=p0, c0=c0, n=n: e.activation(out=zpad[:, c, p0:p0 + n], in_=banks[bz][:, c0:c0 + n], func=AF.Identity),
                           reads=(BANK[bz],), writes=(Z[c],))
                    bfree(bz)
                if ng is not None:
                    for _ in ng:
                        pass
            rst["limit"] = rst["got"] + 2
            for s_ in (sa, sg, sz):
                ring_done(s_)
            wo0, WO0, so0 = ring_get(("cout", j_i, 0))
            wo1, WO1, so1 = ring_get(("cout", j_i, 1))
            free_slots = [i for i in range(NRING) if not rst["held"][i]]
            assert len(free_slots) == 2, free_slots
            for i in free_slots:
                rst["held"][i] = True
            dgs = [ring[i] for i in free_slots]
            DGs = [RING[i] for i in free_slots]
            hcs, HCs = hts, [Res("hc0"), Res("hc1")]
            obdw = PM["bdw"][0] + j_i * 4
            olng = PM["lng"][0] + j_i * 4
            olnb = PM["lnb"][0] + j_i * 4
            odw = PM["dw"][0] + j_i * 4 * 31
            segs = []
            for t in range(NT):
                sl = segs_of_tile(t)
                for i, (s_, t0_, n) in enumerate(sl):
                    segs.append((s_, t0_, n, t, i == len(sl) - 1))
            dgi = [0]

            pending = []

            def build_dg(c):
                di = dgi[0] % 2
                dgi[0] += 1
                dg, DG = dgs[di], DGs[di]
                wc0 = odw + c * 31
                op("dve", lambda e: e.tensor_tensor(
                    out=dg[:, 0:31 * 128].rearrange("p (t m) -> p t m", t=31),
                    in0=ident[:, :].unsqueeze(1).broadcast_to([128, 31, 128]),
                    in1=params[:, wc0:wc0 + 31].unsqueeze(2).broadcast_to([128, 31, 128]), op=ALU.mult),
                   reads=(CONST, P_), writes=(DG,))
                return dg, DG

            def prebuild():
                pending.append(build_dg(0))
                pending.append(build_dg(1))

            def conv_taps(seg):
                s_, t0_, n, t, _ = seg
                p0 = _padpos(t0_)
                cb = [balloc() for _ in range(4)]
                for c in range(4):
                    dg, DG = pending.pop(0) if pending else build_dg(c)

                    def pe(e, c=c, dg=dg):
                        ins = None
                        for tap in range(31):
                            ins = e.matmul(banks[cb[c]][:, 0:n], lhsT=dg[:, tap * 128:(tap + 1) * 128],
                                           rhs=upad[:, c, p0 + tap - 15:p0 + tap - 15 + n], start=(tap == 0), stop=(tap == 30))
                        return ins
                    op("pe", pe, reads=(DG, U[c]), writes=(BANK[cb[c]],))
                return cb

            def evac(seg, cb):
                n = seg[2]
                for c in range(4):
                    op("act", lambda e, c=c: e.activation(out=cvt[c][:, 0:n], in_=banks[cb[c]][:, 0:n], func=AF.Identity,
                                                          bias=params[:, obdw + c:obdw + c + 1], scale=1.0),
                       reads=(BANK[cb[c]], P_), writes=(CVT[c],))
                    bfree(cb[c])

            def pool_seg(seg, hc, HC):
                s_, t0_, n, t, _ = seg
                p0 = _padpos(t0_)
                c0 = t0_ - t * TS
                seq_o, seq_n = SEQS[s_]
                at_start = (t0_ == seq_o)
                at_end = (t0_ + n == seq_o + seq_n)
                for gi, w in enumerate(POOL_W):
                    lo = w // 2
                    hi = w - lo - 1
                    bs_ = balloc(); bz = balloc()

                    def pe(e, gi=gi, lo=lo, hi=hi, bs_=bs_):
                        ins = None
                        for si, sft in enumerate(range(-lo, hi + 1)):
                            ins = e.matmul(banks[bs_][:, 0:n], lhsT=pw[:, gi * 128:(gi + 1) * 128], rhs=zpad[:, gi, p0 + sft:p0 + sft + n],
                                           start=(si == 0), stop=(sft == hi))
                        return ins
                    op("pe", pe, reads=(PW, Z[gi]), writes=(BANK[bs_],))
                    op("pe", lambda e, gi=gi, bz=bz: e.matmul(banks[bz][:, 0:n], lhsT=pw[:, gi * 128:(gi + 1) * 128], rhs=zpad[:, gi, p0:p0 + n], start=True, stop=True),
                       reads=(PW, Z[gi]), writes=(BANK[bz],))
                    pswc = o_psw + j_i * 4 + gi
                    npc = o_np + j_i * 4 + gi
                    op("act", lambda e, bs_=bs_, pswc=pswc: e.activation(out=tmp[2][:, 0:n], in_=banks[bs_][:, 0:n], func=AF.Identity, scale=der[:, pswc:pswc + 1]),
                       reads=(BANK[bs_], DER), writes=(TMP[2],))
                    if at_start and lo > 0:
                        op("dve", lambda e, gi=gi, lo=lo: e.tensor_tensor(out=tmp[2][:, 0:lo], in0=tmp[2][:, 0:lo], in1=pcorr[:, gi * 16:gi * 16 + lo], op=ALU.mult),
                           reads=(TMP[2], CONST), writes=(TMP[2],))
                    if at_end and hi > 0:
                        op("dve", lambda e, gi=gi, hi=hi: e.tensor_tensor(out=tmp[2][:, n - hi:n], in0=tmp[2][:, n - hi:n],
                                                                      in1=pcorr[:, gi * 16 + 16 - hi:gi * 16 + 16], op=ALU.mult),
                           reads=(TMP[2], CONST), writes=(TMP[2],))
                    op("dve", lambda e, gi=gi, bz=bz, npc=npc: e.scalar_tensor_tensor(out=hc[:, 4 + gi, c0:c0 + n], in0=banks[bz][:, 0:n], scalar=der[:, npc:npc + 1],
                                                                               in1=tmp[2][:, 0:n], op0=ALU.mult, op1=ALU.add),
                       reads=(BANK[bz], DER, TMP[2]), writes=(HC,))
                    bfree(bs_); bfree(bz)

            def ln_seg(seg, hc, HC):
                s_, t0_, n, t, _ = seg
                c0 = t0_ - t * TS
                bm = balloc(); bq = balloc()
                for c in range(4):
                    op("act", lambda e, c=c: e.activation(out=sq[0][:, 0:n], in_=cvt[c][:, 0:n], func=AF.Identity), reads=(CVT[c],), writes=(SQ[0],))
                    op("pe", lambda e, c=c: e.matmul(banks[bm][:, 0:n], lhsT=ones1[:, :], rhs=sq[0][:, 0:n], start=(c == 0), stop=(c == 3), skip_group_check=True),
                       reads=(SQ[0], CONST), writes=(BANK[bm],))
                    op("act", lambda e, c=c: e.activation(out=sq[1][:, 0:n], in_=cvt[c][:, 0:n], func=AF.Square), reads=(CVT[c],), writes=(SQ[1],))
                    op("pe", lambda e, c=c: e.matmul(banks[bq][:, 0:n], lhsT=ones1[:, :], rhs=sq[1][:, 0:n], start=(c == 0), stop=(c == 3), skip_group_check=True),
                       reads=(SQ[1], CONST), writes=(BANK[bq],))
                op("act", lambda e: e.activation(out=tmp[0][:, 0:n], in_=banks[bm][:, 0:n], func=AF.Identity, scale=1.0 / 512.0), reads=(BANK[bm],), writes=(TMP[0],))
                op("dve", lambda e: e.tensor_tensor(out=tmp[1][:, 0:n], in0=tmp[0][:, 0:n], in1=tmp[0][:, 0:n], op=ALU.mult), reads=(TMP[0],), writes=(TMP[1],))
                op("dve", lambda e: e.scalar_tensor_tensor(out=tmp[1][:, 0:n], in0=banks[bq][:, 0:n], scalar=1.0 / 512.0, in1=tmp[1][:, 0:n],
                                                           op0=ALU.mult, op1=ALU.subtract), reads=(BANK[bq], TMP[1]), writes=(TMP[1],))
                op("act", lambda e: e.activation(out=rstd[:, 0:n], in_=tmp[1][:, 0:n], func=AF.Ln, bias=epst[:, 0:1], scale=1.0), reads=(TMP[1], CONST), writes=(RSTD,))
                op("act", lambda e: e.activation(out=rstd[:, 0:n], in_=rstd[:, 0:n], func=AF.Exp, scale=-0.5), reads=(RSTD,), writes=(RSTD,))
                bfree(bm); bfree(bq)
                for c in range(4):
                    op("dve", lambda e, c=c: e.tensor_tensor(out=cvt[c][:, 0:n], in0=cvt[c][:, 0:n], in1=tmp[0][:, 0:n], op=ALU.subtract),
                       reads=(CVT[c], TMP[0]), writes=(CVT[c],))
                    op("dve", lambda e, c=c: e.tensor_tensor(out=cvt[c][:, 0:n], in0=cvt[c][:, 0:n], in1=rstd[:, 0:n], op=ALU.mult),
                       reads=(CVT[c], RSTD), writes=(CVT[c],))
                    op("act", lambda e, c=c: e.activation(out=hc[:, c, c0:c0 + n], in_=cvt[c][:, 0:n], func=AF.Silu,
                                                          bias=params[:, olnb + c:olnb + c + 1], scale=params[:, olng + c:olng + c + 1]),
                       reads=(CVT[c], P_), writes=(HC,))

            def outproj(t, hc, HC):
                for j in range(8):
                    wt, WR = (wo0, WO0) if j < 4 else (wo1, WO1)
                    jj = j % 4
                    b0 = balloc()

                    def pe_o(e, b0=b0, wt=wt, jj=jj):
                        ins = None
                        for kc in range(8):
                            ins = e.matmul(banks[b0][:, :], lhsT=wt[:, kc * 512 + jj * 128:kc * 512 + jj * 128 + 128], rhs=hc[:, kc, :],
                                           start=(kc == 0), stop=(kc == 7))
                        return ins
                    op("pe", pe_o, reads=(WR, HC), writes=(BANK[b0],))
                    resid_add(l, 0, j, t, b0)
                    bfree(b0)

            prebuild()
            cb = conv_taps(segs[0])
            prebuild()
            for i, seg in enumerate(segs):
                t = seg[3]
                hc, HC = hcs[t % 2], HCs[t % 2]
                evac(seg, cb)
                if i + 1 < len(segs):
                    cb = conv_taps(segs[i + 1])
                pool_seg(seg, hc, HC)
                ln_seg(seg, hc, HC)
                if i + 2 < len(segs):
                    prebuild()
                if seg[4]:
                    outproj(t, hc, HC)
            for i in free_slots:
                rst["held"][i] = False
            rst["limit"] = None
            ring_done(so0)
            ring_done(so1)
            trk.barrier()

        nsub = 2 * DEPTH if DEBUG_STOP < 0 else DEBUG_STOP
        for g in range(6):
            adaln_group(0, g)
        adaln_finish(0, halves=(0,))
        ada0_rest = [(lambda g=g: adaln_group(0, g)) for g in range(6, 12)] + [lambda: adaln_finish(0, halves=(1,))]
        sub = 0
        for l in range(DEPTH):
            if sub >= nsub:
                break
            if l % 2 == 0:
                attn_group(l, 0)
                attn_group(l, 1, extra=(ada0_rest if l == 0 else None))
            else:
                conv_layer(l)
            sub += 1
            if sub >= nsub:
                break
            nxt = [(l + 1, g) for g in range(12)] if l + 1 < DEPTH else []
            mlp(l, nxt)
            trk.barrier()
            sub += 1

        for t in range(NT):
            tok = slice(t * TS, (t + 1) * TS)
            if DEBUG_STOP >= 0:
                for k in range(8):
                    trk.dma("sp", out_sems.next(), o_yT[:, k, tok], x[:, k, tok], reads=(X[k][t],))
                continue
            b = balloc()
            for k in range(8):
                op("act", lambda e, k=k: e.activation(out=sq[k % 2][:, :], in_=x[:, k, tok], func=AF.Square), reads=(X[k][t],), writes=(SQ[k % 2],))
                op("pe", lambda e, k=k: e.matmul(banks[b][:, :], lhsT=onesm[:, :], rhs=sq[k % 2][:, :], start=(k == 0), stop=(k == 7), skip_group_check=True),
                   reads=(SQ[k % 2], CONST), writes=(BANK[b],))
            op("act", lambda e: e.activation(out=rstd[:, :], in_=banks[b][:, :], func=AF.Ln, bias=epst[:, 0:1], scale=1.0), reads=(BANK[b], CONST), writes=(RSTD,))
            bfree(b)
            op("act", lambda e: e.activation(out=rstd[:, :], in_=rstd[:, :], func=AF.Exp, scale=-0.5), reads=(RSTD,), writes=(RSTD,))
            fo_ = PM["finalg"][0]
            for k in range(8):
                ci = k % 4
                op("dve", lambda e, k=k, ci=ci: e.scalar_tensor_tensor(out=cvt[ci][:, :], in0=x[:, k, tok], scalar=params[:, fo_ + k:fo_ + k + 1],
                                                                     in1=rstd[:, :], op0=ALU.mult, op1=ALU.mult),
                   reads=(X[k][t], RSTD, P_), writes=(CVT[ci],))
                trk.dma("sp", out_sems.next(), o_yT[:, k, tok], cvt[ci][:, :], reads=(CVT[ci],))
        trk.final_wait("sp")
    return nc, wlist


def _count_images():
    n = 0
    for l in range(DEPTH):
        n += 12
        n += 14 if l % 2 == 0 else 5
        n += 16
    return n


N_IMAGES = _count_images()


def _img_k1024(w, col_idx):
    img = np.zeros((128, 8, 512), np.float32)
    col_idx = np.asarray(col_idx)
    valid = col_idx >= 0
    sel = w[:, col_idx[valid]]
    img[:, :, np.nonzero(valid)[0]] = sel.reshape(8, 128, -1).transpose(1, 0, 2)
    return img.reshape(128, SLOT)


def _build_images(wl, inp):
    imgs = np.zeros((N_IMAGES, 128, SLOT), np.float32)
    sw64 = lambda d: (d + 32) % 64
    for n, key in enumerate(wl):
        kind = key[0]
        if kind == "wmod":
            _, l, g = key
            imgs[n] = _img_k1024(inp["w_mod"][l], np.arange(g * 512, (g + 1) * 512))
        elif kind in ("w1",):
            _, l, g = key
            imgs[n] = _img_k1024(inp["mlp_w1"][l], np.arange(g * 512, (g + 1) * 512))
        elif kind == "w2":
            _, l, g = key
            w = inp["mlp_w2"][l][g * 512:(g + 1) * 512]
            imgs[n] = w.reshape(4, 128, 1024).transpose(1, 0, 2).reshape(128, SLOT)
        elif kind == "win":
            _, e, g = key
            w = inp["attn_w_in"][e]
            if g == 0:
                idx = -np.ones(512, np.int64)
                idx[0:192] = 768 + np.arange(192)
                idx[192:320] = 960 + np.arange(128)
                idx[320:352] = 1088 + np.arange(32)
                idx[352:384] = 1088 + (np.arange(32) + 16) % 32
            elif g in (1, 2):
                idx = np.zeros(512, np.int64)
                for c in range(4):
                    for p in range(128):
                        h = c if p < 64 else 4 + c
                        d = p % 64
                        if g == 2:
                            d = sw64(d)
                        idx[c * 128 + p] = h * 64 + d
            else:
                idx = -np.ones(512, np.int64)
                for p in range(128):
                    kh, d = p // 64, p % 64
                    idx[p] = 512 + kh * 64 + d
                    idx[128 + p] = 512 + kh * 64 + sw64(d)
                    idx[256 + p] = 640 + p
            imgs[n] = _img_k1024(w, idx)
        elif kind == "wqkv":
            _, e = key
            img = np.zeros((128, SLOT), np.float32)
            wq = inp["mla_w_qb"][e]
            qcols = np.zeros((8, 128), np.int64)
            for h in range(8):
                qcols[h, 0:64] = h * 96 + np.arange(64)
                qcols[h, 64:96] = h * 96 + 64 + np.arange(32)
                qcols[h, 96:128] = h * 96 + 64 + (np.arange(32) + 16) % 32
            wqa = wq[:, qcols.reshape(-1)]
            img[:, 0:1024] = wqa[0:128]
            img[0:64, 1024:2048] = wqa[128:192]
            wkv = inp["mla_w_kvb"][e]
            for h in range(8):
                img[:, 2048 + h * 64:2048 + (h + 1) * 64] = wkv[:, h * 128:h * 128 + 64]
                img[:, 2560 + h * 64:2560 + (h + 1) * 64] = wkv[:, h * 128 + 64:h * 128 + 128]
            imgs[n] = img
        elif kind == "wout":
            _, e, g = key
            w = inp["attn_w_out"][e]
            rows = np.zeros(1024, np.int64)
            for c in range(4):
                for p in range(128):
                    h = c if p < 64 else 4 + c
                    rows[c * 128 + p] = h * 64 + p % 64
            for c in range(4):
                for p in range(128):
                    h = 2 * c + (p // 64)
                    rows[512 + c * 128 + p] = 512 + h * 64 + p % 64
            wp = w[rows]
            imgs[n] = _img_k1024(wp, np.arange(g * 512, (g + 1) * 512))
        elif kind == "cin":
            _, j, g = key
            imgs[n] = _img_k1024(inp["conv_w_in"][j], np.arange(g * 512, (g + 1) * 512))
        elif kind == "cout":
            _, j, g = key
            imgs[n] = _img_k1024(inp["conv_w_out"][j], np.arange(g * 512, (g + 1) * 512))
        else:
            raise KeyError(key)
    return imgs


_CACHE = {}


def kernel(**inputs):
    inp = {k: np.asarray(v) for k, v in inputs.items()}
    if "prog" not in _CACHE:
        _CACHE["prog"] = build_program()
    nc, wl = _CACHE["prog"]
    assert len(wl) == N_IMAGES or DEBUG_STOP >= 0, (len(wl), N_IMAGES)
    consts = _consts()
    imgs = _build_images(wl, inp)

    def fm(v):
        return np.ascontiguousarray(v.reshape(8, 128).T)

    poolw = np.ascontiguousarray(inp["pool_w"].transpose(0, 2, 1, 3).reshape(2, 128, 512))
    dwp = np.zeros((128, 2, 4, 31), np.float32)
    for j in range(2):
        for c in range(4):
            dwp[:, j, c, :] = inp["conv_dw"][j][:, c * 128:(c + 1) * 128].T
    in_maps = []
    for i in range(8):
        toks = np.concatenate([inp["x_prompt"][2 * i], inp["x_prompt"][2 * i + 1], inp["x_sample"][i]], axis=0)
        xT = np.ascontiguousarray(toks.reshape(NTOK, 8, 128).transpose(2, 1, 0))
        P = np.zeros((128, PM["_n"]), np.float32)

        def put(name, arr):
            o, n = PM[name]
            P[:, o:o + n] = arr.reshape(128, n)
        put("bmod", np.stack([inp["b_mod"][l].reshape(48, 128).T for l in range(4)], 1))
        put("normg", np.stack([np.stack([fm(inp["norm_g"][l, w]) for w in range(2)], 1) for l in range(4)], 1))
        put("finalg", fm(inp["final_g"]))
        put("cT", np.stack([fm(inp["c_ctx"]), fm(inp["c"][i])], 2))
        qn = np.zeros((128, 2, 2), np.float32)
        for e in range(2):
            qn[:, e, 0] = inp["mla_q_norm"][e, 0:128]
            qn[0:64, e, 1] = inp["mla_q_norm"][e, 128:192]
        put("qnorm", qn)
        put("kvnorm", np.stack([inp["mla_kv_norm"][e] for e in range(2)], 1))
        put("sink", np.broadcast_to(inp["attn_sink"].reshape(1, 16), (128, 16)))
        for nm, src in (("bdw", "conv_dw_b"), ("lng", "conv_ln_g"), ("lnb", "conv_ln_b"), ("pscale", "pool_scale")):
            put(nm, np.stack([inp[src][j].reshape(4, 128).T for j in range(2)], 1))
        put("dw", dwp)
        m = {
            "xT": xT, "params": P, "wstream": imgs,
            "ropeA": consts["ropeA"], "ropeB": consts["ropeB"], "maskb": consts["maskb"], "ident": consts["ident"],
            "pcorr": consts["pcorr"],
            "ckT": np.ascontiguousarray(inp["cache_win_k"][i].reshape(2, NCTX, 128).transpose(0, 2, 1)),
            "cv": np.ascontiguousarray(inp["cache_win_v"][i].reshape(2, 2, 128, 128)),
            "cckvT": np.ascontiguousarray(inp["cache_mla_ckv"][i].transpose(0, 2, 1)),
            "ckrT": np.ascontiguousarray(inp["cache_mla_krope"][i].transpose(0, 2, 1)),
            "poolw": poolw,
        }
        in_maps.append(m)
    res = run_bass_kernel_spmd(nc, in_maps, core_ids=list(range(8)))
    R = res.results
    y_prompt = np.zeros((16, 256, D), np.float32)
    y_sample = np.zeros((8, 2048, D), np.float32)
    nk = np.zeros((16, 2, 256, 2, 64), np.float32)
    nv = np.zeros((16, 2, 256, 2, 64), np.float32)
    nckv = np.zeros((16, 2, 256, 128), np.float32)
    nkr = np.zeros((16, 2, 256, 32), np.float32)
    for i in range(8):
        r = R[i]
        y = np.asarray(r["yT"]).transpose(2, 1, 0).reshape(NTOK, D)
        y_prompt[2 * i] = y[0:256]
        y_prompt[2 * i + 1] = y[256:512]
        y_sample[i] = y[512:]
        kT = np.asarray(r["okT"])
        v = np.asarray(r["ov"])
        ck = np.asarray(r["ockvT"])
        kr = np.asarray(r["okrT"])
        for s in range(2):
            b = 2 * i + s
            for e in range(2):
                nk[b, e] = kT[e][:, s * 256:(s + 1) * 256].T.reshape(256, 2, 64)
                nv[b, e] = v[e][s * 256:(s + 1) * 256].reshape(256, 2, 64)
                nckv[b, e] = ck[e][:, s * 256:(s + 1) * 256].T
                nkr[b, e] = kr[e][:, s * 256:(s + 1) * 256].T
    return (y_prompt, y_sample, nk, nv, nckv, nkr)
```
